# Optimizing a Trainium2 kernel written in Bass

```python
import jax
import jax.numpy as jnp
from jax import lax
import numpy as np

D_MODEL = 1024
BATCH = 8
SEQ = 2048
DEPTH = 4

GRID_W = 64
CTX_LEN = 256
N_EVEN = (DEPTH + 1) // 2
N_ODD = DEPTH // 2
D_FF = 4 * D_MODEL
N_MOD = 6
EPS = 1e-6
F32 = jnp.float32

MLA_HEADS = 8
MLA_NOPE = 64
MLA_ROPE = 32
MLA_V = 64
MLA_QK = MLA_NOPE + MLA_ROPE
MLA_Q_RANK = 3 * D_MODEL // 8
MLA_KV_RANK = D_MODEL // 4
ROPE_THETA = 10000.0
Q_BLOCK = 128

GLA_HEADS = 4
GLA_DK = 64
GLA_DV = 128
GLA_GATE_RANK = 16
GLA_GATE_NORM = 16.0
GLA_CHUNK = 64

GDN_HEADS = 8
GDN_DK = 64
GDN_DV = 128
GDN_CONV = 3
GDN_CHUNK = 64

EVEN_SPLITS = (MLA_Q_RANK, MLA_KV_RANK, MLA_ROPE,
               GLA_HEADS * GLA_DK, GLA_HEADS * GLA_DK, GLA_HEADS * GLA_DV, GLA_HEADS * GLA_DV,
               2 * GLA_GATE_RANK)
EVEN_IN = 2240
MIX_WIDTH = MLA_HEADS * MLA_V + GLA_HEADS * GLA_DV
GDN_QKV = 2 * GDN_HEADS * GDN_DK + GDN_HEADS * GDN_DV
ODD_SPLITS = (GDN_QKV, GDN_HEADS * GDN_DV, 2 * GDN_HEADS, 2 * GDN_HEADS)
ODD_IN = 3104
GDN_WIDTH = GDN_HEADS * GDN_DV

kernel_name = "hybrid_mla_gla_gdn_dit_prefix"


def rms_norm(x, w):
    xf = x.astype(F32)
    y = xf * lax.rsqrt(jnp.mean(xf * xf, axis=-1, keepdims=True) + EPS)
    return (y * w.astype(F32)).astype(x.dtype)


def l2norm(t):
    tf = t.astype(F32)
    return (tf * lax.rsqrt(jnp.sum(tf * tf, axis=-1, keepdims=True) + EPS)).astype(t.dtype)


def split_cols(y, sizes):
    out, start = [], 0
    for s in sizes:
        out.append(y[..., start:start + s])
        start += s
    return out


def modulate(h, shift, scale):
    return h * (1.0 + scale) + shift


def flip_seq(ts):
    return tuple(jnp.flip(t, axis=2) for t in ts)


def axial_rope_tables(n_tokens):
    rows = n_tokens // GRID_W
    row = jnp.repeat(jnp.arange(rows, dtype=F32), GRID_W)
    col = jnp.tile(jnp.arange(GRID_W, dtype=F32), rows)
    axis_dim = MLA_ROPE // 2
    inv_freq = ROPE_THETA ** (-jnp.arange(0, axis_dim, 2, dtype=F32) / axis_dim)
    ang = jnp.concatenate([row[:, None] * inv_freq, col[:, None] * inv_freq], axis=-1)
    return jnp.cos(ang), jnp.sin(ang)


def apply_axial_rope(x, cos, sin):
    shp = x.shape
    nf = MLA_ROPE // 4
    xf = x.astype(F32).reshape(shp[:-1] + (2, 2, nf))
    x1, x2 = xf[..., 0, :], xf[..., 1, :]
    cos = cos.reshape(-1, 2, nf)
    sin = sin.reshape(-1, 2, nf)
    out = jnp.stack([x1 * cos - x2 * sin, x2 * cos + x1 * sin], axis=-2)
    return out.reshape(shp).astype(x.dtype)


def mla_attend(q, k, v):
    B, H, Lq, D = q.shape
    scale = MLA_QK ** -0.5
    nb = Lq // Q_BLOCK
    qb = jnp.moveaxis(q.reshape(B, H, nb, Q_BLOCK, D), 2, 0)

    def one(qblk):
        s = jnp.einsum('bhqd,bhkd->bhqk', qblk, k).astype(F32) * scale
        p = jax.nn.softmax(s, axis=-1).astype(v.dtype)
        return jnp.einsum('bhqk,bhkv->bhqv', p, v)

    o = lax.map(one, qb)
    return jnp.moveaxis(o, 0, 2).reshape(B, H, Lq, v.shape[-1])


def gla_chunked(q, k, v, logg, s0):
    out_dtype = v.dtype
    q, k, v, logg = (t.astype(F32) for t in (q, k, v, logg))
    B, H, L, DK = q.shape
    DV = v.shape[-1]
    C = GLA_CHUNK
    n = L // C
    q = q.reshape(B, H, n, C, DK) * DK ** -0.5
    k = k.reshape(B, H, n, C, DK)
    v = v.reshape(B, H, n, C, DV)
    b = jnp.cumsum(logg.reshape(B, H, n, C, DK), axis=3)
    q_dec = q * jnp.exp(b)
    k_inv = k * jnp.exp(-b)
    k_end = k * jnp.exp(b[..., -1:, :] - b)
    g_end = jnp.exp(b[..., -1, :])
    incl = jnp.tril(jnp.ones((C, C), bool))
    a = jnp.where(incl, jnp.einsum('bhnid,bhnjd->bhnij', q_dec, k_inv), 0.0)
    o_intra = jnp.einsum('bhnij,bhnjv->bhniv', a, v)

    def step(s, xs):
        qd_c, ke_c, v_c, ge_c = xs
        o = jnp.einsum('bhcd,bhdv->bhcv', qd_c, s)
        s = s * ge_c[..., None] + jnp.einsum('bhcd,bhcv->bhdv', ke_c, v_c)
        return s, o

    xs = tuple(jnp.moveaxis(t, 2, 0) for t in (q_dec, k_end, v, g_end))
    s_fin, o_inter = lax.scan(step, s0, xs)
    o = o_intra + jnp.moveaxis(o_inter, 0, 2)
    return o.reshape(B, H, L, DV).astype(out_dtype), s_fin


def gdn_chunked(q, k, v, log_a, beta, s0):
    out_dtype = v.dtype
    q, k, v, log_a, beta = (t.astype(F32) for t in (q, k, v, log_a, beta))
    B, H, L, DK = q.shape
    DV = v.shape[-1]
    C = GDN_CHUNK
    n = L // C
    q = q.reshape(B, H, n, C, DK) * DK ** -0.5
    k = k.reshape(B, H, n, C, DK)
    v = v.reshape(B, H, n, C, DV)
    beta = beta.reshape(B, H, n, C)[..., None]
    g = jnp.cumsum(log_a.reshape(B, H, n, C), axis=-1)
    incl = jnp.tril(jnp.ones((C, C), bool))
    strict = jnp.tril(jnp.ones((C, C), bool), -1)
    diff = g[..., :, None] - g[..., None, :]
    decay = jnp.where(incl, jnp.exp(jnp.where(incl, diff, 0.0)), 0.0)
    kb = k * beta
    a_mat = jnp.where(strict, jnp.einsum('bhnid,bhnjd->bhnij', kb, k) * decay, 0.0) + jnp.eye(C, dtype=F32)
    rhs = jnp.concatenate([v * beta, kb * jnp.exp(g)[..., None]], axis=-1)
    sol = lax.linalg.triangular_solve(a_mat, rhs, left_side=True, lower=True)
    u, w = sol[..., :DV], sol[..., DV:]
    qk = jnp.einsum('bhnid,bhnjd->bhnij', q, k) * decay
    q_dec = q * jnp.exp(g)[..., None]
    k_end = k * jnp.exp(g[..., -1:] - g)[..., None]
    g_end = jnp.exp(g[..., -1])

    def step(s, xs):
        qk_c, qd_c, w_c, u_c, ke_c, ge_c = xs
        v_new = u_c - jnp.einsum('bhcd,bhdv->bhcv', w_c, s)
        o = jnp.einsum('bhcd,bhdv->bhcv', qd_c, s) + jnp.einsum('bhcs,bhsv->bhcv', qk_c, v_new)
        s = s * ge_c[..., None, None] + jnp.einsum('bhcd,bhcv->bhdv', ke_c, v_new)
        return s, o

    xs = tuple(jnp.moveaxis(t, 2, 0) for t in (qk, q_dec, w, u, k_end, g_end))
    s_fin, o = lax.scan(step, s0, xs)
    o = jnp.moveaxis(o, 0, 2).reshape(B, H, L, DV)
    return o.astype(out_dtype), s_fin


def bidir_scan(scan_fn, ctx_f, ctx_b, lat_f, lat_b, s0):
    o_cf, s_f = scan_fn(*ctx_f, s0)
    o_cb, s_b = scan_fn(*flip_seq(ctx_b), s0)
    o_lf, _ = scan_fn(*lat_f, s_f)
    o_lb, _ = scan_fn(*flip_seq(lat_b), s_b)
    return o_lf + jnp.flip(o_lb, axis=2), (o_cf, o_cb)


def even_project(y, q_norm, w_uq, kv_norm, w_ukv, gate_up, gate_bias, rope):
    cq, ckv, kpe, gq, gk, gv, gg, glow = split_cols(y, EVEN_SPLITS)
    B, L, _ = y.shape
    q = (rms_norm(cq, q_norm) @ w_uq).reshape(B, L, MLA_HEADS, MLA_QK).transpose(0, 2, 1, 3)
    q_nope, q_pe = q[..., :MLA_NOPE], q[..., MLA_NOPE:]
    kv = (rms_norm(ckv, kv_norm) @ w_ukv).reshape(B, L, MLA_HEADS, MLA_NOPE + MLA_V).transpose(0, 2, 1, 3)
    k_nope, mv = kv[..., :MLA_NOPE], kv[..., MLA_NOPE:]
    if rope is not None:
        q_pe = apply_axial_rope(q_pe, *rope)
        kpe = apply_axial_rope(kpe, *rope)
    mq = jnp.concatenate([q_nope, q_pe], axis=-1)
    mk = jnp.concatenate([k_nope, jnp.broadcast_to(kpe[:, None], (B, MLA_HEADS, L, MLA_ROPE))], axis=-1)
    glow = glow.reshape(B, L, 2, GLA_GATE_RANK)
    logit = jnp.einsum('blzr,zrk->zblk', glow, gate_up) + gate_bias[:, None, None, :]
    logg = jax.nn.log_sigmoid(logit.astype(F32)) / GLA_GATE_NORM
    logg = logg.reshape(2, B, L, GLA_HEADS, GLA_DK).transpose(0, 1, 3, 2, 4)
    heads = lambda t, d: t.reshape(B, L, -1, d).transpose(0, 2, 1, 3)
    return (mq, mk, mv, heads(gq, GLA_DK), heads(gk, GLA_DK), heads(gv, GLA_DV), gg, logg[0], logg[1])


def even_merge(att, gla, gg, o_norm, w_out):
    B, _, L, _ = att.shape
    att = att.transpose(0, 2, 1, 3).reshape(B, L, MLA_HEADS * MLA_V)
    gla = rms_norm(gla.transpose(0, 2, 1, 3), o_norm) * jax.nn.silu(gg.reshape(B, L, GLA_HEADS, GLA_DV))
    return jnp.concatenate([att, gla.reshape(B, L, GLA_HEADS * GLA_DV)], axis=-1) @ w_out


def even_mixer(a_lat, a_ctx, w_in, q_norm, w_uq, kv_norm, w_ukv, gate_up, gate_bias, o_norm, w_out, rope, need_ctx):
    mq_l, mk_l, mv_l, gq_l, gk_l, gv_l, gg_l, lf_l, lb_l = even_project(
        a_lat @ w_in, q_norm, w_uq, kv_norm, w_ukv, gate_up, gate_bias, rope)
    mq_c, mk_c, mv_c, gq_c, gk_c, gv_c, gg_c, lf_c, lb_c = even_project(
        a_ctx @ w_in, q_norm, w_uq, kv_norm, w_ukv, gate_up, gate_bias, None)
    att_l = mla_attend(mq_l, jnp.concatenate([mk_l, mk_c], axis=2), jnp.concatenate([mv_l, mv_c], axis=2))
    B = a_lat.shape[0]
    s0 = jnp.zeros((B, GLA_HEADS, GLA_DK, GLA_DV), F32)
    gla_l, (o_cf, o_cb) = bidir_scan(gla_chunked, (gq_c, gk_c, gv_c, lf_c), (gq_c, gk_c, gv_c, lb_c),
                                     (gq_l, gk_l, gv_l, lf_l), (gq_l, gk_l, gv_l, lb_l), s0)
    o_lat = even_merge(att_l, gla_l, gg_l, o_norm, w_out)
    o_ctx = None
    if need_ctx:
        att_c = mla_attend(mq_c, mk_c, mv_c)
        o_ctx = even_merge(att_c, o_cf + jnp.flip(o_cb, axis=2), gg_c, o_norm, w_out)
    return o_lat, o_ctx


def short_conv(x, w):
    ch = x.shape[-1]
    y = lax.conv_general_dilated(x, w[:, None, :], window_strides=(1,),
                                 padding=[((GDN_CONV - 1) // 2, GDN_CONV // 2)],
                                 dimension_numbers=('NWC', 'WIO', 'NWC'), feature_group_count=ch)
    return jax.nn.silu(y)


def odd_project(a, w_in, conv_w, a_log, dt_bias):
    B, L, _ = a.shape
    qkv, g, a_in, b_in = split_cols(a @ w_in, ODD_SPLITS)
    q, k, v = split_cols(short_conv(qkv, conv_w), (GDN_HEADS * GDN_DK, GDN_HEADS * GDN_DK, GDN_HEADS * GDN_DV))
    heads = lambda t, d: t.reshape(B, L, GDN_HEADS, d).transpose(0, 2, 1, 3)
    q = l2norm(heads(q, GDN_DK))
    k = l2norm(heads(k, GDN_DK))
    v = heads(v, GDN_DV)
    a_in = a_in.reshape(B, L, 2, GDN_HEADS).astype(F32)
    log_a = -jnp.exp(a_log.astype(F32)) * jax.nn.softplus(a_in + dt_bias.astype(F32))
    beta = jax.nn.sigmoid(b_in.reshape(B, L, 2, GDN_HEADS).astype(F32))
    return q, k, v, g, log_a.transpose(2, 0, 3, 1), beta.transpose(2, 0, 3, 1)


def odd_merge(o, g, o_norm, w_out):
    B, _, L, _ = o.shape
    o = rms_norm(o.transpose(0, 2, 1, 3), o_norm) * jax.nn.silu(g.reshape(B, L, GDN_HEADS, GDN_DV))
    return o.reshape(B, L, GDN_WIDTH) @ w_out


def odd_mixer(a_lat, a_ctx, w_in, conv_w, a_log, dt_bias, o_norm, w_out, need_ctx):
    lq, lk, lv, lg, lla, lbe = odd_project(a_lat, w_in, conv_w, a_log, dt_bias)
    cq, ck, cv, cg, cla, cbe = odd_project(a_ctx, w_in, conv_w, a_log, dt_bias)
    B = a_lat.shape[0]
    s0 = jnp.zeros((B, GDN_HEADS, GDN_DK, GDN_DV), F32)
    o_l, (o_cf, o_cb) = bidir_scan(gdn_chunked, (cq, ck, cv, cla[0], cbe[0]), (cq, ck, cv, cla[1], cbe[1]),
                                   (lq, lk, lv, lla[0], lbe[0]), (lq, lk, lv, lla[1], lbe[1]), s0)
    o_lat = odd_merge(o_l, lg, o_norm, w_out)
    o_ctx = None
    if need_ctx:
        o_ctx = odd_merge(o_cf + jnp.flip(o_cb, axis=2), cg, o_norm, w_out)
    return o_lat, o_ctx


def sq_relu_mlp(h, w1, w2):
    return jnp.square(jax.nn.relu(h @ w1)) @ w2


def setup_inputs(seed: int = 0) -> dict:
    key = jax.random.key(seed)
    ks = iter(jax.random.split(key, 40))
    nrm = lambda shape, scale: jax.random.normal(next(ks), shape, F32) * scale
    gain = lambda shape: 1.0 + 0.05 * jax.random.normal(next(ks), shape, F32)
    d = D_MODEL
    dt = jnp.exp(jax.random.uniform(next(ks), (N_ODD, 2, GDN_HEADS), F32, np.log(1e-3), np.log(1e-1)))
    return {
        "x": nrm((BATCH, SEQ, d), 1.0),
        "c": nrm((BATCH, d), 1.0),
        "ctx": nrm((BATCH, CTX_LEN, d), 1.0),
        "c_ctx": nrm((d,), 1.0),
        "ada_w": nrm((DEPTH, d, N_MOD * d), 0.5 * d ** -0.5),
        "ada_b": nrm((DEPTH, N_MOD * d), 0.02),
        "norm1_w": gain((DEPTH, d)),
        "norm2_w": gain((DEPTH, d)),
        "mlp_w1": nrm((DEPTH, d, D_FF), d ** -0.5),
        "mlp_w2": nrm((DEPTH, D_FF, d), D_FF ** -0.5),
        "even_w_in": nrm((N_EVEN, d, EVEN_IN), d ** -0.5),
        "mla_q_norm": gain((N_EVEN, MLA_Q_RANK)),
        "mla_w_uq": nrm((N_EVEN, MLA_Q_RANK, MLA_HEADS * MLA_QK), MLA_Q_RANK ** -0.5),
        "mla_kv_norm": gain((N_EVEN, MLA_KV_RANK)),
        "mla_w_ukv": nrm((N_EVEN, MLA_KV_RANK, MLA_HEADS * (MLA_NOPE + MLA_V)), MLA_KV_RANK ** -0.5),
        "gla_gate_up": nrm((N_EVEN, 2, GLA_GATE_RANK, GLA_HEADS * GLA_DK), GLA_GATE_RANK ** -0.5),
        "gla_gate_bias": nrm((N_EVEN, 2, GLA_HEADS * GLA_DK), 0.1),
        "gla_o_norm": gain((N_EVEN, GLA_DV)),
        "even_w_out": nrm((N_EVEN, MIX_WIDTH, d), MIX_WIDTH ** -0.5),
        "gdn_w_in": nrm((N_ODD, d, ODD_IN), d ** -0.5),
        "gdn_conv_w": nrm((N_ODD, GDN_CONV, GDN_QKV), GDN_CONV ** -0.5),
        "gdn_a_log": jnp.log(jax.random.uniform(next(ks), (N_ODD, 2, GDN_HEADS), F32, 1.0, 16.0)),
        "gdn_dt_bias": dt + jnp.log(-jnp.expm1(-dt)),
        "gdn_o_norm": gain((N_ODD, GDN_DV)),
        "gdn_w_out": nrm((N_ODD, GDN_WIDTH, d), GDN_WIDTH ** -0.5),
        "final_norm": gain((d,)),
    }


def reference(x, c, ctx, c_ctx, ada_w, ada_b, norm1_w, norm2_w, mlp_w1, mlp_w2,
              even_w_in, mla_q_norm, mla_w_uq, mla_kv_norm, mla_w_ukv, gla_gate_up, gla_gate_bias,
              gla_o_norm, even_w_out, gdn_w_in, gdn_conv_w, gdn_a_log, gdn_dt_bias, gdn_o_norm,
              gdn_w_out, final_norm):
    rope = axial_rope_tables(x.shape[1])
    h_lat, h_ctx = x, ctx
    for layer in range(DEPTH):
        need_ctx = layer < DEPTH - 1
        ml = [m[:, None, :] for m in jnp.split(jax.nn.silu(c) @ ada_w[layer] + ada_b[layer], N_MOD, axis=-1)]
        mc = jnp.split(jax.nn.silu(c_ctx) @ ada_w[layer] + ada_b[layer], N_MOD, axis=-1)
        a_lat = modulate(rms_norm(h_lat, norm1_w[layer]), ml[0], ml[1])
        a_ctx = modulate(rms_norm(h_ctx, norm1_w[layer]), mc[0], mc[1])
        i = layer // 2
        if layer % 2 == 0:
            o_lat, o_ctx = even_mixer(a_lat, a_ctx, even_w_in[i], mla_q_norm[i], mla_w_uq[i], mla_kv_norm[i],
                                      mla_w_ukv[i], gla_gate_up[i], gla_gate_bias[i], gla_o_norm[i],
                                      even_w_out[i], rope, need_ctx)
        else:
            o_lat, o_ctx = odd_mixer(a_lat, a_ctx, gdn_w_in[i], gdn_conv_w[i], gdn_a_log[i], gdn_dt_bias[i],
                                     gdn_o_norm[i], gdn_w_out[i], need_ctx)
        h_lat = h_lat + ml[2] * o_lat
        h_lat = h_lat + ml[5] * sq_relu_mlp(modulate(rms_norm(h_lat, norm2_w[layer]), ml[3], ml[4]),
                                            mlp_w1[layer], mlp_w2[layer])
        if need_ctx:
            h_ctx = h_ctx + mc[2] * o_ctx
            h_ctx = h_ctx + mc[5] * sq_relu_mlp(modulate(rms_norm(h_ctx, norm2_w[layer]), mc[3], mc[4]),
                                                mlp_w1[layer], mlp_w2[layer])
    return rms_norm(h_lat, final_norm)
```

```python
import contextlib
import numpy as np
import concourse.bass as bass
import concourse.mybir as mybir
from concourse.bass_utils import run_bass_kernel_spmd

F32 = mybir.dt.float32
BF16 = mybir.dt.bfloat16
AF = mybir.ActivationFunctionType
ALU = mybir.AluOpType

SAME_ENGINE_SYNC = True
SEM_ROTATE = 30000
DBG_SKIP = set()
USE_AB32 = False
DBG_STOP = None


class _Stop(Exception):
    pass


def _chk(name):
    if DBG_STOP == name:
        raise _Stop()


class _Eng:
    def __init__(self, name, handle, is_dma_only=False):
        self.name = name
        self.h = handle
        self.sem = None
        self.count = 0
        self.waited = {}
        self.pending = False


class Prog:
    def __init__(self, nc, es, n_dma_sems=40):
        self.nc = nc
        self.es = es
        self.engs = {
            "pe": _Eng("pe", nc.tensor),
            "act": _Eng("act", nc.scalar),
            "dve": _Eng("dve", nc.vector),
            "pool": _Eng("pool", nc.gpsimd),
            "sp": _Eng("sp", nc.sync),
        }
        self.semid = 0
        for e in self.engs.values():
            e.sem = self._newsem()
        self.dma_sems = [[self._newsem(), 0] for _ in range(n_dma_sems)]
        self.dma_rr = 0
        self.recs = {}
        self.n_inst = 0
        self.n_wait = 0
        self.out_tokens = []

    def _newsem(self):
        self.semid += 1
        return self.es.enter_context(self.nc.semaphore("s%d" % self.semid))

    @staticmethod
    def _box(ap):
        t = ap.tensor
        name = t.name
        shape = tuple(t.shape)
        pat = ap.ap
        off = ap.offset
        sp = str(ap.space) if not isinstance(ap.space, str) else ap.space
        if "DRAM" in sp.upper() or "HBM" in sp.upper() or "Dram" in sp:
            W = shape[-1]
            r0, c0 = off // W, off % W
            r1, c1 = r0, c0
            ok = True
            for (s, c) in pat:
                s = abs(s)
                if c <= 1:
                    continue
                if s % W == 0:
                    r1 += (c - 1) * (s // W)
                elif s * (c - 1) < W:
                    c1 += (c - 1) * s
                else:
                    ok = False
            if (not ok) or c1 >= W:
                ext = 1
                for (s, c) in pat:
                    ext += (c - 1) * abs(s)
                return (name, 0, 1 << 30, 0, 1 << 30) if True else None
            return (name, r0, r1 + 1, c0, c1 + 1)
        fsz = 1
        for s in shape[1:]:
            fsz *= s
        pstep, pcnt = pat[0]
        if pstep != fsz and pcnt > 1:
            return (name, 0, 128, 0, fsz)
        p0 = off // fsz
        f0 = off % fsz
        ext = 1
        for (s, c) in pat[1:]:
            ext += (c - 1) * abs(s)
        if name.startswith("ps"):
            return (name, (p0 // 32) * 32, ((p0 + pcnt + 31) // 32) * 32, 0, fsz)
        return (name, p0, p0 + pcnt, f0, f0 + ext)

    def _deps(self, eng, reads, writes, is_dma):
        deps = {}
        boxes_r = [self._box(a) for a in reads]
        boxes_w = [self._box(a) for a in writes]
        for kind, boxes in (("r", boxes_r), ("w", boxes_w)):
            for b in boxes:
                lst = self.recs.get(b[0])
                if not lst:
                    continue
                for rec in lst:
                    if kind == "r" and rec[4] == "r":
                        continue
                    if rec[0] >= b[2] or b[1] >= rec[1] or rec[2] >= b[4] or b[3] >= rec[3]:
                        continue
                    if (not is_dma) and (not rec[8]) and rec[7] == eng.name and (eng.name == "pe" or not SAME_ENGINE_SYNC):
                        continue
                    s, v = rec[5], rec[6]
                    k = id(s)
                    if k not in deps or deps[k][1] < v:
                        deps[k] = (s, v)
        return deps, boxes_r, boxes_w

    def _record(self, eng, boxes_r, boxes_w, sem, val, is_dma):
        for b in boxes_w:
            lst = self.recs.setdefault(b[0], [])
            lst[:] = [r for r in lst if not (b[1] <= r[0] and r[1] <= b[2] and b[3] <= r[2] and r[3] <= b[4])]
            lst.append([b[1], b[2], b[3], b[4], "w", sem, val, eng.name, is_dma])
        for b in boxes_r:
            lst = self.recs.setdefault(b[0], [])
            hit = False
            if not is_dma:
                for r in lst:
                    if r[4] == "r" and r[7] == eng.name and not r[8] and r[0] == b[1] and r[1] == b[2] and r[2] == b[3] and r[3] == b[4]:
                        r[5], r[6] = sem, val
                        hit = True
                        break
            if not hit:
                lst.append([b[1], b[2], b[3], b[4], "r", sem, val, eng.name, is_dma])
                if len(lst) > 96:
                    self._compact(lst)

    @staticmethod
    def _compact(lst):
        keep = [r for r in lst if r[4] == "w"]
        merged = {}
        for r in lst:
            if r[4] != "r":
                continue
            k = (r[7], id(r[5]), r[8])
            m = merged.get(k)
            if m is None:
                merged[k] = list(r)
            else:
                m[0] = min(m[0], r[0]); m[1] = max(m[1], r[1]); m[2] = min(m[2], r[2]); m[3] = max(m[3], r[3])
                m[6] = max(m[6], r[6])
        lst[:] = keep + list(merged.values())

    def _emit_waits(self, eng, deps):
        for (s, v) in deps.values():
            k = id(s)
            if eng.waited.get(k, 0) >= v:
                continue
            eng.h.wait_ge(s, v)
            eng.waited[k] = v
            self.n_wait += 1

    def op(self, en, fn, reads=(), writes=(), inc=True):
        eng = self.engs[en]
        deps, br, bw = self._deps(eng, reads, writes, False)
        self._emit_waits(eng, deps)
        ins = fn(eng.h)
        self.n_inst += 1
        if inc:
            if eng.count >= SEM_ROTATE and not eng.pending:
                eng.sem = self._newsem()
                eng.count = 0
            eng.count += 1
            ins.then_inc(eng.sem, 1)
            tok = (eng.sem, eng.count)
            eng.pending = False
        else:
            tok = (eng.sem, eng.count + 1)
            eng.pending = True
        self._record(eng, br, bw, tok[0], tok[1], False)
        return ins

    def dma(self, out, in_, q="sp", is_output=False, **kw):
        eng = self.engs[q]
        deps, br, bw = self._deps(eng, [in_], [out], True)
        slot = self.dma_sems[self.dma_rr]
        self.dma_rr = (self.dma_rr + 1) % len(self.dma_sems)
        if slot[1] > 0:
            deps[id(slot[0])] = (slot[0], slot[1])
        self._emit_waits(eng, deps)
        slot[1] += 16
        eng.h.dma_start(out=out, in_=in_, **kw).then_inc(slot[0], 16)
        self.n_inst += 1
        self._record(eng, br, bw, slot[0], slot[1], True)
        if is_output:
            self.out_tokens.append((slot[0], slot[1]))

    def finish(self):
        eng = self.engs["sp"]
        deps = {}
        for slot in self.dma_sems:
            if slot[1] > 0:
                deps[id(slot[0])] = (slot[0], slot[1])
        self._emit_waits(eng, deps)

    def barrier(self):
        deps = {}
        for e in self.engs.values():
            if e.count > 0:
                deps[id(e.sem)] = (e.sem, e.count)
        for slot in self.dma_sems:
            if slot[1] > 0:
                deps[id(slot[0])] = (slot[0], slot[1])
        for e in self.engs.values():
            d = {k: v for k, v in deps.items() if k != id(e.sem)}
            self._emit_waits(e, d)
        self.recs = {}


D = 1024
KC = 8
SEQ = 2048
NCTX = 256
TOK = NCTX + SEQ
NT = TOK // 128
DEPTH = 4
DFF = 4096
EPS = 1e-6
BLOCKS = [(0, 256), (256, 512), (768, 512), (1280, 512), (1792, 512)]
EVEN_IN = 2240
ODD_IN = 3104
AX = mybir.AxisListType.X

WEIGHT_NAMES = ["ada_w", "ada_b", "norm1_w", "norm2_w", "mlp_w1", "mlp_w2", "even_w_in", "mla_q_norm", "mla_w_uq",
                "mla_kv_norm", "mla_w_ukv", "gla_gate_up", "gla_gate_bias", "gla_o_norm", "even_w_out", "gdn_w_in",
                "gdn_conv_w", "gdn_a_log", "gdn_dt_bias", "gdn_o_norm", "gdn_w_out", "final_norm"]
WEIGHT_SHAPES = {
    "ada_w": [4, 1024, 6144], "ada_b": [4, 6144], "norm1_w": [4, 1024], "norm2_w": [4, 1024],
    "mlp_w1": [4, 1024, 4096], "mlp_w2": [4, 4096, 1024], "even_w_in": [2, 1024, 2240], "mla_q_norm": [2, 384],
    "mla_w_uq": [2, 384, 768], "mla_kv_norm": [2, 256], "mla_w_ukv": [2, 256, 1024], "gla_gate_up": [2, 2, 16, 256],
    "gla_gate_bias": [2, 2, 256], "gla_o_norm": [2, 128], "even_w_out": [2, 1024, 1024], "gdn_w_in": [2, 1024, 3104],
    "gdn_conv_w": [2, 3, 2048], "gdn_a_log": [2, 2, 8], "gdn_dt_bias": [2, 2, 8], "gdn_o_norm": [2, 128],
    "gdn_w_out": [2, 1024, 1024], "final_norm": [1024],
}


def host_consts():
    i = np.arange(128)
    same = (i[:, None] // 64) == (i[None, :] // 64)
    Mf = (same & (i[:, None] <= i[None, :])).astype(np.float32)
    Mb = (same & (i[:, None] >= i[None, :])).astype(np.float32)
    Sf = (same & (i[:, None] > i[None, :])).astype(np.float32)
    Sb = (same & (i[:, None] < i[None, :])).astype(np.float32)
    CI = np.stack([(i < 64), (i >= 64)], 1).astype(np.float32)
    ident = np.eye(128, dtype=np.float32)
    cols = [ident, Mf, Mb, Sf, Sb, CI,
            np.concatenate([Mf, CI], 1) / -16.0, np.concatenate([Mb, CI], 1) / -16.0, Sf / -16.0, Sb / -16.0,
            np.concatenate([Mf, CI], 1), np.concatenate([Mb, CI], 1)]
    cst = np.concatenate(cols, 1).astype(np.float32)
    rows = SEQ // 64
    row = np.repeat(np.arange(rows, dtype=np.float32), 64)
    col = np.tile(np.arange(64, dtype=np.float32), rows)
    inv = (10000.0 ** (-np.arange(0, 16, 2, dtype=np.float32) / 16)).astype(np.float32)
    ang = np.concatenate([row[:, None] * inv, col[:, None] * inv], -1).astype(np.float32)
    cos, sin = np.cos(ang).astype(np.float32), np.sin(ang).astype(np.float32)
    C = np.zeros((32, SEQ), np.float32)
    S = np.zeros((32, SEQ), np.float32)
    for ax in range(2):
        for half in range(2):
            for f in range(8):
                d = ax * 16 + half * 8 + f
                C[d] = cos[:, ax * 8 + f]
                S[d] = (-sin[:, ax * 8 + f]) if half == 0 else sin[:, ax * 8 + f]
    rope = np.stack([C, S], 0)
    r = np.arange(128)
    ms = []
    for b in (1, 2, 4, 8, 16, 32):
        same2 = (r[:, None] // (2 * b)) == (r[None, :] // (2 * b))
        ur = same2 & ((r[:, None] % (2 * b)) < b) & ((r[None, :] % (2 * b)) >= b)
        ms.append(ur.astype(np.float32))
    bmask = np.concatenate(ms + [m.T for m in ms], 1).astype(np.float32)
    return cst, rope, bmask


CST_OFF = {}
_o = 0
for _n, _w in [("ident", 128), ("Mf", 128), ("Mb", 128), ("Sf", 128), ("Sb", 128), ("CI", 2), ("glaRf", 130), ("glaRb", 130),
               ("glaAf", 128), ("glaAb", 128), ("gdnRf", 130), ("gdnRb", 130)]:
    CST_OFF[_n] = (_o, _w)
    _o += _w
CST_W = _o


class Ctx:
    pass


def build_program(n_layers=DEPTH, fake_mixer=False, debug_h=False):
    nc = bass.Bass("TRN2", target_bir_lowering=False)
    g = Ctx()
    g.nc = nc
    dr = {}
    dr["x"] = nc.dram_tensor("x", [SEQ, D], F32, kind="ExternalInput").ap()
    dr["ctx"] = nc.dram_tensor("ctx", [NCTX, D], F32, kind="ExternalInput").ap()
    dr["cc"] = nc.dram_tensor("cc", [16, 128], F32, kind="ExternalInput").ap()
    dr["cst"] = nc.dram_tensor("cst", [128, CST_W], F32, kind="ExternalInput").ap()
    dr["rope"] = nc.dram_tensor("rope", [2, 32, SEQ], F32, kind="ExternalInput").ap()
    dr["bmask"] = nc.dram_tensor("bmask", [128, 12 * 128], F32, kind="ExternalInput").ap()
    for n in WEIGHT_NAMES:
        dr[n] = nc.dram_tensor(n, WEIGHT_SHAPES[n], F32, kind="ExternalInput").ap()
    dr["out"] = nc.dram_tensor("out", [SEQ, D], F32, kind="ExternalOutput").ap()
    dr["H"] = nc.dram_tensor("Hs", [KC, 128, TOK], F32).ap()
    dr["MIX"] = nc.dram_tensor("MIXs", [KC, 128, TOK], BF16).ap()
    if debug_h:
        dr["dbg"] = nc.dram_tensor("dbg", [KC, 128, TOK], F32, kind="ExternalOutput").ap()
    g.dr = dr
    es = contextlib.ExitStack()
    with es:
        P = Prog(nc, es)
        g.P = P
        g.es = es
        g.ps = [es.enter_context(nc.psum_tensor("psb%d" % i, [128, 512], F32)) for i in range(7)]
        g.psbf = es.enter_context(nc.psum_tensor("psbf", [128, 1024], BF16))
        g.ps_rr = 0
        _emit(g, n_layers, fake_mixer, debug_h)
        P.finish()
        g.stats = (P.n_inst, P.n_wait, P.semid)
    return nc, g


_UNIQ = [0]


def _sb(g, name, shape, dt=F32, stack=None):
    _UNIQ[0] += 1
    return (stack or g.es).enter_context(g.nc.sbuf_tensor("sb_%s_%d" % (name, _UNIQ[0]), shape, dt))


def _psum(g, kind="rot"):
    if kind == "rot":
        t = g.ps[g.ps_rr]
        g.ps_rr = (g.ps_rr + 1) % 4
        return t
    if kind == "acc":
        g.ps_acc = 1 - getattr(g, "ps_acc", 0)
        return g.ps[4 + g.ps_acc]
    return g.ps[6]


def _mm(g, out, lhsT, rhs, start=True, stop=True, inc=True):
    g.P.op("pe", lambda e: e.matmul(out, lhsT, rhs, start=start, stop=stop), [lhsT, rhs], [out], inc=inc)


def _mmk(g, out, pairs):
    n = len(pairs)
    for i, (l, r) in enumerate(pairs):
        _mm(g, out, l, r, start=(i == 0), stop=(i == n - 1), inc=(i == n - 1))


def _tr(g, out, in_, ident):
    g.P.op("pe", lambda e: e.transpose(out, in_, ident), [in_, ident], [out])


def _act(g, out, in_, func, scale=1.0, bias=0.0, extra_reads=()):
    rd = [in_] + [a for a in (scale, bias) if not isinstance(a, (int, float))] + list(extra_reads)
    g.P.op("act", lambda e: e.activation(out=out, in_=in_, func=func, bias=bias, scale=scale), rd, [out])


def _tt(g, en, out, in0, in1, op):
    g.P.op(en, lambda e: e.tensor_tensor(out, in0, in1, op=op), [in0, in1], [out])


def _stt(g, en, out, in0, scalar, in1, op0, op1):
    rd = [in0, in1] + ([scalar] if not isinstance(scalar, (int, float)) else [])
    g.P.op(en, lambda e: e.scalar_tensor_tensor(out=out, in0=in0, scalar=scalar, in1=in1, op0=op0, op1=op1), rd, [out])


def _ts(g, en, out, in0, s1, s2, op0, op1=None):
    rd = [in0] + [a for a in (s1, s2) if a is not None and not isinstance(a, (int, float))]
    if op1 is None:
        g.P.op(en, lambda e: e.tensor_scalar(out, in0, s1, None, op0=op0), rd, [out])
    else:
        g.P.op(en, lambda e: e.tensor_scalar(out, in0, s1, s2, op0=op0, op1=op1), rd, [out])


def _copy(g, en, out, in_):
    if en == "act":
        g.P.op("act", lambda e: e.copy(out, in_), [in_], [out])
    else:
        g.P.op(en, lambda e: e.tensor_copy(out, in_), [in_], [out])


def _memset(g, en, out, val):
    g.P.op(en, lambda e: e.memset(out, val), [], [out])


def _load_cast(g, dst, src):
    shp = list(dst.shape)
    if len(shp) == 2:
        pieces = [(dst[:, c:min(c + 1024, shp[1])], src[:, c:min(c + 1024, shp[1])]) for c in range(0, shp[1], 1024)]
    else:
        inner = shp[2]
        assert len(shp) == 3 and inner <= 1024
        step = max(1, 1024 // inner)
        pieces = [(dst[:, a:min(a + step, shp[1]), :], src[:, a:min(a + step, shp[1]), :]) for a in range(0, shp[1], step)]
    for d, s_ in pieces:
        stg = g.stage.get()
        n = 1
        for x in d.shape[1:]:
            n *= x
        if len(d.shape) == 3:
            v = stg[:, 0:n].rearrange("p (a b) -> p a b", b=d.shape[2])
        else:
            v = stg[:, 0:n]
        g.P.dma(v, s_)
        g.cast_rr = 1 - getattr(g, "cast_rr", 0)
        _copy(g, "pool" if g.cast_rr else "act", d, v)


def _rstd_from_sum(g, out_sb, ps_in, inv_n, eps_col):
    _act(g, out_sb, ps_in, AF.Sqrt, scale=inv_n, bias=eps_col)
    g.P.op("dve", lambda e: e.reciprocal(out_sb, out_sb), [out_sb], [out_sb])


class Pool_:
    def __init__(self, g, name, shape, dt, n, stack=None, zero=False):
        self.tiles = [_sb(g, "%s%d" % (name, i), shape, dt, stack) for i in range(n)]
        self.i = 0
        if zero:
            for t in self.tiles:
                _memset(g, "pool", t[:], 0.0)

    def get(self):
        t = self.tiles[self.i]
        self.i = (self.i + 1) % len(self.tiles)
        return t


def _cst(g, name):
    o, w = CST_OFF[name]
    return g.cst[:, o:o + w]


def _setup(g):
    P, dr = g.P, g.dr
    g.cst = _sb(g, "cst", [128, CST_W])
    P.dma(g.cst[:], dr["cst"])
    g.ident = _cst(g, "ident")
    g.ident_bf = _sb(g, "ident_bf", [128, 128], BF16)
    _copy(g, "dve", g.ident_bf[:], g.ident)
    g.ones_bf = _sb(g, "ones_bf", [128, 128], BF16)
    _memset(g, "pool", g.ones_bf[:], 1.0)
    g.ones_f = _sb(g, "ones_f", [128, 128])
    _memset(g, "pool", g.ones_f[:], 1.0)
    g.eps = _sb(g, "eps", [128, 1])
    _memset(g, "pool", g.eps[:], EPS)
    rows = []
    rows.append(("cc", dr["cc"], 16))
    for L in range(DEPTH):
        rows.append(("ada_b%d" % L, dr["ada_b"][L].rearrange("(r c) -> r c", c=128), 48))
        rows.append(("n1w%d" % L, dr["norm1_w"][L].rearrange("(r c) -> r c", c=128), 8))
        rows.append(("n2w%d" % L, dr["norm2_w"][L].rearrange("(r c) -> r c", c=128), 8))
    for i in range(2):
        rows.append(("qn%d" % i, dr["mla_q_norm"][i].rearrange("(r c) -> r c", c=128), 3))
        rows.append(("kvn%d" % i, dr["mla_kv_norm"][i].rearrange("(r c) -> r c", c=128), 2))
        rows.append(("conv%d" % i, dr["gdn_conv_w"][i].rearrange("t (r c) -> (t r) c", c=128), 48))
    rows.append(("fn", dr["final_norm"].rearrange("(r c) -> r c", c=128), 8))
    tot = sum(r[2] for r in rows)
    ntile = (tot + 127) // 128
    g.cols = _sb(g, "cols", [128, ntile * 128])
    g.coloff = {}
    with contextlib.ExitStack() as st:
        rt = [_sb(g, "rowst%d" % i, [128, 128], F32, st) for i in range(ntile)]
        for t in rt:
            _memset(g, "dve", t[:], 0.0)
        r = 0
        for key, ap, n in rows:
            g.coloff[key] = r
            done = 0
            while done < n:
                t, p = divmod(r + done, 128)
                m = min(n - done, 128 - p)
                P.dma(rt[t][p:p + m, :], ap[done:done + m, :])
                done += m
            r += n
        for t in range(ntile):
            ps = _psum(g)
            _tr(g, ps[:, 0:128], rt[t][:], g.ident)
            _copy(g, "dve", g.cols[:, t * 128:(t + 1) * 128], ps[:, 0:128])
        P.barrier()
    g.sc = _sb(g, "sc", [128, KC, 2])
    _act(g, g.sc[:, :, 0], g.cols[:, 0:8], AF.Silu)
    _act(g, g.sc[:, :, 1], g.cols[:, 8:16], AF.Silu)
    g.mod = [_sb(g, "mod%d" % L, [128, 48, 2]) for L in range(DEPTH)]
    g.comb1 = [_sb(g, "comb1_%d" % L, [128, KC, 2]) for L in range(DEPTH)]
    g.comb2 = [_sb(g, "comb2_%d" % L, [128, KC, 2]) for L in range(DEPTH)]
    g.adaw_pool = Pool_(g, "adaw", [128, KC, 128], F32, 2)
    g.stage = Pool_(g, "stage", [128, 1024], F32, 2)


def _col(g, key, n):
    o = g.coloff[key]
    return g.cols[:, o:o + n]


def _adaln_steps(g, L):
    P, dr = g.P, g.dr
    src = dr["ada_w"][L].rearrange("(k p) n -> p k n", p=128)
    for cb in range(48):
        wt = g.adaw_pool.get()
        P.dma(wt[:], src[:, :, cb * 128:(cb + 1) * 128])
        ps = _psum(g)
        _mmk(g, ps[:, 0:2], [(wt[:, k, :], g.sc[:, k, :]) for k in range(KC)])
        bias = _col(g, "ada_b%d" % L, 48)[:, cb:cb + 1]
        _tt(g, "dve", g.mod[L][:, cb, :], ps[:, 0:2], bias.to_broadcast([128, 2]), ALU.add)
        if cb == 47:
            n1 = _col(g, "n1w%d" % L, 8).unsqueeze(2).to_broadcast([128, KC, 2])
            n2 = _col(g, "n2w%d" % L, 8).unsqueeze(2).to_broadcast([128, KC, 2])
            _stt(g, "dve", g.comb1[L][:], g.mod[L][:, 8:16, :], 1.0, n1, ALU.add, ALU.mult)
            _stt(g, "dve", g.comb2[L][:], g.mod[L][:, 32:40, :], 1.0, n2, ALU.add, ALU.mult)
        yield


def _mix_view(g, s0, n):
    return g.dr["MIX"].rearrange("k p t -> p k t")[:, :, s0:s0 + n]


def _h_view(g, s0, n):
    return g.dr["H"].rearrange("k p t -> p k t")[:, :, s0:s0 + n]


def _load_inputs(g):
    P, dr = g.P, g.dr
    with contextlib.ExitStack() as st:
        xin = Pool_(g, "xin", [128, D], F32, 2, st)
        hb = Pool_(g, "hb0_", [128, KC, 128], F32, 2, st)
        for t in range(NT):
            src = dr["ctx"][t * 128:(t + 1) * 128, :] if t < 2 else dr["x"][(t - 2) * 128:(t - 1) * 128, :]
            xt = xin.get()
            P.dma(xt[:], src)
            ht = hb.get()
            for half in range(2):
                ps = _psum(g)
                for q in range(4):
                    k = half * 4 + q
                    _tr(g, ps[:, q * 128:(q + 1) * 128], xt[:, k * 128:(k + 1) * 128], g.ident)
                _copy(g, "act" if half == 0 else "dve", ht[:, half * 4:(half + 1) * 4, :],
                      ps[:].rearrange("p (q t) -> p q t", t=128))
            P.dma(_h_view(g, t * 128, 128), ht[:])
        P.barrier()


def _norm_tmps(g, st):
    return (Pool_(g, "sqn", [128, 512], BF16, 2, st), Pool_(g, "tmn", [128, 512], F32, 3, st), _sb(g, "rstdn", [128, 512], F32, st))


def _norm_mod(g, hblk, N, comb, shift, out_bf, tmps, ab=None):
    sqp, tp, rstd = tmps
    ps = _psum(g)
    for k in range(KC):
        sq = sqp.get()
        _act(g, sq[:, :N], hblk[:, k, :N], AF.Square)
        _mm(g, ps[:, :N], g.ones_bf[:], sq[:, :N], start=(k == 0), stop=(k == KC - 1))
    _rstd_from_sum(g, rstd[:, :N], ps[:, :N], 1.0 / D, g.eps[:])
    if ab is not None:
        w32, AB, t0, a32p = ab
        psab = _psum(g, "misc")
    for k in range(KC):
        t = tp.get()
        _stt(g, "dve", t[:, :N], hblk[:, k, :N], comb[:, k:k + 1], rstd[:, :N], ALU.mult, ALU.mult)
        if ab is None:
            _act(g, out_bf[:, k, :N], t[:, :N], AF.Identity, scale=1.0, bias=shift[:, k:k + 1])
        else:
            a32 = a32p.get()
            _act(g, a32[:, :N], t[:, :N], AF.Identity, scale=1.0, bias=shift[:, k:k + 1])
            _copy(g, "pool", out_bf[:, k, :N], a32[:, :N])
            for tt in range(N // 128):
                _mm(g, psab[:, tt * 32:(tt + 1) * 32], a32[:, tt * 128:(tt + 1) * 128], w32[:, k, :], start=(k == 0 and tt == 0), stop=(k == KC - 1 and tt == N // 128 - 1))
    if ab is not None:
        nt = N // 128
        _copy(g, "act", AB[:, t0:t0 + nt, :], psab[:, 0:nt * 32].rearrange("p (t c) -> p t c", c=32))


def _emit(g, n_layers, fake_mixer, debug_h):
    P, dr = g.P, g.dr
    _setup(g)
    ada = _adaln_steps(g, 0)
    for _ in ada:
        pass
    _load_inputs(g)
    for L in range(n_layers):
        last = (L == DEPTH - 1)
        need_ctx = not last
        nxt = _adaln_steps(g, L + 1) if L + 1 < n_layers else iter(())
        with contextlib.ExitStack() as lstA:
            lat = _mla_alloc(g, lstA) if (L % 2 == 0 and not fake_mixer) else None
            with contextlib.ExitStack() as lst:
                aT = _sb(g, "aT", [128, KC, TOK], BF16, lst)
                AB = None
                if L % 2 == 1 and not fake_mixer and USE_AB32:
                    AB = _sb(g, "AB", [128, NT, 32], F32, lst)
                with contextlib.ExitStack() as st:
                    hbp = Pool_(g, "hbn", [128, KC, 512], F32, 2, st)
                    tmps = _norm_tmps(g, st)
                    ab = None
                    if AB is not None:
                        w32 = _sb(g, "wab32", [128, KC, 32], F32, st)
                        P.dma(w32[:], dr["gdn_w_in"][L // 2].rearrange("(k p) n -> p k n", p=128)[:, :, O_A:O_A + 32])
                        a32p = Pool_(g, "a32", [128, 512], F32, 2, st)
                    for bi, (s0, N) in enumerate(BLOCKS):
                        s = 1 if bi == 0 else 0
                        hb = hbp.get()
                        P.dma(hb[:, :, :N], _h_view(g, s0, N))
                        if AB is not None:
                            ab = (w32, AB, s0 // 128, a32p)
                        _norm_mod(g, hb, N, g.comb1[L][:, :, s], g.mod[L][:, 0:8, s], aT[:, :, s0:s0 + N], tmps, ab)
                    P.barrier()
                if fake_mixer:
                    for (s0, N) in BLOCKS:
                        P.dma(_mix_view(g, s0, N), aT[:, :, s0:s0 + N])
                elif L % 2 == 0:
                    _even_mixer_a(g, L, aT, lat, lst)
                else:
                    _odd_mixer(g, L, aT, need_ctx, AB)
                P.barrier()
            if L % 2 == 0 and not fake_mixer and "attn" not in DBG_SKIP:
                _mla_attention(g, L, lat)
                P.barrier()
        wname = "even_w_out" if L % 2 == 0 else "gdn_w_out"
        with contextlib.ExitStack() as st:
            wo = _sb(g, "wo", [128, KC, D], BF16, st)
            _load_cast(g, wo[:], dr[wname][L // 2].rearrange("(k p) n -> p k n", p=128))
            hbp = Pool_(g, "hba", [128, KC, 512], F32, 2, st)
            mxp = Pool_(g, "mxb", [128, KC, 512], BF16, 2, st)
            for bi, (s0, N) in enumerate(BLOCKS):
                if bi == 0 and not need_ctx:
                    continue
                s = 1 if bi == 0 else 0
                hb = hbp.get()
                P.dma(hb[:, :, :N], _h_view(g, s0, N))
                mx = mxp.get()
                P.dma(mx[:, :, :N], _mix_view(g, s0, N))
                for m in range(KC):
                    ps = _psum(g)
                    _mmk(g, ps[:, :N], [(wo[:, k, m * 128:(m + 1) * 128], mx[:, k, :N]) for k in range(KC)])
                    _stt(g, "dve", hb[:, m, :N], ps[:, :N], g.mod[L][:, 16 + m, s:s + 1], hb[:, m, :N], ALU.mult, ALU.add)
                P.dma(_h_view(g, s0, N), hb[:, :, :N])
                for _ in range(5):
                    next(nxt, None)
            P.barrier()
        if debug_h == ("mix", L):
            _dump_h(g)
            return
        with contextlib.ExitStack() as st:
            w2 = _sb(g, "w2", [128, 32, D], BF16, st)
            w2src = dr["mlp_w2"][L].rearrange("(f p) n -> p f n", p=128)
            _load_cast(g, w2[:], w2src)
            w1p = Pool_(g, "w1p", [128, KC, 512], BF16, 2, st)
            w1src = dr["mlp_w1"][L].rearrange("(k p) n -> p k n", p=128)
            hbp = Pool_(g, "hbm", [128, KC, 512], F32, 1, st)
            a2 = _sb(g, "a2", [128, KC, 512], BF16, st)
            h1 = _sb(g, "h1", [128, 32, 512], BF16, st)
            rl = Pool_(g, "rl", [128, 512], F32, 3, st)
            tmps = _norm_tmps(g, st)
            for bi, (s0, N) in enumerate(BLOCKS):
                if bi == 0 and not need_ctx:
                    continue
                s = 1 if bi == 0 else 0
                hb = hbp.get()
                P.dma(hb[:, :, :N], _h_view(g, s0, N))
                _norm_mod(g, hb, N, g.comb2[L][:, :, s], g.mod[L][:, 24:32, s], a2, tmps)
                for fg in range(8):
                    w1 = w1p.get()
                    _load_cast(g, w1[:], w1src[:, :, fg * 512:(fg + 1) * 512])
                    for f in range(4):
                        ps = _psum(g)
                        _mmk(g, ps[:, :N], [(w1[:, k, f * 128:(f + 1) * 128], a2[:, k, :N]) for k in range(KC)])
                        r = rl.get()
                        _act(g, r[:, :N], ps[:, :N], AF.Relu)
                        _tt(g, "pool", h1[:, fg * 4 + f, :N], r[:, :N], r[:, :N], ALU.mult)
                for m in range(KC):
                    ps = _psum(g)
                    _mmk(g, ps[:, :N], [(w2[:, f, m * 128:(m + 1) * 128], h1[:, f, :N]) for f in range(32)])
                    _stt(g, "dve", hb[:, m, :N], ps[:, :N], g.mod[L][:, 40 + m, s:s + 1], hb[:, m, :N], ALU.mult, ALU.add)
                P.dma(_h_view(g, s0, N), hb[:, :, :N])
                for _ in range(5):
                    next(nxt, None)
            for _ in nxt:
                pass
            P.barrier()
        if debug_h == ("mlp", L):
            _dump_h(g)
            return
    if debug_h:
        _dump_h(g)
        return
    _final(g)


def _dump_h(g):
    P = g.P
    with contextlib.ExitStack() as st:
        hb = Pool_(g, "hbd", [128, KC, 512], F32, 2, st)
        for (s0, N) in BLOCKS:
            t = hb.get()
            P.dma(t[:, :, :N], _h_view(g, s0, N))
            P.dma(g.dr["dbg"].rearrange("k p t -> p k t")[:, :, s0:s0 + N], t[:, :, :N], is_output=True)
        P.dma(g.dr["out"][0:128, :], g.cst[:, 0:D], is_output=True)


def _final(g):
    P, dr = g.P, g.dr
    with contextlib.ExitStack() as st:
        hbp = Pool_(g, "hbf", [128, KC, 512], F32, 2, st)
        sq = _sb(g, "sqf", [128, KC, 512], BF16, st)
        yb = _sb(g, "yf", [128, KC, 512], F32, st)
        rstd = _sb(g, "rstdf", [128, 512], F32, st)
        otp = Pool_(g, "ot", [128, D], F32, 2, st)
        fn = _col(g, "fn", 8)
        for (s0, N) in BLOCKS[1:]:
            hb = hbp.get()
            P.dma(hb[:, :, :N], _h_view(g, s0, N))
            _act(g, sq[:, :, :N], hb[:, :, :N], AF.Square)
            ps = _psum(g)
            _mmk(g, ps[:, :N], [(g.ones_bf[:], sq[:, k, :N]) for k in range(KC)])
            _rstd_from_sum(g, rstd[:, :N], ps[:, :N], 1.0 / D, g.eps[:])
            for k in range(KC):
                _stt(g, "dve", yb[:, k, :N], hb[:, k, :N], fn[:, k:k + 1], rstd[:, :N], ALU.mult, ALU.mult)
            for tt in range(N // 128):
                ot = otp.get()
                for half in range(2):
                    ps = _psum(g)
                    for q in range(4):
                        k = half * 4 + q
                        _tr(g, ps[:, q * 128:(q + 1) * 128], yb[:, k, tt * 128:(tt + 1) * 128], g.ident)
                    _copy(g, "act" if half == 0 else "dve", ot[:, half * 512:(half + 1) * 512], ps[:])
                r0 = s0 - NCTX + tt * 128
                P.dma(dr["out"][r0:r0 + 128, :], ot[:], is_output=True)


C_CQ, C_CKV, C_KPE, C_GQ, C_GK, C_GV, C_GG, C_GLOW = 0, 384, 640, 672, 928, 1184, 1696, 2208
GLA_FWD = list(range(NT))
GLA_BWD = [1, 0] + list(range(NT - 1, 1, -1))


def _mla_alloc(g, st):
    lat = Ctx()
    lat.cqn = _sb(g, "cqn", [128, 3, TOK], BF16, st)
    lat.ckvn = _sb(g, "ckvn", [128, 2, TOK], BF16, st)
    lat.kpeT = _sb(g, "kpeT", [128, TOK], BF16, st)
    return lat


def _rope_tables(g, lat, st):
    lat.ropeC = _sb(g, "ropeC", [128, SEQ], F32, st)
    lat.ropeS = _sb(g, "ropeS", [128, SEQ], F32, st)
    g.P.dma(lat.ropeC[64:96, :], g.dr["rope"][0])
    g.P.dma(lat.ropeS[64:96, :], g.dr["rope"][1])


def _swap_halves(g, en, dst, src):
    d = dst.rearrange("p h (a t f) -> p h a t f", a=2, t=2)
    s_ = src.rearrange("p h (a t f) -> p h a t f", a=2, t=2)
    _copy(g, en, d[:, :, :, 0, :], s_[:, :, :, 1, :])
    _copy(g, en, d[:, :, :, 1, :], s_[:, :, :, 0, :])


def _even_mixer_a(g, L, aT, lat, st):
    P, dr = g.P, g.dr
    i = L // 2
    win = _sb(g, "win", [128, KC, EVEN_IN], BF16, st)
    wsrc = dr["even_w_in"][i].rearrange("(k p) n -> p k n", p=128)
    for k in range(KC):
        _load_cast(g, win[:, k, :], wsrc[:, k, :])
    with contextlib.ExitStack() as s1:
        _rope_tables(g, lat, s1)
        winrot = _sb(g, "winrot", [128, KC, 32], BF16, s1)
        _swap_halves(g, "pool", winrot[:], win[:, :, C_KPE:C_KPE + 32])
        cqf = _sb(g, "cqf", [128, 5, 512], F32, s1)
        sqp = Pool_(g, "sql", [128, 512], BF16, 2, s1)
        rq = _sb(g, "rq", [128, 512], F32, s1)
        rkv = _sb(g, "rkv", [128, 512], F32, s1)
        t1p = Pool_(g, "t1l", [128, 512], F32, 2, s1)
        qn = _col(g, "qn%d" % i, 3)
        kvn = _col(g, "kvn%d" % i, 2)
        for bi, (s0, N) in enumerate(BLOCKS):
            rhs = [aT[:, k, s0:s0 + N] for k in range(KC)]
            pss_q = _psum(g, "acc")
            pss_kv = _psum(g, "acc")
            for fc in range(5):
                ps = _psum(g)
                _mmk(g, ps[:, :N], [(win[:, k, fc * 128:(fc + 1) * 128], rhs[k]) for k in range(KC)])
                _copy(g, "act", cqf[:, fc, :N], ps[:, :N])
                sq = sqp.get()
                _tt(g, "pool", sq[:, :N], cqf[:, fc, :N], cqf[:, fc, :N], ALU.mult)
                if fc < 3:
                    _mm(g, pss_q[:, :N], g.ones_bf[:], sq[:, :N], start=(fc == 0), stop=(fc == 2))
                else:
                    _mm(g, pss_kv[:, :N], g.ones_bf[:], sq[:, :N], start=(fc == 3), stop=(fc == 4))
            _rstd_from_sum(g, rq[:, :N], pss_q[:, :N], 1.0 / 384, g.eps[:])
            _rstd_from_sum(g, rkv[:, :N], pss_kv[:, :N], 1.0 / 256, g.eps[:])
            for fc in range(3):
                _stt(g, "dve", lat.cqn[:, fc, s0:s0 + N], cqf[:, fc, :N], qn[:, fc:fc + 1], rq[:, :N], ALU.mult, ALU.mult)
            for fc in range(2):
                _stt(g, "dve", lat.ckvn[:, fc, s0:s0 + N], cqf[:, 3 + fc, :N], kvn[:, fc:fc + 1], rkv[:, :N], ALU.mult, ALU.mult)
            ps = _psum(g)
            _mmk(g, ps[64:96, :N], [(win[:, k, C_KPE:C_KPE + 32], rhs[k]) for k in range(KC)])
            if bi == 0:
                _copy(g, "act", lat.kpeT[64:96, s0:s0 + N], ps[64:96, :N])
            else:
                ps2 = _psum(g)
                _mmk(g, ps2[64:96, :N], [(winrot[:, k, :], rhs[k]) for k in range(KC)])
                p0 = s0 - NCTX
                t1 = t1p.get()
                t2 = t1p.get()
                _tt(g, "dve", t1[64:96, :N], ps[64:96, :N], lat.ropeC[64:96, p0:p0 + N], ALU.mult)
                _tt(g, "dve", t2[64:96, :N], ps2[64:96, :N], lat.ropeS[64:96, p0:p0 + N], ALU.mult)
                _tt(g, "pool", lat.kpeT[64:96, s0:s0 + N], t1[64:96, :N], t2[64:96, :N], ALU.add)
        P.barrier()
    with contextlib.ExitStack() as s2:
        if "gla" not in DBG_SKIP:
            try:
                _gla(g, i, aT, win, s2)
            except _Stop:
                pass
        P.barrier()


def _gla(g, i, aT, win, st):
    P, dr = g.P, g.dr
    gup = _sb(g, "gup", [16, 2, 256], F32, st)
    P.dma(gup[:], dr["gla_gate_up"][i].rearrange("z r n -> r z n"))
    gbias = _sb(g, "gbias", [128, 2, 256], F32, st)
    P.dma(gbias[:], dr["gla_gate_bias"][i].partition_broadcast(128))
    onorm = _sb(g, "onorm", [128, 128], F32, st)
    P.dma(onorm[:], dr["gla_o_norm"][i].partition_broadcast(128))
    Of = _sb(g, "Of", [128, NT, 512], F32, st)
    S = _sb(g, "S", [128, 2, 128], F32, st)
    Sbf = _sb(g, "Sbf", [128, 2, 128], BF16, st)
    glowT = Pool_(g, "glowT", [16, 128], F32, 2, st)
    tl = Pool_(g, "tl", [128, 256], F32, 2, st)
    Lz = Pool_(g, "Lz", [128, 256], F32, 2, st)
    Eq = Pool_(g, "Eq", [128, 2, 128], F32, 2, st)
    Ek = Pool_(g, "Ek", [128, 2, 128], F32, 2, st)
    gend = Pool_(g, "gend", [128, 2, 2], F32, 2, st)
    kendE = Pool_(g, "kendE", [128, 256], F32, 2, st)
    qdecT = Pool_(g, "qdecT", [128, 4, 128], BF16, 2, st, zero=True)
    kinvT = Pool_(g, "kinvT", [128, 4, 128], BF16, 2, st, zero=True)
    kend = Pool_(g, "kend", [128, 2, 256], BF16, 2, st)
    CI2 = _cst(g, "CI").unsqueeze(2).to_broadcast([128, 2, 256])
    Vg = Pool_(g, "Vg", [128, 512], BF16, 2, st)
    AmT = Pool_(g, "AmT", [128, 4, 128], BF16, 2, st)
    G2 = Pool_(g, "G2", [128, 512], F32, 1, st)
    ob = Pool_(g, "ob", [128, 512], F32, 1, st)
    sqo = Pool_(g, "sqo", [128, 512], F32, 1, st)
    ss4 = Pool_(g, "ss4", [128, 4], F32, 2, st)
    mtok = Pool_(g, "mtok", [128, 512], BF16, 2, st)
    mt = Pool_(g, "mt", [128, 4, 128], BF16, 2, st)
    LN8 = float(np.log(0.125))
    for z in range(2):
        R_ = _cst(g, "glaRf" if z == 0 else "glaRb")
        A_ = _cst(g, "glaAf" if z == 0 else "glaAb")
        Mz = _cst(g, "Mf" if z == 0 else "Mb")
        _memset(g, "pool", S[:], 0.0)
        _memset(g, "pool", Sbf[:], 0.0)
        for t in (GLA_FWD if z == 0 else GLA_BWD):
            ts = slice(t * 128, (t + 1) * 128)
            at = [aT[:, k, ts] for k in range(KC)]
            ps = _psum(g)
            c0 = C_GLOW + 16 * z
            _mmk(g, ps[0:16, 0:128], [(win[:, k, c0:c0 + 16], at[k]) for k in range(KC)])
            gl = glowT.get()
            _copy(g, "act", gl[:], ps[0:16, 0:128])
            _chk("gla_%s_%d" % ("a", z))
            ps = _psum(g)
            _mm(g, ps[:, 0:256], gl[:], gup[:, z, :])
            tt_ = tl.get()
            _tt(g, "dve", tt_[:], ps[:, 0:256], gbias[:, z, :], ALU.add)
            L_ = Lz.get()
            _act(g, tt_[:], tt_[:], AF.Exp, scale=-1.0)
            _act(g, L_[:], tt_[:], AF.Ln, scale=1.0, bias=1.0)
            _chk("gla_%s_%d" % ("b", z))
            ps = _psum(g)
            for hp in range(2):
                _mm(g, ps[:, hp * 130:(hp + 1) * 130], L_[:, hp * 128:(hp + 1) * 128], R_)
            pv = ps[:, 0:260].rearrange("p (h c) -> p h c", c=130)
            eq, ek, ge = Eq.get(), Ek.get(), gend.get()
            _act(g, eq[:], pv[:, :, 0:128], AF.Exp, scale=1.0, bias=LN8)
            _act(g, ek[:], pv[:, :, 0:128], AF.Exp, scale=-1.0)
            _act(g, ge[:], pv[:, :, 128:130], AF.Exp)
            _chk("gla_%s_%d" % ("c", z))
            ps = _psum(g)
            _mm(g, ps[:, 0:256], A_, L_[:])
            ke = kendE.get()
            _act(g, ke[:], ps[:, 0:256], AF.Exp)
            _chk("gla_%s_%d" % ("d", z))
            ps = _psum(g)
            for hp in range(2):
                _mmk(g, ps[:, hp * 128:(hp + 1) * 128], [(win[:, k, C_GQ + hp * 128:C_GQ + (hp + 1) * 128], at[k]) for k in range(KC)])
                _mmk(g, ps[:, 256 + hp * 128:256 + (hp + 1) * 128], [(win[:, k, C_GK + hp * 128:C_GK + (hp + 1) * 128], at[k]) for k in range(KC)])
            qd, ki = qdecT.get(), kinvT.get()
            for (lo, hi, par) in ((0, 64, 0), (64, 128, 1)):
                _tt(g, "dve", qd[lo:hi, par::2, :], ps[lo:hi, 0:256].rearrange("p (h c) -> p h c", c=128), eq[lo:hi, :, :], ALU.mult)
                _tt(g, "dve", ki[lo:hi, par::2, :], ps[lo:hi, 256:512].rearrange("p (h c) -> p h c", c=128), ek[lo:hi, :, :], ALU.mult)
            _chk("gla_%s_%d" % ("e", z))
            ps = _psum(g)
            _mmk(g, ps[:, 0:256], [(at[k], win[:, k, C_GK:C_GK + 256]) for k in range(KC)])
            kn = kend.get()
            _tt(g, "dve", ke[:], ps[:, 0:256], ke[:], ALU.mult)
            _tt(g, "pool", kn[:], ke[:].unsqueeze(1).to_broadcast([128, 2, 256]), CI2, ALU.mult)
            _chk("gla_%s_%d" % ("f", z))
            ps = _psum(g)
            _mmk(g, ps[:, 0:512], [(at[k], win[:, k, C_GV:C_GV + 512]) for k in range(KC)])
            vg = Vg.get()
            _copy(g, "act", vg[:], ps[:, 0:512])
            _chk("gla_%s_%d" % ("g", z))
            ps = _psum(g)
            for h in range(4):
                hp, hb = h // 2, (h % 2) * 64
                _mm(g, ps[:, h * 128:(h + 1) * 128], ki[:, h, :], qd[:, h, :])
            am = AmT.get()
            if "h_dve" not in DBG_SKIP:
                _tt(g, "dve", am[:], ps[:].rearrange("p (h c) -> p h c", c=128), Mz.unsqueeze(1).to_broadcast([128, 4, 128]), ALU.mult)
            else:
                _copy(g, "act", am[:], ps[:].rearrange("p (h c) -> p h c", c=128))
            _chk("gla_%s_%d" % ("h", z))
            if z == 1:
                ps = _psum(g)
                _mmk(g, ps[:, 0:512], [(at[k], win[:, k, C_GG:C_GG + 512]) for k in range(KC)])
                g2 = G2.get()
                _act(g, g2[:], ps[:, 0:512], AF.Silu)
                _tt(g, "pool", g2[:].rearrange("p (h c) -> p h c", c=128), g2[:].rearrange("p (h c) -> p h c", c=128),
                    onorm[:].unsqueeze(1).to_broadcast([128, 4, 128]), ALU.mult)
            _chk("gla_%s_%d" % ("i", z))
            pso = _psum(g, "acc")
            for c in ((0, 1) if z == 0 else (1, 0)):
                cb = c * 64
                psS = _psum(g, "misc")
                for h in range(4):
                    hp, hb = h // 2, (h % 2) * 64
                    _mm(g, pso[cb:cb + 64, h * 128:(h + 1) * 128], qd[:, h, cb:cb + 64], Sbf[:, hp, :], start=True, stop=False)
                    _mm(g, pso[cb:cb + 64, h * 128:(h + 1) * 128], am[:, h, cb:cb + 64], vg[:, h * 128:(h + 1) * 128], start=False, stop=True)
                    _mm(g, psS[hb:hb + 64, hp * 128:(hp + 1) * 128], kn[:, c, h * 64:(h + 1) * 64], vg[:, h * 128:(h + 1) * 128])
                for hp in range(2):
                    _stt(g, "dve", S[:, hp, :], S[:, hp, :], ge[:, hp, c:c + 1], psS[:, hp * 128:(hp + 1) * 128], ALU.mult, ALU.add)
                _copy(g, "pool", Sbf[:], S[:])
            _chk("gla_%s_%d" % ("j", z))
            if z == 0:
                _copy(g, "act", Of[:, t, :], pso[:])
            else:
                o = ob.get()
                _tt(g, "dve", o[:], pso[:], Of[:, t, :], ALU.add)
                sq = sqo.get()
                _tt(g, "pool", sq[:], o[:], o[:], ALU.mult)
                s4 = ss4.get()
                P.op("dve", lambda e: e.tensor_reduce(s4[:], sq[:].rearrange("p (h c) -> p h c", c=128), axis=AX, op=ALU.add), [sq[:]], [s4[:]])
                _rstd_from_sum(g, s4[:], s4[:], 1.0 / 128, g.eps[:])
                o3 = o[:].rearrange("p (h c) -> p h c", c=128)
                _tt(g, "dve", o3, o3, s4[:].unsqueeze(2).to_broadcast([128, 4, 128]), ALU.mult)
                mk = mtok.get()
                _tt(g, "pool", mk[:], o[:], g2[:], ALU.mult)
                for h in range(4):
                    _tr(g, g.psbf[:, h * 128:(h + 1) * 128], mk[:, h * 128:(h + 1) * 128], g.ident_bf[:])
                m_ = mt.get()
                _copy(g, "act", m_[:], g.psbf[:, 0:512].rearrange("p (h c) -> p h c", c=128))
                P.dma(g.dr["MIX"].rearrange("k p t -> p k t")[:, 4:8, ts], m_[:])
            _chk("gla_k_%d" % z)


def _mla_attention(g, L, lat):
    P, dr = g.P, g.dr
    i = L // 2
    with contextlib.ExitStack() as st0:
      QT = _sb(g, "QT", [128, 8, TOK], BF16, st0)
      KT = _sb(g, "KT", [128, 8, TOK], BF16, st0)
      VA = _sb(g, "VA", [128, NT, 8, 128], BF16, st0)
      with contextlib.ExitStack() as st:
        _rope_tables(g, lat, st)
        wuq = _sb(g, "wuq", [128, 3, 768], BF16, st)
        _load_cast(g, wuq[:], dr["mla_w_uq"][i].rearrange("(k p) n -> p k n", p=128))
        wuqr = _sb(g, "wuqr", [128, 3, 768], BF16, st)
        _copy(g, "pool", wuqr[:], wuq[:])
        for k in range(3):
            _swap_halves(g, "pool", wuqr[:, k, :].rearrange("p (h c) -> p h c", c=96)[:, :, 64:96],
                         wuq[:, k, :].rearrange("p (h c) -> p h c", c=96)[:, :, 64:96])
        wukv = _sb(g, "wukv", [128, 2, 1024], BF16, st)
        _load_cast(g, wukv[:], dr["mla_w_ukv"][i].rearrange("(k p) n -> p k n", p=128))
        wv = _sb(g, "wv", [128, 2, 512], BF16, st)
        for k in range(2):
            _copy(g, "pool", wv[:, k, :].rearrange("p (h c) -> p h c", c=64), wukv[:, k, :].rearrange("p (h c) -> p h c", c=128)[:, :, 64:128])
        _memset(g, "pool", VA[:, :, :, 64:128].rearrange("p t h c -> p (t h) c"), 1.0)
        t1p = Pool_(g, "t1a", [128, 512], F32, 2, st)
        for bi, (s0, N) in enumerate(BLOCKS):
            sl = slice(s0, s0 + N)
            p0 = s0 - NCTX
            for h in range(8):
                ps = _psum(g)
                _mmk(g, ps[0:96, :N], [(wuq[:, fc, h * 96:(h + 1) * 96], lat.cqn[:, fc, sl]) for fc in range(3)])
                _copy(g, "act", QT[0:64, h, sl], ps[0:64, :N])
                if bi == 0:
                    _copy(g, "act", QT[64:96, h, sl], ps[64:96, :N])
                else:
                    ps2 = _psum(g)
                    _mmk(g, ps2[0:96, :N], [(wuqr[:, fc, h * 96:(h + 1) * 96], lat.cqn[:, fc, sl]) for fc in range(3)])
                    t1, t2 = t1p.get(), t1p.get()
                    _tt(g, "dve", t1[64:96, :N], ps[64:96, :N], lat.ropeC[64:96, p0:p0 + N], ALU.mult)
                    _tt(g, "dve", t2[64:96, :N], ps2[64:96, :N], lat.ropeS[64:96, p0:p0 + N], ALU.mult)
                    _tt(g, "pool", QT[64:96, h, sl], t1[64:96, :N], t2[64:96, :N], ALU.add)
                ps = _psum(g)
                _mmk(g, ps[0:64, :N], [(wukv[:, c, h * 128:h * 128 + 64], lat.ckvn[:, c, sl]) for c in range(2)])
                _copy(g, "act", KT[0:64, h, sl], ps[0:64, :N])
                _copy(g, "pool", KT[64:96, h, sl], lat.kpeT[64:96, sl])
            for tt_ in range(N // 128):
                t = s0 // 128 + tt_
                ps = _psum(g)
                _mmk(g, ps[:, 0:512], [(lat.ckvn[:, c, t * 128:(t + 1) * 128], wv[:, c, :]) for c in range(2)])
                _copy(g, "dve", VA[:, t, :, 0:64], ps[:, 0:512].rearrange("p (h c) -> p h c", c=64))
        P.barrier()
      with contextlib.ExitStack() as st:
        scale = float(96 ** -0.5)
        pT = Pool_(g, "pT", [128, 512], BF16, 3, st)
        rec = Pool_(g, "rec", [128, 512], F32, 2, st)
        atile = Pool_(g, "atile", [128, 512], BF16, 2, st)
        for bi, (s0, N) in enumerate(BLOCKS):
            sl = slice(s0, s0 + N)
            nk = 2 if bi == 0 else NT
            for h in range(8):
                if h % 2 == 0:
                    at_ = atile.get()
                pso = _psum(g, "acc")
                for kc in range(nk):
                    pss = _psum(g)
                    _mm(g, pss[:, :N], KT[0:96, h, kc * 128:(kc + 1) * 128], QT[0:96, h, sl])
                    p_ = pT.get()
                    _act(g, p_[:, :N], pss[:, :N], AF.Exp, scale=scale)
                    _mm(g, pso[:, :N], VA[:, kc, h, :], p_[:, :N], start=(kc == 0), stop=(kc == nk - 1))
                r_ = rec.get()
                P.op("dve", lambda e: e.reciprocal(r_[64:128, :N], pso[64:128, :N]), [pso[64:128, :N]], [r_[64:128, :N]])
                hb = (h % 2) * 64
                _tt(g, "dve", at_[hb:hb + 64, :N], pso[0:64, :N], r_[64:128, :N], ALU.mult)
                if h % 2 == 1:
                    P.dma(g.dr["MIX"].rearrange("k p t -> p k t")[:, h // 2, sl], at_[:, :N])


O_Q, O_K, O_V, O_G, O_A, O_B = 0, 512, 1024, 2048, 3072, 3088


def _odd_mixer(g, L, aT, need_ctx, AB):
    P, dr = g.P, g.dr
    i = L // 2
    wsrc = dr["gdn_w_in"][i].rearrange("(k p) n -> p k n", p=128)
    with contextlib.ExitStack() as st:
        blockones = _sb(g, "blockones", [128, 128], BF16, st)
        _memset(g, "pool", blockones[:], 0.0)
        _memset(g, "pool", blockones[0:64, 0:64], 1.0)
        _memset(g, "pool", blockones[64:128, 64:128], 1.0)
        wab = AB
        if AB is None:
            wab = _sb(g, "wab", [128, KC, 32], BF16, st)
            _load_cast(g, wab[:], wsrc[:, :, O_A:O_A + 32])
        dtb = _sb(g, "dtb", [128, 16], F32, st)
        P.dma(dtb[:], dr["gdn_dt_bias"][i].rearrange("z h -> (z h)").partition_broadcast(128))
        negA = _sb(g, "negA", [128, 16], F32, st)
        P.dma(negA[:], dr["gdn_a_log"][i].rearrange("z h -> (z h)").partition_broadcast(128))
        _act(g, negA[:], negA[:], AF.Exp)
        _ts(g, "dve", negA[:], negA[:], -1.0, None, ALU.mult)
        onorm = _sb(g, "onormd", [128, 128], F32, st)
        P.dma(onorm[:], dr["gdn_o_norm"][i].partition_broadcast(128))
        g.bmask = _sb(g, "bmask", [128, 12, 128], BF16, st)
        _load_cast(g, g.bmask[:], dr["bmask"].rearrange("p (m c) -> p m c", c=128))
        for hf in range(2):
            with contextlib.ExitStack() as sh:
                qT = _sb(g, "gqT", [128, 2, TOK], BF16, sh)
                kT = _sb(g, "gkT", [128, 2, TOK], BF16, sh)
                Vt = _sb(g, "gVt", [128, NT, 512], BF16, sh)
                Kt = _sb(g, "gKt", [128, NT, 256], BF16, sh)
                Of = _sb(g, "gOf", [128, NT, 512], F32, sh)
                wg = _sb(g, "gwg", [128, KC, 512], BF16, sh)
                _load_cast(g, wg[:], wsrc[:, :, O_G + hf * 512:O_G + (hf + 1) * 512])
                with contextlib.ExitStack() as s1:
                    _gdn_conv_stage(g, i, hf, aT, wsrc, qT, kT, Vt, Kt, blockones, s1)
                    P.barrier()
                with contextlib.ExitStack() as s2:
                    _gdn_scan(g, hf, aT, qT, kT, Vt, Kt, Of, wg, wab, dtb, negA, onorm, need_ctx, s2)
                    P.barrier()


def _gdn_conv_stage(g, i, hf, aT, wsrc, qT, kT, Vt, Kt, blockones, st):
    P = g.P
    xl = _sb(g, "xl", [128, SEQ + 2], F32, st)
    xc = _sb(g, "xc", [128, NCTX + 2], F32, st)
    for buf, n in ((xl, SEQ), (xc, NCTX)):
        _memset(g, "pool", buf[:, 0:1], 0.0)
        _memset(g, "pool", buf[:, n + 1:n + 2], 0.0)
    tcv = _sb(g, "tcv", [128, SEQ], F32, st)
    ybf = _sb(g, "ybf", [128, TOK], BF16, st)
    yf = _sb(g, "yf32", [128, 512], F32, st)
    sqp = _sb(g, "sqc", [128, 512], BF16, st)
    rn = _sb(g, "rnc", [128, 512], F32, st)
    wtp = Pool_(g, "wcv", [128, KC, 128], BF16, 2, st)
    cols = _col(g, "conv%d" % i, 48)
    chunks = [("q", O_Q // 128 + 2 * hf + j, j) for j in range(2)] + [("k", O_K // 128 + 2 * hf + j, j) for j in range(2)] + \
             [("v", O_V // 128 + 4 * hf + j, j) for j in range(4)]
    for kind, cc, j in chunks:
        wt = wtp.get()
        _load_cast(g, wt[:], wsrc[:, :, cc * 128:(cc + 1) * 128])
        for bi, (s0, N) in enumerate(BLOCKS):
            ps = _psum(g)
            _mmk(g, ps[:, :N], [(wt[:, k, :], aT[:, k, s0:s0 + N]) for k in range(KC)])
            if bi == 0:
                _copy(g, "act", xc[:, 1:1 + N], ps[:, :N])
            else:
                p0 = s0 - NCTX
                _copy(g, "act", xl[:, 1 + p0:1 + p0 + N], ps[:, :N])
        w0, w1, w2 = cols[:, cc:cc + 1], cols[:, 16 + cc:17 + cc], cols[:, 32 + cc:33 + cc]
        for buf, n, o0 in ((xc, NCTX, 0), (xl, SEQ, NCTX)):
            t = tcv[:, 0:n]
            _ts(g, "pool", t, buf[:, 1:1 + n], w1, None, ALU.mult)
            _stt(g, "dve", t, buf[:, 0:n], w0, t, ALU.mult, ALU.add)
            _stt(g, "dve", t, buf[:, 2:2 + n], w2, t, ALU.mult, ALU.add)
            if kind == "v":
                _act(g, ybf[:, o0:o0 + n], t, AF.Silu)
            else:
                for c0 in range(0, n, 512):
                    m = min(512, n - c0)
                    _act(g, yf[:, :m], t[:, c0:c0 + m], AF.Silu)
                    _tt(g, "pool", sqp[:, :m], yf[:, :m], yf[:, :m], ALU.mult)
                    ps = _psum(g)
                    _mm(g, ps[:, :m], blockones[:], sqp[:, :m])
                    _rstd_from_sum(g, rn[:, :m], ps[:, :m], 1.0, g.eps[:])
                    dst = (qT if kind == "q" else kT)[:, j, o0 + c0:o0 + c0 + m]
                    if kind == "q":
                        _stt(g, "dve", dst, yf[:, :m], 0.125, rn[:, :m], ALU.mult, ALU.mult)
                    else:
                        _tt(g, "dve", dst, yf[:, :m], rn[:, :m], ALU.mult)
        if kind in ("k", "v"):
            src = kT[:, j, :] if kind == "k" else ybf[:]
            for t4 in range(0, NT, 4):
                nt = min(4, NT - t4)
                for q in range(nt):
                    _tr(g, g.psbf[:, q * 128:(q + 1) * 128], src[:, (t4 + q) * 128:(t4 + q + 1) * 128], g.ident_bf[:])
                dst = (Kt[:, t4:t4 + nt, j * 128:(j + 1) * 128] if kind == "k" else Vt[:, t4:t4 + nt, j * 128:(j + 1) * 128])
                _copy(g, "act", dst, g.psbf[:, 0:nt * 128].rearrange("p (q c) -> p q c", c=128))


def _gdn_scan(g, hf, aT, qT, kT, Vt, Kt, Of, wg, wab, dtb, negA, onorm, need_ctx, st):
    P = g.P
    S = _sb(g, "dS", [128, 2, 128], F32, st)
    Sbf = _sb(g, "dSbf", [128, 2, 128], BF16, st)
    sm = Pool_(g, "dsm", [128, 32], F32, 2, st)
    larep = _sb(g, "larep", [128, 4, 64], F32, st)
    Eg = _sb(g, "dEg", [128, 2, 128], F32, st)
    gend = Pool_(g, "dgend", [128, 2, 2], F32, 2, st)
    qdecT = _sb(g, "dqdec", [128, 4, 128], BF16, st)
    _memset(g, "pool", qdecT[:], 0.0)
    kz = _sb(g, "dkz", [128, 4, 128], BF16, st)
    _memset(g, "pool", kz[:], 0.0)
    CI4 = _cst(g, "CI").unsqueeze(2).to_broadcast([128, 2, 256])
    rhsd = _sb(g, "rhsd", [128, 4, 128], F32, st)
    dec = _sb(g, "ddec", [128, 4, 128], F32, st)
    tG = _sb(g, "dtG", [128, 4, 128], F32, st)
    qkm = _sb(g, "dqkm", [128, 4, 128], BF16, st)
    Mp = Pool_(g, "dM", [128, 4, 128], BF16, 2, st)
    Np = Pool_(g, "dN", [128, 4, 128], BF16, 2, st)
    Tnp = Pool_(g, "dTn", [128, 4, 128], BF16, 2, st)
    NQ = Pool_(g, "dNQ", [128, 4, 256], BF16, 2, st)
    vb = _sb(g, "dvb", [128, 512], BF16, st)
    kbg = _sb(g, "dkbg", [128, 4, 64], BF16, st)
    kend = _sb(g, "dkend", [128, 4, 64], BF16, st)
    kend2 = _sb(g, "dkend2", [128, 2, 256], BF16, st)
    u = _sb(g, "du", [128, 512], F32, st)
    wT = _sb(g, "dwT", [128, 4, 128], BF16, st)
    _memset(g, "pool", wT[:], 0.0)
    vnew = _sb(g, "dvnew", [128, 512], BF16, st)
    _memset(g, "pool", vnew[:], 0.0)
    G2 = _sb(g, "dG2", [128, 512], F32, st)
    ob = _sb(g, "dob", [128, 512], F32, st)
    sqo = _sb(g, "dsqo", [128, 512], F32, st)
    mtok = _sb(g, "dmtok", [128, 512], BF16, st)
    mt = Pool_(g, "dmt", [128, 4, 128], BF16, 2, st)
    ident_bc = g.ident.unsqueeze(1).to_broadcast([128, 4, 128])
    for z in range(2):
        R_ = _cst(g, "gdnRf" if z == 0 else "gdnRb")
        Mz = _cst(g, "Mf" if z == 0 else "Mb")
        Sz = _cst(g, "Sf" if z == 0 else "Sb")
        Sz_o = _cst(g, "Sb" if z == 0 else "Sf")
        _memset(g, "pool", S[:], 0.0)
        _memset(g, "pool", Sbf[:], 0.0)
        for t in (GLA_FWD if z == 0 else GLA_BWD):
            ts = slice(t * 128, (t + 1) * 128)
            at = [aT[:, k, ts] for k in range(KC)]
            want_out = need_ctx or t >= 2
            if USE_AB32:
                ps = wab[:, t, :]
            else:
                ps = _psum(g)
                _mmk(g, ps[:, 0:32], [(at[k], wab[:, k, :]) for k in range(KC)])
            s_ = sm.get()
            ca = z * 8 + hf * 4
            _tt(g, "dve", s_[:, 0:4], ps[:, ca:ca + 4], dtb[:, ca:ca + 4], ALU.add)
            _act(g, s_[:, 0:4], s_[:, 0:4], AF.Exp)
            _act(g, s_[:, 0:4], s_[:, 0:4], AF.Ln, scale=1.0, bias=1.0)
            _tt(g, "dve", s_[:, 0:4], s_[:, 0:4], negA[:, ca:ca + 4], ALU.mult)
            _act(g, s_[:, 4:8], ps[:, 16 + ca:16 + ca + 4], AF.Sigmoid)
            la, beta = s_[:, 0:4], s_[:, 4:8]
            _copy(g, "pool", larep[:], la.unsqueeze(2).to_broadcast([128, 4, 64]))
            ps = _psum(g)
            lr = larep[:].rearrange("p h d -> p (h d)")
            for hp in range(2):
                _mm(g, ps[:, hp * 130:(hp + 1) * 130], lr[:, hp * 128:(hp + 1) * 128], R_)
            pv = ps[:, 0:260].rearrange("p (h c) -> p h c", c=130)
            ge = gend.get()
            _act(g, Eg[:], pv[:, :, 0:128], AF.Exp)
            _act(g, ge[:], pv[:, :, 128:130], AF.Exp)
            for (lo, hi, par) in ((0, 64, 0), (64, 128, 1)):
                _tt(g, "dve", qdecT[lo:hi, par::2, :], qT[lo:hi, :, ts], Eg[lo:hi, :, :], ALU.mult)
                _copy(g, "pool", kz[lo:hi, par::2, :], kT[lo:hi, :, ts])
            ps = _psum(g)
            _mm(g, ps[:, 0:4], Mz, la)
            _mm(g, ps[:, 4:8], Sz, la)
            _act(g, s_[:, 8:16], ps[:, 0:8], AF.Exp)
            eg, ekend = s_[:, 8:12], s_[:, 12:16]
            _tt(g, "dve", s_[:, 16:20], beta, eg, ALU.mult)
            bg = s_[:, 16:20]
            kt3 = Kt[:, t, :].rearrange("p (h d) -> p h d", d=64)
            _tt(g, "pool", kend[:], kt3, ekend.unsqueeze(2).to_broadcast([128, 4, 64]), ALU.mult)
            _tt(g, "pool", kend2[:], kend[:].rearrange("p h d -> p (h d)").unsqueeze(1).to_broadcast([128, 2, 256]), CI4, ALU.mult)
            _tt(g, "pool", kbg[:], kt3, bg.unsqueeze(2).to_broadcast([128, 4, 64]), ALU.mult)
            _tt(g, "pool", vb[:].rearrange("p (h c) -> p h c", c=128), Vt[:, t, :].rearrange("p (h c) -> p h c", c=128),
                beta.unsqueeze(2).to_broadcast([128, 4, 128]), ALU.mult)
            la_bc = la.unsqueeze(2).to_broadcast([128, 4, 128])
            _tt(g, "pool", rhsd[:], Mz.unsqueeze(1).to_broadcast([128, 4, 128]), la_bc, ALU.mult)
            ps = _psum(g)
            _mm(g, ps[:, 0:512], Sz, rhsd[:].rearrange("p h c -> p (h c)"))
            _act(g, dec[:].rearrange("p h c -> p (h c)"), ps[:, 0:512], AF.Exp)
            ps = _psum(g)
            for h in range(4):
                hp, hb = h // 2, (h % 2) * 64
                _mm(g, ps[:, h * 128:(h + 1) * 128], kz[:, h, :], qT[:, hp, ts])
            _tt(g, "dve", tG[:], ps[:].rearrange("p (h c) -> p h c", c=128), Mz.unsqueeze(1).to_broadcast([128, 4, 128]), ALU.mult)
            _tt(g, "dve", qkm[:], tG[:], dec[:], ALU.mult)
            _tt(g, "pool", rhsd[:], Sz.unsqueeze(1).to_broadcast([128, 4, 128]), la_bc, ALU.mult)
            ps = _psum(g)
            _mm(g, ps[:, 0:512], Mz, rhsd[:].rearrange("p h c -> p (h c)"))
            _act(g, dec[:].rearrange("p h c -> p (h c)"), ps[:, 0:512], AF.Exp)
            _tt(g, "pool", dec[:], dec[:], beta.unsqueeze(2).to_broadcast([128, 4, 128]), ALU.mult)
            ps = _psum(g)
            for h in range(4):
                hp, hb = h // 2, (h % 2) * 64
                _mm(g, ps[:, h * 128:(h + 1) * 128], kz[:, h, :], kz[:, h, :])
            _tt(g, "dve", tG[:], ps[:].rearrange("p (h c) -> p h c", c=128), Sz.unsqueeze(1).to_broadcast([128, 4, 128]), ALU.mult)
            M_ = Mp.get()
            _tt(g, "dve", M_[:], tG[:], dec[:], ALU.mult)
            for h in range(4):
                _tr(g, g.psbf[:, h * 128:(h + 1) * 128], M_[:, h, :], g.ident_bf[:])
            N_ = Np.get()
            _copy(g, "act", N_[:], g.psbf[:, 0:512].rearrange("p (h c) -> p h c", c=128))
            TT = NQ.get()
            for lv, bsz in enumerate((1, 2, 4, 8, 16, 32)):
                mk_ = g.bmask[:, (lv if z == 0 else 6 + lv), :].unsqueeze(1).to_broadcast([128, 4, 128])
                TT2 = NQ.get()
                if lv == 0:
                    _tt(g, "dve", tG[:], N_[:], mk_, ALU.mult)
                    _tt(g, "pool", TT2[:, :, 128:256], ident_bc, tG[:], ALU.subtract)
                else:
                    ps = _psum(g)
                    for h in range(4):
                        _mm(g, ps[:, h * 128:(h + 1) * 128], M_[:, h, :], TT[:, h, 128:256])
                    _copy(g, "act", TT[:, :, 0:128], ps[:].rearrange("p (h c) -> p h c", c=128))
                    ps = _psum(g)
                    for h in range(4):
                        _mm(g, ps[:, h * 128:(h + 1) * 128], Tn[:, h, :], TT[:, h, 0:128])
                    _tt(g, "dve", tG[:], ps[:].rearrange("p (h c) -> p h c", c=128), mk_, ALU.mult)
                    _tt(g, "pool", TT2[:, :, 128:256], TT[:, :, 128:256], tG[:], ALU.subtract)
                TT = TT2
                if bsz != 32:
                    for h in range(4):
                        _tr(g, g.psbf[:, h * 128:(h + 1) * 128], TT[:, h, 128:256], g.ident_bf[:])
                    Tn = Tnp.get()
                    _copy(g, "act", Tn[:], g.psbf[:, 0:512].rearrange("p (h c) -> p h c", c=128))
            nq = TT
            ps = _psum(g)
            for h in range(4):
                _mm(g, ps[:, h * 128:(h + 1) * 128], nq[:, h, 128:256], vb[:, h * 128:(h + 1) * 128])
            _copy(g, "act", u[:], ps[:])
            ps = _psum(g)
            for h in range(4):
                hp, hb = h // 2, (h % 2) * 64
                _mm(g, ps[hb:hb + 64, hp * 128:(hp + 1) * 128], kbg[:, h, :], nq[:, h, 128:256])
            for (lo, hi, par) in ((0, 64, 0), (64, 128, 1)):
                _copy(g, "act", wT[lo:hi, par::2, :], ps[lo:hi, 0:256].rearrange("p (h c) -> p h c", c=128))
            if z == 1 and want_out:
                ps = _psum(g)
                _mmk(g, ps[:, 0:512], [(at[k], wg[:, k, :]) for k in range(KC)])
                _act(g, G2[:], ps[:, 0:512], AF.Silu)
                _tt(g, "pool", G2[:].rearrange("p (h c) -> p h c", c=128), G2[:].rearrange("p (h c) -> p h c", c=128),
                    onorm[:].unsqueeze(1).to_broadcast([128, 4, 128]), ALU.mult)
            pso = _psum(g, "acc")
            for c in ((0, 1) if z == 0 else (1, 0)):
                cb = c * 64
                psw = _psum(g)
                for h in range(4):
                    hp, hb = h // 2, (h % 2) * 64
                    _mm(g, psw[cb:cb + 64, h * 128:(h + 1) * 128], wT[:, h, cb:cb + 64], Sbf[:, hp, :])
                _tt(g, "dve", vnew[cb:cb + 64, :], u[cb:cb + 64, :], psw[cb:cb + 64, :], ALU.subtract)
                psS = _psum(g, "misc")
                for h in range(4):
                    hp, hb = h // 2, (h % 2) * 64
                    if want_out:
                        _mm(g, pso[cb:cb + 64, h * 128:(h + 1) * 128], qdecT[:, h, cb:cb + 64], Sbf[:, hp, :], start=True, stop=False)
                        _mm(g, pso[cb:cb + 64, h * 128:(h + 1) * 128], qkm[:, h, cb:cb + 64], vnew[:, h * 128:(h + 1) * 128], start=False, stop=True)
                    _mm(g, psS[hb:hb + 64, hp * 128:(hp + 1) * 128], kend2[:, c, h * 64:(h + 1) * 64], vnew[:, h * 128:(h + 1) * 128])
                for hp in range(2):
                    _stt(g, "dve", S[:, hp, :], S[:, hp, :], ge[:, hp, c:c + 1], psS[:, hp * 128:(hp + 1) * 128], ALU.mult, ALU.add)
                _copy(g, "pool", Sbf[:], S[:])
            if not want_out:
                continue
            if z == 0:
                _copy(g, "act", Of[:, t, :], pso[:])
            else:
                _tt(g, "dve", ob[:], pso[:], Of[:, t, :], ALU.add)
                _tt(g, "pool", sqo[:], ob[:], ob[:], ALU.mult)
                P.op("dve", lambda e: e.tensor_reduce(s_[:, 24:28], sqo[:].rearrange("p (h c) -> p h c", c=128), axis=AX, op=ALU.add), [sqo[:]], [s_[:, 24:28]])
                _rstd_from_sum(g, s_[:, 24:28], s_[:, 24:28], 1.0 / 128, g.eps[:])
                o3 = ob[:].rearrange("p (h c) -> p h c", c=128)
                _tt(g, "dve", o3, o3, s_[:, 24:28].unsqueeze(2).to_broadcast([128, 4, 128]), ALU.mult)
                _tt(g, "pool", mtok[:], ob[:], G2[:], ALU.mult)
                for h in range(4):
                    _tr(g, g.psbf[:, h * 128:(h + 1) * 128], mtok[:, h * 128:(h + 1) * 128], g.ident_bf[:])
                m_ = mt.get()
                _copy(g, "act", m_[:], g.psbf[:, 0:512].rearrange("p (h c) -> p h c", c=128))
                P.dma(g.dr["MIX"].rearrange("k p t -> p k t")[:, 4 * hf:4 * hf + 4, ts], m_[:])


def make_in_maps(inputs):
    cst, rope, bmask = host_consts()
    maps = []
    for b in range(8):
        m = {
            "x": np.ascontiguousarray(inputs["x"][b], dtype=np.float32),
            "ctx": np.ascontiguousarray(inputs["ctx"][b], dtype=np.float32),
            "cc": np.ascontiguousarray(np.concatenate([np.asarray(inputs["c"][b]).reshape(8, 128),
                                                       np.asarray(inputs["c_ctx"]).reshape(8, 128)], 0), dtype=np.float32),
            "cst": cst, "rope": rope, "bmask": bmask,
        }
        for n in WEIGHT_NAMES:
            m[n] = np.ascontiguousarray(inputs[n], dtype=np.float32)
        maps.append(m)
    return maps


_PROG_CACHE = {}


def kernel(**inputs):
    if "nc" not in _PROG_CACHE:
        _PROG_CACHE["nc"] = build_program()[0]
    nc = _PROG_CACHE["nc"]
    in_maps = make_in_maps(inputs)
    res = run_bass_kernel_spmd(nc, in_maps, core_ids=list(range(8)))
    out = np.stack([np.asarray(res.results[b]["out"], dtype=np.float32) for b in range(8)], 0)
    return out
```

```python
import contextlib
import numpy as np
import concourse.bass as bass
import concourse.mybir as mybir
from concourse.bass_utils import run_bass_kernel_spmd

F32 = mybir.dt.float32
BF16 = mybir.dt.bfloat16
AF = mybir.ActivationFunctionType
ALU = mybir.AluOpType

SAME_ENGINE_SYNC = True
SEM_ROTATE = 30000
DBG_SKIP = set()
USE_AB32 = False
DBG_STOP = None


class _Stop(Exception):
    pass


def _chk(name):
    if DBG_STOP == name:
        raise _Stop()


class _Eng:
    def __init__(self, name, handle, is_dma_only=False):
        self.name = name
        self.h = handle
        self.sem = None
        self.count = 0
        self.waited = {}
        self.pending = False


class Prog:
    def __init__(self, nc, es, n_dma_sems=40):
        self.nc = nc
        self.es = es
        self.engs = {
            "pe": _Eng("pe", nc.tensor),
            "act": _Eng("act", nc.scalar),
            "dve": _Eng("dve", nc.vector),
            "pool": _Eng("pool", nc.gpsimd),
            "sp": _Eng("sp", nc.sync),
        }
        self.semid = 0
        for e in self.engs.values():
            e.sem = self._newsem()
        self.dma_sems = [[self._newsem(), 0] for _ in range(n_dma_sems)]
        self.dma_rr = 0
        self.recs = {}
        self.n_inst = 0
        self.n_wait = 0
        self.out_tokens = []

    def _newsem(self):
        self.semid += 1
        return self.es.enter_context(self.nc.semaphore("s%d" % self.semid))

    @staticmethod
    def _box(ap):
        t = ap.tensor
        name = t.name
        shape = tuple(t.shape)
        pat = ap.ap
        off = ap.offset
        sp = str(ap.space) if not isinstance(ap.space, str) else ap.space
        if "DRAM" in sp.upper() or "HBM" in sp.upper() or "Dram" in sp:
            W = shape[-1]
            r0, c0 = off // W, off % W
            r1, c1 = r0, c0
            ok = True
            for (s, c) in pat:
                s = abs(s)
                if c <= 1:
                    continue
                if s % W == 0:
                    r1 += (c - 1) * (s // W)
                elif s * (c - 1) < W:
                    c1 += (c - 1) * s
                else:
                    ok = False
            if (not ok) or c1 >= W:
                ext = 1
                for (s, c) in pat:
                    ext += (c - 1) * abs(s)
                return (name, 0, 1 << 30, 0, 1 << 30) if True else None
            return (name, r0, r1 + 1, c0, c1 + 1)
        fsz = 1
        for s in shape[1:]:
            fsz *= s
        pstep, pcnt = pat[0]
        if pstep != fsz and pcnt > 1:
            return (name, 0, 128, 0, fsz)
        p0 = off // fsz
        f0 = off % fsz
        ext = 1
        for (s, c) in pat[1:]:
            ext += (c - 1) * abs(s)
        if name.startswith("ps"):
            return (name, (p0 // 32) * 32, ((p0 + pcnt + 31) // 32) * 32, 0, fsz)
        return (name, p0, p0 + pcnt, f0, f0 + ext)

    def _deps(self, eng, reads, writes, is_dma):
        deps = {}
        boxes_r = [self._box(a) for a in reads]
        boxes_w = [self._box(a) for a in writes]
        for kind, boxes in (("r", boxes_r), ("w", boxes_w)):
            for b in boxes:
                lst = self.recs.get(b[0])
                if not lst:
                    continue
                for rec in lst:
                    if kind == "r" and rec[4] == "r":
                        continue
                    if rec[0] >= b[2] or b[1] >= rec[1] or rec[2] >= b[4] or b[3] >= rec[3]:
                        continue
                    if (not is_dma) and (not rec[8]) and rec[7] == eng.name and (eng.name == "pe" or not SAME_ENGINE_SYNC):
                        continue
                    s, v = rec[5], rec[6]
                    k = id(s)
                    if k not in deps or deps[k][1] < v:
                        deps[k] = (s, v)
        return deps, boxes_r, boxes_w

    def _record(self, eng, boxes_r, boxes_w, sem, val, is_dma):
        for b in boxes_w:
            lst = self.recs.setdefault(b[0], [])
            lst[:] = [r for r in lst if not (b[1] <= r[0] and r[1] <= b[2] and b[3] <= r[2] and r[3] <= b[4])]
            lst.append([b[1], b[2], b[3], b[4], "w", sem, val, eng.name, is_dma])
        for b in boxes_r:
            lst = self.recs.setdefault(b[0], [])
            hit = False
            if not is_dma:
                for r in lst:
                    if r[4] == "r" and r[7] == eng.name and not r[8] and r[0] == b[1] and r[1] == b[2] and r[2] == b[3] and r[3] == b[4]:
                        r[5], r[6] = sem, val
                        hit = True
                        break
            if not hit:
                lst.append([b[1], b[2], b[3], b[4], "r", sem, val, eng.name, is_dma])
                if len(lst) > 96:
                    self._compact(lst)

    @staticmethod
    def _compact(lst):
        keep = [r for r in lst if r[4] == "w"]
        merged = {}
        for r in lst:
            if r[4] != "r":
                continue
            k = (r[7], id(r[5]), r[8])
            m = merged.get(k)
            if m is None:
                merged[k] = list(r)
            else:
                m[0] = min(m[0], r[0]); m[1] = max(m[1], r[1]); m[2] = min(m[2], r[2]); m[3] = max(m[3], r[3])
                m[6] = max(m[6], r[6])
        lst[:] = keep + list(merged.values())

    def _emit_waits(self, eng, deps):
        for (s, v) in deps.values():
            k = id(s)
            if eng.waited.get(k, 0) >= v:
                continue
            eng.h.wait_ge(s, v)
            eng.waited[k] = v
            self.n_wait += 1

    def op(self, en, fn, reads=(), writes=(), inc=True):
        eng = self.engs[en]
        deps, br, bw = self._deps(eng, reads, writes, False)
        self._emit_waits(eng, deps)
        ins = fn(eng.h)
        self.n_inst += 1
        if inc:
            if eng.count >= SEM_ROTATE and not eng.pending:
                eng.sem = self._newsem()
                eng.count = 0
            eng.count += 1
            ins.then_inc(eng.sem, 1)
            tok = (eng.sem, eng.count)
            eng.pending = False
        else:
            tok = (eng.sem, eng.count + 1)
            eng.pending = True
        self._record(eng, br, bw, tok[0], tok[1], False)
        return ins

    def dma(self, out, in_, q="sp", is_output=False, **kw):
        eng = self.engs[q]
        deps, br, bw = self._deps(eng, [in_], [out], True)
        slot = self.dma_sems[self.dma_rr]
        self.dma_rr = (self.dma_rr + 1) % len(self.dma_sems)
        if slot[1] > 0:
            deps[id(slot[0])] = (slot[0], slot[1])
        self._emit_waits(eng, deps)
        slot[1] += 16
        eng.h.dma_start(out=out, in_=in_, **kw).then_inc(slot[0], 16)
        self.n_inst += 1
        self._record(eng, br, bw, slot[0], slot[1], True)
        if is_output:
            self.out_tokens.append((slot[0], slot[1]))

    def finish(self):
        eng = self.engs["sp"]
        deps = {}
        for slot in self.dma_sems:
            if slot[1] > 0:
                deps[id(slot[0])] = (slot[0], slot[1])
        self._emit_waits(eng, deps)

    def barrier(self):
        deps = {}
        for e in self.engs.values():
            if e.count > 0:
                deps[id(e.sem)] = (e.sem, e.count)
        for slot in self.dma_sems:
            if slot[1] > 0:
                deps[id(slot[0])] = (slot[0], slot[1])
        for e in self.engs.values():
            d = {k: v for k, v in deps.items() if k != id(e.sem)}
            self._emit_waits(e, d)
        self.recs = {}


D = 1024
KC = 8
SEQ = 2048
NCTX = 256
TOK = NCTX + SEQ
NT = TOK // 128
DEPTH = 4
DFF = 4096
EPS = 1e-6
BLOCKS = [(0, 256), (256, 512), (768, 512), (1280, 512), (1792, 512)]
EVEN_IN = 2240
ODD_IN = 3104
AX = mybir.AxisListType.X

WEIGHT_NAMES = ["ada_w", "ada_b", "norm1_w", "norm2_w", "mlp_w1", "mlp_w2", "even_w_in", "mla_q_norm", "mla_w_uq",
                "mla_kv_norm", "mla_w_ukv", "gla_gate_up", "gla_gate_bias", "gla_o_norm", "even_w_out", "gdn_w_in",
                "gdn_conv_w", "gdn_a_log", "gdn_dt_bias", "gdn_o_norm", "gdn_w_out", "final_norm"]
WEIGHT_SHAPES = {
    "ada_w": [4, 1024, 6144], "ada_b": [4, 6144], "norm1_w": [4, 1024], "norm2_w": [4, 1024],
    "mlp_w1": [4, 1024, 4096], "mlp_w2": [4, 4096, 1024], "even_w_in": [2, 1024, 2240], "mla_q_norm": [2, 384],
    "mla_w_uq": [2, 384, 768], "mla_kv_norm": [2, 256], "mla_w_ukv": [2, 256, 1024], "gla_gate_up": [2, 2, 16, 256],
    "gla_gate_bias": [2, 2, 256], "gla_o_norm": [2, 128], "even_w_out": [2, 1024, 1024], "gdn_w_in": [2, 1024, 3104],
    "gdn_conv_w": [2, 3, 2048], "gdn_a_log": [2, 2, 8], "gdn_dt_bias": [2, 2, 8], "gdn_o_norm": [2, 128],
    "gdn_w_out": [2, 1024, 1024], "final_norm": [1024],
}


def host_consts():
    i = np.arange(128)
    same = (i[:, None] // 64) == (i[None, :] // 64)
    Mf = (same & (i[:, None] <= i[None, :])).astype(np.float32)
    Mb = (same & (i[:, None] >= i[None, :])).astype(np.float32)
    Sf = (same & (i[:, None] > i[None, :])).astype(np.float32)
    Sb = (same & (i[:, None] < i[None, :])).astype(np.float32)
    CI = np.stack([(i < 64), (i >= 64)], 1).astype(np.float32)
    ident = np.eye(128, dtype=np.float32)
    cols = [ident, Mf, Mb, Sf, Sb, CI,
            np.concatenate([Mf, CI], 1) / -16.0, np.concatenate([Mb, CI], 1) / -16.0, Sf / -16.0, Sb / -16.0,
            np.concatenate([Mf, CI], 1), np.concatenate([Mb, CI], 1)]
    cst = np.concatenate(cols, 1).astype(np.float32)
    rows = SEQ // 64
    row = np.repeat(np.arange(rows, dtype=np.float32), 64)
    col = np.tile(np.arange(64, dtype=np.float32), rows)
    inv = (10000.0 ** (-np.arange(0, 16, 2, dtype=np.float32) / 16)).astype(np.float32)
    ang = np.concatenate([row[:, None] * inv, col[:, None] * inv], -1).astype(np.float32)
    cos, sin = np.cos(ang).astype(np.float32), np.sin(ang).astype(np.float32)
    C = np.zeros((32, SEQ), np.float32)
    S = np.zeros((32, SEQ), np.float32)
    for ax in range(2):
        for half in range(2):
            for f in range(8):
                d = ax * 16 + half * 8 + f
                C[d] = cos[:, ax * 8 + f]
                S[d] = (-sin[:, ax * 8 + f]) if half == 0 else sin[:, ax * 8 + f]
    rope = np.stack([C, S], 0)
    r = np.arange(128)
    ms = []
    for b in (1, 2, 4, 8, 16, 32):
        same2 = (r[:, None] // (2 * b)) == (r[None, :] // (2 * b))
        ur = same2 & ((r[:, None] % (2 * b)) < b) & ((r[None, :] % (2 * b)) >= b)
        ms.append(ur.astype(np.float32))
    bmask = np.concatenate(ms + [m.T for m in ms], 1).astype(np.float32)
    return cst, rope, bmask


CST_OFF = {}
_o = 0
for _n, _w in [("ident", 128), ("Mf", 128), ("Mb", 128), ("Sf", 128), ("Sb", 128), ("CI", 2), ("glaRf", 130), ("glaRb", 130),
               ("glaAf", 128), ("glaAb", 128), ("gdnRf", 130), ("gdnRb", 130)]:
    CST_OFF[_n] = (_o, _w)
    _o += _w
CST_W = _o


class Ctx:
    pass


def build_program(n_layers=DEPTH, fake_mixer=False, debug_h=False):
    nc = bass.Bass("TRN2", target_bir_lowering=False)
    g = Ctx()
    g.nc = nc
    dr = {}
    dr["x"] = nc.dram_tensor("x", [SEQ, D], F32, kind="ExternalInput").ap()
    dr["ctx"] = nc.dram_tensor("ctx", [NCTX, D], F32, kind="ExternalInput").ap()
    dr["cc"] = nc.dram_tensor("cc", [16, 128], F32, kind="ExternalInput").ap()
    dr["cst"] = nc.dram_tensor("cst", [128, CST_W], F32, kind="ExternalInput").ap()
    dr["rope"] = nc.dram_tensor("rope", [2, 32, SEQ], F32, kind="ExternalInput").ap()
    dr["bmask"] = nc.dram_tensor("bmask", [128, 12 * 128], F32, kind="ExternalInput").ap()
    for n in WEIGHT_NAMES:
        dr[n] = nc.dram_tensor(n, WEIGHT_SHAPES[n], F32, kind="ExternalInput").ap()
    dr["out"] = nc.dram_tensor("out", [SEQ, D], F32, kind="ExternalOutput").ap()
    dr["H"] = nc.dram_tensor("Hs", [KC, 128, TOK], F32).ap()
    dr["MIX"] = nc.dram_tensor("MIXs", [KC, 128, TOK], BF16).ap()
    if debug_h:
        dr["dbg"] = nc.dram_tensor("dbg", [KC, 128, TOK], F32, kind="ExternalOutput").ap()
    g.dr = dr
    es = contextlib.ExitStack()
    with es:
        P = Prog(nc, es)
        g.P = P
        g.es = es
        g.ps = [es.enter_context(nc.psum_tensor("psb%d" % i, [128, 512], F32)) for i in range(7)]
        g.psbf = es.enter_context(nc.psum_tensor("psbf", [128, 1024], BF16))
        g.ps_rr = 0
        _emit(g, n_layers, fake_mixer, debug_h)
        P.finish()
        g.stats = (P.n_inst, P.n_wait, P.semid)
    return nc, g


_UNIQ = [0]


def _sb(g, name, shape, dt=F32, stack=None):
    _UNIQ[0] += 1
    return (stack or g.es).enter_context(g.nc.sbuf_tensor("sb_%s_%d" % (name, _UNIQ[0]), shape, dt))


def _psum(g, kind="rot"):
    if kind == "rot":
        t = g.ps[g.ps_rr]
        g.ps_rr = (g.ps_rr + 1) % 4
        return t
    if kind == "acc":
        g.ps_acc = 1 - getattr(g, "ps_acc", 0)
        return g.ps[4 + g.ps_acc]
    return g.ps[6]


def _mm(g, out, lhsT, rhs, start=True, stop=True, inc=True):
    g.P.op("pe", lambda e: e.matmul(out, lhsT, rhs, start=start, stop=stop), [lhsT, rhs], [out], inc=inc)


def _mmk(g, out, pairs):
    n = len(pairs)
    for i, (l, r) in enumerate(pairs):
        _mm(g, out, l, r, start=(i == 0), stop=(i == n - 1), inc=(i == n - 1))


def _tr(g, out, in_, ident):
    g.P.op("pe", lambda e: e.transpose(out, in_, ident), [in_, ident], [out])


def _act(g, out, in_, func, scale=1.0, bias=0.0, extra_reads=()):
    rd = [in_] + [a for a in (scale, bias) if not isinstance(a, (int, float))] + list(extra_reads)
    g.P.op("act", lambda e: e.activation(out=out, in_=in_, func=func, bias=bias, scale=scale), rd, [out])


def _tt(g, en, out, in0, in1, op):
    g.P.op(en, lambda e: e.tensor_tensor(out, in0, in1, op=op), [in0, in1], [out])


def _stt(g, en, out, in0, scalar, in1, op0, op1):
    rd = [in0, in1] + ([scalar] if not isinstance(scalar, (int, float)) else [])
    g.P.op(en, lambda e: e.scalar_tensor_tensor(out=out, in0=in0, scalar=scalar, in1=in1, op0=op0, op1=op1), rd, [out])


def _ts(g, en, out, in0, s1, s2, op0, op1=None):
    rd = [in0] + [a for a in (s1, s2) if a is not None and not isinstance(a, (int, float))]
    if op1 is None:
        g.P.op(en, lambda e: e.tensor_scalar(out, in0, s1, None, op0=op0), rd, [out])
    else:
        g.P.op(en, lambda e: e.tensor_scalar(out, in0, s1, s2, op0=op0, op1=op1), rd, [out])


def _copy(g, en, out, in_):
    if en == "act":
        g.P.op("act", lambda e: e.copy(out, in_), [in_], [out])
    else:
        g.P.op(en, lambda e: e.tensor_copy(out, in_), [in_], [out])


def _memset(g, en, out, val):
    g.P.op(en, lambda e: e.memset(out, val), [], [out])


def _load_cast(g, dst, src):
    shp = list(dst.shape)
    if len(shp) == 2:
        pieces = [(dst[:, c:min(c + 1024, shp[1])], src[:, c:min(c + 1024, shp[1])]) for c in range(0, shp[1], 1024)]
    else:
        inner = shp[2]
        assert len(shp) == 3 and inner <= 1024
        step = max(1, 1024 // inner)
        pieces = [(dst[:, a:min(a + step, shp[1]), :], src[:, a:min(a + step, shp[1]), :]) for a in range(0, shp[1], step)]
    for d, s_ in pieces:
        stg = g.stage.get()
        n = 1
        for x in d.shape[1:]:
            n *= x
        if len(d.shape) == 3:
            v = stg[:, 0:n].rearrange("p (a b) -> p a b", b=d.shape[2])
        else:
            v = stg[:, 0:n]
        g.P.dma(v, s_)
        g.cast_rr = 1 - getattr(g, "cast_rr", 0)
        _copy(g, "pool" if g.cast_rr else "act", d, v)


def _rstd_from_sum(g, out_sb, ps_in, inv_n, eps_col):
    _act(g, out_sb, ps_in, AF.Sqrt, scale=inv_n, bias=eps_col)
    g.P.op("dve", lambda e: e.reciprocal(out_sb, out_sb), [out_sb], [out_sb])


class Pool_:
    def __init__(self, g, name, shape, dt, n, stack=None, zero=False):
        self.tiles = [_sb(g, "%s%d" % (name, i), shape, dt, stack) for i in range(n)]
        self.i = 0
        if zero:
            for t in self.tiles:
                _memset(g, "pool", t[:], 0.0)

    def get(self):
        t = self.tiles[self.i]
        self.i = (self.i + 1) % len(self.tiles)
        return t


def _cst(g, name):
    o, w = CST_OFF[name]
    return g.cst[:, o:o + w]


def _setup(g):
    P, dr = g.P, g.dr
    g.cst = _sb(g, "cst", [128, CST_W])
    P.dma(g.cst[:], dr["cst"])
    g.ident = _cst(g, "ident")
    g.ident_bf = _sb(g, "ident_bf", [128, 128], BF16)
    _copy(g, "dve", g.ident_bf[:], g.ident)
    g.ones_bf = _sb(g, "ones_bf", [128, 128], BF16)
    _memset(g, "pool", g.ones_bf[:], 1.0)
    g.ones_f = _sb(g, "ones_f", [128, 128])
    _memset(g, "pool", g.ones_f[:], 1.0)
    g.eps = _sb(g, "eps", [128, 1])
    _memset(g, "pool", g.eps[:], EPS)
    rows = []
    rows.append(("cc", dr["cc"], 16))
    for L in range(DEPTH):
        rows.append(("ada_b%d" % L, dr["ada_b"][L].rearrange("(r c) -> r c", c=128), 48))
        rows.append(("n1w%d" % L, dr["norm1_w"][L].rearrange("(r c) -> r c", c=128), 8))
        rows.append(("n2w%d" % L, dr["norm2_w"][L].rearrange("(r c) -> r c", c=128), 8))
    for i in range(2):
        rows.append(("qn%d" % i, dr["mla_q_norm"][i].rearrange("(r c) -> r c", c=128), 3))
        rows.append(("kvn%d" % i, dr["mla_kv_norm"][i].rearrange("(r c) -> r c", c=128), 2))
        rows.append(("conv%d" % i, dr["gdn_conv_w"][i].rearrange("t (r c) -> (t r) c", c=128), 48))
    rows.append(("fn", dr["final_norm"].rearrange("(r c) -> r c", c=128), 8))
    tot = sum(r[2] for r in rows)
    ntile = (tot + 127) // 128
    g.cols = _sb(g, "cols", [128, ntile * 128])
    g.coloff = {}
    with contextlib.ExitStack() as st:
        rt = [_sb(g, "rowst%d" % i, [128, 128], F32, st) for i in range(ntile)]
        for t in rt:
            _memset(g, "dve", t[:], 0.0)
        r = 0
        for key, ap, n in rows:
            g.coloff[key] = r
            done = 0
            while done < n:
                t, p = divmod(r + done, 128)
                m = min(n - done, 128 - p)
                P.dma(rt[t][p:p + m, :], ap[done:done + m, :])
                done += m
            r += n
        for t in range(ntile):
            ps = _psum(g)
            _tr(g, ps[:, 0:128], rt[t][:], g.ident)
            _copy(g, "dve", g.cols[:, t * 128:(t + 1) * 128], ps[:, 0:128])
        P.barrier()
    g.sc = _sb(g, "sc", [128, KC, 2])
    _act(g, g.sc[:, :, 0], g.cols[:, 0:8], AF.Silu)
    _act(g, g.sc[:, :, 1], g.cols[:, 8:16], AF.Silu)
    g.mod = [_sb(g, "mod%d" % L, [128, 48, 2]) for L in range(DEPTH)]
    g.comb1 = [_sb(g, "comb1_%d" % L, [128, KC, 2]) for L in range(DEPTH)]
    g.comb2 = [_sb(g, "comb2_%d" % L, [128, KC, 2]) for L in range(DEPTH)]
    g.adaw_pool = Pool_(g, "adaw", [128, KC, 128], F32, 2)
    g.stage = Pool_(g, "stage", [128, 1024], F32, 2)


def _col(g, key, n):
    o = g.coloff[key]
    return g.cols[:, o:o + n]


def _adaln_steps(g, L):
    P, dr = g.P, g.dr
    src = dr["ada_w"][L].rearrange("(k p) n -> p k n", p=128)
    for cb in range(48):
        wt = g.adaw_pool.get()
        P.dma(wt[:], src[:, :, cb * 128:(cb + 1) * 128])
        ps = _psum(g)
        _mmk(g, ps[:, 0:2], [(wt[:, k, :], g.sc[:, k, :]) for k in range(KC)])
        bias = _col(g, "ada_b%d" % L, 48)[:, cb:cb + 1]
        _tt(g, "dve", g.mod[L][:, cb, :], ps[:, 0:2], bias.to_broadcast([128, 2]), ALU.add)
        if cb == 47:
            n1 = _col(g, "n1w%d" % L, 8).unsqueeze(2).to_broadcast([128, KC, 2])
            n2 = _col(g, "n2w%d" % L, 8).unsqueeze(2).to_broadcast([128, KC, 2])
            _stt(g, "dve", g.comb1[L][:], g.mod[L][:, 8:16, :], 1.0, n1, ALU.add, ALU.mult)
            _stt(g, "dve", g.comb2[L][:], g.mod[L][:, 32:40, :], 1.0, n2, ALU.add, ALU.mult)
        yield


def _mix_view(g, s0, n):
    return g.dr["MIX"].rearrange("k p t -> p k t")[:, :, s0:s0 + n]


def _h_view(g, s0, n):
    return g.dr["H"].rearrange("k p t -> p k t")[:, :, s0:s0 + n]


def _load_inputs(g):
    P, dr = g.P, g.dr
    with contextlib.ExitStack() as st:
        xin = Pool_(g, "xin", [128, D], F32, 2, st)
        hb = Pool_(g, "hb0_", [128, KC, 128], F32, 2, st)
        for t in range(NT):
            src = dr["ctx"][t * 128:(t + 1) * 128, :] if t < 2 else dr["x"][(t - 2) * 128:(t - 1) * 128, :]
            xt = xin.get()
            P.dma(xt[:], src)
            ht = hb.get()
            for half in range(2):
                ps = _psum(g)
                for q in range(4):
                    k = half * 4 + q
                    _tr(g, ps[:, q * 128:(q + 1) * 128], xt[:, k * 128:(k + 1) * 128], g.ident)
                _copy(g, "act" if half == 0 else "dve", ht[:, half * 4:(half + 1) * 4, :],
                      ps[:].rearrange("p (q t) -> p q t", t=128))
            P.dma(_h_view(g, t * 128, 128), ht[:])
        P.barrier()


def _norm_tmps(g, st):
    return (Pool_(g, "sqn", [128, 512], BF16, 2, st), Pool_(g, "tmn", [128, 512], F32, 3, st), _sb(g, "rstdn", [128, 512], F32, st))


def _norm_mod(g, hblk, N, comb, shift, out_bf, tmps, ab=None):
    sqp, tp, rstd = tmps
    ps = _psum(g)
    for k in range(KC):
        sq = sqp.get()
        _act(g, sq[:, :N], hblk[:, k, :N], AF.Square)
        _mm(g, ps[:, :N], g.ones_bf[:], sq[:, :N], start=(k == 0), stop=(k == KC - 1))
    _rstd_from_sum(g, rstd[:, :N], ps[:, :N], 1.0 / D, g.eps[:])
    if ab is not None:
        w32, AB, t0, a32p = ab
        psab = _psum(g, "misc")
    for k in range(KC):
        t = tp.get()
        _stt(g, "dve", t[:, :N], hblk[:, k, :N], comb[:, k:k + 1], rstd[:, :N], ALU.mult, ALU.mult)
        if ab is None:
            _act(g, out_bf[:, k, :N], t[:, :N], AF.Identity, scale=1.0, bias=shift[:, k:k + 1])
        else:
            a32 = a32p.get()
            _act(g, a32[:, :N], t[:, :N], AF.Identity, scale=1.0, bias=shift[:, k:k + 1])
            _copy(g, "pool", out_bf[:, k, :N], a32[:, :N])
            for tt in range(N // 128):
                _mm(g, psab[:, tt * 32:(tt + 1) * 32], a32[:, tt * 128:(tt + 1) * 128], w32[:, k, :], start=(k == 0 and tt == 0), stop=(k == KC - 1 and tt == N // 128 - 1))
    if ab is not None:
        nt = N // 128
        _copy(g, "act", AB[:, t0:t0 + nt, :], psab[:, 0:nt * 32].rearrange("p (t c) -> p t c", c=32))


def _emit(g, n_layers, fake_mixer, debug_h):
    P, dr = g.P, g.dr
    _setup(g)
    ada = _adaln_steps(g, 0)
    for _ in ada:
        pass
    _load_inputs(g)
    for L in range(n_layers):
        last = (L == DEPTH - 1)
        need_ctx = not last
        nxt = _adaln_steps(g, L + 1) if L + 1 < n_layers else iter(())
        with contextlib.ExitStack() as lstA:
            lat = _mla_alloc(g, lstA) if (L % 2 == 0 and not fake_mixer) else None
            with contextlib.ExitStack() as lst:
                aT = _sb(g, "aT", [128, KC, TOK], BF16, lst)
                AB = None
                if L % 2 == 1 and not fake_mixer and USE_AB32:
                    AB = _sb(g, "AB", [128, NT, 32], F32, lst)
                with contextlib.ExitStack() as st:
                    hbp = Pool_(g, "hbn", [128, KC, 512], F32, 2, st)
                    tmps = _norm_tmps(g, st)
                    ab = None
                    if AB is not None:
                        w32 = _sb(g, "wab32", [128, KC, 32], F32, st)
                        P.dma(w32[:], dr["gdn_w_in"][L // 2].rearrange("(k p) n -> p k n", p=128)[:, :, O_A:O_A + 32])
                        a32p = Pool_(g, "a32", [128, 512], F32, 2, st)
                    for bi, (s0, N) in enumerate(BLOCKS):
                        s = 1 if bi == 0 else 0
                        hb = hbp.get()
                        P.dma(hb[:, :, :N], _h_view(g, s0, N))
                        if AB is not None:
                            ab = (w32, AB, s0 // 128, a32p)
                        _norm_mod(g, hb, N, g.comb1[L][:, :, s], g.mod[L][:, 0:8, s], aT[:, :, s0:s0 + N], tmps, ab)
                    P.barrier()
                if fake_mixer:
                    for (s0, N) in BLOCKS:
                        P.dma(_mix_view(g, s0, N), aT[:, :, s0:s0 + N])
                elif L % 2 == 0:
                    _even_mixer_a(g, L, aT, lat, lst)
                else:
                    _odd_mixer(g, L, aT, need_ctx, AB)
                P.barrier()
            if L % 2 == 0 and not fake_mixer and "attn" not in DBG_SKIP:
                _mla_attention(g, L, lat)
                P.barrier()
        wname = "even_w_out" if L % 2 == 0 else "gdn_w_out"
        with contextlib.ExitStack() as st:
            wo = _sb(g, "wo", [128, KC, D], BF16, st)
            _load_cast(g, wo[:], dr[wname][L // 2].rearrange("(k p) n -> p k n", p=128))
            hbp = Pool_(g, "hba", [128, KC, 512], F32, 2, st)
            mxp = Pool_(g, "mxb", [128, KC, 512], BF16, 2, st)
            for bi, (s0, N) in enumerate(BLOCKS):
                if bi == 0 and not need_ctx:
                    continue
                s = 1 if bi == 0 else 0
                hb = hbp.get()
                P.dma(hb[:, :, :N], _h_view(g, s0, N))
                mx = mxp.get()
                P.dma(mx[:, :, :N], _mix_view(g, s0, N))
                for m in range(KC):
                    ps = _psum(g)
                    _mmk(g, ps[:, :N], [(wo[:, k, m * 128:(m + 1) * 128], mx[:, k, :N]) for k in range(KC)])
                    _stt(g, "dve", hb[:, m, :N], ps[:, :N], g.mod[L][:, 16 + m, s:s + 1], hb[:, m, :N], ALU.mult, ALU.add)
                P.dma(_h_view(g, s0, N), hb[:, :, :N])
                for _ in range(5):
                    next(nxt, None)
            P.barrier()
        if debug_h == ("mix", L):
            _dump_h(g)
            return
        with contextlib.ExitStack() as st:
            w2 = _sb(g, "w2", [128, 32, D], BF16, st)
            w2src = dr["mlp_w2"][L].rearrange("(f p) n -> p f n", p=128)
            _load_cast(g, w2[:], w2src)
            w1p = Pool_(g, "w1p", [128, KC, 512], BF16, 2, st)
            w1src = dr["mlp_w1"][L].rearrange("(k p) n -> p k n", p=128)
            hbp = Pool_(g, "hbm", [128, KC, 512], F32, 1, st)
            a2 = _sb(g, "a2", [128, KC, 512], BF16, st)
            h1 = _sb(g, "h1", [128, 32, 512], BF16, st)
            rl = Pool_(g, "rl", [128, 512], F32, 3, st)
            tmps = _norm_tmps(g, st)
            for bi, (s0, N) in enumerate(BLOCKS):
                if bi == 0 and not need_ctx:
                    continue
                s = 1 if bi == 0 else 0
                hb = hbp.get()
                P.dma(hb[:, :, :N], _h_view(g, s0, N))
                _norm_mod(g, hb, N, g.comb2[L][:, :, s], g.mod[L][:, 24:32, s], a2, tmps)
                for fg in range(8):
                    w1 = w1p.get()
                    _load_cast(g, w1[:], w1src[:, :, fg * 512:(fg + 1) * 512])
                    for f in range(4):
                        ps = _psum(g)
                        _mmk(g, ps[:, :N], [(w1[:, k, f * 128:(f + 1) * 128], a2[:, k, :N]) for k in range(KC)])
                        r = rl.get()
                        _act(g, r[:, :N], ps[:, :N], AF.Relu)
                        _tt(g, "pool", h1[:, fg * 4 + f, :N], r[:, :N], r[:, :N], ALU.mult)
                for m in range(KC):
                    ps = _psum(g)
                    _mmk(g, ps[:, :N], [(w2[:, f, m * 128:(m + 1) * 128], h1[:, f, :N]) for f in range(32)])
                    _stt(g, "dve", hb[:, m, :N], ps[:, :N], g.mod[L][:, 40 + m, s:s + 1], hb[:, m, :N], ALU.mult, ALU.add)
                P.dma(_h_view(g, s0, N), hb[:, :, :N])
                for _ in range(5):
                    next(nxt, None)
            for _ in nxt:
                pass
            P.barrier()
        if debug_h == ("mlp", L):
            _dump_h(g)
            return
    if debug_h:
        _dump_h(g)
        return
    _final(g)


def _dump_h(g):
    P = g.P
    with contextlib.ExitStack() as st:
        hb = Pool_(g, "hbd", [128, KC, 512], F32, 2, st)
        for (s0, N) in BLOCKS:
            t = hb.get()
            P.dma(t[:, :, :N], _h_view(g, s0, N))
            P.dma(g.dr["dbg"].rearrange("k p t -> p k t")[:, :, s0:s0 + N], t[:, :, :N], is_output=True)
        P.dma(g.dr["out"][0:128, :], g.cst[:, 0:D], is_output=True)


def _final(g):
    P, dr = g.P, g.dr
    with contextlib.ExitStack() as st:
        hbp = Pool_(g, "hbf", [128, KC, 512], F32, 2, st)
        sq = _sb(g, "sqf", [128, KC, 512], BF16, st)
        yb = _sb(g, "yf", [128, KC, 512], F32, st)
        rstd = _sb(g, "rstdf", [128, 512], F32, st)
        otp = Pool_(g, "ot", [128, D], F32, 2, st)
        fn = _col(g, "fn", 8)
        for (s0, N) in BLOCKS[1:]:
            hb = hbp.get()
            P.dma(hb[:, :, :N], _h_view(g, s0, N))
            _act(g, sq[:, :, :N], hb[:, :, :N], AF.Square)
            ps = _psum(g)
            _mmk(g, ps[:, :N], [(g.ones_bf[:], sq[:, k, :N]) for k in range(KC)])
            _rstd_from_sum(g, rstd[:, :N], ps[:, :N], 1.0 / D, g.eps[:])
            for k in range(KC):
                _stt(g, "dve", yb[:, k, :N], hb[:, k, :N], fn[:, k:k + 1], rstd[:, :N], ALU.mult, ALU.mult)
            for tt in range(N // 128):
                ot = otp.get()
                for half in range(2):
                    ps = _psum(g)
                    for q in range(4):
                        k = half * 4 + q
                        _tr(g, ps[:, q * 128:(q + 1) * 128], yb[:, k, tt * 128:(tt + 1) * 128], g.ident)
                    _copy(g, "act" if half == 0 else "dve", ot[:, half * 512:(half + 1) * 512], ps[:])
                r0 = s0 - NCTX + tt * 128
                P.dma(dr["out"][r0:r0 + 128, :], ot[:], is_output=True)


C_CQ, C_CKV, C_KPE, C_GQ, C_GK, C_GV, C_GG, C_GLOW = 0, 384, 640, 672, 928, 1184, 1696, 2208
GLA_FWD = list(range(NT))
GLA_BWD = [1, 0] + list(range(NT - 1, 1, -1))


def _mla_alloc(g, st):
    lat = Ctx()
    lat.cqn = _sb(g, "cqn", [128, 3, TOK], BF16, st)
    lat.ckvn = _sb(g, "ckvn", [128, 2, TOK], BF16, st)
    lat.kpeT = _sb(g, "kpeT", [128, TOK], BF16, st)
    return lat


def _rope_tables(g, lat, st):
    lat.ropeC = _sb(g, "ropeC", [128, SEQ], F32, st)
    lat.ropeS = _sb(g, "ropeS", [128, SEQ], F32, st)
    g.P.dma(lat.ropeC[64:96, :], g.dr["rope"][0])
    g.P.dma(lat.ropeS[64:96, :], g.dr["rope"][1])


def _swap_halves(g, en, dst, src):
    d = dst.rearrange("p h (a t f) -> p h a t f", a=2, t=2)
    s_ = src.rearrange("p h (a t f) -> p h a t f", a=2, t=2)
    _copy(g, en, d[:, :, :, 0, :], s_[:, :, :, 1, :])
    _copy(g, en, d[:, :, :, 1, :], s_[:, :, :, 0, :])


def _even_mixer_a(g, L, aT, lat, st):
    P, dr = g.P, g.dr
    i = L // 2
    win = _sb(g, "win", [128, KC, EVEN_IN], BF16, st)
    wsrc = dr["even_w_in"][i].rearrange("(k p) n -> p k n", p=128)
    for k in range(KC):
        _load_cast(g, win[:, k, :], wsrc[:, k, :])
    with contextlib.ExitStack() as s1:
        _rope_tables(g, lat, s1)
        winrot = _sb(g, "winrot", [128, KC, 32], BF16, s1)
        _swap_halves(g, "pool", winrot[:], win[:, :, C_KPE:C_KPE + 32])
        cqf = _sb(g, "cqf", [128, 5, 512], F32, s1)
        sqp = Pool_(g, "sql", [128, 512], BF16, 2, s1)
        rq = _sb(g, "rq", [128, 512], F32, s1)
        rkv = _sb(g, "rkv", [128, 512], F32, s1)
        t1p = Pool_(g, "t1l", [128, 512], F32, 2, s1)
        qn = _col(g, "qn%d" % i, 3)
        kvn = _col(g, "kvn%d" % i, 2)
        for bi, (s0, N) in enumerate(BLOCKS):
            rhs = [aT[:, k, s0:s0 + N] for k in range(KC)]
            pss_q = _psum(g, "acc")
            pss_kv = _psum(g, "acc")
            for fc in range(5):
                ps = _psum(g)
                _mmk(g, ps[:, :N], [(win[:, k, fc * 128:(fc + 1) * 128], rhs[k]) for k in range(KC)])
                _copy(g, "act", cqf[:, fc, :N], ps[:, :N])
                sq = sqp.get()
                _tt(g, "pool", sq[:, :N], cqf[:, fc, :N], cqf[:, fc, :N], ALU.mult)
                if fc < 3:
                    _mm(g, pss_q[:, :N], g.ones_bf[:], sq[:, :N], start=(fc == 0), stop=(fc == 2))
                else:
                    _mm(g, pss_kv[:, :N], g.ones_bf[:], sq[:, :N], start=(fc == 3), stop=(fc == 4))
            _rstd_from_sum(g, rq[:, :N], pss_q[:, :N], 1.0 / 384, g.eps[:])
            _rstd_from_sum(g, rkv[:, :N], pss_kv[:, :N], 1.0 / 256, g.eps[:])
            for fc in range(3):
                _stt(g, "dve", lat.cqn[:, fc, s0:s0 + N], cqf[:, fc, :N], qn[:, fc:fc + 1], rq[:, :N], ALU.mult, ALU.mult)
            for fc in range(2):
                _stt(g, "dve", lat.ckvn[:, fc, s0:s0 + N], cqf[:, 3 + fc, :N], kvn[:, fc:fc + 1], rkv[:, :N], ALU.mult, ALU.mult)
            ps = _psum(g)
            _mmk(g, ps[64:96, :N], [(win[:, k, C_KPE:C_KPE + 32], rhs[k]) for k in range(KC)])
            if bi == 0:
                _copy(g, "act", lat.kpeT[64:96, s0:s0 + N], ps[64:96, :N])
            else:
                ps2 = _psum(g)
                _mmk(g, ps2[64:96, :N], [(winrot[:, k, :], rhs[k]) for k in range(KC)])
                p0 = s0 - NCTX
                t1 = t1p.get()
                t2 = t1p.get()
                _tt(g, "dve", t1[64:96, :N], ps[64:96, :N], lat.ropeC[64:96, p0:p0 + N], ALU.mult)
                _tt(g, "dve", t2[64:96, :N], ps2[64:96, :N], lat.ropeS[64:96, p0:p0 + N], ALU.mult)
                _tt(g, "pool", lat.kpeT[64:96, s0:s0 + N], t1[64:96, :N], t2[64:96, :N], ALU.add)
        P.barrier()
    with contextlib.ExitStack() as s2:
        if "gla" not in DBG_SKIP:
            try:
                _gla(g, i, aT, win, s2)
            except _Stop:
                pass
        P.barrier()


def _gla(g, i, aT, win, st):
    P, dr = g.P, g.dr
    gup = _sb(g, "gup", [16, 2, 256], F32, st)
    P.dma(gup[:], dr["gla_gate_up"][i].rearrange("z r n -> r z n"))
    gbias = _sb(g, "gbias", [128, 2, 256], F32, st)
    P.dma(gbias[:], dr["gla_gate_bias"][i].partition_broadcast(128))
    onorm = _sb(g, "onorm", [128, 128], F32, st)
    P.dma(onorm[:], dr["gla_o_norm"][i].partition_broadcast(128))
    Of = _sb(g, "Of", [128, NT, 512], F32, st)
    S = _sb(g, "S", [128, 2, 128], F32, st)
    Sbf = _sb(g, "Sbf", [128, 2, 128], BF16, st)
    glowT = Pool_(g, "glowT", [16, 128], F32, 2, st)
    tl = Pool_(g, "tl", [128, 256], F32, 2, st)
    Lz = Pool_(g, "Lz", [128, 256], F32, 2, st)
    Eq = Pool_(g, "Eq", [128, 2, 128], F32, 2, st)
    Ek = Pool_(g, "Ek", [128, 2, 128], F32, 2, st)
    gend = Pool_(g, "gend", [128, 2, 2], F32, 2, st)
    kendE = Pool_(g, "kendE", [128, 256], F32, 2, st)
    qdecT = Pool_(g, "qdecT", [128, 4, 128], BF16, 2, st, zero=True)
    kinvT = Pool_(g, "kinvT", [128, 4, 128], BF16, 2, st, zero=True)
    kend = Pool_(g, "kend", [128, 2, 256], BF16, 2, st)
    CI2 = _cst(g, "CI").unsqueeze(2).to_broadcast([128, 2, 256])
    Vg = Pool_(g, "Vg", [128, 512], BF16, 2, st)
    AmT = Pool_(g, "AmT", [128, 4, 128], BF16, 2, st)
    G2 = Pool_(g, "G2", [128, 512], F32, 1, st)
    ob = Pool_(g, "ob", [128, 512], F32, 1, st)
    sqo = Pool_(g, "sqo", [128, 512], F32, 1, st)
    ss4 = Pool_(g, "ss4", [128, 4], F32, 2, st)
    mtok = Pool_(g, "mtok", [128, 512], BF16, 2, st)
    mt = Pool_(g, "mt", [128, 4, 128], BF16, 2, st)
    LN8 = float(np.log(0.125))
    for z in range(2):
        R_ = _cst(g, "glaRf" if z == 0 else "glaRb")
        A_ = _cst(g, "glaAf" if z == 0 else "glaAb")
        Mz = _cst(g, "Mf" if z == 0 else "Mb")
        _memset(g, "pool", S[:], 0.0)
        _memset(g, "pool", Sbf[:], 0.0)
        for t in (GLA_FWD if z == 0 else GLA_BWD):
            ts = slice(t * 128, (t + 1) * 128)
            at = [aT[:, k, ts] for k in range(KC)]
            ps = _psum(g)
            c0 = C_GLOW + 16 * z
            _mmk(g, ps[0:16, 0:128], [(win[:, k, c0:c0 + 16], at[k]) for k in range(KC)])
            gl = glowT.get()
            _copy(g, "act", gl[:], ps[0:16, 0:128])
            _chk("gla_%s_%d" % ("a", z))
            ps = _psum(g)
            _mm(g, ps[:, 0:256], gl[:], gup[:, z, :])
            tt_ = tl.get()
            _tt(g, "dve", tt_[:], ps[:, 0:256], gbias[:, z, :], ALU.add)
            L_ = Lz.get()
            _act(g, tt_[:], tt_[:], AF.Exp, scale=-1.0)
            _act(g, L_[:], tt_[:], AF.Ln, scale=1.0, bias=1.0)
            _chk("gla_%s_%d" % ("b", z))
            ps = _psum(g)
            for hp in range(2):
                _mm(g, ps[:, hp * 130:(hp + 1) * 130], L_[:, hp * 128:(hp + 1) * 128], R_)
            pv = ps[:, 0:260].rearrange("p (h c) -> p h c", c=130)
            eq, ek, ge = Eq.get(), Ek.get(), gend.get()
            _act(g, eq[:], pv[:, :, 0:128], AF.Exp, scale=1.0, bias=LN8)
            _act(g, ek[:], pv[:, :, 0:128], AF.Exp, scale=-1.0)
            _act(g, ge[:], pv[:, :, 128:130], AF.Exp)
            _chk("gla_%s_%d" % ("c", z))
            ps = _psum(g)
            _mm(g, ps[:, 0:256], A_, L_[:])
            ke = kendE.get()
            _act(g, ke[:], ps[:, 0:256], AF.Exp)
            _chk("gla_%s_%d" % ("d", z))
            ps = _psum(g)
            for hp in range(2):
                _mmk(g, ps[:, hp * 128:(hp + 1) * 128], [(win[:, k, C_GQ + hp * 128:C_GQ + (hp + 1) * 128], at[k]) for k in range(KC)])
                _mmk(g, ps[:, 256 + hp * 128:256 + (hp + 1) * 128], [(win[:, k, C_GK + hp * 128:C_GK + (hp + 1) * 128], at[k]) for k in range(KC)])
            qd, ki = qdecT.get(), kinvT.get()
            for (lo, hi, par) in ((0, 64, 0), (64, 128, 1)):
                _tt(g, "dve", qd[lo:hi, par::2, :], ps[lo:hi, 0:256].rearrange("p (h c) -> p h c", c=128), eq[lo:hi, :, :], ALU.mult)
                _tt(g, "dve", ki[lo:hi, par::2, :], ps[lo:hi, 256:512].rearrange("p (h c) -> p h c", c=128), ek[lo:hi, :, :], ALU.mult)
            _chk("gla_%s_%d" % ("e", z))
            ps = _psum(g)
            _mmk(g, ps[:, 0:256], [(at[k], win[:, k, C_GK:C_GK + 256]) for k in range(KC)])
            kn = kend.get()
            _tt(g, "dve", ke[:], ps[:, 0:256], ke[:], ALU.mult)
            _tt(g, "pool", kn[:], ke[:].unsqueeze(1).to_broadcast([128, 2, 256]), CI2, ALU.mult)
            _chk("gla_%s_%d" % ("f", z))
            ps = _psum(g)
            _mmk(g, ps[:, 0:512], [(at[k], win[:, k, C_GV:C_GV + 512]) for k in range(KC)])
            vg = Vg.get()
            _copy(g, "act", vg[:], ps[:, 0:512])
            _chk("gla_%s_%d" % ("g", z))
            ps = _psum(g)
            for h in range(4):
                hp, hb = h // 2, (h % 2) * 64
                _mm(g, ps[:, h * 128:(h + 1) * 128], ki[:, h, :], qd[:, h, :])
            am = AmT.get()
            if "h_dve" not in DBG_SKIP:
                _tt(g, "dve", am[:], ps[:].rearrange("p (h c) -> p h c", c=128), Mz.unsqueeze(1).to_broadcast([128, 4, 128]), ALU.mult)
            else:
                _copy(g, "act", am[:], ps[:].rearrange("p (h c) -> p h c", c=128))
            _chk("gla_%s_%d" % ("h", z))
            if z == 1:
                ps = _psum(g)
                _mmk(g, ps[:, 0:512], [(at[k], win[:, k, C_GG:C_GG + 512]) for k in range(KC)])
                g2 = G2.get()
                _act(g, g2[:], ps[:, 0:512], AF.Silu)
                _tt(g, "pool", g2[:].rearrange("p (h c) -> p h c", c=128), g2[:].rearrange("p (h c) -> p h c", c=128),
                    onorm[:].unsqueeze(1).to_broadcast([128, 4, 128]), ALU.mult)
            _chk("gla_%s_%d" % ("i", z))
            pso = _psum(g, "acc")
            for c in ((0, 1) if z == 0 else (1, 0)):
                cb = c * 64
                psS = _psum(g, "misc")
                for h in range(4):
                    hp, hb = h // 2, (h % 2) * 64
                    _mm(g, pso[cb:cb + 64, h * 128:(h + 1) * 128], qd[:, h, cb:cb + 64], Sbf[:, hp, :], start=True, stop=False)
                    _mm(g, pso[cb:cb + 64, h * 128:(h + 1) * 128], am[:, h, cb:cb + 64], vg[:, h * 128:(h + 1) * 128], start=False, stop=True)
                    _mm(g, psS[hb:hb + 64, hp * 128:(hp + 1) * 128], kn[:, c, h * 64:(h + 1) * 64], vg[:, h * 128:(h + 1) * 128])
                for hp in range(2):
                    _stt(g, "dve", S[:, hp, :], S[:, hp, :], ge[:, hp, c:c + 1], psS[:, hp * 128:(hp + 1) * 128], ALU.mult, ALU.add)
                _copy(g, "pool", Sbf[:], S[:])
            _chk("gla_%s_%d" % ("j", z))
            if z == 0:
                _copy(g, "act", Of[:, t, :], pso[:])
            else:
                o = ob.get()
                _tt(g, "dve", o[:], pso[:], Of[:, t, :], ALU.add)
                sq = sqo.get()
                _tt(g, "pool", sq[:], o[:], o[:], ALU.mult)
                s4 = ss4.get()
                P.op("dve", lambda e: e.tensor_reduce(s4[:], sq[:].rearrange("p (h c) -> p h c", c=128), axis=AX, op=ALU.add), [sq[:]], [s4[:]])
                _rstd_from_sum(g, s4[:], s4[:], 1.0 / 128, g.eps[:])
                o3 = o[:].rearrange("p (h c) -> p h c", c=128)
                _tt(g, "dve", o3, o3, s4[:].unsqueeze(2).to_broadcast([128, 4, 128]), ALU.mult)
                mk = mtok.get()
                _tt(g, "pool", mk[:], o[:], g2[:], ALU.mult)
                for h in range(4):
                    _tr(g, g.psbf[:, h * 128:(h + 1) * 128], mk[:, h * 128:(h + 1) * 128], g.ident_bf[:])
                m_ = mt.get()
                _copy(g, "act", m_[:], g.psbf[:, 0:512].rearrange("p (h c) -> p h c", c=128))
                P.dma(g.dr["MIX"].rearrange("k p t -> p k t")[:, 4:8, ts], m_[:])
            _chk("gla_k_%d" % z)


def _mla_attention(g, L, lat):
    P, dr = g.P, g.dr
    i = L // 2
    with contextlib.ExitStack() as st0:
      QT = _sb(g, "QT", [128, 8, TOK], BF16, st0)
      KT = _sb(g, "KT", [128, 8, TOK], BF16, st0)
      VA = _sb(g, "VA", [128, NT, 8, 128], BF16, st0)
      with contextlib.ExitStack() as st:
        _rope_tables(g, lat, st)
        wuq = _sb(g, "wuq", [128, 3, 768], BF16, st)
        _load_cast(g, wuq[:], dr["mla_w_uq"][i].rearrange("(k p) n -> p k n", p=128))
        wuqr = _sb(g, "wuqr", [128, 3, 768], BF16, st)
        _copy(g, "pool", wuqr[:], wuq[:])
        for k in range(3):
            _swap_halves(g, "pool", wuqr[:, k, :].rearrange("p (h c) -> p h c", c=96)[:, :, 64:96],
                         wuq[:, k, :].rearrange("p (h c) -> p h c", c=96)[:, :, 64:96])
        wukv = _sb(g, "wukv", [128, 2, 1024], BF16, st)
        _load_cast(g, wukv[:], dr["mla_w_ukv"][i].rearrange("(k p) n -> p k n", p=128))
        wv = _sb(g, "wv", [128, 2, 512], BF16, st)
        for k in range(2):
            _copy(g, "pool", wv[:, k, :].rearrange("p (h c) -> p h c", c=64), wukv[:, k, :].rearrange("p (h c) -> p h c", c=128)[:, :, 64:128])
        _memset(g, "pool", VA[:, :, :, 64:128].rearrange("p t h c -> p (t h) c"), 1.0)
        t1p = Pool_(g, "t1a", [128, 512], F32, 2, st)
        for bi, (s0, N) in enumerate(BLOCKS):
            sl = slice(s0, s0 + N)
            p0 = s0 - NCTX
            for h in range(8):
                ps = _psum(g)
                _mmk(g, ps[0:96, :N], [(wuq[:, fc, h * 96:(h + 1) * 96], lat.cqn[:, fc, sl]) for fc in range(3)])
                _copy(g, "act", QT[0:64, h, sl], ps[0:64, :N])
                if bi == 0:
                    _copy(g, "act", QT[64:96, h, sl], ps[64:96, :N])
                else:
                    ps2 = _psum(g)
                    _mmk(g, ps2[0:96, :N], [(wuqr[:, fc, h * 96:(h + 1) * 96], lat.cqn[:, fc, sl]) for fc in range(3)])
                    t1, t2 = t1p.get(), t1p.get()
                    _tt(g, "dve", t1[64:96, :N], ps[64:96, :N], lat.ropeC[64:96, p0:p0 + N], ALU.mult)
                    _tt(g, "dve", t2[64:96, :N], ps2[64:96, :N], lat.ropeS[64:96, p0:p0 + N], ALU.mult)
                    _tt(g, "pool", QT[64:96, h, sl], t1[64:96, :N], t2[64:96, :N], ALU.add)
                ps = _psum(g)
                _mmk(g, ps[0:64, :N], [(wukv[:, c, h * 128:h * 128 + 64], lat.ckvn[:, c, sl]) for c in range(2)])
                _copy(g, "act", KT[0:64, h, sl], ps[0:64, :N])
                _copy(g, "pool", KT[64:96, h, sl], lat.kpeT[64:96, sl])
            for tt_ in range(N // 128):
                t = s0 // 128 + tt_
                ps = _psum(g)
                _mmk(g, ps[:, 0:512], [(lat.ckvn[:, c, t * 128:(t + 1) * 128], wv[:, c, :]) for c in range(2)])
                _copy(g, "dve", VA[:, t, :, 0:64], ps[:, 0:512].rearrange("p (h c) -> p h c", c=64))
        P.barrier()
      with contextlib.ExitStack() as st:
        scale = float(96 ** -0.5)
        pT = Pool_(g, "pT", [128, 512], BF16, 3, st)
        rec = Pool_(g, "rec", [128, 512], F32, 2, st)
        atile = Pool_(g, "atile", [128, 512], BF16, 2, st)
        for bi, (s0, N) in enumerate(BLOCKS):
            sl = slice(s0, s0 + N)
            nk = 2 if bi == 0 else NT
            for h in range(8):
                if h % 2 == 0:
                    at_ = atile.get()
                pso = _psum(g, "acc")
                for kc in range(nk):
                    pss = _psum(g)
                    _mm(g, pss[:, :N], KT[0:96, h, kc * 128:(kc + 1) * 128], QT[0:96, h, sl])
                    p_ = pT.get()
                    _act(g, p_[:, :N], pss[:, :N], AF.Exp, scale=scale)
                    _mm(g, pso[:, :N], VA[:, kc, h, :], p_[:, :N], start=(kc == 0), stop=(kc == nk - 1))
                r_ = rec.get()
                P.op("dve", lambda e: e.reciprocal(r_[64:128, :N], pso[64:128, :N]), [pso[64:128, :N]], [r_[64:128, :N]])
                hb = (h % 2) * 64
                _tt(g, "dve", at_[hb:hb + 64, :N], pso[0:64, :N], r_[64:128, :N], ALU.mult)
                if h % 2 == 1:
                    P.dma(g.dr["MIX"].rearrange("k p t -> p k t")[:, h // 2, sl], at_[:, :N])


O_Q, O_K, O_V, O_G, O_A, O_B = 0, 512, 1024, 2048, 3072, 3088


def _odd_mixer(g, L, aT, need_ctx, AB):
    P, dr = g.P, g.dr
    i = L // 2
    wsrc = dr["gdn_w_in"][i].rearrange("(k p) n -> p k n", p=128)
    with contextlib.ExitStack() as st:
        blockones = _sb(g, "blockones", [128, 128], BF16, st)
        _memset(g, "pool", blockones[:], 0.0)
        _memset(g, "pool", blockones[0:64, 0:64], 1.0)
        _memset(g, "pool", blockones[64:128, 64:128], 1.0)
        wab = AB
        if AB is None:
            wab = _sb(g, "wab", [128, KC, 32], BF16, st)
            _load_cast(g, wab[:], wsrc[:, :, O_A:O_A + 32])
        dtb = _sb(g, "dtb", [128, 16], F32, st)
        P.dma(dtb[:], dr["gdn_dt_bias"][i].rearrange("z h -> (z h)").partition_broadcast(128))
        negA = _sb(g, "negA", [128, 16], F32, st)
        P.dma(negA[:], dr["gdn_a_log"][i].rearrange("z h -> (z h)").partition_broadcast(128))
        _act(g, negA[:], negA[:], AF.Exp)
        _ts(g, "dve", negA[:], negA[:], -1.0, None, ALU.mult)
        onorm = _sb(g, "onormd", [128, 128], F32, st)
        P.dma(onorm[:], dr["gdn_o_norm"][i].partition_broadcast(128))
        g.bmask = _sb(g, "bmask", [128, 12, 128], BF16, st)
        _load_cast(g, g.bmask[:], dr["bmask"].rearrange("p (m c) -> p m c", c=128))
        for hf in range(2):
            with contextlib.ExitStack() as sh:
                qT = _sb(g, "gqT", [128, 2, TOK], BF16, sh)
                kT = _sb(g, "gkT", [128, 2, TOK], BF16, sh)
                Vt = _sb(g, "gVt", [128, NT, 512], BF16, sh)
                Kt = _sb(g, "gKt", [128, NT, 256], BF16, sh)
                Of = _sb(g, "gOf", [128, NT, 512], F32, sh)
                wg = _sb(g, "gwg", [128, KC, 512], BF16, sh)
                _load_cast(g, wg[:], wsrc[:, :, O_G + hf * 512:O_G + (hf + 1) * 512])
                with contextlib.ExitStack() as s1:
                    _gdn_conv_stage(g, i, hf, aT, wsrc, qT, kT, Vt, Kt, blockones, s1)
                    P.barrier()
                with contextlib.ExitStack() as s2:
                    _gdn_scan(g, hf, aT, qT, kT, Vt, Kt, Of, wg, wab, dtb, negA, onorm, need_ctx, s2)
                    P.barrier()


def _gdn_conv_stage(g, i, hf, aT, wsrc, qT, kT, Vt, Kt, blockones, st):
    P = g.P
    xl = _sb(g, "xl", [128, SEQ + 2], F32, st)
    xc = _sb(g, "xc", [128, NCTX + 2], F32, st)
    for buf, n in ((xl, SEQ), (xc, NCTX)):
        _memset(g, "pool", buf[:, 0:1], 0.0)
        _memset(g, "pool", buf[:, n + 1:n + 2], 0.0)
    tcv = _sb(g, "tcv", [128, SEQ], F32, st)
    ybf = _sb(g, "ybf", [128, TOK], BF16, st)
    yf = _sb(g, "yf32", [128, 512], F32, st)
    sqp = _sb(g, "sqc", [128, 512], BF16, st)
    rn = _sb(g, "rnc", [128, 512], F32, st)
    wtp = Pool_(g, "wcv", [128, KC, 128], BF16, 2, st)
    cols = _col(g, "conv%d" % i, 48)
    chunks = [("q", O_Q // 128 + 2 * hf + j, j) for j in range(2)] + [("k", O_K // 128 + 2 * hf + j, j) for j in range(2)] + \
             [("v", O_V // 128 + 4 * hf + j, j) for j in range(4)]
    for kind, cc, j in chunks:
        wt = wtp.get()
        _load_cast(g, wt[:], wsrc[:, :, cc * 128:(cc + 1) * 128])
        for bi, (s0, N) in enumerate(BLOCKS):
            ps = _psum(g)
            _mmk(g, ps[:, :N], [(wt[:, k, :], aT[:, k, s0:s0 + N]) for k in range(KC)])
            if bi == 0:
                _copy(g, "act", xc[:, 1:1 + N], ps[:, :N])
            else:
                p0 = s0 - NCTX
                _copy(g, "act", xl[:, 1 + p0:1 + p0 + N], ps[:, :N])
        w0, w1, w2 = cols[:, cc:cc + 1], cols[:, 16 + cc:17 + cc], cols[:, 32 + cc:33 + cc]
        for buf, n, o0 in ((xc, NCTX, 0), (xl, SEQ, NCTX)):
            t = tcv[:, 0:n]
            _ts(g, "pool", t, buf[:, 1:1 + n], w1, None, ALU.mult)
            _stt(g, "dve", t, buf[:, 0:n], w0, t, ALU.mult, ALU.add)
            _stt(g, "dve", t, buf[:, 2:2 + n], w2, t, ALU.mult, ALU.add)
            if kind == "v":
                _act(g, ybf[:, o0:o0 + n], t, AF.Silu)
            else:
                for c0 in range(0, n, 512):
                    m = min(512, n - c0)
                    _act(g, yf[:, :m], t[:, c0:c0 + m], AF.Silu)
                    _tt(g, "pool", sqp[:, :m], yf[:, :m], yf[:, :m], ALU.mult)
                    ps = _psum(g)
                    _mm(g, ps[:, :m], blockones[:], sqp[:, :m])
                    _rstd_from_sum(g, rn[:, :m], ps[:, :m], 1.0, g.eps[:])
                    dst = (qT if kind == "q" else kT)[:, j, o0 + c0:o0 + c0 + m]
                    if kind == "q":
                        _stt(g, "dve", dst, yf[:, :m], 0.125, rn[:, :m], ALU.mult, ALU.mult)
                    else:
                        _tt(g, "dve", dst, yf[:, :m], rn[:, :m], ALU.mult)
        if kind in ("k", "v"):
            src = kT[:, j, :] if kind == "k" else ybf[:]
            for t4 in range(0, NT, 4):
                nt = min(4, NT - t4)
                for q in range(nt):
                    _tr(g, g.psbf[:, q * 128:(q + 1) * 128], src[:, (t4 + q) * 128:(t4 + q + 1) * 128], g.ident_bf[:])
                dst = (Kt[:, t4:t4 + nt, j * 128:(j + 1) * 128] if kind == "k" else Vt[:, t4:t4 + nt, j * 128:(j + 1) * 128])
                _copy(g, "act", dst, g.psbf[:, 0:nt * 128].rearrange("p (q c) -> p q c", c=128))


def _gdn_scan(g, hf, aT, qT, kT, Vt, Kt, Of, wg, wab, dtb, negA, onorm, need_ctx, st):
    P = g.P
    S = _sb(g, "dS", [128, 2, 128], F32, st)
    Sbf = _sb(g, "dSbf", [128, 2, 128], BF16, st)
    sm = Pool_(g, "dsm", [128, 32], F32, 2, st)
    larep = _sb(g, "larep", [128, 4, 64], F32, st)
    Eg = _sb(g, "dEg", [128, 2, 128], F32, st)
    gend = Pool_(g, "dgend", [128, 2, 2], F32, 2, st)
    qdecT = _sb(g, "dqdec", [128, 4, 128], BF16, st)
    _memset(g, "pool", qdecT[:], 0.0)
    kz = _sb(g, "dkz", [128, 4, 128], BF16, st)
    _memset(g, "pool", kz[:], 0.0)
    CI4 = _cst(g, "CI").unsqueeze(2).to_broadcast([128, 2, 256])
    rhsd = _sb(g, "rhsd", [128, 4, 128], F32, st)
    dec = _sb(g, "ddec", [128, 4, 128], F32, st)
    tG = _sb(g, "dtG", [128, 4, 128], F32, st)
    qkm = _sb(g, "dqkm", [128, 4, 128], BF16, st)
    Mp = Pool_(g, "dM", [128, 4, 128], BF16, 2, st)
    Np = Pool_(g, "dN", [128, 4, 128], BF16, 2, st)
    Tnp = Pool_(g, "dTn", [128, 4, 128], BF16, 2, st)
    Amp = Pool_(g, "dAm", [128, 4, 128], BF16, 4, st)
    Y2p = Pool_(g, "dY2", [128, 4, 128], BF16, 2, st)
    NQ = Pool_(g, "dNQ", [128, 4, 256], BF16, 2, st)
    vb = _sb(g, "dvb", [128, 512], BF16, st)
    kbg = _sb(g, "dkbg", [128, 4, 64], BF16, st)
    kend = _sb(g, "dkend", [128, 4, 64], BF16, st)
    kend2 = _sb(g, "dkend2", [128, 2, 256], BF16, st)
    u = _sb(g, "du", [128, 512], F32, st)
    wT = _sb(g, "dwT", [128, 4, 128], BF16, st)
    _memset(g, "pool", wT[:], 0.0)
    vnew = _sb(g, "dvnew", [128, 512], BF16, st)
    _memset(g, "pool", vnew[:], 0.0)
    G2 = _sb(g, "dG2", [128, 512], F32, st)
    ob = _sb(g, "dob", [128, 512], F32, st)
    sqo = _sb(g, "dsqo", [128, 512], F32, st)
    mtok = _sb(g, "dmtok", [128, 512], BF16, st)
    mt = Pool_(g, "dmt", [128, 4, 128], BF16, 2, st)
    ident_bc = g.ident.unsqueeze(1).to_broadcast([128, 4, 128])
    for z in range(2):
        R_ = _cst(g, "gdnRf" if z == 0 else "gdnRb")
        Mz = _cst(g, "Mf" if z == 0 else "Mb")
        Sz = _cst(g, "Sf" if z == 0 else "Sb")
        Sz_o = _cst(g, "Sb" if z == 0 else "Sf")
        _memset(g, "pool", S[:], 0.0)
        _memset(g, "pool", Sbf[:], 0.0)
        for t in (GLA_FWD if z == 0 else GLA_BWD):
            ts = slice(t * 128, (t + 1) * 128)
            at = [aT[:, k, ts] for k in range(KC)]
            want_out = need_ctx or t >= 2
            if USE_AB32:
                ps = wab[:, t, :]
            else:
                ps = _psum(g)
                _mmk(g, ps[:, 0:32], [(at[k], wab[:, k, :]) for k in range(KC)])
            s_ = sm.get()
            ca = z * 8 + hf * 4
            _tt(g, "dve", s_[:, 0:4], ps[:, ca:ca + 4], dtb[:, ca:ca + 4], ALU.add)
            _act(g, s_[:, 0:4], s_[:, 0:4], AF.Exp)
            _act(g, s_[:, 0:4], s_[:, 0:4], AF.Ln, scale=1.0, bias=1.0)
            _tt(g, "dve", s_[:, 0:4], s_[:, 0:4], negA[:, ca:ca + 4], ALU.mult)
            _act(g, s_[:, 4:8], ps[:, 16 + ca:16 + ca + 4], AF.Sigmoid)
            la, beta = s_[:, 0:4], s_[:, 4:8]
            _copy(g, "pool", larep[:], la.unsqueeze(2).to_broadcast([128, 4, 64]))
            ps = _psum(g)
            lr = larep[:].rearrange("p h d -> p (h d)")
            for hp in range(2):
                _mm(g, ps[:, hp * 130:(hp + 1) * 130], lr[:, hp * 128:(hp + 1) * 128], R_)
            pv = ps[:, 0:260].rearrange("p (h c) -> p h c", c=130)
            ge = gend.get()
            _act(g, Eg[:], pv[:, :, 0:128], AF.Exp)
            _act(g, ge[:], pv[:, :, 128:130], AF.Exp)
            for (lo, hi, par) in ((0, 64, 0), (64, 128, 1)):
                _tt(g, "dve", qdecT[lo:hi, par::2, :], qT[lo:hi, :, ts], Eg[lo:hi, :, :], ALU.mult)
                _copy(g, "pool", kz[lo:hi, par::2, :], kT[lo:hi, :, ts])
            ps = _psum(g)
            _mm(g, ps[:, 0:4], Mz, la)
            _mm(g, ps[:, 4:8], Sz, la)
            _act(g, s_[:, 8:16], ps[:, 0:8], AF.Exp)
            eg, ekend = s_[:, 8:12], s_[:, 12:16]
            _tt(g, "dve", s_[:, 16:20], beta, eg, ALU.mult)
            bg = s_[:, 16:20]
            kt3 = Kt[:, t, :].rearrange("p (h d) -> p h d", d=64)
            _tt(g, "pool", kend[:], kt3, ekend.unsqueeze(2).to_broadcast([128, 4, 64]), ALU.mult)
            _tt(g, "pool", kend2[:], kend[:].rearrange("p h d -> p (h d)").unsqueeze(1).to_broadcast([128, 2, 256]), CI4, ALU.mult)
            _tt(g, "pool", kbg[:], kt3, bg.unsqueeze(2).to_broadcast([128, 4, 64]), ALU.mult)
            _tt(g, "pool", vb[:].rearrange("p (h c) -> p h c", c=128), Vt[:, t, :].rearrange("p (h c) -> p h c", c=128),
                beta.unsqueeze(2).to_broadcast([128, 4, 128]), ALU.mult)
            la_bc = la.unsqueeze(2).to_broadcast([128, 4, 128])
            _tt(g, "pool", rhsd[:], Mz.unsqueeze(1).to_broadcast([128, 4, 128]), la_bc, ALU.mult)
            ps = _psum(g)
            _mm(g, ps[:, 0:512], Sz, rhsd[:].rearrange("p h c -> p (h c)"))
            _act(g, dec[:].rearrange("p h c -> p (h c)"), ps[:, 0:512], AF.Exp)
            ps = _psum(g)
            for h in range(4):
                hp, hb = h // 2, (h % 2) * 64
                _mm(g, ps[:, h * 128:(h + 1) * 128], kz[:, h, :], qT[:, hp, ts])
            _tt(g, "dve", tG[:], ps[:].rearrange("p (h c) -> p h c", c=128), Mz.unsqueeze(1).to_broadcast([128, 4, 128]), ALU.mult)
            _tt(g, "dve", qkm[:], tG[:], dec[:], ALU.mult)
            _tt(g, "pool", rhsd[:], Sz.unsqueeze(1).to_broadcast([128, 4, 128]), la_bc, ALU.mult)
            ps = _psum(g)
            _mm(g, ps[:, 0:512], Mz, rhsd[:].rearrange("p h c -> p (h c)"))
            _act(g, dec[:].rearrange("p h c -> p (h c)"), ps[:, 0:512], AF.Exp)
            _tt(g, "pool", dec[:], dec[:], beta.unsqueeze(2).to_broadcast([128, 4, 128]), ALU.mult)
            ps = _psum(g)
            for h in range(4):
                hp, hb = h // 2, (h % 2) * 64
                _mm(g, ps[:, h * 128:(h + 1) * 128], kz[:, h, :], kz[:, h, :])
            _tt(g, "dve", tG[:], ps[:].rearrange("p (h c) -> p h c", c=128), Sz.unsqueeze(1).to_broadcast([128, 4, 128]), ALU.mult)
            M_ = Mp.get()
            _tt(g, "dve", M_[:], tG[:], dec[:], ALU.mult)
            for h in range(4):
                _tr(g, g.psbf[:, h * 128:(h + 1) * 128], M_[:, h, :], g.ident_bf[:])
            N_ = Np.get()
            _copy(g, "act", N_[:], g.psbf[:, 0:512].rearrange("p (h c) -> p h c", c=128))
            TT = NQ.get()
            Tn = Tnp.get()
            for lv, bsz in enumerate((1, 2, 4, 8, 16, 32)):
                last = (bsz == 32)
                mk_ = g.bmask[:, (lv if z == 0 else 6 + lv), :].unsqueeze(1).to_broadcast([128, 4, 128])
                mkT = g.bmask[:, (6 + lv if z == 0 else lv), :].unsqueeze(1).to_broadcast([128, 4, 128])
                Am = Amp.get()
                _tt(g, "pool", Am[:], M_[:], mkT, ALU.mult)
                if lv == 0 or not last:
                    Nm = Amp.get()
                    _tt(g, "pool", Nm[:], N_[:], mk_, ALU.mult)
                if lv == 0:
                    _tt(g, "dve", TT[:, :, 128:256], ident_bc, Nm[:], ALU.subtract)
                    _tt(g, "dve", Tn[:], ident_bc, Am[:], ALU.subtract)
                    continue
                TT2 = NQ.get()
                psY = _psum(g)
                for h in range(4):
                    _mm(g, psY[:, h * 128:(h + 1) * 128], Am[:, h, :], TT[:, h, 128:256])
                _copy(g, "act", TT[:, :, 0:128], psY[:].rearrange("p (h c) -> p h c", c=128))
                if not last:
                    psY2 = _psum(g)
                    for h in range(4):
                        _mm(g, psY2[:, h * 128:(h + 1) * 128], Nm[:, h, :], Tn[:, h, :])
                    y2 = Y2p.get()
                    _copy(g, "dve", y2[:], psY2[:].rearrange("p (h c) -> p h c", c=128))
                psZ = _psum(g)
                for h in range(4):
                    _mm(g, psZ[:, h * 128:(h + 1) * 128], Tn[:, h, :], TT[:, h, 0:128])
                _tt(g, "dve", TT2[:, :, 128:256], TT[:, :, 128:256], psZ[:].rearrange("p (h c) -> p h c", c=128), ALU.subtract)
                if not last:
                    psZ2 = _psum(g)
                    for h in range(4):
                        _mm(g, psZ2[:, h * 128:(h + 1) * 128], TT[:, h, 128:256], y2[:, h, :])
                    Tn2 = Tnp.get()
                    _tt(g, "dve", Tn2[:], Tn[:], psZ2[:].rearrange("p (h c) -> p h c", c=128), ALU.subtract)
                    Tn = Tn2
                TT = TT2
            nq = TT
            ps = _psum(g)
            for h in range(4):
                _mm(g, ps[:, h * 128:(h + 1) * 128], nq[:, h, 128:256], vb[:, h * 128:(h + 1) * 128])
            _copy(g, "act", u[:], ps[:])
            ps = _psum(g)
            for h in range(4):
                hp, hb = h // 2, (h % 2) * 64
                _mm(g, ps[hb:hb + 64, hp * 128:(hp + 1) * 128], kbg[:, h, :], nq[:, h, 128:256])
            for (lo, hi, par) in ((0, 64, 0), (64, 128, 1)):
                _copy(g, "act", wT[lo:hi, par::2, :], ps[lo:hi, 0:256].rearrange("p (h c) -> p h c", c=128))
            if z == 1 and want_out:
                ps = _psum(g)
                _mmk(g, ps[:, 0:512], [(at[k], wg[:, k, :]) for k in range(KC)])
                _act(g, G2[:], ps[:, 0:512], AF.Silu)
                _tt(g, "pool", G2[:].rearrange("p (h c) -> p h c", c=128), G2[:].rearrange("p (h c) -> p h c", c=128),
                    onorm[:].unsqueeze(1).to_broadcast([128, 4, 128]), ALU.mult)
            pso = _psum(g, "acc")
            for c in ((0, 1) if z == 0 else (1, 0)):
                cb = c * 64
                psw = _psum(g)
                for h in range(4):
                    hp, hb = h // 2, (h % 2) * 64
                    _mm(g, psw[cb:cb + 64, h * 128:(h + 1) * 128], wT[:, h, cb:cb + 64], Sbf[:, hp, :])
                _tt(g, "dve", vnew[cb:cb + 64, :], u[cb:cb + 64, :], psw[cb:cb + 64, :], ALU.subtract)
                psS = _psum(g, "misc")
                for h in range(4):
                    hp, hb = h // 2, (h % 2) * 64
                    if want_out:
                        _mm(g, pso[cb:cb + 64, h * 128:(h + 1) * 128], qdecT[:, h, cb:cb + 64], Sbf[:, hp, :], start=True, stop=False)
                        _mm(g, pso[cb:cb + 64, h * 128:(h + 1) * 128], qkm[:, h, cb:cb + 64], vnew[:, h * 128:(h + 1) * 128], start=False, stop=True)
                    _mm(g, psS[hb:hb + 64, hp * 128:(hp + 1) * 128], kend2[:, c, h * 64:(h + 1) * 64], vnew[:, h * 128:(h + 1) * 128])
                for hp in range(2):
                    _stt(g, "dve", S[:, hp, :], S[:, hp, :], ge[:, hp, c:c + 1], psS[:, hp * 128:(hp + 1) * 128], ALU.mult, ALU.add)
                _copy(g, "pool", Sbf[:], S[:])
            if not want_out:
                continue
            if z == 0:
                _copy(g, "act", Of[:, t, :], pso[:])
            else:
                _tt(g, "dve", ob[:], pso[:], Of[:, t, :], ALU.add)
                _tt(g, "pool", sqo[:], ob[:], ob[:], ALU.mult)
                P.op("dve", lambda e: e.tensor_reduce(s_[:, 24:28], sqo[:].rearrange("p (h c) -> p h c", c=128), axis=AX, op=ALU.add), [sqo[:]], [s_[:, 24:28]])
                _rstd_from_sum(g, s_[:, 24:28], s_[:, 24:28], 1.0 / 128, g.eps[:])
                o3 = ob[:].rearrange("p (h c) -> p h c", c=128)
                _tt(g, "dve", o3, o3, s_[:, 24:28].unsqueeze(2).to_broadcast([128, 4, 128]), ALU.mult)
                _tt(g, "pool", mtok[:], ob[:], G2[:], ALU.mult)
                for h in range(4):
                    _tr(g, g.psbf[:, h * 128:(h + 1) * 128], mtok[:, h * 128:(h + 1) * 128], g.ident_bf[:])
                m_ = mt.get()
                _copy(g, "act", m_[:], g.psbf[:, 0:512].rearrange("p (h c) -> p h c", c=128))
                P.dma(g.dr["MIX"].rearrange("k p t -> p k t")[:, 4 * hf:4 * hf + 4, ts], m_[:])


def make_in_maps(inputs):
    cst, rope, bmask = host_consts()
    maps = []
    for b in range(8):
        m = {
            "x": np.ascontiguousarray(inputs["x"][b], dtype=np.float32),
            "ctx": np.ascontiguousarray(inputs["ctx"][b], dtype=np.float32),
            "cc": np.ascontiguousarray(np.concatenate([np.asarray(inputs["c"][b]).reshape(8, 128),
                                                       np.asarray(inputs["c_ctx"]).reshape(8, 128)], 0), dtype=np.float32),
            "cst": cst, "rope": rope, "bmask": bmask,
        }
        for n in WEIGHT_NAMES:
            m[n] = np.ascontiguousarray(inputs[n], dtype=np.float32)
        maps.append(m)
    return maps


_PROG_CACHE = {}


def kernel(**inputs):
    if "nc" not in _PROG_CACHE:
        import os
        _PROG_CACHE["nc"] = build_program(n_layers=int(os.environ.get("K_LAYERS", DEPTH)))[0]
    nc = _PROG_CACHE["nc"]
    in_maps = make_in_maps(inputs)
    res = run_bass_kernel_spmd(nc, in_maps, core_ids=list(range(8)))
    out = np.stack([np.asarray(res.results[b]["out"], dtype=np.float32) for b in range(8)], 0)
    return out
```

```python
import contextlib
import numpy as np
import concourse.bass as bass
import concourse.mybir as mybir
from concourse.bass_utils import run_bass_kernel_spmd

F32 = mybir.dt.float32
BF16 = mybir.dt.bfloat16
AF = mybir.ActivationFunctionType
ALU = mybir.AluOpType

SAME_ENGINE_SYNC = True
SEM_ROTATE = 30000
DBG_SKIP = set()
USE_AB32 = False
DBG_STOP = None


class _Stop(Exception):
    pass


def _chk(name):
    if DBG_STOP == name:
        raise _Stop()


class _Eng:
    def __init__(self, name, handle, is_dma_only=False):
        self.name = name
        self.h = handle
        self.sem = None
        self.count = 0
        self.waited = {}
        self.pending = False


class Prog:
    def __init__(self, nc, es, n_dma_sems=40):
        self.nc = nc
        self.es = es
        self.engs = {
            "pe": _Eng("pe", nc.tensor),
            "act": _Eng("act", nc.scalar),
            "dve": _Eng("dve", nc.vector),
            "pool": _Eng("pool", nc.gpsimd),
            "sp": _Eng("sp", nc.sync),
        }
        self.semid = 0
        for e in self.engs.values():
            e.sem = self._newsem()
        self.dma_sems = [[self._newsem(), 0] for _ in range(n_dma_sems)]
        self.dma_rr = 0
        self.recs = {}
        self.n_inst = 0
        self.n_wait = 0
        self.out_tokens = []

    def _newsem(self):
        self.semid += 1
        return self.es.enter_context(self.nc.semaphore("s%d" % self.semid))

    @staticmethod
    def _box(ap):
        t = ap.tensor
        name = t.name
        shape = tuple(t.shape)
        pat = ap.ap
        off = ap.offset
        sp = str(ap.space) if not isinstance(ap.space, str) else ap.space
        if "DRAM" in sp.upper() or "HBM" in sp.upper() or "Dram" in sp:
            W = shape[-1]
            r0, c0 = off // W, off % W
            r1, c1 = r0, c0
            ok = True
            for (s, c) in pat:
                s = abs(s)
                if c <= 1:
                    continue
                if s % W == 0:
                    r1 += (c - 1) * (s // W)
                elif s * (c - 1) < W:
                    c1 += (c - 1) * s
                else:
                    ok = False
            if (not ok) or c1 >= W:
                ext = 1
                for (s, c) in pat:
                    ext += (c - 1) * abs(s)
                return (name, 0, 1 << 30, 0, 1 << 30) if True else None
            return (name, r0, r1 + 1, c0, c1 + 1)
        fsz = 1
        for s in shape[1:]:
            fsz *= s
        pstep, pcnt = pat[0]
        if pstep != fsz and pcnt > 1:
            return (name, 0, 128, 0, fsz)
        p0 = off // fsz
        f0 = off % fsz
        ext = 1
        for (s, c) in pat[1:]:
            ext += (c - 1) * abs(s)
        if name.startswith("ps"):
            return (name, (p0 // 32) * 32, ((p0 + pcnt + 31) // 32) * 32, 0, fsz)
        return (name, p0, p0 + pcnt, f0, f0 + ext)

    def _deps(self, eng, reads, writes, is_dma):
        deps = {}
        boxes_r = [self._box(a) for a in reads]
        boxes_w = [self._box(a) for a in writes]
        for kind, boxes in (("r", boxes_r), ("w", boxes_w)):
            for b in boxes:
                lst = self.recs.get(b[0])
                if not lst:
                    continue
                for rec in lst:
                    if kind == "r" and rec[4] == "r":
                        continue
                    if rec[0] >= b[2] or b[1] >= rec[1] or rec[2] >= b[4] or b[3] >= rec[3]:
                        continue
                    if (not is_dma) and (not rec[8]) and rec[7] == eng.name and (eng.name == "pe" or not SAME_ENGINE_SYNC):
                        continue
                    s, v = rec[5], rec[6]
                    k = id(s)
                    if k not in deps or deps[k][1] < v:
                        deps[k] = (s, v)
        return deps, boxes_r, boxes_w

    def _record(self, eng, boxes_r, boxes_w, sem, val, is_dma):
        for b in boxes_w:
            lst = self.recs.setdefault(b[0], [])
            lst[:] = [r for r in lst if not (b[1] <= r[0] and r[1] <= b[2] and b[3] <= r[2] and r[3] <= b[4])]
            lst.append([b[1], b[2], b[3], b[4], "w", sem, val, eng.name, is_dma])
        for b in boxes_r:
            lst = self.recs.setdefault(b[0], [])
            hit = False
            if not is_dma:
                for r in lst:
                    if r[4] == "r" and r[7] == eng.name and not r[8] and r[0] == b[1] and r[1] == b[2] and r[2] == b[3] and r[3] == b[4]:
                        r[5], r[6] = sem, val
                        hit = True
                        break
            if not hit:
                lst.append([b[1], b[2], b[3], b[4], "r", sem, val, eng.name, is_dma])
                if len(lst) > 96:
                    self._compact(lst)

    @staticmethod
    def _compact(lst):
        keep = [r for r in lst if r[4] == "w"]
        merged = {}
        for r in lst:
            if r[4] != "r":
                continue
            k = (r[7], id(r[5]), r[8])
            m = merged.get(k)
            if m is None:
                merged[k] = list(r)
            else:
                m[0] = min(m[0], r[0]); m[1] = max(m[1], r[1]); m[2] = min(m[2], r[2]); m[3] = max(m[3], r[3])
                m[6] = max(m[6], r[6])
        lst[:] = keep + list(merged.values())

    def _emit_waits(self, eng, deps):
        for (s, v) in deps.values():
            k = id(s)
            if eng.waited.get(k, 0) >= v:
                continue
            eng.h.wait_ge(s, v)
            eng.waited[k] = v
            self.n_wait += 1

    def op(self, en, fn, reads=(), writes=(), inc=True):
        eng = self.engs[en]
        deps, br, bw = self._deps(eng, reads, writes, False)
        self._emit_waits(eng, deps)
        ins = fn(eng.h)
        self.n_inst += 1
        if inc:
            if eng.count >= SEM_ROTATE and not eng.pending:
                eng.sem = self._newsem()
                eng.count = 0
            eng.count += 1
            ins.then_inc(eng.sem, 1)
            tok = (eng.sem, eng.count)
            eng.pending = False
        else:
            tok = (eng.sem, eng.count + 1)
            eng.pending = True
        self._record(eng, br, bw, tok[0], tok[1], False)
        return ins

    def dma(self, out, in_, q="sp", is_output=False, **kw):
        eng = self.engs[q]
        deps, br, bw = self._deps(eng, [in_], [out], True)
        slot = self.dma_sems[self.dma_rr]
        self.dma_rr = (self.dma_rr + 1) % len(self.dma_sems)
        if slot[1] > 0:
            deps[id(slot[0])] = (slot[0], slot[1])
        self._emit_waits(eng, deps)
        slot[1] += 16
        eng.h.dma_start(out=out, in_=in_, **kw).then_inc(slot[0], 16)
        self.n_inst += 1
        self._record(eng, br, bw, slot[0], slot[1], True)
        if is_output:
            self.out_tokens.append((slot[0], slot[1]))

    def finish(self):
        eng = self.engs["sp"]
        deps = {}
        for slot in self.dma_sems:
            if slot[1] > 0:
                deps[id(slot[0])] = (slot[0], slot[1])
        self._emit_waits(eng, deps)

    def barrier(self):
        deps = {}
        for e in self.engs.values():
            if e.count > 0:
                deps[id(e.sem)] = (e.sem, e.count)
        for slot in self.dma_sems:
            if slot[1] > 0:
                deps[id(slot[0])] = (slot[0], slot[1])
        for e in self.engs.values():
            d = {k: v for k, v in deps.items() if k != id(e.sem)}
            self._emit_waits(e, d)
        self.recs = {}


D = 1024
KC = 8
SEQ = 2048
NCTX = 256
TOK = NCTX + SEQ
NT = TOK // 128
DEPTH = 4
DFF = 4096
EPS = 1e-6
BLOCKS = [(0, 256), (256, 512), (768, 512), (1280, 512), (1792, 512)]
EVEN_IN = 2240
ODD_IN = 3104
AX = mybir.AxisListType.X

WEIGHT_NAMES = ["ada_w", "ada_b", "norm1_w", "norm2_w", "mlp_w1", "mlp_w2", "even_w_in", "mla_q_norm", "mla_w_uq",
                "mla_kv_norm", "mla_w_ukv", "gla_gate_up", "gla_gate_bias", "gla_o_norm", "even_w_out", "gdn_w_in",
                "gdn_conv_w", "gdn_a_log", "gdn_dt_bias", "gdn_o_norm", "gdn_w_out", "final_norm"]
WEIGHT_SHAPES = {
    "ada_w": [4, 1024, 6144], "ada_b": [4, 6144], "norm1_w": [4, 1024], "norm2_w": [4, 1024],
    "mlp_w1": [4, 1024, 4096], "mlp_w2": [4, 4096, 1024], "even_w_in": [2, 1024, 2240], "mla_q_norm": [2, 384],
    "mla_w_uq": [2, 384, 768], "mla_kv_norm": [2, 256], "mla_w_ukv": [2, 256, 1024], "gla_gate_up": [2, 2, 16, 256],
    "gla_gate_bias": [2, 2, 256], "gla_o_norm": [2, 128], "even_w_out": [2, 1024, 1024], "gdn_w_in": [2, 1024, 3104],
    "gdn_conv_w": [2, 3, 2048], "gdn_a_log": [2, 2, 8], "gdn_dt_bias": [2, 2, 8], "gdn_o_norm": [2, 128],
    "gdn_w_out": [2, 1024, 1024], "final_norm": [1024],
}


def host_consts():
    i = np.arange(128)
    same = (i[:, None] // 64) == (i[None, :] // 64)
    Mf = (same & (i[:, None] <= i[None, :])).astype(np.float32)
    Mb = (same & (i[:, None] >= i[None, :])).astype(np.float32)
    Sf = (same & (i[:, None] > i[None, :])).astype(np.float32)
    Sb = (same & (i[:, None] < i[None, :])).astype(np.float32)
    CI = np.stack([(i < 64), (i >= 64)], 1).astype(np.float32)
    ident = np.eye(128, dtype=np.float32)
    cols = [ident, Mf, Mb, Sf, Sb, CI,
            np.concatenate([Mf, CI], 1) / -16.0, np.concatenate([Mb, CI], 1) / -16.0, Sf / -16.0, Sb / -16.0,
            np.concatenate([Mf, CI], 1), np.concatenate([Mb, CI], 1)]
    cst = np.concatenate(cols, 1).astype(np.float32)
    rows = SEQ // 64
    row = np.repeat(np.arange(rows, dtype=np.float32), 64)
    col = np.tile(np.arange(64, dtype=np.float32), rows)
    inv = (10000.0 ** (-np.arange(0, 16, 2, dtype=np.float32) / 16)).astype(np.float32)
    ang = np.concatenate([row[:, None] * inv, col[:, None] * inv], -1).astype(np.float32)
    cos, sin = np.cos(ang).astype(np.float32), np.sin(ang).astype(np.float32)
    C = np.zeros((32, SEQ), np.float32)
    S = np.zeros((32, SEQ), np.float32)
    for ax in range(2):
        for half in range(2):
            for f in range(8):
                d = ax * 16 + half * 8 + f
                C[d] = cos[:, ax * 8 + f]
                S[d] = (-sin[:, ax * 8 + f]) if half == 0 else sin[:, ax * 8 + f]
    rope = np.stack([C, S], 0)
    r = np.arange(128)
    ms = []
    for b in (1, 2, 4, 8, 16, 32):
        same2 = (r[:, None] // (2 * b)) == (r[None, :] // (2 * b))
        ur = same2 & ((r[:, None] % (2 * b)) < b) & ((r[None, :] % (2 * b)) >= b)
        ms.append(ur.astype(np.float32))
    bmask = np.concatenate(ms + [m.T for m in ms], 1).astype(np.float32)
    return cst, rope, bmask


CST_OFF = {}
_o = 0
for _n, _w in [("ident", 128), ("Mf", 128), ("Mb", 128), ("Sf", 128), ("Sb", 128), ("CI", 2), ("glaRf", 130), ("glaRb", 130),
               ("glaAf", 128), ("glaAb", 128), ("gdnRf", 130), ("gdnRb", 130)]:
    CST_OFF[_n] = (_o, _w)
    _o += _w
CST_W = _o


class Ctx:
    pass


def build_program(n_layers=DEPTH, fake_mixer=False, debug_h=False):
    nc = bass.Bass("TRN2", target_bir_lowering=False)
    g = Ctx()
    g.nc = nc
    dr = {}
    dr["x"] = nc.dram_tensor("x", [SEQ, D], F32, kind="ExternalInput").ap()
    dr["ctx"] = nc.dram_tensor("ctx", [NCTX, D], F32, kind="ExternalInput").ap()
    dr["cc"] = nc.dram_tensor("cc", [16, 128], F32, kind="ExternalInput").ap()
    dr["cst"] = nc.dram_tensor("cst", [128, CST_W], F32, kind="ExternalInput").ap()
    dr["rope"] = nc.dram_tensor("rope", [2, 32, SEQ], F32, kind="ExternalInput").ap()
    dr["bmask"] = nc.dram_tensor("bmask", [128, 12 * 128], F32, kind="ExternalInput").ap()
    for n in WEIGHT_NAMES:
        dr[n] = nc.dram_tensor(n, WEIGHT_SHAPES[n], F32, kind="ExternalInput").ap()
    dr["out"] = nc.dram_tensor("out", [SEQ, D], F32, kind="ExternalOutput").ap()
    dr["H"] = nc.dram_tensor("Hs", [KC, 128, TOK], F32).ap()
    dr["MIX"] = nc.dram_tensor("MIXs", [KC, 128, TOK], BF16).ap()
    if debug_h:
        dr["dbg"] = nc.dram_tensor("dbg", [KC, 128, TOK], F32, kind="ExternalOutput").ap()
    g.dr = dr
    es = contextlib.ExitStack()
    with es:
        P = Prog(nc, es)
        g.P = P
        g.es = es
        g.ps = [es.enter_context(nc.psum_tensor("psb%d" % i, [128, 512], F32)) for i in range(7)]
        g.psbf = es.enter_context(nc.psum_tensor("psbf", [128, 1024], BF16))
        g.ps_rr = 0
        _emit(g, n_layers, fake_mixer, debug_h)
        P.finish()
        g.stats = (P.n_inst, P.n_wait, P.semid)
    return nc, g


_UNIQ = [0]


def _sb(g, name, shape, dt=F32, stack=None):
    _UNIQ[0] += 1
    return (stack or g.es).enter_context(g.nc.sbuf_tensor("sb_%s_%d" % (name, _UNIQ[0]), shape, dt))


def _psum(g, kind="rot"):
    if kind == "rot":
        t = g.ps[g.ps_rr]
        g.ps_rr = (g.ps_rr + 1) % getattr(g, "rot_n", 4)
        return t
    if kind == "acc":
        g.ps_acc = 1 - getattr(g, "ps_acc", 0)
        return g.ps[4 + g.ps_acc]
    return g.ps[6]


def _mm(g, out, lhsT, rhs, start=True, stop=True, inc=True):
    g.P.op("pe", lambda e: e.matmul(out, lhsT, rhs, start=start, stop=stop), [lhsT, rhs], [out], inc=inc)


def _mmk(g, out, pairs):
    n = len(pairs)
    for i, (l, r) in enumerate(pairs):
        _mm(g, out, l, r, start=(i == 0), stop=(i == n - 1), inc=(i == n - 1))


def _tr(g, out, in_, ident):
    g.P.op("pe", lambda e: e.transpose(out, in_, ident), [in_, ident], [out])


def _act(g, out, in_, func, scale=1.0, bias=0.0, extra_reads=()):
    rd = [in_] + [a for a in (scale, bias) if not isinstance(a, (int, float))] + list(extra_reads)
    g.P.op("act", lambda e: e.activation(out=out, in_=in_, func=func, bias=bias, scale=scale), rd, [out])


def _tt(g, en, out, in0, in1, op):
    g.P.op(en, lambda e: e.tensor_tensor(out, in0, in1, op=op), [in0, in1], [out])


def _stt(g, en, out, in0, scalar, in1, op0, op1):
    rd = [in0, in1] + ([scalar] if not isinstance(scalar, (int, float)) else [])
    g.P.op(en, lambda e: e.scalar_tensor_tensor(out=out, in0=in0, scalar=scalar, in1=in1, op0=op0, op1=op1), rd, [out])


def _ts(g, en, out, in0, s1, s2, op0, op1=None):
    rd = [in0] + [a for a in (s1, s2) if a is not None and not isinstance(a, (int, float))]
    if op1 is None:
        g.P.op(en, lambda e: e.tensor_scalar(out, in0, s1, None, op0=op0), rd, [out])
    else:
        g.P.op(en, lambda e: e.tensor_scalar(out, in0, s1, s2, op0=op0, op1=op1), rd, [out])


def _copy(g, en, out, in_):
    if en == "act":
        g.P.op("act", lambda e: e.copy(out, in_), [in_], [out])
    else:
        g.P.op(en, lambda e: e.tensor_copy(out, in_), [in_], [out])


def _memset(g, en, out, val):
    g.P.op(en, lambda e: e.memset(out, val), [], [out])


def _load_cast(g, dst, src):
    shp = list(dst.shape)
    if len(shp) == 2:
        pieces = [(dst[:, c:min(c + 1024, shp[1])], src[:, c:min(c + 1024, shp[1])]) for c in range(0, shp[1], 1024)]
    else:
        inner = shp[2]
        assert len(shp) == 3 and inner <= 1024
        step = max(1, 1024 // inner)
        pieces = [(dst[:, a:min(a + step, shp[1]), :], src[:, a:min(a + step, shp[1]), :]) for a in range(0, shp[1], step)]
    for d, s_ in pieces:
        stg = g.stage.get()
        n = 1
        for x in d.shape[1:]:
            n *= x
        if len(d.shape) == 3:
            v = stg[:, 0:n].rearrange("p (a b) -> p a b", b=d.shape[2])
        else:
            v = stg[:, 0:n]
        g.P.dma(v, s_)
        g.cast_rr = 1 - getattr(g, "cast_rr", 0)
        _copy(g, "pool" if g.cast_rr else "act", d, v)


def _rstd_from_sum(g, out_sb, ps_in, inv_n, eps_col):
    _act(g, out_sb, ps_in, AF.Sqrt, scale=inv_n, bias=eps_col)
    g.P.op("dve", lambda e: e.reciprocal(out_sb, out_sb), [out_sb], [out_sb])


class Pool_:
    def __init__(self, g, name, shape, dt, n, stack=None, zero=False):
        self.tiles = [_sb(g, "%s%d" % (name, i), shape, dt, stack) for i in range(n)]
        self.i = 0
        if zero:
            for t in self.tiles:
                _memset(g, "pool", t[:], 0.0)

    def get(self):
        t = self.tiles[self.i]
        self.i = (self.i + 1) % len(self.tiles)
        return t


def _cst(g, name):
    o, w = CST_OFF[name]
    return g.cst[:, o:o + w]


def _setup(g):
    P, dr = g.P, g.dr
    g.cst = _sb(g, "cst", [128, CST_W])
    P.dma(g.cst[:], dr["cst"])
    g.ident = _cst(g, "ident")
    g.ident_bf = _sb(g, "ident_bf", [128, 128], BF16)
    _copy(g, "dve", g.ident_bf[:], g.ident)
    g.ones_bf = _sb(g, "ones_bf", [128, 128], BF16)
    _memset(g, "pool", g.ones_bf[:], 1.0)
    g.ones_f = _sb(g, "ones_f", [128, 128])
    _memset(g, "pool", g.ones_f[:], 1.0)
    g.eps = _sb(g, "eps", [128, 1])
    _memset(g, "pool", g.eps[:], EPS)
    rows = []
    rows.append(("cc", dr["cc"], 16))
    for L in range(DEPTH):
        rows.append(("ada_b%d" % L, dr["ada_b"][L].rearrange("(r c) -> r c", c=128), 48))
        rows.append(("n1w%d" % L, dr["norm1_w"][L].rearrange("(r c) -> r c", c=128), 8))
        rows.append(("n2w%d" % L, dr["norm2_w"][L].rearrange("(r c) -> r c", c=128), 8))
    for i in range(2):
        rows.append(("qn%d" % i, dr["mla_q_norm"][i].rearrange("(r c) -> r c", c=128), 3))
        rows.append(("kvn%d" % i, dr["mla_kv_norm"][i].rearrange("(r c) -> r c", c=128), 2))
        rows.append(("conv%d" % i, dr["gdn_conv_w"][i].rearrange("t (r c) -> (t r) c", c=128), 48))
    rows.append(("fn", dr["final_norm"].rearrange("(r c) -> r c", c=128), 8))
    tot = sum(r[2] for r in rows)
    ntile = (tot + 127) // 128
    g.cols = _sb(g, "cols", [128, ntile * 128])
    g.coloff = {}
    with contextlib.ExitStack() as st:
        rt = [_sb(g, "rowst%d" % i, [128, 128], F32, st) for i in range(ntile)]
        for t in rt:
            _memset(g, "dve", t[:], 0.0)
        r = 0
        for key, ap, n in rows:
            g.coloff[key] = r
            done = 0
            while done < n:
                t, p = divmod(r + done, 128)
                m = min(n - done, 128 - p)
                P.dma(rt[t][p:p + m, :], ap[done:done + m, :])
                done += m
            r += n
        for t in range(ntile):
            ps = _psum(g)
            _tr(g, ps[:, 0:128], rt[t][:], g.ident)
            _copy(g, "dve", g.cols[:, t * 128:(t + 1) * 128], ps[:, 0:128])
        P.barrier()
    g.sc = _sb(g, "sc", [128, KC, 2])
    _act(g, g.sc[:, :, 0], g.cols[:, 0:8], AF.Silu)
    _act(g, g.sc[:, :, 1], g.cols[:, 8:16], AF.Silu)
    g.mod = [_sb(g, "mod%d" % L, [128, 48, 2]) for L in range(DEPTH)]
    g.comb1 = [_sb(g, "comb1_%d" % L, [128, KC, 2]) for L in range(DEPTH)]
    g.comb2 = [_sb(g, "comb2_%d" % L, [128, KC, 2]) for L in range(DEPTH)]
    g.adaw_pool = Pool_(g, "adaw", [128, KC, 128], F32, 2)
    g.stage = Pool_(g, "stage", [128, 1024], F32, 2)


def _col(g, key, n):
    o = g.coloff[key]
    return g.cols[:, o:o + n]


def _adaln_steps(g, L):
    P, dr = g.P, g.dr
    src = dr["ada_w"][L].rearrange("(k p) n -> p k n", p=128)
    for cb in range(48):
        wt = g.adaw_pool.get()
        P.dma(wt[:], src[:, :, cb * 128:(cb + 1) * 128])
        ps = _psum(g)
        _mmk(g, ps[:, 0:2], [(wt[:, k, :], g.sc[:, k, :]) for k in range(KC)])
        bias = _col(g, "ada_b%d" % L, 48)[:, cb:cb + 1]
        _tt(g, "dve", g.mod[L][:, cb, :], ps[:, 0:2], bias.to_broadcast([128, 2]), ALU.add)
        if cb == 47:
            n1 = _col(g, "n1w%d" % L, 8).unsqueeze(2).to_broadcast([128, KC, 2])
            n2 = _col(g, "n2w%d" % L, 8).unsqueeze(2).to_broadcast([128, KC, 2])
            _stt(g, "dve", g.comb1[L][:], g.mod[L][:, 8:16, :], 1.0, n1, ALU.add, ALU.mult)
            _stt(g, "dve", g.comb2[L][:], g.mod[L][:, 32:40, :], 1.0, n2, ALU.add, ALU.mult)
        yield


def _mix_view(g, s0, n):
    return g.dr["MIX"].rearrange("k p t -> p k t")[:, :, s0:s0 + n]


def _h_view(g, s0, n):
    return g.dr["H"].rearrange("k p t -> p k t")[:, :, s0:s0 + n]


def _load_inputs(g):
    P, dr = g.P, g.dr
    with contextlib.ExitStack() as st:
        xin = Pool_(g, "xin", [128, D], F32, 2, st)
        hb = Pool_(g, "hb0_", [128, KC, 128], F32, 2, st)
        for t in range(NT):
            src = dr["ctx"][t * 128:(t + 1) * 128, :] if t < 2 else dr["x"][(t - 2) * 128:(t - 1) * 128, :]
            xt = xin.get()
            P.dma(xt[:], src)
            ht = hb.get()
            for half in range(2):
                ps = _psum(g)
                for q in range(4):
                    k = half * 4 + q
                    _tr(g, ps[:, q * 128:(q + 1) * 128], xt[:, k * 128:(k + 1) * 128], g.ident)
                _copy(g, "act" if half == 0 else "dve", ht[:, half * 4:(half + 1) * 4, :],
                      ps[:].rearrange("p (q t) -> p q t", t=128))
            P.dma(_h_view(g, t * 128, 128), ht[:])
        P.barrier()


def _norm_tmps(g, st):
    return (Pool_(g, "sqn", [128, 512], BF16, 2, st), Pool_(g, "tmn", [128, 512], F32, 3, st), _sb(g, "rstdn", [128, 512], F32, st))


def _norm_mod(g, hblk, N, comb, shift, out_bf, tmps, ab=None):
    sqp, tp, rstd = tmps
    ps = _psum(g)
    for k in range(KC):
        sq = sqp.get()
        _act(g, sq[:, :N], hblk[:, k, :N], AF.Square)
        _mm(g, ps[:, :N], g.ones_bf[:], sq[:, :N], start=(k == 0), stop=(k == KC - 1))
    _rstd_from_sum(g, rstd[:, :N], ps[:, :N], 1.0 / D, g.eps[:])
    if ab is not None:
        w32, AB, t0, a32p = ab
        psab = _psum(g, "misc")
    for k in range(KC):
        t = tp.get()
        _stt(g, "dve", t[:, :N], hblk[:, k, :N], comb[:, k:k + 1], rstd[:, :N], ALU.mult, ALU.mult)
        if ab is None:
            _act(g, out_bf[:, k, :N], t[:, :N], AF.Identity, scale=1.0, bias=shift[:, k:k + 1])
        else:
            a32 = a32p.get()
            _act(g, a32[:, :N], t[:, :N], AF.Identity, scale=1.0, bias=shift[:, k:k + 1])
            _copy(g, "pool", out_bf[:, k, :N], a32[:, :N])
            for tt in range(N // 128):
                _mm(g, psab[:, tt * 32:(tt + 1) * 32], a32[:, tt * 128:(tt + 1) * 128], w32[:, k, :], start=(k == 0 and tt == 0), stop=(k == KC - 1 and tt == N // 128 - 1))
    if ab is not None:
        nt = N // 128
        _copy(g, "act", AB[:, t0:t0 + nt, :], psab[:, 0:nt * 32].rearrange("p (t c) -> p t c", c=32))


def _emit(g, n_layers, fake_mixer, debug_h):
    P, dr = g.P, g.dr
    _setup(g)
    ada = _adaln_steps(g, 0)
    for _ in ada:
        pass
    _load_inputs(g)
    for L in range(n_layers):
        last = (L == DEPTH - 1)
        need_ctx = not last
        nxt = _adaln_steps(g, L + 1) if L + 1 < n_layers else iter(())
        with contextlib.ExitStack() as lstA:
            lat = _mla_alloc(g, lstA) if (L % 2 == 0 and not fake_mixer) else None
            with contextlib.ExitStack() as lst:
                aT = _sb(g, "aT", [128, KC, TOK], BF16, lst)
                AB = None
                if L % 2 == 1 and not fake_mixer and USE_AB32:
                    AB = _sb(g, "AB", [128, NT, 32], F32, lst)
                with contextlib.ExitStack() as st:
                    hbp = Pool_(g, "hbn", [128, KC, 512], F32, 2, st)
                    tmps = _norm_tmps(g, st)
                    ab = None
                    if AB is not None:
                        w32 = _sb(g, "wab32", [128, KC, 32], F32, st)
                        P.dma(w32[:], dr["gdn_w_in"][L // 2].rearrange("(k p) n -> p k n", p=128)[:, :, O_A:O_A + 32])
                        a32p = Pool_(g, "a32", [128, 512], F32, 2, st)
                    for bi, (s0, N) in enumerate(BLOCKS):
                        s = 1 if bi == 0 else 0
                        hb = hbp.get()
                        P.dma(hb[:, :, :N], _h_view(g, s0, N))
                        if AB is not None:
                            ab = (w32, AB, s0 // 128, a32p)
                        _norm_mod(g, hb, N, g.comb1[L][:, :, s], g.mod[L][:, 0:8, s], aT[:, :, s0:s0 + N], tmps, ab)
                    P.barrier()
                if fake_mixer:
                    for (s0, N) in BLOCKS:
                        P.dma(_mix_view(g, s0, N), aT[:, :, s0:s0 + N])
                elif L % 2 == 0:
                    _even_mixer_a(g, L, aT, lat, lst)
                else:
                    _odd_mixer(g, L, aT, need_ctx, AB)
                P.barrier()
            if L % 2 == 0 and not fake_mixer and "attn" not in DBG_SKIP:
                _mla_attention(g, L, lat)
                P.barrier()
        wname = "even_w_out" if L % 2 == 0 else "gdn_w_out"
        with contextlib.ExitStack() as st:
            wo = _sb(g, "wo", [128, KC, D], BF16, st)
            _load_cast(g, wo[:], dr[wname][L // 2].rearrange("(k p) n -> p k n", p=128))
            hbp = Pool_(g, "hba", [128, KC, 512], F32, 2, st)
            mxp = Pool_(g, "mxb", [128, KC, 512], BF16, 2, st)
            for bi, (s0, N) in enumerate(BLOCKS):
                if bi == 0 and not need_ctx:
                    continue
                s = 1 if bi == 0 else 0
                hb = hbp.get()
                P.dma(hb[:, :, :N], _h_view(g, s0, N))
                mx = mxp.get()
                P.dma(mx[:, :, :N], _mix_view(g, s0, N))
                for m in range(KC):
                    ps = _psum(g)
                    _mmk(g, ps[:, :N], [(wo[:, k, m * 128:(m + 1) * 128], mx[:, k, :N]) for k in range(KC)])
                    _stt(g, "dve", hb[:, m, :N], ps[:, :N], g.mod[L][:, 16 + m, s:s + 1], hb[:, m, :N], ALU.mult, ALU.add)
                P.dma(_h_view(g, s0, N), hb[:, :, :N])
                for _ in range(5):
                    next(nxt, None)
            P.barrier()
        if debug_h == ("mix", L):
            _dump_h(g)
            return
        with contextlib.ExitStack() as st:
            w2 = _sb(g, "w2", [128, 32, D], BF16, st)
            w2src = dr["mlp_w2"][L].rearrange("(f p) n -> p f n", p=128)
            _load_cast(g, w2[:], w2src)
            w1p = Pool_(g, "w1p", [128, KC, 512], BF16, 2, st)
            w1src = dr["mlp_w1"][L].rearrange("(k p) n -> p k n", p=128)
            hbp = Pool_(g, "hbm", [128, KC, 512], F32, 1, st)
            a2 = _sb(g, "a2", [128, KC, 512], BF16, st)
            h1 = _sb(g, "h1", [128, 32, 512], BF16, st)
            rl = Pool_(g, "rl", [128, 512], F32, 3, st)
            tmps = _norm_tmps(g, st)
            for bi, (s0, N) in enumerate(BLOCKS):
                if bi == 0 and not need_ctx:
                    continue
                s = 1 if bi == 0 else 0
                hb = hbp.get()
                P.dma(hb[:, :, :N], _h_view(g, s0, N))
                _norm_mod(g, hb, N, g.comb2[L][:, :, s], g.mod[L][:, 24:32, s], a2, tmps)
                for fg in range(8):
                    w1 = w1p.get()
                    _load_cast(g, w1[:], w1src[:, :, fg * 512:(fg + 1) * 512])
                    for f in range(4):
                        ps = _psum(g)
                        _mmk(g, ps[:, :N], [(w1[:, k, f * 128:(f + 1) * 128], a2[:, k, :N]) for k in range(KC)])
                        r = rl.get()
                        _act(g, r[:, :N], ps[:, :N], AF.Relu)
                        _tt(g, "pool", h1[:, fg * 4 + f, :N], r[:, :N], r[:, :N], ALU.mult)
                for m in range(KC):
                    ps = _psum(g)
                    _mmk(g, ps[:, :N], [(w2[:, f, m * 128:(m + 1) * 128], h1[:, f, :N]) for f in range(32)])
                    _stt(g, "dve", hb[:, m, :N], ps[:, :N], g.mod[L][:, 40 + m, s:s + 1], hb[:, m, :N], ALU.mult, ALU.add)
                P.dma(_h_view(g, s0, N), hb[:, :, :N])
                for _ in range(5):
                    next(nxt, None)
            for _ in nxt:
                pass
            P.barrier()
        if debug_h == ("mlp", L):
            _dump_h(g)
            return
    if debug_h:
        _dump_h(g)
        return
    _final(g)


def _dump_h(g):
    P = g.P
    with contextlib.ExitStack() as st:
        hb = Pool_(g, "hbd", [128, KC, 512], F32, 2, st)
        for (s0, N) in BLOCKS:
            t = hb.get()
            P.dma(t[:, :, :N], _h_view(g, s0, N))
            P.dma(g.dr["dbg"].rearrange("k p t -> p k t")[:, :, s0:s0 + N], t[:, :, :N], is_output=True)
        P.dma(g.dr["out"][0:128, :], g.cst[:, 0:D], is_output=True)


def _final(g):
    P, dr = g.P, g.dr
    with contextlib.ExitStack() as st:
        hbp = Pool_(g, "hbf", [128, KC, 512], F32, 2, st)
        sq = _sb(g, "sqf", [128, KC, 512], BF16, st)
        yb = _sb(g, "yf", [128, KC, 512], F32, st)
        rstd = _sb(g, "rstdf", [128, 512], F32, st)
        otp = Pool_(g, "ot", [128, D], F32, 2, st)
        fn = _col(g, "fn", 8)
        for (s0, N) in BLOCKS[1:]:
            hb = hbp.get()
            P.dma(hb[:, :, :N], _h_view(g, s0, N))
            _act(g, sq[:, :, :N], hb[:, :, :N], AF.Square)
            ps = _psum(g)
            _mmk(g, ps[:, :N], [(g.ones_bf[:], sq[:, k, :N]) for k in range(KC)])
            _rstd_from_sum(g, rstd[:, :N], ps[:, :N], 1.0 / D, g.eps[:])
            for k in range(KC):
                _stt(g, "dve", yb[:, k, :N], hb[:, k, :N], fn[:, k:k + 1], rstd[:, :N], ALU.mult, ALU.mult)
            for tt in range(N // 128):
                ot = otp.get()
                for half in range(2):
                    ps = _psum(g)
                    for q in range(4):
                        k = half * 4 + q
                        _tr(g, ps[:, q * 128:(q + 1) * 128], yb[:, k, tt * 128:(tt + 1) * 128], g.ident)
                    _copy(g, "act" if half == 0 else "dve", ot[:, half * 512:(half + 1) * 512], ps[:])
                r0 = s0 - NCTX + tt * 128
                P.dma(dr["out"][r0:r0 + 128, :], ot[:], is_output=True)


C_CQ, C_CKV, C_KPE, C_GQ, C_GK, C_GV, C_GG, C_GLOW = 0, 384, 640, 672, 928, 1184, 1696, 2208
GLA_FWD = list(range(NT))
GLA_BWD = [1, 0] + list(range(NT - 1, 1, -1))


def _mla_alloc(g, st):
    lat = Ctx()
    lat.cqn = _sb(g, "cqn", [128, 3, TOK], BF16, st)
    lat.ckvn = _sb(g, "ckvn", [128, 2, TOK], BF16, st)
    lat.kpeT = _sb(g, "kpeT", [128, TOK], BF16, st)
    return lat


def _rope_tables(g, lat, st):
    lat.ropeC = _sb(g, "ropeC", [128, SEQ], F32, st)
    lat.ropeS = _sb(g, "ropeS", [128, SEQ], F32, st)
    g.P.dma(lat.ropeC[64:96, :], g.dr["rope"][0])
    g.P.dma(lat.ropeS[64:96, :], g.dr["rope"][1])


def _swap_halves(g, en, dst, src):
    d = dst.rearrange("p h (a t f) -> p h a t f", a=2, t=2)
    s_ = src.rearrange("p h (a t f) -> p h a t f", a=2, t=2)
    _copy(g, en, d[:, :, :, 0, :], s_[:, :, :, 1, :])
    _copy(g, en, d[:, :, :, 1, :], s_[:, :, :, 0, :])


def _even_mixer_a(g, L, aT, lat, st):
    P, dr = g.P, g.dr
    i = L // 2
    win = _sb(g, "win", [128, KC, EVEN_IN], BF16, st)
    wsrc = dr["even_w_in"][i].rearrange("(k p) n -> p k n", p=128)
    for k in range(KC):
        _load_cast(g, win[:, k, :], wsrc[:, k, :])
    with contextlib.ExitStack() as s1:
        _rope_tables(g, lat, s1)
        winrot = _sb(g, "winrot", [128, KC, 32], BF16, s1)
        _swap_halves(g, "pool", winrot[:], win[:, :, C_KPE:C_KPE + 32])
        cqf = _sb(g, "cqf", [128, 5, 512], F32, s1)
        sqp = Pool_(g, "sql", [128, 512], BF16, 2, s1)
        rq = _sb(g, "rq", [128, 512], F32, s1)
        rkv = _sb(g, "rkv", [128, 512], F32, s1)
        t1p = Pool_(g, "t1l", [128, 512], F32, 2, s1)
        qn = _col(g, "qn%d" % i, 3)
        kvn = _col(g, "kvn%d" % i, 2)
        for bi, (s0, N) in enumerate(BLOCKS):
            rhs = [aT[:, k, s0:s0 + N] for k in range(KC)]
            pss_q = _psum(g, "acc")
            pss_kv = _psum(g, "acc")
            for fc in range(5):
                ps = _psum(g)
                _mmk(g, ps[:, :N], [(win[:, k, fc * 128:(fc + 1) * 128], rhs[k]) for k in range(KC)])
                _copy(g, "act", cqf[:, fc, :N], ps[:, :N])
                sq = sqp.get()
                _tt(g, "pool", sq[:, :N], cqf[:, fc, :N], cqf[:, fc, :N], ALU.mult)
                if fc < 3:
                    _mm(g, pss_q[:, :N], g.ones_bf[:], sq[:, :N], start=(fc == 0), stop=(fc == 2))
                else:
                    _mm(g, pss_kv[:, :N], g.ones_bf[:], sq[:, :N], start=(fc == 3), stop=(fc == 4))
            _rstd_from_sum(g, rq[:, :N], pss_q[:, :N], 1.0 / 384, g.eps[:])
            _rstd_from_sum(g, rkv[:, :N], pss_kv[:, :N], 1.0 / 256, g.eps[:])
            for fc in range(3):
                _stt(g, "dve", lat.cqn[:, fc, s0:s0 + N], cqf[:, fc, :N], qn[:, fc:fc + 1], rq[:, :N], ALU.mult, ALU.mult)
            for fc in range(2):
                _stt(g, "dve", lat.ckvn[:, fc, s0:s0 + N], cqf[:, 3 + fc, :N], kvn[:, fc:fc + 1], rkv[:, :N], ALU.mult, ALU.mult)
            ps = _psum(g)
            _mmk(g, ps[64:96, :N], [(win[:, k, C_KPE:C_KPE + 32], rhs[k]) for k in range(KC)])
            if bi == 0:
                _copy(g, "act", lat.kpeT[64:96, s0:s0 + N], ps[64:96, :N])
            else:
                ps2 = _psum(g)
                _mmk(g, ps2[64:96, :N], [(winrot[:, k, :], rhs[k]) for k in range(KC)])
                p0 = s0 - NCTX
                t1 = t1p.get()
                t2 = t1p.get()
                _tt(g, "dve", t1[64:96, :N], ps[64:96, :N], lat.ropeC[64:96, p0:p0 + N], ALU.mult)
                _tt(g, "dve", t2[64:96, :N], ps2[64:96, :N], lat.ropeS[64:96, p0:p0 + N], ALU.mult)
                _tt(g, "pool", lat.kpeT[64:96, s0:s0 + N], t1[64:96, :N], t2[64:96, :N], ALU.add)
        P.barrier()
    with contextlib.ExitStack() as s2:
        if "gla" not in DBG_SKIP:
            try:
                _gla(g, i, aT, win, s2)
            except _Stop:
                pass
        P.barrier()


def _gla(g, i, aT, win, st):
    P, dr = g.P, g.dr
    gup = _sb(g, "gup", [16, 2, 256], F32, st)
    P.dma(gup[:], dr["gla_gate_up"][i].rearrange("z r n -> r z n"))
    gbias = _sb(g, "gbias", [128, 2, 256], F32, st)
    P.dma(gbias[:], dr["gla_gate_bias"][i].partition_broadcast(128))
    onorm = _sb(g, "onorm", [128, 128], F32, st)
    P.dma(onorm[:], dr["gla_o_norm"][i].partition_broadcast(128))
    Of = _sb(g, "Of", [128, NT, 512], F32, st)
    S = _sb(g, "S", [128, 2, 128], F32, st)
    Sbf = _sb(g, "Sbf", [128, 2, 128], BF16, st)
    glowT = Pool_(g, "glowT", [16, 128], F32, 2, st)
    tl = Pool_(g, "tl", [128, 256], F32, 2, st)
    Lz = Pool_(g, "Lz", [128, 256], F32, 2, st)
    Eq = Pool_(g, "Eq", [128, 2, 128], F32, 2, st)
    Ek = Pool_(g, "Ek", [128, 2, 128], F32, 2, st)
    gend = Pool_(g, "gend", [128, 2, 2], F32, 2, st)
    kendE = Pool_(g, "kendE", [128, 256], F32, 2, st)
    qdecT = Pool_(g, "qdecT", [128, 4, 128], BF16, 2, st, zero=True)
    kinvT = Pool_(g, "kinvT", [128, 4, 128], BF16, 2, st, zero=True)
    kend = Pool_(g, "kend", [128, 2, 256], BF16, 2, st)
    CI2 = _cst(g, "CI").unsqueeze(2).to_broadcast([128, 2, 256])
    Vg = Pool_(g, "Vg", [128, 512], BF16, 2, st)
    AmT = Pool_(g, "AmT", [128, 4, 128], BF16, 2, st)
    G2 = Pool_(g, "G2", [128, 512], F32, 1, st)
    ob = Pool_(g, "ob", [128, 512], F32, 1, st)
    sqo = Pool_(g, "sqo", [128, 512], F32, 1, st)
    ss4 = Pool_(g, "ss4", [128, 4], F32, 2, st)
    mtok = Pool_(g, "mtok", [128, 512], BF16, 2, st)
    mt = Pool_(g, "mt", [128, 4, 128], BF16, 2, st)
    LN8 = float(np.log(0.125))
    for z in range(2):
        R_ = _cst(g, "glaRf" if z == 0 else "glaRb")
        A_ = _cst(g, "glaAf" if z == 0 else "glaAb")
        Mz = _cst(g, "Mf" if z == 0 else "Mb")
        _memset(g, "pool", S[:], 0.0)
        _memset(g, "pool", Sbf[:], 0.0)
        for t in (GLA_FWD if z == 0 else GLA_BWD):
            ts = slice(t * 128, (t + 1) * 128)
            at = [aT[:, k, ts] for k in range(KC)]
            ps = _psum(g)
            c0 = C_GLOW + 16 * z
            _mmk(g, ps[0:16, 0:128], [(win[:, k, c0:c0 + 16], at[k]) for k in range(KC)])
            gl = glowT.get()
            _copy(g, "act", gl[:], ps[0:16, 0:128])
            _chk("gla_%s_%d" % ("a", z))
            ps = _psum(g)
            _mm(g, ps[:, 0:256], gl[:], gup[:, z, :])
            tt_ = tl.get()
            _tt(g, "dve", tt_[:], ps[:, 0:256], gbias[:, z, :], ALU.add)
            L_ = Lz.get()
            _act(g, tt_[:], tt_[:], AF.Exp, scale=-1.0)
            _act(g, L_[:], tt_[:], AF.Ln, scale=1.0, bias=1.0)
            _chk("gla_%s_%d" % ("b", z))
            ps = _psum(g)
            for hp in range(2):
                _mm(g, ps[:, hp * 130:(hp + 1) * 130], L_[:, hp * 128:(hp + 1) * 128], R_)
            pv = ps[:, 0:260].rearrange("p (h c) -> p h c", c=130)
            eq, ek, ge = Eq.get(), Ek.get(), gend.get()
            _act(g, eq[:], pv[:, :, 0:128], AF.Exp, scale=1.0, bias=LN8)
            _act(g, ek[:], pv[:, :, 0:128], AF.Exp, scale=-1.0)
            _act(g, ge[:], pv[:, :, 128:130], AF.Exp)
            _chk("gla_%s_%d" % ("c", z))
            ps = _psum(g)
            _mm(g, ps[:, 0:256], A_, L_[:])
            ke = kendE.get()
            _act(g, ke[:], ps[:, 0:256], AF.Exp)
            _chk("gla_%s_%d" % ("d", z))
            ps = _psum(g)
            for hp in range(2):
                _mmk(g, ps[:, hp * 128:(hp + 1) * 128], [(win[:, k, C_GQ + hp * 128:C_GQ + (hp + 1) * 128], at[k]) for k in range(KC)])
                _mmk(g, ps[:, 256 + hp * 128:256 + (hp + 1) * 128], [(win[:, k, C_GK + hp * 128:C_GK + (hp + 1) * 128], at[k]) for k in range(KC)])
            qd, ki = qdecT.get(), kinvT.get()
            for (lo, hi, par) in ((0, 64, 0), (64, 128, 1)):
                _tt(g, "dve", qd[lo:hi, par::2, :], ps[lo:hi, 0:256].rearrange("p (h c) -> p h c", c=128), eq[lo:hi, :, :], ALU.mult)
                _tt(g, "dve", ki[lo:hi, par::2, :], ps[lo:hi, 256:512].rearrange("p (h c) -> p h c", c=128), ek[lo:hi, :, :], ALU.mult)
            _chk("gla_%s_%d" % ("e", z))
            ps = _psum(g)
            _mmk(g, ps[:, 0:256], [(at[k], win[:, k, C_GK:C_GK + 256]) for k in range(KC)])
            kn = kend.get()
            _tt(g, "dve", ke[:], ps[:, 0:256], ke[:], ALU.mult)
            _tt(g, "pool", kn[:], ke[:].unsqueeze(1).to_broadcast([128, 2, 256]), CI2, ALU.mult)
            _chk("gla_%s_%d" % ("f", z))
            ps = _psum(g)
            _mmk(g, ps[:, 0:512], [(at[k], win[:, k, C_GV:C_GV + 512]) for k in range(KC)])
            vg = Vg.get()
            _copy(g, "act", vg[:], ps[:, 0:512])
            _chk("gla_%s_%d" % ("g", z))
            ps = _psum(g)
            for h in range(4):
                hp, hb = h // 2, (h % 2) * 64
                _mm(g, ps[:, h * 128:(h + 1) * 128], ki[:, h, :], qd[:, h, :])
            am = AmT.get()
            if "h_dve" not in DBG_SKIP:
                _tt(g, "dve", am[:], ps[:].rearrange("p (h c) -> p h c", c=128), Mz.unsqueeze(1).to_broadcast([128, 4, 128]), ALU.mult)
            else:
                _copy(g, "act", am[:], ps[:].rearrange("p (h c) -> p h c", c=128))
            _chk("gla_%s_%d" % ("h", z))
            if z == 1:
                ps = _psum(g)
                _mmk(g, ps[:, 0:512], [(at[k], win[:, k, C_GG:C_GG + 512]) for k in range(KC)])
                g2 = G2.get()
                _act(g, g2[:], ps[:, 0:512], AF.Silu)
                _tt(g, "pool", g2[:].rearrange("p (h c) -> p h c", c=128), g2[:].rearrange("p (h c) -> p h c", c=128),
                    onorm[:].unsqueeze(1).to_broadcast([128, 4, 128]), ALU.mult)
            _chk("gla_%s_%d" % ("i", z))
            pso = _psum(g, "acc")
            for c in ((0, 1) if z == 0 else (1, 0)):
                cb = c * 64
                psS = _psum(g, "misc")
                for h in range(4):
                    hp, hb = h // 2, (h % 2) * 64
                    _mm(g, pso[cb:cb + 64, h * 128:(h + 1) * 128], qd[:, h, cb:cb + 64], Sbf[:, hp, :], start=True, stop=False)
                    _mm(g, pso[cb:cb + 64, h * 128:(h + 1) * 128], am[:, h, cb:cb + 64], vg[:, h * 128:(h + 1) * 128], start=False, stop=True)
                    _mm(g, psS[hb:hb + 64, hp * 128:(hp + 1) * 128], kn[:, c, h * 64:(h + 1) * 64], vg[:, h * 128:(h + 1) * 128])
                for hp in range(2):
                    _stt(g, "dve", S[:, hp, :], S[:, hp, :], ge[:, hp, c:c + 1], psS[:, hp * 128:(hp + 1) * 128], ALU.mult, ALU.add)
                _copy(g, "pool", Sbf[:], S[:])
            _chk("gla_%s_%d" % ("j", z))
            if z == 0:
                _copy(g, "act", Of[:, t, :], pso[:])
            else:
                o = ob.get()
                _tt(g, "dve", o[:], pso[:], Of[:, t, :], ALU.add)
                sq = sqo.get()
                _tt(g, "pool", sq[:], o[:], o[:], ALU.mult)
                s4 = ss4.get()
                P.op("dve", lambda e: e.tensor_reduce(s4[:], sq[:].rearrange("p (h c) -> p h c", c=128), axis=AX, op=ALU.add), [sq[:]], [s4[:]])
                _rstd_from_sum(g, s4[:], s4[:], 1.0 / 128, g.eps[:])
                o3 = o[:].rearrange("p (h c) -> p h c", c=128)
                _tt(g, "dve", o3, o3, s4[:].unsqueeze(2).to_broadcast([128, 4, 128]), ALU.mult)
                mk = mtok.get()
                _tt(g, "pool", mk[:], o[:], g2[:], ALU.mult)
                for h in range(4):
                    _tr(g, g.psbf[:, h * 128:(h + 1) * 128], mk[:, h * 128:(h + 1) * 128], g.ident_bf[:])
                m_ = mt.get()
                _copy(g, "act", m_[:], g.psbf[:, 0:512].rearrange("p (h c) -> p h c", c=128))
                P.dma(g.dr["MIX"].rearrange("k p t -> p k t")[:, 4:8, ts], m_[:])
            _chk("gla_k_%d" % z)


def _mla_attention(g, L, lat):
    P, dr = g.P, g.dr
    i = L // 2
    with contextlib.ExitStack() as st0:
      QT = _sb(g, "QT", [128, 8, TOK], BF16, st0)
      KT = _sb(g, "KT", [128, 8, TOK], BF16, st0)
      VA = _sb(g, "VA", [128, NT, 8, 128], BF16, st0)
      with contextlib.ExitStack() as st:
        _rope_tables(g, lat, st)
        wuq = _sb(g, "wuq", [128, 3, 768], BF16, st)
        _load_cast(g, wuq[:], dr["mla_w_uq"][i].rearrange("(k p) n -> p k n", p=128))
        wuqr = _sb(g, "wuqr", [128, 3, 768], BF16, st)
        _copy(g, "pool", wuqr[:], wuq[:])
        for k in range(3):
            _swap_halves(g, "pool", wuqr[:, k, :].rearrange("p (h c) -> p h c", c=96)[:, :, 64:96],
                         wuq[:, k, :].rearrange("p (h c) -> p h c", c=96)[:, :, 64:96])
        wukv = _sb(g, "wukv", [128, 2, 1024], BF16, st)
        _load_cast(g, wukv[:], dr["mla_w_ukv"][i].rearrange("(k p) n -> p k n", p=128))
        wv = _sb(g, "wv", [128, 2, 512], BF16, st)
        for k in range(2):
            _copy(g, "pool", wv[:, k, :].rearrange("p (h c) -> p h c", c=64), wukv[:, k, :].rearrange("p (h c) -> p h c", c=128)[:, :, 64:128])
        _memset(g, "pool", VA[:, :, :, 64:128].rearrange("p t h c -> p (t h) c"), 1.0)
        t1p = Pool_(g, "t1a", [128, 512], F32, 2, st)
        for bi, (s0, N) in enumerate(BLOCKS):
            sl = slice(s0, s0 + N)
            p0 = s0 - NCTX
            for h in range(8):
                ps = _psum(g)
                _mmk(g, ps[0:96, :N], [(wuq[:, fc, h * 96:(h + 1) * 96], lat.cqn[:, fc, sl]) for fc in range(3)])
                _copy(g, "act", QT[0:64, h, sl], ps[0:64, :N])
                if bi == 0:
                    _copy(g, "act", QT[64:96, h, sl], ps[64:96, :N])
                else:
                    ps2 = _psum(g)
                    _mmk(g, ps2[0:96, :N], [(wuqr[:, fc, h * 96:(h + 1) * 96], lat.cqn[:, fc, sl]) for fc in range(3)])
                    t1, t2 = t1p.get(), t1p.get()
                    _tt(g, "dve", t1[64:96, :N], ps[64:96, :N], lat.ropeC[64:96, p0:p0 + N], ALU.mult)
                    _tt(g, "dve", t2[64:96, :N], ps2[64:96, :N], lat.ropeS[64:96, p0:p0 + N], ALU.mult)
                    _tt(g, "pool", QT[64:96, h, sl], t1[64:96, :N], t2[64:96, :N], ALU.add)
                ps = _psum(g)
                _mmk(g, ps[0:64, :N], [(wukv[:, c, h * 128:h * 128 + 64], lat.ckvn[:, c, sl]) for c in range(2)])
                _copy(g, "act", KT[0:64, h, sl], ps[0:64, :N])
                _copy(g, "pool", KT[64:96, h, sl], lat.kpeT[64:96, sl])
            for tt_ in range(N // 128):
                t = s0 // 128 + tt_
                ps = _psum(g)
                _mmk(g, ps[:, 0:512], [(lat.ckvn[:, c, t * 128:(t + 1) * 128], wv[:, c, :]) for c in range(2)])
                _copy(g, "dve", VA[:, t, :, 0:64], ps[:, 0:512].rearrange("p (h c) -> p h c", c=64))
        P.barrier()
      with contextlib.ExitStack() as st:
        scale = float(96 ** -0.5)
        pT = Pool_(g, "pT", [128, 512], BF16, 3, st)
        rec = Pool_(g, "rec", [128, 512], F32, 2, st)
        atile = Pool_(g, "atile", [128, 512], BF16, 2, st)
        for bi, (s0, N) in enumerate(BLOCKS):
            sl = slice(s0, s0 + N)
            nk = 2 if bi == 0 else NT
            for h in range(8):
                if h % 2 == 0:
                    at_ = atile.get()
                pso = _psum(g, "acc")
                for kc in range(nk):
                    pss = _psum(g)
                    _mm(g, pss[:, :N], KT[0:96, h, kc * 128:(kc + 1) * 128], QT[0:96, h, sl])
                    p_ = pT.get()
                    _act(g, p_[:, :N], pss[:, :N], AF.Exp, scale=scale)
                    _mm(g, pso[:, :N], VA[:, kc, h, :], p_[:, :N], start=(kc == 0), stop=(kc == nk - 1))
                r_ = rec.get()
                P.op("dve", lambda e: e.reciprocal(r_[64:128, :N], pso[64:128, :N]), [pso[64:128, :N]], [r_[64:128, :N]])
                hb = (h % 2) * 64
                _tt(g, "dve", at_[hb:hb + 64, :N], pso[0:64, :N], r_[64:128, :N], ALU.mult)
                if h % 2 == 1:
                    P.dma(g.dr["MIX"].rearrange("k p t -> p k t")[:, h // 2, sl], at_[:, :N])


O_Q, O_K, O_V, O_G, O_A, O_B = 0, 512, 1024, 2048, 3072, 3088


def _odd_mixer(g, L, aT, need_ctx, AB):
    import itertools
    P, dr = g.P, g.dr
    i = L // 2
    wsrc = dr["gdn_w_in"][i].rearrange("(k p) n -> p k n", p=128)
    with contextlib.ExitStack() as st:
        blockones = _sb(g, "blockones", [128, 128], BF16, st)
        _memset(g, "pool", blockones[:], 0.0)
        _memset(g, "pool", blockones[0:64, 0:64], 1.0)
        _memset(g, "pool", blockones[64:128, 64:128], 1.0)
        wab = _sb(g, "wab", [128, KC, 32], BF16, st)
        _load_cast(g, wab[:], wsrc[:, :, O_A:O_A + 32])
        dtb = _sb(g, "dtb", [128, 16], F32, st)
        P.dma(dtb[:], dr["gdn_dt_bias"][i].rearrange("z h -> (z h)").partition_broadcast(128))
        negA = _sb(g, "negA", [128, 16], F32, st)
        P.dma(negA[:], dr["gdn_a_log"][i].rearrange("z h -> (z h)").partition_broadcast(128))
        _act(g, negA[:], negA[:], AF.Exp)
        _ts(g, "dve", negA[:], negA[:], -1.0, None, ALU.mult)
        onorm = _sb(g, "onormd", [128, 128], F32, st)
        P.dma(onorm[:], dr["gdn_o_norm"][i].partition_broadcast(128))
        g.bmask = _sb(g, "bmask", [128, 12, 128], BF16, st)
        _load_cast(g, g.bmask[:], dr["bmask"].rearrange("p (m c) -> p m c", c=128))
        nhp = 1
        NH = 2 * nhp
        for pair in range(2):
            with contextlib.ExitStack() as sh:
                units = []
                for u_ in range(2):
                    q0 = (2 * pair + u_) * nhp
                    d = Ctx()
                    d.q0 = q0
                    d.qT = _sb(g, "gqT", [128, nhp, TOK], BF16, sh)
                    d.kT = _sb(g, "gkT", [128, nhp, TOK], BF16, sh)
                    d.Vt = _sb(g, "gVt", [128, NT, NH * 128], BF16, sh)
                    d.Kt = _sb(g, "gKt", [128, NT, NH * 64], BF16, sh)
                    d.Of = _sb(g, "gOf", [128, NT, NH * 128], F32, sh)
                    d.wg = _sb(g, "gwg", [128, KC, NH * 128], BF16, sh)
                    _load_cast(g, d.wg[:], wsrc[:, :, O_G + 2 * q0 * 128:O_G + (2 * q0 + NH) * 128])
                    d.banks = (g.ps[3 + 2 * u_], g.ps[4 + 2 * u_])
                    units.append(d)
                with contextlib.ExitStack() as s1:
                    tmp = _gdn_conv_tmps(g, s1)
                    for d in units:
                        _gdn_conv_stage(g, i, d.q0, nhp, aT, wsrc, d.qT, d.kT, d.Vt, d.Kt, blockones, tmp)
                    P.barrier()
                with contextlib.ExitStack() as s2:
                    g.rot_n = 3
                    g.ps_rr = 0
                    gens = [_gdn_scan(g, d, nhp, aT, wab, dtb, negA, onorm, need_ctx, s2) for d in units]
                    for _ in itertools.zip_longest(*gens):
                        pass
                    g.rot_n = 4
                    g.ps_rr = 0
                    P.barrier()


def _gdn_conv_tmps(g, st):
    t = Ctx()
    t.xl = _sb(g, "xl", [128, SEQ + 2], F32, st)
    t.xc = _sb(g, "xc", [128, NCTX + 2], F32, st)
    for buf, n in ((t.xl, SEQ), (t.xc, NCTX)):
        _memset(g, "pool", buf[:, 0:1], 0.0)
        _memset(g, "pool", buf[:, n + 1:n + 2], 0.0)
    t.tcv = _sb(g, "tcv", [128, SEQ], F32, st)
    t.ybf = _sb(g, "ybf", [128, TOK], BF16, st)
    t.yf = _sb(g, "yf32", [128, 512], F32, st)
    t.sqp = _sb(g, "sqc", [128, 512], BF16, st)
    t.rn = _sb(g, "rnc", [128, 512], F32, st)
    t.wtp = Pool_(g, "wcv", [128, KC, 128], BF16, 2, st)
    return t


def _gdn_conv_stage(g, i, q0, nhp, aT, wsrc, qT, kT, Vt, Kt, blockones, tmp):
    P = g.P
    xl, xc, tcv, ybf, yf, sqp, rn, wtp = tmp.xl, tmp.xc, tmp.tcv, tmp.ybf, tmp.yf, tmp.sqp, tmp.rn, tmp.wtp
    cols = _col(g, "conv%d" % i, 48)
    chunks = [("q", O_Q // 128 + q0 + j, j) for j in range(nhp)] + [("k", O_K // 128 + q0 + j, j) for j in range(nhp)] + \
             [("v", O_V // 128 + 2 * q0 + j, j) for j in range(2 * nhp)]
    for kind, cc, j in chunks:
        wt = wtp.get()
        _load_cast(g, wt[:], wsrc[:, :, cc * 128:(cc + 1) * 128])
        for bi, (s0, N) in enumerate(BLOCKS):
            ps = _psum(g)
            _mmk(g, ps[:, :N], [(wt[:, k, :], aT[:, k, s0:s0 + N]) for k in range(KC)])
            if bi == 0:
                _copy(g, "act", xc[:, 1:1 + N], ps[:, :N])
            else:
                p0 = s0 - NCTX
                _copy(g, "act", xl[:, 1 + p0:1 + p0 + N], ps[:, :N])
        w0, w1, w2 = cols[:, cc:cc + 1], cols[:, 16 + cc:17 + cc], cols[:, 32 + cc:33 + cc]
        for buf, n, o0 in ((xc, NCTX, 0), (xl, SEQ, NCTX)):
            t = tcv[:, 0:n]
            _ts(g, "pool", t, buf[:, 1:1 + n], w1, None, ALU.mult)
            _stt(g, "dve", t, buf[:, 0:n], w0, t, ALU.mult, ALU.add)
            _stt(g, "dve", t, buf[:, 2:2 + n], w2, t, ALU.mult, ALU.add)
            if kind == "v":
                _act(g, ybf[:, o0:o0 + n], t, AF.Silu)
            else:
                for c0 in range(0, n, 512):
                    m = min(512, n - c0)
                    _act(g, yf[:, :m], t[:, c0:c0 + m], AF.Silu)
                    _tt(g, "pool", sqp[:, :m], yf[:, :m], yf[:, :m], ALU.mult)
                    ps = _psum(g)
                    _mm(g, ps[:, :m], blockones[:], sqp[:, :m])
                    _rstd_from_sum(g, rn[:, :m], ps[:, :m], 1.0, g.eps[:])
                    dst = (qT if kind == "q" else kT)[:, j, o0 + c0:o0 + c0 + m]
                    if kind == "q":
                        _stt(g, "dve", dst, yf[:, :m], 0.125, rn[:, :m], ALU.mult, ALU.mult)
                    else:
                        _tt(g, "dve", dst, yf[:, :m], rn[:, :m], ALU.mult)
        if kind in ("k", "v"):
            src = kT[:, j, :] if kind == "k" else ybf[:]
            for t4 in range(0, NT, 4):
                nt = min(4, NT - t4)
                for q in range(nt):
                    _tr(g, g.psbf[:, q * 128:(q + 1) * 128], src[:, (t4 + q) * 128:(t4 + q + 1) * 128], g.ident_bf[:])
                dst = (Kt[:, t4:t4 + nt, j * 128:(j + 1) * 128] if kind == "k" else Vt[:, t4:t4 + nt, j * 128:(j + 1) * 128])
                _copy(g, "act", dst, g.psbf[:, 0:nt * 128].rearrange("p (q c) -> p q c", c=128))


def _gdn_scan(g, d, nhp, aT, wab, dtb, negA, onorm, need_ctx, st):
    P = g.P
    NH = 2 * nhp
    W = NH * 128
    KW = NH * 64
    q0, qT, kT, Vt, Kt, Of, wg = d.q0, d.qT, d.kT, d.Vt, d.Kt, d.Of, d.wg
    bank_o, bank_s = d.banks
    S = _sb(g, "dS", [128, nhp, 128], F32, st)
    Sbf = _sb(g, "dSbf", [128, nhp, 128], BF16, st)
    sm = Pool_(g, "dsm", [128, 32], F32, 2, st)
    larep = _sb(g, "larep", [128, NH, 64], F32, st)
    Eg = _sb(g, "dEg", [128, nhp, 128], F32, st)
    gend = Pool_(g, "dgend", [128, nhp, 2], F32, 2, st)
    qdecT = _sb(g, "dqdec", [128, NH, 128], BF16, st)
    _memset(g, "pool", qdecT[:], 0.0)
    kz = _sb(g, "dkz", [128, NH, 128], BF16, st)
    _memset(g, "pool", kz[:], 0.0)
    CI4 = _cst(g, "CI").unsqueeze(2).to_broadcast([128, 2, KW])
    rhsd = _sb(g, "rhsd", [128, NH, 128], F32, st)
    dec = _sb(g, "ddec", [128, NH, 128], F32, st)
    tG = _sb(g, "dtG", [128, NH, 128], F32, st)
    qkm = _sb(g, "dqkm", [128, NH, 128], BF16, st)
    Mp = Pool_(g, "dM", [128, NH, 128], BF16, 2, st)
    Np = Pool_(g, "dN", [128, NH, 128], BF16, 2, st)
    Tnp = Pool_(g, "dTn", [128, NH, 128], BF16, 2, st)
    Amp = Pool_(g, "dAm", [128, NH, 128], BF16, 4, st)
    Y2p = Pool_(g, "dY2", [128, NH, 128], BF16, 2, st)
    NQ = Pool_(g, "dNQ", [128, NH, 256], BF16, 2, st)
    vb = _sb(g, "dvb", [128, W], BF16, st)
    kbg = _sb(g, "dkbg", [128, NH, 64], BF16, st)
    kend = _sb(g, "dkend", [128, NH, 64], BF16, st)
    kend2 = _sb(g, "dkend2", [128, 2, KW], BF16, st)
    u = _sb(g, "du", [128, W], F32, st)
    wT = _sb(g, "dwT", [128, NH, 128], BF16, st)
    _memset(g, "pool", wT[:], 0.0)
    vnew = _sb(g, "dvnew", [128, W], BF16, st)
    _memset(g, "pool", vnew[:], 0.0)
    G2 = _sb(g, "dG2", [128, W], F32, st)
    ob = _sb(g, "dob", [128, W], F32, st)
    sqo = _sb(g, "dsqo", [128, W], F32, st)
    mtok = _sb(g, "dmtok", [128, W], BF16, st)
    mt = Pool_(g, "dmt", [128, NH, 128], BF16, 2, st)
    ident_bc = g.ident.unsqueeze(1).to_broadcast([128, NH, 128])
    v3 = lambda ap: ap.rearrange("p (h c) -> p h c", c=128)
    halves = ((0, 64, 0), (64, 128, 1))
    yield
    for z in range(2):
        R_ = _cst(g, "gdnRf" if z == 0 else "gdnRb")
        Mz = _cst(g, "Mf" if z == 0 else "Mb")
        Sz = _cst(g, "Sf" if z == 0 else "Sb")
        Mz_bc = Mz.unsqueeze(1).to_broadcast([128, NH, 128])
        Sz_bc = Sz.unsqueeze(1).to_broadcast([128, NH, 128])
        _memset(g, "pool", S[:], 0.0)
        _memset(g, "pool", Sbf[:], 0.0)
        for t in (GLA_FWD if z == 0 else GLA_BWD):
            ts = slice(t * 128, (t + 1) * 128)
            at = [aT[:, k, ts] for k in range(KC)]
            want_out = need_ctx or t >= 2
            ps = _psum(g)
            _mmk(g, ps[:, 0:32], [(at[k], wab[:, k, :]) for k in range(KC)])
            s_ = sm.get()
            ca = z * 8 + 2 * q0
            la, beta = s_[:, 0:NH], s_[:, NH:2 * NH]
            egk = s_[:, 2 * NH:4 * NH]
            eg, ekend = s_[:, 2 * NH:3 * NH], s_[:, 3 * NH:4 * NH]
            bg = s_[:, 4 * NH:5 * NH]
            rs = s_[:, 5 * NH:6 * NH]
            _tt(g, "dve", la, ps[:, ca:ca + NH], dtb[:, ca:ca + NH], ALU.add)
            _act(g, beta, ps[:, 16 + ca:16 + ca + NH], AF.Sigmoid)
            yield
            _act(g, la, la, AF.Exp)
            _act(g, la, la, AF.Ln, scale=1.0, bias=1.0)
            _tt(g, "dve", la, la, negA[:, ca:ca + NH], ALU.mult)
            _copy(g, "pool", larep[:], la.unsqueeze(2).to_broadcast([128, NH, 64]))
            yield
            ps = _psum(g)
            lr = larep[:].rearrange("p h d -> p (h d)")
            for hp in range(nhp):
                _mm(g, ps[:, hp * 130:(hp + 1) * 130], lr[:, hp * 128:(hp + 1) * 128], R_)
            pv = ps[:, 0:130 * nhp].rearrange("p (h c) -> p h c", c=130)
            ge = gend.get()
            _act(g, Eg[:], pv[:, :, 0:128], AF.Exp)
            _act(g, ge[:], pv[:, :, 128:130], AF.Exp)
            yield
            for (lo, hi, par) in halves:
                _tt(g, "dve", qdecT[lo:hi, par::2, :], qT[lo:hi, :, ts], Eg[lo:hi, :, :], ALU.mult)
                _copy(g, "pool", kz[lo:hi, par::2, :], kT[lo:hi, :, ts])
            ps = _psum(g)
            _mm(g, ps[:, 0:NH], Mz, la)
            _mm(g, ps[:, NH:2 * NH], Sz, la)
            _act(g, egk, ps[:, 0:2 * NH], AF.Exp)
            yield
            _tt(g, "dve", bg, beta, eg, ALU.mult)
            kt3 = Kt[:, t, :].rearrange("p (h d) -> p h d", d=64)
            _tt(g, "pool", kend[:], kt3, ekend.unsqueeze(2).to_broadcast([128, NH, 64]), ALU.mult)
            _tt(g, "pool", kend2[:], kend[:].rearrange("p h d -> p (h d)").unsqueeze(1).to_broadcast([128, 2, KW]), CI4, ALU.mult)
            _tt(g, "pool", kbg[:], kt3, bg.unsqueeze(2).to_broadcast([128, NH, 64]), ALU.mult)
            _tt(g, "pool", v3(vb[:]), v3(Vt[:, t, :]), beta.unsqueeze(2).to_broadcast([128, NH, 128]), ALU.mult)
            la_bc = la.unsqueeze(2).to_broadcast([128, NH, 128])
            _tt(g, "pool", rhsd[:], Mz_bc, la_bc, ALU.mult)
            yield
            ps = _psum(g)
            _mm(g, ps[:, 0:W], Sz, rhsd[:].rearrange("p h c -> p (h c)"))
            _act(g, dec[:].rearrange("p h c -> p (h c)"), ps[:, 0:W], AF.Exp)
            yield
            ps = _psum(g)
            for h in range(NH):
                _mm(g, ps[:, h * 128:(h + 1) * 128], kz[:, h, :], qT[:, h // 2, ts])
            _tt(g, "dve", tG[:], v3(ps[:, 0:W]), Mz_bc, ALU.mult)
            yield
            _tt(g, "dve", qkm[:], tG[:], dec[:], ALU.mult)
            _tt(g, "pool", rhsd[:], Sz_bc, la_bc, ALU.mult)
            yield
            ps = _psum(g)
            _mm(g, ps[:, 0:W], Mz, rhsd[:].rearrange("p h c -> p (h c)"))
            _act(g, dec[:].rearrange("p h c -> p (h c)"), ps[:, 0:W], AF.Exp)
            yield
            _tt(g, "pool", dec[:], dec[:], beta.unsqueeze(2).to_broadcast([128, NH, 128]), ALU.mult)
            ps = _psum(g)
            for h in range(NH):
                _mm(g, ps[:, h * 128:(h + 1) * 128], kz[:, h, :], kz[:, h, :])
            _tt(g, "dve", tG[:], v3(ps[:, 0:W]), Sz_bc, ALU.mult)
            yield
            M_ = Mp.get()
            _tt(g, "dve", M_[:], tG[:], dec[:], ALU.mult)
            for h in range(NH):
                _tr(g, g.psbf[:, h * 128:(h + 1) * 128], M_[:, h, :], g.ident_bf[:])
            N_ = Np.get()
            _copy(g, "act", N_[:], v3(g.psbf[:, 0:W]))
            yield
            TT = NQ.get()
            Tn = Tnp.get()
            for lv, bsz in enumerate((1, 2, 4, 8, 16, 32)):
                last = (bsz == 32)
                mk_ = g.bmask[:, (lv if z == 0 else 6 + lv), :].unsqueeze(1).to_broadcast([128, NH, 128])
                mkT = g.bmask[:, (6 + lv if z == 0 else lv), :].unsqueeze(1).to_broadcast([128, NH, 128])
                Am = Amp.get()
                _tt(g, "pool", Am[:], M_[:], mkT, ALU.mult)
                if lv == 0 or not last:
                    Nm = Amp.get()
                    _tt(g, "pool", Nm[:], N_[:], mk_, ALU.mult)
                if lv == 0:
                    _tt(g, "dve", TT[:, :, 128:256], ident_bc, Nm[:], ALU.subtract)
                    _tt(g, "dve", Tn[:], ident_bc, Am[:], ALU.subtract)
                    yield
                    continue
                TT2 = NQ.get()
                psY = _psum(g)
                for h in range(NH):
                    _mm(g, psY[:, h * 128:(h + 1) * 128], Am[:, h, :], TT[:, h, 128:256])
                _copy(g, "act", TT[:, :, 0:128], v3(psY[:, 0:W]))
                yield
                if not last:
                    psY2 = _psum(g)
                    for h in range(NH):
                        _mm(g, psY2[:, h * 128:(h + 1) * 128], Nm[:, h, :], Tn[:, h, :])
                    y2 = Y2p.get()
                    _copy(g, "dve", y2[:], v3(psY2[:, 0:W]))
                    yield
                psZ = _psum(g)
                for h in range(NH):
                    _mm(g, psZ[:, h * 128:(h + 1) * 128], Tn[:, h, :], TT[:, h, 0:128])
                _tt(g, "dve", TT2[:, :, 128:256], TT[:, :, 128:256], v3(psZ[:, 0:W]), ALU.subtract)
                yield
                if not last:
                    psZ2 = _psum(g)
                    for h in range(NH):
                        _mm(g, psZ2[:, h * 128:(h + 1) * 128], TT[:, h, 128:256], y2[:, h, :])
                    Tn2 = Tnp.get()
                    _tt(g, "dve", Tn2[:], Tn[:], v3(psZ2[:, 0:W]), ALU.subtract)
                    Tn = Tn2
                    yield
                TT = TT2
            nq = TT
            ps = _psum(g)
            for h in range(NH):
                _mm(g, ps[:, h * 128:(h + 1) * 128], nq[:, h, 128:256], vb[:, h * 128:(h + 1) * 128])
            _copy(g, "act", u[:], ps[:, 0:W])
            yield
            ps = _psum(g)
            for h in range(NH):
                hp, hb = h // 2, (h % 2) * 64
                _mm(g, ps[hb:hb + 64, hp * 128:(hp + 1) * 128], kbg[:, h, :], nq[:, h, 128:256])
            for (lo, hi, par) in halves:
                _copy(g, "act", wT[lo:hi, par::2, :], v3(ps[lo:hi, 0:128 * nhp]))
            yield
            if z == 1 and want_out:
                ps = _psum(g)
                _mmk(g, ps[:, 0:W], [(at[k], wg[:, k, :]) for k in range(KC)])
                _act(g, G2[:], ps[:, 0:W], AF.Silu)
                _tt(g, "pool", v3(G2[:]), v3(G2[:]), onorm[:].unsqueeze(1).to_broadcast([128, NH, 128]), ALU.mult)
                yield
            pso = bank_o
            for c in ((0, 1) if z == 0 else (1, 0)):
                cb = c * 64
                psw = _psum(g)
                for h in range(NH):
                    _mm(g, psw[cb:cb + 64, h * 128:(h + 1) * 128], wT[:, h, cb:cb + 64], Sbf[:, h // 2, :])
                _tt(g, "dve", vnew[cb:cb + 64, :], u[cb:cb + 64, :], psw[cb:cb + 64, 0:W], ALU.subtract)
                yield
                psS = bank_s
                for h in range(NH):
                    hp, hb = h // 2, (h % 2) * 64
                    if want_out:
                        _mm(g, pso[cb:cb + 64, h * 128:(h + 1) * 128], qdecT[:, h, cb:cb + 64], Sbf[:, hp, :], start=True, stop=False)
                        _mm(g, pso[cb:cb + 64, h * 128:(h + 1) * 128], qkm[:, h, cb:cb + 64], vnew[:, h * 128:(h + 1) * 128], start=False, stop=True)
                    _mm(g, psS[hb:hb + 64, hp * 128:(hp + 1) * 128], kend2[:, c, h * 64:(h + 1) * 64], vnew[:, h * 128:(h + 1) * 128])
                for hp in range(nhp):
                    _stt(g, "dve", S[:, hp, :], S[:, hp, :], ge[:, hp, c:c + 1], psS[:, hp * 128:(hp + 1) * 128], ALU.mult, ALU.add)
                _copy(g, "pool", Sbf[:], S[:])
                yield
            if not want_out:
                continue
            if z == 0:
                _copy(g, "act", Of[:, t, :], pso[:, 0:W])
                yield
            else:
                _tt(g, "dve", ob[:], pso[:, 0:W], Of[:, t, :], ALU.add)
                _tt(g, "pool", sqo[:], ob[:], ob[:], ALU.mult)
                P.op("dve", lambda e: e.tensor_reduce(rs, v3(sqo[:]), axis=AX, op=ALU.add), [sqo[:]], [rs])
                _rstd_from_sum(g, rs, rs, 1.0 / 128, g.eps[:])
                yield
                _tt(g, "dve", v3(ob[:]), v3(ob[:]), rs.unsqueeze(2).to_broadcast([128, NH, 128]), ALU.mult)
                _tt(g, "pool", mtok[:], ob[:], G2[:], ALU.mult)
                for h in range(NH):
                    _tr(g, g.psbf[:, h * 128:(h + 1) * 128], mtok[:, h * 128:(h + 1) * 128], g.ident_bf[:])
                m_ = mt.get()
                _copy(g, "act", m_[:], v3(g.psbf[:, 0:W]))
                P.dma(g.dr["MIX"].rearrange("k p t -> p k t")[:, 2 * q0:2 * q0 + NH, ts], m_[:])
                yield


def make_in_maps(inputs):
    cst, rope, bmask = host_consts()
    maps = []
    for b in range(8):
        m = {
            "x": np.ascontiguousarray(inputs["x"][b], dtype=np.float32),
            "ctx": np.ascontiguousarray(inputs["ctx"][b], dtype=np.float32),
            "cc": np.ascontiguousarray(np.concatenate([np.asarray(inputs["c"][b]).reshape(8, 128),
                                                       np.asarray(inputs["c_ctx"]).reshape(8, 128)], 0), dtype=np.float32),
            "cst": cst, "rope": rope, "bmask": bmask,
        }
        for n in WEIGHT_NAMES:
            m[n] = np.ascontiguousarray(inputs[n], dtype=np.float32)
        maps.append(m)
    return maps


_PROG_CACHE = {}


def kernel(**inputs):
    if "nc" not in _PROG_CACHE:
        import os
        _PROG_CACHE["nc"] = build_program(n_layers=int(os.environ.get("K_LAYERS", DEPTH)))[0]
    nc = _PROG_CACHE["nc"]
    in_maps = make_in_maps(inputs)
    res = run_bass_kernel_spmd(nc, in_maps, core_ids=list(range(8)))
    out = np.stack([np.asarray(res.results[b]["out"], dtype=np.float32) for b in range(8)], 0)
    return out
```

```python
import contextlib
import numpy as np
import concourse.bass as bass
import concourse.mybir as mybir
from concourse.bass_utils import run_bass_kernel_spmd

F32 = mybir.dt.float32
BF16 = mybir.dt.bfloat16
AF = mybir.ActivationFunctionType
ALU = mybir.AluOpType

SAME_ENGINE_SYNC = True
SEM_ROTATE = 30000
DBG_SKIP = set()
USE_AB32 = False
DBG_STOP = None


class _Stop(Exception):
    pass


def _chk(name):
    if DBG_STOP == name:
        raise _Stop()


class _Eng:
    def __init__(self, name, handle, is_dma_only=False):
        self.name = name
        self.h = handle
        self.sem = None
        self.count = 0
        self.waited = {}
        self.pending = False


class Prog:
    def __init__(self, nc, es, n_dma_sems=40):
        self.nc = nc
        self.es = es
        self.engs = {
            "pe": _Eng("pe", nc.tensor),
            "act": _Eng("act", nc.scalar),
            "dve": _Eng("dve", nc.vector),
            "pool": _Eng("pool", nc.gpsimd),
            "sp": _Eng("sp", nc.sync),
        }
        self.semid = 0
        for e in self.engs.values():
            e.sem = self._newsem()
        self.dma_sems = [[self._newsem(), 0] for _ in range(n_dma_sems)]
        self.dma_rr = 0
        self.recs = {}
        self.snap = {}
        self.n_inst = 0
        self.n_wait = 0
        self.out_tokens = []

    def _newsem(self):
        self.semid += 1
        return self.es.enter_context(self.nc.semaphore("s%d" % self.semid))

    @staticmethod
    def _box(ap):
        t = ap.tensor
        name = t.name
        shape = tuple(t.shape)
        pat = ap.ap
        off = ap.offset
        sp = str(ap.space) if not isinstance(ap.space, str) else ap.space
        if "DRAM" in sp.upper() or "HBM" in sp.upper() or "Dram" in sp:
            W = shape[-1]
            r0, c0 = off // W, off % W
            r1, c1 = r0, c0
            ok = True
            for (s, c) in pat:
                s = abs(s)
                if c <= 1:
                    continue
                if s % W == 0:
                    r1 += (c - 1) * (s // W)
                elif s * (c - 1) < W:
                    c1 += (c - 1) * s
                else:
                    ok = False
            if (not ok) or c1 >= W:
                ext = 1
                for (s, c) in pat:
                    ext += (c - 1) * abs(s)
                return (name, 0, 1 << 30, 0, 1 << 30) if True else None
            return (name, r0, r1 + 1, c0, c1 + 1)
        fsz = 1
        for s in shape[1:]:
            fsz *= s
        pstep, pcnt = pat[0]
        if pstep != fsz and pcnt > 1:
            return (name, 0, 128, 0, fsz)
        p0 = off // fsz
        f0 = off % fsz
        ext = 1
        for (s, c) in pat[1:]:
            ext += (c - 1) * abs(s)
        if name.startswith("ps"):
            return (name, (p0 // 32) * 32, ((p0 + pcnt + 31) // 32) * 32, 0, fsz)
        return (name, p0, p0 + pcnt, f0, f0 + ext)

    def _deps(self, eng, reads, writes, is_dma):
        deps = {}
        boxes_r = [self._box(a) for a in reads]
        boxes_w = [self._box(a) for a in writes]
        for kind, boxes in (("r", boxes_r), ("w", boxes_w)):
            for b in boxes:
                lst = self.recs.get(b[0])
                if not lst:
                    continue
                for rec in lst:
                    if kind == "r" and rec[4] == "r":
                        continue
                    if rec[0] >= b[2] or b[1] >= rec[1] or rec[2] >= b[4] or b[3] >= rec[3]:
                        continue
                    if (not is_dma) and (not rec[8]) and rec[7] == eng.name and (eng.name == "pe" or not SAME_ENGINE_SYNC):
                        continue
                    s, v = rec[5], rec[6]
                    k = id(s)
                    if k not in deps or deps[k][1] < v:
                        deps[k] = (s, v)
        return deps, boxes_r, boxes_w

    def _record(self, eng, boxes_r, boxes_w, sem, val, is_dma):
        for b in boxes_w:
            lst = self.recs.setdefault(b[0], [])
            lst[:] = [r for r in lst if not (b[1] <= r[0] and r[1] <= b[2] and b[3] <= r[2] and r[3] <= b[4])]
            lst.append([b[1], b[2], b[3], b[4], "w", sem, val, eng.name, is_dma])
        for b in boxes_r:
            lst = self.recs.setdefault(b[0], [])
            hit = False
            if not is_dma:
                for r in lst:
                    if r[4] == "r" and r[7] == eng.name and not r[8] and r[0] == b[1] and r[1] == b[2] and r[2] == b[3] and r[3] == b[4]:
                        r[5], r[6] = sem, val
                        hit = True
                        break
            if not hit:
                lst.append([b[1], b[2], b[3], b[4], "r", sem, val, eng.name, is_dma])
                if len(lst) > 96:
                    self._compact(lst)

    @staticmethod
    def _compact(lst):
        keep = [r for r in lst if r[4] == "w"]
        merged = {}
        for r in lst:
            if r[4] != "r":
                continue
            k = (r[7], id(r[5]), r[8])
            m = merged.get(k)
            if m is None:
                merged[k] = list(r)
            else:
                m[0] = min(m[0], r[0]); m[1] = max(m[1], r[1]); m[2] = min(m[2], r[2]); m[3] = max(m[3], r[3])
                m[6] = max(m[6], r[6])
        lst[:] = keep + list(merged.values())

    def _emit_waits(self, eng, deps):
        for (s, v) in sorted(deps.values(), key=lambda sv: -sv[1]):
            k = id(s)
            if eng.waited.get(k, 0) >= v:
                continue
            eng.h.wait_ge(s, v)
            eng.waited[k] = v
            self.n_wait += 1
            sn = self.snap.get((k, v))
            if sn:
                w = eng.waited
                for k2, v2 in sn.items():
                    if w.get(k2, 0) < v2:
                        w[k2] = v2

    def op(self, en, fn, reads=(), writes=(), inc=True):
        eng = self.engs[en]
        deps, br, bw = self._deps(eng, reads, writes, False)
        self._emit_waits(eng, deps)
        ins = fn(eng.h)
        self.n_inst += 1
        if inc:
            if eng.count >= SEM_ROTATE and not eng.pending:
                eng.sem = self._newsem()
                eng.count = 0
            eng.count += 1
            ins.then_inc(eng.sem, 1)
            tok = (eng.sem, eng.count)
            eng.pending = False
            self.snap[(id(eng.sem), eng.count)] = dict(eng.waited)
        else:
            tok = (eng.sem, eng.count + 1)
            eng.pending = True
        self._record(eng, br, bw, tok[0], tok[1], False)
        return ins

    def dma(self, out, in_, q="sp", is_output=False, **kw):
        eng = self.engs[q]
        deps, br, bw = self._deps(eng, [in_], [out], True)
        slot = self.dma_sems[self.dma_rr]
        self.dma_rr = (self.dma_rr + 1) % len(self.dma_sems)
        if slot[1] > 0:
            deps[id(slot[0])] = (slot[0], slot[1])
        self._emit_waits(eng, deps)
        slot[1] += 16
        eng.h.dma_start(out=out, in_=in_, **kw).then_inc(slot[0], 16)
        self.snap[(id(slot[0]), slot[1])] = dict(eng.waited)
        self.n_inst += 1
        self._record(eng, br, bw, slot[0], slot[1], True)
        if is_output:
            self.out_tokens.append((slot[0], slot[1]))

    def finish(self):
        eng = self.engs["sp"]
        deps = {}
        for slot in self.dma_sems:
            if slot[1] > 0:
                deps[id(slot[0])] = (slot[0], slot[1])
        self._emit_waits(eng, deps)

    def barrier(self):
        deps = {}
        for e in self.engs.values():
            if e.count > 0:
                deps[id(e.sem)] = (e.sem, e.count)
        for slot in self.dma_sems:
            if slot[1] > 0:
                deps[id(slot[0])] = (slot[0], slot[1])
        for e in self.engs.values():
            d = {k: v for k, v in deps.items() if k != id(e.sem)}
            self._emit_waits(e, d)
        self.recs = {}


D = 1024
KC = 8
SEQ = 2048
NCTX = 256
TOK = NCTX + SEQ
NT = TOK // 128
DEPTH = 4
DFF = 4096
EPS = 1e-6
BLOCKS = [(0, 256), (256, 512), (768, 512), (1280, 512), (1792, 512)]
EVEN_IN = 2240
ODD_IN = 3104
AX = mybir.AxisListType.X

WEIGHT_NAMES = ["ada_w", "ada_b", "norm1_w", "norm2_w", "mlp_w1", "mlp_w2", "even_w_in", "mla_q_norm", "mla_w_uq",
                "mla_kv_norm", "mla_w_ukv", "gla_gate_up", "gla_gate_bias", "gla_o_norm", "even_w_out", "gdn_w_in",
                "gdn_conv_w", "gdn_a_log", "gdn_dt_bias", "gdn_o_norm", "gdn_w_out", "final_norm"]
WEIGHT_SHAPES = {
    "ada_w": [4, 1024, 6144], "ada_b": [4, 6144], "norm1_w": [4, 1024], "norm2_w": [4, 1024],
    "mlp_w1": [4, 1024, 4096], "mlp_w2": [4, 4096, 1024], "even_w_in": [2, 1024, 2240], "mla_q_norm": [2, 384],
    "mla_w_uq": [2, 384, 768], "mla_kv_norm": [2, 256], "mla_w_ukv": [2, 256, 1024], "gla_gate_up": [2, 2, 16, 256],
    "gla_gate_bias": [2, 2, 256], "gla_o_norm": [2, 128], "even_w_out": [2, 1024, 1024], "gdn_w_in": [2, 1024, 3104],
    "gdn_conv_w": [2, 3, 2048], "gdn_a_log": [2, 2, 8], "gdn_dt_bias": [2, 2, 8], "gdn_o_norm": [2, 128],
    "gdn_w_out": [2, 1024, 1024], "final_norm": [1024],
}


def host_consts():
    i = np.arange(128)
    same = (i[:, None] // 64) == (i[None, :] // 64)
    Mf = (same & (i[:, None] <= i[None, :])).astype(np.float32)
    Mb = (same & (i[:, None] >= i[None, :])).astype(np.float32)
    Sf = (same & (i[:, None] > i[None, :])).astype(np.float32)
    Sb = (same & (i[:, None] < i[None, :])).astype(np.float32)
    CI = np.stack([(i < 64), (i >= 64)], 1).astype(np.float32)
    ident = np.eye(128, dtype=np.float32)
    cols = [ident, Mf, Mb, Sf, Sb, CI,
            np.concatenate([Mf, CI], 1) / -16.0, np.concatenate([Mb, CI], 1) / -16.0, Sf / -16.0, Sb / -16.0,
            np.concatenate([Mf, CI], 1), np.concatenate([Mb, CI], 1)]
    cst = np.concatenate(cols, 1).astype(np.float32)
    rows = SEQ // 64
    row = np.repeat(np.arange(rows, dtype=np.float32), 64)
    col = np.tile(np.arange(64, dtype=np.float32), rows)
    inv = (10000.0 ** (-np.arange(0, 16, 2, dtype=np.float32) / 16)).astype(np.float32)
    ang = np.concatenate([row[:, None] * inv, col[:, None] * inv], -1).astype(np.float32)
    cos, sin = np.cos(ang).astype(np.float32), np.sin(ang).astype(np.float32)
    C = np.zeros((32, SEQ), np.float32)
    S = np.zeros((32, SEQ), np.float32)
    for ax in range(2):
        for half in range(2):
            for f in range(8):
                d = ax * 16 + half * 8 + f
                C[d] = cos[:, ax * 8 + f]
                S[d] = (-sin[:, ax * 8 + f]) if half == 0 else sin[:, ax * 8 + f]
    rope = np.stack([C, S], 0)
    r = np.arange(128)
    ms = []
    for b in (1, 2, 4, 8, 16, 32):
        same2 = (r[:, None] // (2 * b)) == (r[None, :] // (2 * b))
        ur = same2 & ((r[:, None] % (2 * b)) < b) & ((r[None, :] % (2 * b)) >= b)
        ms.append(ur.astype(np.float32))
    bmask = np.concatenate(ms + [m.T for m in ms], 1).astype(np.float32)
    return cst, rope, bmask


CST_OFF = {}
_o = 0
for _n, _w in [("ident", 128), ("Mf", 128), ("Mb", 128), ("Sf", 128), ("Sb", 128), ("CI", 2), ("glaRf", 130), ("glaRb", 130),
               ("glaAf", 128), ("glaAb", 128), ("gdnRf", 130), ("gdnRb", 130)]:
    CST_OFF[_n] = (_o, _w)
    _o += _w
CST_W = _o


class Ctx:
    pass


def build_program(n_layers=DEPTH, fake_mixer=False, debug_h=False):
    nc = bass.Bass("TRN2", target_bir_lowering=False)
    g = Ctx()
    g.nc = nc
    dr = {}
    dr["x"] = nc.dram_tensor("x", [SEQ, D], F32, kind="ExternalInput").ap()
    dr["ctx"] = nc.dram_tensor("ctx", [NCTX, D], F32, kind="ExternalInput").ap()
    dr["cc"] = nc.dram_tensor("cc", [16, 128], F32, kind="ExternalInput").ap()
    dr["cst"] = nc.dram_tensor("cst", [128, CST_W], F32, kind="ExternalInput").ap()
    dr["rope"] = nc.dram_tensor("rope", [2, 32, SEQ], F32, kind="ExternalInput").ap()
    dr["bmask"] = nc.dram_tensor("bmask", [128, 12 * 128], F32, kind="ExternalInput").ap()
    for n in WEIGHT_NAMES:
        dr[n] = nc.dram_tensor(n, WEIGHT_SHAPES[n], F32, kind="ExternalInput").ap()
    dr["out"] = nc.dram_tensor("out", [SEQ, D], F32, kind="ExternalOutput").ap()
    dr["H"] = nc.dram_tensor("Hs", [KC, 128, TOK], F32).ap()
    dr["MIX"] = nc.dram_tensor("MIXs", [KC, 128, TOK], BF16).ap()
    if debug_h:
        dr["dbg"] = nc.dram_tensor("dbg", [KC, 128, TOK], F32, kind="ExternalOutput").ap()
    g.dr = dr
    es = contextlib.ExitStack()
    with es:
        P = Prog(nc, es)
        g.P = P
        g.es = es
        g.ps = [es.enter_context(nc.psum_tensor("psb%d" % i, [128, 512], F32)) for i in range(7)]
        g.psbf = es.enter_context(nc.psum_tensor("psbf", [128, 1024], BF16))
        g.ps_rr = 0
        _emit(g, n_layers, fake_mixer, debug_h)
        P.finish()
        g.stats = (P.n_inst, P.n_wait, P.semid)
    return nc, g


_UNIQ = [0]


def _sb(g, name, shape, dt=F32, stack=None):
    _UNIQ[0] += 1
    return (stack or g.es).enter_context(g.nc.sbuf_tensor("sb_%s_%d" % (name, _UNIQ[0]), shape, dt))


def _psum(g, kind="rot"):
    if kind == "rot":
        t = g.ps[g.ps_rr]
        g.ps_rr = (g.ps_rr + 1) % getattr(g, "rot_n", 4)
        return t
    if kind == "acc":
        g.ps_acc = 1 - getattr(g, "ps_acc", 0)
        return g.ps[4 + g.ps_acc]
    return g.ps[6]


def _mm(g, out, lhsT, rhs, start=True, stop=True, inc=True):
    g.P.op("pe", lambda e: e.matmul(out, lhsT, rhs, start=start, stop=stop), [lhsT, rhs], [out], inc=inc)


def _mmk(g, out, pairs):
    n = len(pairs)
    for i, (l, r) in enumerate(pairs):
        _mm(g, out, l, r, start=(i == 0), stop=(i == n - 1), inc=(i == n - 1))


def _tr(g, out, in_, ident):
    g.P.op("pe", lambda e: e.transpose(out, in_, ident), [in_, ident], [out])


def _act(g, out, in_, func, scale=1.0, bias=0.0, extra_reads=()):
    rd = [in_] + [a for a in (scale, bias) if not isinstance(a, (int, float))] + list(extra_reads)
    g.P.op("act", lambda e: e.activation(out=out, in_=in_, func=func, bias=bias, scale=scale), rd, [out])


def _tt(g, en, out, in0, in1, op):
    g.P.op(en, lambda e: e.tensor_tensor(out, in0, in1, op=op), [in0, in1], [out])


def _stt(g, en, out, in0, scalar, in1, op0, op1):
    rd = [in0, in1] + ([scalar] if not isinstance(scalar, (int, float)) else [])
    g.P.op(en, lambda e: e.scalar_tensor_tensor(out=out, in0=in0, scalar=scalar, in1=in1, op0=op0, op1=op1), rd, [out])


def _ts(g, en, out, in0, s1, s2, op0, op1=None):
    rd = [in0] + [a for a in (s1, s2) if a is not None and not isinstance(a, (int, float))]
    if op1 is None:
        g.P.op(en, lambda e: e.tensor_scalar(out, in0, s1, None, op0=op0), rd, [out])
    else:
        g.P.op(en, lambda e: e.tensor_scalar(out, in0, s1, s2, op0=op0, op1=op1), rd, [out])


def _copy(g, en, out, in_):
    if en == "act":
        g.P.op("act", lambda e: e.copy(out, in_), [in_], [out])
    else:
        g.P.op(en, lambda e: e.tensor_copy(out, in_), [in_], [out])


def _memset(g, en, out, val):
    g.P.op(en, lambda e: e.memset(out, val), [], [out])


def _load_cast(g, dst, src):
    shp = list(dst.shape)
    if len(shp) == 2:
        pieces = [(dst[:, c:min(c + 1024, shp[1])], src[:, c:min(c + 1024, shp[1])]) for c in range(0, shp[1], 1024)]
    else:
        inner = shp[2]
        assert len(shp) == 3 and inner <= 1024
        step = max(1, 1024 // inner)
        pieces = [(dst[:, a:min(a + step, shp[1]), :], src[:, a:min(a + step, shp[1]), :]) for a in range(0, shp[1], step)]
    for d, s_ in pieces:
        stg = g.stage.get()
        n = 1
        for x in d.shape[1:]:
            n *= x
        if len(d.shape) == 3:
            v = stg[:, 0:n].rearrange("p (a b) -> p a b", b=d.shape[2])
        else:
            v = stg[:, 0:n]
        g.P.dma(v, s_)
        g.cast_rr = 1 - getattr(g, "cast_rr", 0)
        _copy(g, "pool" if g.cast_rr else "act", d, v)


def _rstd_from_sum(g, out_sb, ps_in, inv_n, eps_col):
    _act(g, out_sb, ps_in, AF.Sqrt, scale=inv_n, bias=eps_col)
    g.P.op("dve", lambda e: e.reciprocal(out_sb, out_sb), [out_sb], [out_sb])


class Pool_:
    def __init__(self, g, name, shape, dt, n, stack=None, zero=False):
        self.tiles = [_sb(g, "%s%d" % (name, i), shape, dt, stack) for i in range(n)]
        self.i = 0
        if zero:
            for t in self.tiles:
                _memset(g, "pool", t[:], 0.0)

    def get(self):
        t = self.tiles[self.i]
        self.i = (self.i + 1) % len(self.tiles)
        return t


def _cst(g, name):
    o, w = CST_OFF[name]
    return g.cst[:, o:o + w]


def _setup(g):
    P, dr = g.P, g.dr
    g.cst = _sb(g, "cst", [128, CST_W])
    P.dma(g.cst[:], dr["cst"])
    g.ident = _cst(g, "ident")
    g.ident_bf = _sb(g, "ident_bf", [128, 128], BF16)
    _copy(g, "dve", g.ident_bf[:], g.ident)
    g.ones_bf = _sb(g, "ones_bf", [128, 128], BF16)
    _memset(g, "pool", g.ones_bf[:], 1.0)
    g.ones_f = _sb(g, "ones_f", [128, 128])
    _memset(g, "pool", g.ones_f[:], 1.0)
    g.eps = _sb(g, "eps", [128, 1])
    _memset(g, "pool", g.eps[:], EPS)
    rows = []
    rows.append(("cc", dr["cc"], 16))
    for L in range(DEPTH):
        rows.append(("ada_b%d" % L, dr["ada_b"][L].rearrange("(r c) -> r c", c=128), 48))
        rows.append(("n1w%d" % L, dr["norm1_w"][L].rearrange("(r c) -> r c", c=128), 8))
        rows.append(("n2w%d" % L, dr["norm2_w"][L].rearrange("(r c) -> r c", c=128), 8))
    for i in range(2):
        rows.append(("qn%d" % i, dr["mla_q_norm"][i].rearrange("(r c) -> r c", c=128), 3))
        rows.append(("kvn%d" % i, dr["mla_kv_norm"][i].rearrange("(r c) -> r c", c=128), 2))
        rows.append(("conv%d" % i, dr["gdn_conv_w"][i].rearrange("t (r c) -> (t r) c", c=128), 48))
    rows.append(("fn", dr["final_norm"].rearrange("(r c) -> r c", c=128), 8))
    tot = sum(r[2] for r in rows)
    ntile = (tot + 127) // 128
    g.cols = _sb(g, "cols", [128, ntile * 128])
    g.coloff = {}
    with contextlib.ExitStack() as st:
        rt = [_sb(g, "rowst%d" % i, [128, 128], F32, st) for i in range(ntile)]
        for t in rt:
            _memset(g, "dve", t[:], 0.0)
        r = 0
        for key, ap, n in rows:
            g.coloff[key] = r
            done = 0
            while done < n:
                t, p = divmod(r + done, 128)
                m = min(n - done, 128 - p)
                P.dma(rt[t][p:p + m, :], ap[done:done + m, :])
                done += m
            r += n
        for t in range(ntile):
            ps = _psum(g)
            _tr(g, ps[:, 0:128], rt[t][:], g.ident)
            _copy(g, "dve", g.cols[:, t * 128:(t + 1) * 128], ps[:, 0:128])
        P.barrier()
    g.sc = _sb(g, "sc", [128, KC, 2])
    _act(g, g.sc[:, :, 0], g.cols[:, 0:8], AF.Silu)
    _act(g, g.sc[:, :, 1], g.cols[:, 8:16], AF.Silu)
    g.mod = [_sb(g, "mod%d" % L, [128, 48, 2]) for L in range(DEPTH)]
    g.comb1 = [_sb(g, "comb1_%d" % L, [128, KC, 2]) for L in range(DEPTH)]
    g.comb2 = [_sb(g, "comb2_%d" % L, [128, KC, 2]) for L in range(DEPTH)]
    g.adaw_pool = Pool_(g, "adaw", [128, KC, 128], F32, 2)
    g.stage = Pool_(g, "stage", [128, 1024], F32, 2)


def _col(g, key, n):
    o = g.coloff[key]
    return g.cols[:, o:o + n]


def _adaln_steps(g, L):
    P, dr = g.P, g.dr
    src = dr["ada_w"][L].rearrange("(k p) n -> p k n", p=128)
    for cb in range(48):
        wt = g.adaw_pool.get()
        P.dma(wt[:], src[:, :, cb * 128:(cb + 1) * 128])
        ps = _psum(g)
        _mmk(g, ps[:, 0:2], [(wt[:, k, :], g.sc[:, k, :]) for k in range(KC)])
        bias = _col(g, "ada_b%d" % L, 48)[:, cb:cb + 1]
        _tt(g, "dve", g.mod[L][:, cb, :], ps[:, 0:2], bias.to_broadcast([128, 2]), ALU.add)
        if cb == 47:
            n1 = _col(g, "n1w%d" % L, 8).unsqueeze(2).to_broadcast([128, KC, 2])
            n2 = _col(g, "n2w%d" % L, 8).unsqueeze(2).to_broadcast([128, KC, 2])
            _stt(g, "dve", g.comb1[L][:], g.mod[L][:, 8:16, :], 1.0, n1, ALU.add, ALU.mult)
            _stt(g, "dve", g.comb2[L][:], g.mod[L][:, 32:40, :], 1.0, n2, ALU.add, ALU.mult)
        yield


def _mix_view(g, s0, n):
    return g.dr["MIX"].rearrange("k p t -> p k t")[:, :, s0:s0 + n]


def _h_view(g, s0, n):
    return g.dr["H"].rearrange("k p t -> p k t")[:, :, s0:s0 + n]


def _load_inputs(g):
    P, dr = g.P, g.dr
    with contextlib.ExitStack() as st:
        xin = Pool_(g, "xin", [128, D], F32, 2, st)
        hb = Pool_(g, "hb0_", [128, KC, 128], F32, 2, st)
        for t in range(NT):
            src = dr["ctx"][t * 128:(t + 1) * 128, :] if t < 2 else dr["x"][(t - 2) * 128:(t - 1) * 128, :]
            xt = xin.get()
            P.dma(xt[:], src)
            ht = hb.get()
            for half in range(2):
                ps = _psum(g)
                for q in range(4):
                    k = half * 4 + q
                    _tr(g, ps[:, q * 128:(q + 1) * 128], xt[:, k * 128:(k + 1) * 128], g.ident)
                _copy(g, "act" if half == 0 else "dve", ht[:, half * 4:(half + 1) * 4, :],
                      ps[:].rearrange("p (q t) -> p q t", t=128))
            P.dma(_h_view(g, t * 128, 128), ht[:])
        P.barrier()


def _norm_tmps(g, st):
    return (Pool_(g, "sqn", [128, 512], BF16, 2, st), Pool_(g, "tmn", [128, 512], F32, 3, st), _sb(g, "rstdn", [128, 512], F32, st))


def _norm_mod(g, hblk, N, comb, shift, out_bf, tmps, ab=None):
    sqp, tp, rstd = tmps
    ps = _psum(g)
    for k in range(KC):
        sq = sqp.get()
        _act(g, sq[:, :N], hblk[:, k, :N], AF.Square)
        _mm(g, ps[:, :N], g.ones_bf[:], sq[:, :N], start=(k == 0), stop=(k == KC - 1))
    _rstd_from_sum(g, rstd[:, :N], ps[:, :N], 1.0 / D, g.eps[:])
    if ab is not None:
        w32, AB, t0, a32p = ab
        psab = _psum(g, "misc")
    for k in range(KC):
        t = tp.get()
        _stt(g, "dve", t[:, :N], hblk[:, k, :N], comb[:, k:k + 1], rstd[:, :N], ALU.mult, ALU.mult)
        if ab is None:
            _act(g, out_bf[:, k, :N], t[:, :N], AF.Identity, scale=1.0, bias=shift[:, k:k + 1])
        else:
            a32 = a32p.get()
            _act(g, a32[:, :N], t[:, :N], AF.Identity, scale=1.0, bias=shift[:, k:k + 1])
            _copy(g, "pool", out_bf[:, k, :N], a32[:, :N])
            for tt in range(N // 128):
                _mm(g, psab[:, tt * 32:(tt + 1) * 32], a32[:, tt * 128:(tt + 1) * 128], w32[:, k, :], start=(k == 0 and tt == 0), stop=(k == KC - 1 and tt == N // 128 - 1))
    if ab is not None:
        nt = N // 128
        _copy(g, "act", AB[:, t0:t0 + nt, :], psab[:, 0:nt * 32].rearrange("p (t c) -> p t c", c=32))


def _emit(g, n_layers, fake_mixer, debug_h):
    P, dr = g.P, g.dr
    _setup(g)
    ada = _adaln_steps(g, 0)
    for _ in ada:
        pass
    _load_inputs(g)
    for L in range(n_layers):
        last = (L == DEPTH - 1)
        need_ctx = not last
        nxt = _adaln_steps(g, L + 1) if L + 1 < n_layers else iter(())
        with contextlib.ExitStack() as lstA:
            lat = _mla_alloc(g, lstA) if (L % 2 == 0 and not fake_mixer) else None
            with contextlib.ExitStack() as lst:
                aT = _sb(g, "aT", [128, KC, TOK], BF16, lst)
                AB = None
                if L % 2 == 1 and not fake_mixer and USE_AB32:
                    AB = _sb(g, "AB", [128, NT, 32], F32, lst)
                with contextlib.ExitStack() as st:
                    hbp = Pool_(g, "hbn", [128, KC, 512], F32, 2, st)
                    tmps = _norm_tmps(g, st)
                    ab = None
                    if AB is not None:
                        w32 = _sb(g, "wab32", [128, KC, 32], F32, st)
                        P.dma(w32[:], dr["gdn_w_in"][L // 2].rearrange("(k p) n -> p k n", p=128)[:, :, O_A:O_A + 32])
                        a32p = Pool_(g, "a32", [128, 512], F32, 2, st)
                    for bi, (s0, N) in enumerate(BLOCKS):
                        s = 1 if bi == 0 else 0
                        hb = hbp.get()
                        P.dma(hb[:, :, :N], _h_view(g, s0, N))
                        if AB is not None:
                            ab = (w32, AB, s0 // 128, a32p)
                        _norm_mod(g, hb, N, g.comb1[L][:, :, s], g.mod[L][:, 0:8, s], aT[:, :, s0:s0 + N], tmps, ab)
                    P.barrier()
                if fake_mixer:
                    for (s0, N) in BLOCKS:
                        P.dma(_mix_view(g, s0, N), aT[:, :, s0:s0 + N])
                elif L % 2 == 0:
                    _even_mixer_a(g, L, aT, lat, lst)
                else:
                    _odd_mixer(g, L, aT, need_ctx, AB)
                P.barrier()
            if L % 2 == 0 and not fake_mixer and "attn" not in DBG_SKIP:
                _mla_attention(g, L, lat)
                P.barrier()
        wname = "even_w_out" if L % 2 == 0 else "gdn_w_out"
        with contextlib.ExitStack() as st:
            wo = _sb(g, "wo", [128, KC, D], BF16, st)
            _load_cast(g, wo[:], dr[wname][L // 2].rearrange("(k p) n -> p k n", p=128))
            hbp = Pool_(g, "hba", [128, KC, 512], F32, 2, st)
            mxp = Pool_(g, "mxb", [128, KC, 512], BF16, 2, st)
            for bi, (s0, N) in enumerate(BLOCKS):
                if bi == 0 and not need_ctx:
                    continue
                s = 1 if bi == 0 else 0
                hb = hbp.get()
                P.dma(hb[:, :, :N], _h_view(g, s0, N))
                mx = mxp.get()
                P.dma(mx[:, :, :N], _mix_view(g, s0, N))
                for m in range(KC):
                    ps = _psum(g)
                    _mmk(g, ps[:, :N], [(wo[:, k, m * 128:(m + 1) * 128], mx[:, k, :N]) for k in range(KC)])
                    _stt(g, "dve", hb[:, m, :N], ps[:, :N], g.mod[L][:, 16 + m, s:s + 1], hb[:, m, :N], ALU.mult, ALU.add)
                P.dma(_h_view(g, s0, N), hb[:, :, :N])
                for _ in range(5):
                    next(nxt, None)
            P.barrier()
        if debug_h == ("mix", L):
            _dump_h(g)
            return
        with contextlib.ExitStack() as st:
            w2 = _sb(g, "w2", [128, 32, D], BF16, st)
            w2src = dr["mlp_w2"][L].rearrange("(f p) n -> p f n", p=128)
            _load_cast(g, w2[:], w2src)
            w1p = Pool_(g, "w1p", [128, KC, 512], BF16, 2, st)
            w1src = dr["mlp_w1"][L].rearrange("(k p) n -> p k n", p=128)
            hbp = Pool_(g, "hbm", [128, KC, 512], F32, 1, st)
            a2 = _sb(g, "a2", [128, KC, 512], BF16, st)
            h1 = _sb(g, "h1", [128, 32, 512], BF16, st)
            rl = Pool_(g, "rl", [128, 512], F32, 3, st)
            tmps = _norm_tmps(g, st)
            for bi, (s0, N) in enumerate(BLOCKS):
                if bi == 0 and not need_ctx:
                    continue
                s = 1 if bi == 0 else 0
                hb = hbp.get()
                P.dma(hb[:, :, :N], _h_view(g, s0, N))
                _norm_mod(g, hb, N, g.comb2[L][:, :, s], g.mod[L][:, 24:32, s], a2, tmps)
                for fg in range(8):
                    w1 = w1p.get()
                    _load_cast(g, w1[:], w1src[:, :, fg * 512:(fg + 1) * 512])
                    for f in range(4):
                        ps = _psum(g)
                        _mmk(g, ps[:, :N], [(w1[:, k, f * 128:(f + 1) * 128], a2[:, k, :N]) for k in range(KC)])
                        r = rl.get()
                        _act(g, r[:, :N], ps[:, :N], AF.Relu)
                        _tt(g, "pool", h1[:, fg * 4 + f, :N], r[:, :N], r[:, :N], ALU.mult)
                for m in range(KC):
                    ps = _psum(g)
                    _mmk(g, ps[:, :N], [(w2[:, f, m * 128:(m + 1) * 128], h1[:, f, :N]) for f in range(32)])
                    _stt(g, "dve", hb[:, m, :N], ps[:, :N], g.mod[L][:, 40 + m, s:s + 1], hb[:, m, :N], ALU.mult, ALU.add)
                P.dma(_h_view(g, s0, N), hb[:, :, :N])
                for _ in range(5):
                    next(nxt, None)
            for _ in nxt:
                pass
            P.barrier()
        if debug_h == ("mlp", L):
            _dump_h(g)
            return
    if debug_h:
        _dump_h(g)
        return
    _final(g)


def _dump_h(g):
    P = g.P
    with contextlib.ExitStack() as st:
        hb = Pool_(g, "hbd", [128, KC, 512], F32, 2, st)
        for (s0, N) in BLOCKS:
            t = hb.get()
            P.dma(t[:, :, :N], _h_view(g, s0, N))
            P.dma(g.dr["dbg"].rearrange("k p t -> p k t")[:, :, s0:s0 + N], t[:, :, :N], is_output=True)
        P.dma(g.dr["out"][0:128, :], g.cst[:, 0:D], is_output=True)


def _final(g):
    P, dr = g.P, g.dr
    with contextlib.ExitStack() as st:
        hbp = Pool_(g, "hbf", [128, KC, 512], F32, 2, st)
        sq = _sb(g, "sqf", [128, KC, 512], BF16, st)
        yb = _sb(g, "yf", [128, KC, 512], F32, st)
        rstd = _sb(g, "rstdf", [128, 512], F32, st)
        otp = Pool_(g, "ot", [128, D], F32, 2, st)
        fn = _col(g, "fn", 8)
        for (s0, N) in BLOCKS[1:]:
            hb = hbp.get()
            P.dma(hb[:, :, :N], _h_view(g, s0, N))
            _act(g, sq[:, :, :N], hb[:, :, :N], AF.Square)
            ps = _psum(g)
            _mmk(g, ps[:, :N], [(g.ones_bf[:], sq[:, k, :N]) for k in range(KC)])
            _rstd_from_sum(g, rstd[:, :N], ps[:, :N], 1.0 / D, g.eps[:])
            for k in range(KC):
                _stt(g, "dve", yb[:, k, :N], hb[:, k, :N], fn[:, k:k + 1], rstd[:, :N], ALU.mult, ALU.mult)
            for tt in range(N // 128):
                ot = otp.get()
                for half in range(2):
                    ps = _psum(g)
                    for q in range(4):
                        k = half * 4 + q
                        _tr(g, ps[:, q * 128:(q + 1) * 128], yb[:, k, tt * 128:(tt + 1) * 128], g.ident)
                    _copy(g, "act" if half == 0 else "dve", ot[:, half * 512:(half + 1) * 512], ps[:])
                r0 = s0 - NCTX + tt * 128
                P.dma(dr["out"][r0:r0 + 128, :], ot[:], is_output=True)


C_CQ, C_CKV, C_KPE, C_GQ, C_GK, C_GV, C_GG, C_GLOW = 0, 384, 640, 672, 928, 1184, 1696, 2208
GLA_FWD = list(range(NT))
GLA_BWD = [1, 0] + list(range(NT - 1, 1, -1))


def _mla_alloc(g, st):
    lat = Ctx()
    lat.cqn = _sb(g, "cqn", [128, 3, TOK], BF16, st)
    lat.ckvn = _sb(g, "ckvn", [128, 2, TOK], BF16, st)
    lat.kpeT = _sb(g, "kpeT", [128, TOK], BF16, st)
    return lat


def _rope_tables(g, lat, st):
    lat.ropeC = _sb(g, "ropeC", [128, SEQ], F32, st)
    lat.ropeS = _sb(g, "ropeS", [128, SEQ], F32, st)
    g.P.dma(lat.ropeC[64:96, :], g.dr["rope"][0])
    g.P.dma(lat.ropeS[64:96, :], g.dr["rope"][1])


def _swap_halves(g, en, dst, src):
    d = dst.rearrange("p h (a t f) -> p h a t f", a=2, t=2)
    s_ = src.rearrange("p h (a t f) -> p h a t f", a=2, t=2)
    _copy(g, en, d[:, :, :, 0, :], s_[:, :, :, 1, :])
    _copy(g, en, d[:, :, :, 1, :], s_[:, :, :, 0, :])


def _even_mixer_a(g, L, aT, lat, st):
    P, dr = g.P, g.dr
    i = L // 2
    win = _sb(g, "win", [128, KC, EVEN_IN], BF16, st)
    wsrc = dr["even_w_in"][i].rearrange("(k p) n -> p k n", p=128)
    for k in range(KC):
        _load_cast(g, win[:, k, :], wsrc[:, k, :])
    with contextlib.ExitStack() as s1:
        _rope_tables(g, lat, s1)
        winrot = _sb(g, "winrot", [128, KC, 32], BF16, s1)
        _swap_halves(g, "pool", winrot[:], win[:, :, C_KPE:C_KPE + 32])
        cqf = _sb(g, "cqf", [128, 5, 512], F32, s1)
        sqp = Pool_(g, "sql", [128, 512], BF16, 2, s1)
        rq = _sb(g, "rq", [128, 512], F32, s1)
        rkv = _sb(g, "rkv", [128, 512], F32, s1)
        t1p = Pool_(g, "t1l", [128, 512], F32, 2, s1)
        qn = _col(g, "qn%d" % i, 3)
        kvn = _col(g, "kvn%d" % i, 2)
        for bi, (s0, N) in enumerate(BLOCKS):
            rhs = [aT[:, k, s0:s0 + N] for k in range(KC)]
            pss_q = _psum(g, "acc")
            pss_kv = _psum(g, "acc")
            for fc in range(5):
                ps = _psum(g)
                _mmk(g, ps[:, :N], [(win[:, k, fc * 128:(fc + 1) * 128], rhs[k]) for k in range(KC)])
                _copy(g, "act", cqf[:, fc, :N], ps[:, :N])
                sq = sqp.get()
                _tt(g, "pool", sq[:, :N], cqf[:, fc, :N], cqf[:, fc, :N], ALU.mult)
                if fc < 3:
                    _mm(g, pss_q[:, :N], g.ones_bf[:], sq[:, :N], start=(fc == 0), stop=(fc == 2))
                else:
                    _mm(g, pss_kv[:, :N], g.ones_bf[:], sq[:, :N], start=(fc == 3), stop=(fc == 4))
            _rstd_from_sum(g, rq[:, :N], pss_q[:, :N], 1.0 / 384, g.eps[:])
            _rstd_from_sum(g, rkv[:, :N], pss_kv[:, :N], 1.0 / 256, g.eps[:])
            for fc in range(3):
                _stt(g, "dve", lat.cqn[:, fc, s0:s0 + N], cqf[:, fc, :N], qn[:, fc:fc + 1], rq[:, :N], ALU.mult, ALU.mult)
            for fc in range(2):
                _stt(g, "dve", lat.ckvn[:, fc, s0:s0 + N], cqf[:, 3 + fc, :N], kvn[:, fc:fc + 1], rkv[:, :N], ALU.mult, ALU.mult)
            ps = _psum(g)
            _mmk(g, ps[64:96, :N], [(win[:, k, C_KPE:C_KPE + 32], rhs[k]) for k in range(KC)])
            if bi == 0:
                _copy(g, "act", lat.kpeT[64:96, s0:s0 + N], ps[64:96, :N])
            else:
                ps2 = _psum(g)
                _mmk(g, ps2[64:96, :N], [(winrot[:, k, :], rhs[k]) for k in range(KC)])
                p0 = s0 - NCTX
                t1 = t1p.get()
                t2 = t1p.get()
                _tt(g, "dve", t1[64:96, :N], ps[64:96, :N], lat.ropeC[64:96, p0:p0 + N], ALU.mult)
                _tt(g, "dve", t2[64:96, :N], ps2[64:96, :N], lat.ropeS[64:96, p0:p0 + N], ALU.mult)
                _tt(g, "pool", lat.kpeT[64:96, s0:s0 + N], t1[64:96, :N], t2[64:96, :N], ALU.add)
        P.barrier()
    with contextlib.ExitStack() as s2:
        if "gla" not in DBG_SKIP:
            try:
                _gla(g, i, aT, win, s2)
            except _Stop:
                pass
        P.barrier()


def _gla(g, i, aT, win, st):
    P, dr = g.P, g.dr
    gup = _sb(g, "gup", [16, 2, 256], F32, st)
    P.dma(gup[:], dr["gla_gate_up"][i].rearrange("z r n -> r z n"))
    gbias = _sb(g, "gbias", [128, 2, 256], F32, st)
    P.dma(gbias[:], dr["gla_gate_bias"][i].partition_broadcast(128))
    onorm = _sb(g, "onorm", [128, 128], F32, st)
    P.dma(onorm[:], dr["gla_o_norm"][i].partition_broadcast(128))
    Of = _sb(g, "Of", [128, NT, 512], F32, st)
    S = _sb(g, "S", [128, 2, 128], F32, st)
    Sbf = _sb(g, "Sbf", [128, 2, 128], BF16, st)
    glowT = Pool_(g, "glowT", [16, 128], F32, 2, st)
    tl = Pool_(g, "tl", [128, 256], F32, 2, st)
    Lz = Pool_(g, "Lz", [128, 256], F32, 2, st)
    Eq = Pool_(g, "Eq", [128, 2, 128], F32, 2, st)
    Ek = Pool_(g, "Ek", [128, 2, 128], F32, 2, st)
    gend = Pool_(g, "gend", [128, 2, 2], F32, 2, st)
    kendE = Pool_(g, "kendE", [128, 256], F32, 2, st)
    qdecT = Pool_(g, "qdecT", [128, 4, 128], BF16, 2, st, zero=True)
    kinvT = Pool_(g, "kinvT", [128, 4, 128], BF16, 2, st, zero=True)
    kend = Pool_(g, "kend", [128, 2, 256], BF16, 2, st)
    CI2 = _cst(g, "CI").unsqueeze(2).to_broadcast([128, 2, 256])
    Vg = Pool_(g, "Vg", [128, 512], BF16, 2, st)
    AmT = Pool_(g, "AmT", [128, 4, 128], BF16, 2, st)
    G2 = Pool_(g, "G2", [128, 512], F32, 1, st)
    ob = Pool_(g, "ob", [128, 512], F32, 1, st)
    sqo = Pool_(g, "sqo", [128, 512], F32, 1, st)
    ss4 = Pool_(g, "ss4", [128, 4], F32, 2, st)
    mtok = Pool_(g, "mtok", [128, 512], BF16, 2, st)
    mt = Pool_(g, "mt", [128, 4, 128], BF16, 2, st)
    LN8 = float(np.log(0.125))
    for z in range(2):
        R_ = _cst(g, "glaRf" if z == 0 else "glaRb")
        A_ = _cst(g, "glaAf" if z == 0 else "glaAb")
        Mz = _cst(g, "Mf" if z == 0 else "Mb")
        _memset(g, "pool", S[:], 0.0)
        _memset(g, "pool", Sbf[:], 0.0)
        for t in (GLA_FWD if z == 0 else GLA_BWD):
            ts = slice(t * 128, (t + 1) * 128)
            at = [aT[:, k, ts] for k in range(KC)]
            ps = _psum(g)
            c0 = C_GLOW + 16 * z
            _mmk(g, ps[0:16, 0:128], [(win[:, k, c0:c0 + 16], at[k]) for k in range(KC)])
            gl = glowT.get()
            _copy(g, "act", gl[:], ps[0:16, 0:128])
            _chk("gla_%s_%d" % ("a", z))
            ps = _psum(g)
            _mm(g, ps[:, 0:256], gl[:], gup[:, z, :])
            tt_ = tl.get()
            _tt(g, "dve", tt_[:], ps[:, 0:256], gbias[:, z, :], ALU.add)
            L_ = Lz.get()
            _act(g, tt_[:], tt_[:], AF.Exp, scale=-1.0)
            _act(g, L_[:], tt_[:], AF.Ln, scale=1.0, bias=1.0)
            _chk("gla_%s_%d" % ("b", z))
            ps = _psum(g)
            for hp in range(2):
                _mm(g, ps[:, hp * 130:(hp + 1) * 130], L_[:, hp * 128:(hp + 1) * 128], R_)
            pv = ps[:, 0:260].rearrange("p (h c) -> p h c", c=130)
            eq, ek, ge = Eq.get(), Ek.get(), gend.get()
            _act(g, eq[:], pv[:, :, 0:128], AF.Exp, scale=1.0, bias=LN8)
            _act(g, ek[:], pv[:, :, 0:128], AF.Exp, scale=-1.0)
            _act(g, ge[:], pv[:, :, 128:130], AF.Exp)
            _chk("gla_%s_%d" % ("c", z))
            ps = _psum(g)
            _mm(g, ps[:, 0:256], A_, L_[:])
            ke = kendE.get()
            _act(g, ke[:], ps[:, 0:256], AF.Exp)
            _chk("gla_%s_%d" % ("d", z))
            ps = _psum(g)
            for hp in range(2):
                _mmk(g, ps[:, hp * 128:(hp + 1) * 128], [(win[:, k, C_GQ + hp * 128:C_GQ + (hp + 1) * 128], at[k]) for k in range(KC)])
                _mmk(g, ps[:, 256 + hp * 128:256 + (hp + 1) * 128], [(win[:, k, C_GK + hp * 128:C_GK + (hp + 1) * 128], at[k]) for k in range(KC)])
            qd, ki = qdecT.get(), kinvT.get()
            for (lo, hi, par) in ((0, 64, 0), (64, 128, 1)):
                _tt(g, "dve", qd[lo:hi, par::2, :], ps[lo:hi, 0:256].rearrange("p (h c) -> p h c", c=128), eq[lo:hi, :, :], ALU.mult)
                _tt(g, "dve", ki[lo:hi, par::2, :], ps[lo:hi, 256:512].rearrange("p (h c) -> p h c", c=128), ek[lo:hi, :, :], ALU.mult)
            _chk("gla_%s_%d" % ("e", z))
            ps = _psum(g)
            _mmk(g, ps[:, 0:256], [(at[k], win[:, k, C_GK:C_GK + 256]) for k in range(KC)])
            kn = kend.get()
            _tt(g, "dve", ke[:], ps[:, 0:256], ke[:], ALU.mult)
            _tt(g, "pool", kn[:], ke[:].unsqueeze(1).to_broadcast([128, 2, 256]), CI2, ALU.mult)
            _chk("gla_%s_%d" % ("f", z))
            ps = _psum(g)
            _mmk(g, ps[:, 0:512], [(at[k], win[:, k, C_GV:C_GV + 512]) for k in range(KC)])
            vg = Vg.get()
            _copy(g, "act", vg[:], ps[:, 0:512])
            _chk("gla_%s_%d" % ("g", z))
            ps = _psum(g)
            for h in range(4):
                hp, hb = h // 2, (h % 2) * 64
                _mm(g, ps[:, h * 128:(h + 1) * 128], ki[:, h, :], qd[:, h, :])
            am = AmT.get()
            if "h_dve" not in DBG_SKIP:
                _tt(g, "dve", am[:], ps[:].rearrange("p (h c) -> p h c", c=128), Mz.unsqueeze(1).to_broadcast([128, 4, 128]), ALU.mult)
            else:
                _copy(g, "act", am[:], ps[:].rearrange("p (h c) -> p h c", c=128))
            _chk("gla_%s_%d" % ("h", z))
            if z == 1:
                ps = _psum(g)
                _mmk(g, ps[:, 0:512], [(at[k], win[:, k, C_GG:C_GG + 512]) for k in range(KC)])
                g2 = G2.get()
                _act(g, g2[:], ps[:, 0:512], AF.Silu)
                _tt(g, "pool", g2[:].rearrange("p (h c) -> p h c", c=128), g2[:].rearrange("p (h c) -> p h c", c=128),
                    onorm[:].unsqueeze(1).to_broadcast([128, 4, 128]), ALU.mult)
            _chk("gla_%s_%d" % ("i", z))
            pso = _psum(g, "acc")
            for c in ((0, 1) if z == 0 else (1, 0)):
                cb = c * 64
                psS = _psum(g, "misc")
                for h in range(4):
                    hp, hb = h // 2, (h % 2) * 64
                    _mm(g, pso[cb:cb + 64, h * 128:(h + 1) * 128], qd[:, h, cb:cb + 64], Sbf[:, hp, :], start=True, stop=False)
                    _mm(g, pso[cb:cb + 64, h * 128:(h + 1) * 128], am[:, h, cb:cb + 64], vg[:, h * 128:(h + 1) * 128], start=False, stop=True)
                    _mm(g, psS[hb:hb + 64, hp * 128:(hp + 1) * 128], kn[:, c, h * 64:(h + 1) * 64], vg[:, h * 128:(h + 1) * 128])
                for hp in range(2):
                    _stt(g, "dve", S[:, hp, :], S[:, hp, :], ge[:, hp, c:c + 1], psS[:, hp * 128:(hp + 1) * 128], ALU.mult, ALU.add)
                _copy(g, "pool", Sbf[:], S[:])
            _chk("gla_%s_%d" % ("j", z))
            if z == 0:
                _copy(g, "act", Of[:, t, :], pso[:])
            else:
                o = ob.get()
                _tt(g, "dve", o[:], pso[:], Of[:, t, :], ALU.add)
                sq = sqo.get()
                _tt(g, "pool", sq[:], o[:], o[:], ALU.mult)
                s4 = ss4.get()
                P.op("dve", lambda e: e.tensor_reduce(s4[:], sq[:].rearrange("p (h c) -> p h c", c=128), axis=AX, op=ALU.add), [sq[:]], [s4[:]])
                _rstd_from_sum(g, s4[:], s4[:], 1.0 / 128, g.eps[:])
                o3 = o[:].rearrange("p (h c) -> p h c", c=128)
                _tt(g, "dve", o3, o3, s4[:].unsqueeze(2).to_broadcast([128, 4, 128]), ALU.mult)
                mk = mtok.get()
                _tt(g, "pool", mk[:], o[:], g2[:], ALU.mult)
                for h in range(4):
                    _tr(g, g.psbf[:, h * 128:(h + 1) * 128], mk[:, h * 128:(h + 1) * 128], g.ident_bf[:])
                m_ = mt.get()
                _copy(g, "act", m_[:], g.psbf[:, 0:512].rearrange("p (h c) -> p h c", c=128))
                P.dma(g.dr["MIX"].rearrange("k p t -> p k t")[:, 4:8, ts], m_[:])
            _chk("gla_k_%d" % z)


def _mla_attention(g, L, lat):
    P, dr = g.P, g.dr
    i = L // 2
    with contextlib.ExitStack() as st0:
      QT = _sb(g, "QT", [128, 8, TOK], BF16, st0)
      KT = _sb(g, "KT", [128, 8, TOK], BF16, st0)
      VA = _sb(g, "VA", [128, NT, 8, 128], BF16, st0)
      with contextlib.ExitStack() as st:
        _rope_tables(g, lat, st)
        wuq = _sb(g, "wuq", [128, 3, 768], BF16, st)
        _load_cast(g, wuq[:], dr["mla_w_uq"][i].rearrange("(k p) n -> p k n", p=128))
        wuqr = _sb(g, "wuqr", [128, 3, 768], BF16, st)
        _copy(g, "pool", wuqr[:], wuq[:])
        for k in range(3):
            _swap_halves(g, "pool", wuqr[:, k, :].rearrange("p (h c) -> p h c", c=96)[:, :, 64:96],
                         wuq[:, k, :].rearrange("p (h c) -> p h c", c=96)[:, :, 64:96])
        wukv = _sb(g, "wukv", [128, 2, 1024], BF16, st)
        _load_cast(g, wukv[:], dr["mla_w_ukv"][i].rearrange("(k p) n -> p k n", p=128))
        wv = _sb(g, "wv", [128, 2, 512], BF16, st)
        for k in range(2):
            _copy(g, "pool", wv[:, k, :].rearrange("p (h c) -> p h c", c=64), wukv[:, k, :].rearrange("p (h c) -> p h c", c=128)[:, :, 64:128])
        _memset(g, "pool", VA[:, :, :, 64:128].rearrange("p t h c -> p (t h) c"), 1.0)
        t1p = Pool_(g, "t1a", [128, 512], F32, 2, st)
        for bi, (s0, N) in enumerate(BLOCKS):
            sl = slice(s0, s0 + N)
            p0 = s0 - NCTX
            for h in range(8):
                ps = _psum(g)
                _mmk(g, ps[0:96, :N], [(wuq[:, fc, h * 96:(h + 1) * 96], lat.cqn[:, fc, sl]) for fc in range(3)])
                _copy(g, "act", QT[0:64, h, sl], ps[0:64, :N])
                if bi == 0:
                    _copy(g, "act", QT[64:96, h, sl], ps[64:96, :N])
                else:
                    ps2 = _psum(g)
                    _mmk(g, ps2[0:96, :N], [(wuqr[:, fc, h * 96:(h + 1) * 96], lat.cqn[:, fc, sl]) for fc in range(3)])
                    t1, t2 = t1p.get(), t1p.get()
                    _tt(g, "dve", t1[64:96, :N], ps[64:96, :N], lat.ropeC[64:96, p0:p0 + N], ALU.mult)
                    _tt(g, "dve", t2[64:96, :N], ps2[64:96, :N], lat.ropeS[64:96, p0:p0 + N], ALU.mult)
                    _tt(g, "pool", QT[64:96, h, sl], t1[64:96, :N], t2[64:96, :N], ALU.add)
                ps = _psum(g)
                _mmk(g, ps[0:64, :N], [(wukv[:, c, h * 128:h * 128 + 64], lat.ckvn[:, c, sl]) for c in range(2)])
                _copy(g, "act", KT[0:64, h, sl], ps[0:64, :N])
                _copy(g, "pool", KT[64:96, h, sl], lat.kpeT[64:96, sl])
            for tt_ in range(N // 128):
                t = s0 // 128 + tt_
                ps = _psum(g)
                _mmk(g, ps[:, 0:512], [(lat.ckvn[:, c, t * 128:(t + 1) * 128], wv[:, c, :]) for c in range(2)])
                _copy(g, "dve", VA[:, t, :, 0:64], ps[:, 0:512].rearrange("p (h c) -> p h c", c=64))
        P.barrier()
      with contextlib.ExitStack() as st:
        scale = float(96 ** -0.5)
        pT = Pool_(g, "pT", [128, 512], BF16, 3, st)
        rec = Pool_(g, "rec", [128, 512], F32, 2, st)
        atile = Pool_(g, "atile", [128, 512], BF16, 2, st)
        for bi, (s0, N) in enumerate(BLOCKS):
            sl = slice(s0, s0 + N)
            nk = 2 if bi == 0 else NT
            for h in range(8):
                if h % 2 == 0:
                    at_ = atile.get()
                pso = _psum(g, "acc")
                for kc in range(nk):
                    pss = _psum(g)
                    _mm(g, pss[:, :N], KT[0:96, h, kc * 128:(kc + 1) * 128], QT[0:96, h, sl])
                    p_ = pT.get()
                    _act(g, p_[:, :N], pss[:, :N], AF.Exp, scale=scale)
                    _mm(g, pso[:, :N], VA[:, kc, h, :], p_[:, :N], start=(kc == 0), stop=(kc == nk - 1))
                r_ = rec.get()
                P.op("dve", lambda e: e.reciprocal(r_[64:128, :N], pso[64:128, :N]), [pso[64:128, :N]], [r_[64:128, :N]])
                hb = (h % 2) * 64
                _tt(g, "dve", at_[hb:hb + 64, :N], pso[0:64, :N], r_[64:128, :N], ALU.mult)
                if h % 2 == 1:
                    P.dma(g.dr["MIX"].rearrange("k p t -> p k t")[:, h // 2, sl], at_[:, :N])


O_Q, O_K, O_V, O_G, O_A, O_B = 0, 512, 1024, 2048, 3072, 3088


def _odd_mixer(g, L, aT, need_ctx, AB):
    import itertools
    P, dr = g.P, g.dr
    i = L // 2
    wsrc = dr["gdn_w_in"][i].rearrange("(k p) n -> p k n", p=128)
    with contextlib.ExitStack() as st:
        blockones = _sb(g, "blockones", [128, 128], BF16, st)
        _memset(g, "pool", blockones[:], 0.0)
        _memset(g, "pool", blockones[0:64, 0:64], 1.0)
        _memset(g, "pool", blockones[64:128, 64:128], 1.0)
        wab = _sb(g, "wab", [128, KC, 32], BF16, st)
        _load_cast(g, wab[:], wsrc[:, :, O_A:O_A + 32])
        dtb = _sb(g, "dtb", [128, 16], F32, st)
        P.dma(dtb[:], dr["gdn_dt_bias"][i].rearrange("z h -> (z h)").partition_broadcast(128))
        negA = _sb(g, "negA", [128, 16], F32, st)
        P.dma(negA[:], dr["gdn_a_log"][i].rearrange("z h -> (z h)").partition_broadcast(128))
        _act(g, negA[:], negA[:], AF.Exp)
        _ts(g, "dve", negA[:], negA[:], -1.0, None, ALU.mult)
        onorm = _sb(g, "onormd", [128, 128], F32, st)
        P.dma(onorm[:], dr["gdn_o_norm"][i].partition_broadcast(128))
        g.bmask = _sb(g, "bmask", [128, 12, 128], BF16, st)
        _load_cast(g, g.bmask[:], dr["bmask"].rearrange("p (m c) -> p m c", c=128))
        nhp = 1
        NH = 2 * nhp
        for pair in range(2):
            with contextlib.ExitStack() as sh:
                units = []
                for u_ in range(2):
                    q0 = (2 * pair + u_) * nhp
                    d = Ctx()
                    d.q0 = q0
                    d.qT = _sb(g, "gqT", [128, nhp, TOK], BF16, sh)
                    d.kT = _sb(g, "gkT", [128, nhp, TOK], BF16, sh)
                    d.Vt = _sb(g, "gVt", [128, NT, NH * 128], BF16, sh)
                    d.Kt = _sb(g, "gKt", [128, NT, NH * 64], BF16, sh)
                    d.Of = _sb(g, "gOf", [128, NT, NH * 128], F32, sh)
                    d.wg = _sb(g, "gwg", [128, KC, NH * 128], BF16, sh)
                    _load_cast(g, d.wg[:], wsrc[:, :, O_G + 2 * q0 * 128:O_G + (2 * q0 + NH) * 128])
                    d.banks = (g.ps[3 + 2 * u_], g.ps[4 + 2 * u_])
                    units.append(d)
                with contextlib.ExitStack() as s1:
                    tmp = _gdn_conv_tmps(g, s1)
                    for d in units:
                        _gdn_conv_stage(g, i, d.q0, nhp, aT, wsrc, d.qT, d.kT, d.Vt, d.Kt, blockones, tmp)
                    P.barrier()
                with contextlib.ExitStack() as s2:
                    g.rot_n = 3
                    g.ps_rr = 0
                    gens = [_gdn_scan(g, d, nhp, aT, wab, dtb, negA, onorm, need_ctx, s2) for d in units]
                    for _ in itertools.zip_longest(*gens):
                        pass
                    g.rot_n = 4
                    g.ps_rr = 0
                    P.barrier()


def _gdn_conv_tmps(g, st):
    t = Ctx()
    t.xl = _sb(g, "xl", [128, SEQ + 2], F32, st)
    t.xc = _sb(g, "xc", [128, NCTX + 2], F32, st)
    for buf, n in ((t.xl, SEQ), (t.xc, NCTX)):
        _memset(g, "pool", buf[:, 0:1], 0.0)
        _memset(g, "pool", buf[:, n + 1:n + 2], 0.0)
    t.tcv = _sb(g, "tcv", [128, SEQ], F32, st)
    t.ybf = _sb(g, "ybf", [128, TOK], BF16, st)
    t.yf = _sb(g, "yf32", [128, 512], F32, st)
    t.sqp = _sb(g, "sqc", [128, 512], BF16, st)
    t.rn = _sb(g, "rnc", [128, 512], F32, st)
    t.wtp = Pool_(g, "wcv", [128, KC, 128], BF16, 2, st)
    return t


def _gdn_conv_stage(g, i, q0, nhp, aT, wsrc, qT, kT, Vt, Kt, blockones, tmp):
    P = g.P
    xl, xc, tcv, ybf, yf, sqp, rn, wtp = tmp.xl, tmp.xc, tmp.tcv, tmp.ybf, tmp.yf, tmp.sqp, tmp.rn, tmp.wtp
    cols = _col(g, "conv%d" % i, 48)
    chunks = [("q", O_Q // 128 + q0 + j, j) for j in range(nhp)] + [("k", O_K // 128 + q0 + j, j) for j in range(nhp)] + \
             [("v", O_V // 128 + 2 * q0 + j, j) for j in range(2 * nhp)]
    for kind, cc, j in chunks:
        wt = wtp.get()
        _load_cast(g, wt[:], wsrc[:, :, cc * 128:(cc + 1) * 128])
        for bi, (s0, N) in enumerate(BLOCKS):
            ps = _psum(g)
            _mmk(g, ps[:, :N], [(wt[:, k, :], aT[:, k, s0:s0 + N]) for k in range(KC)])
            if bi == 0:
                _copy(g, "act", xc[:, 1:1 + N], ps[:, :N])
            else:
                p0 = s0 - NCTX
                _copy(g, "act", xl[:, 1 + p0:1 + p0 + N], ps[:, :N])
        w0, w1, w2 = cols[:, cc:cc + 1], cols[:, 16 + cc:17 + cc], cols[:, 32 + cc:33 + cc]
        for buf, n, o0 in ((xc, NCTX, 0), (xl, SEQ, NCTX)):
            t = tcv[:, 0:n]
            _ts(g, "pool", t, buf[:, 1:1 + n], w1, None, ALU.mult)
            _stt(g, "dve", t, buf[:, 0:n], w0, t, ALU.mult, ALU.add)
            _stt(g, "dve", t, buf[:, 2:2 + n], w2, t, ALU.mult, ALU.add)
            if kind == "v":
                _act(g, ybf[:, o0:o0 + n], t, AF.Silu)
            else:
                for c0 in range(0, n, 512):
                    m = min(512, n - c0)
                    _act(g, yf[:, :m], t[:, c0:c0 + m], AF.Silu)
                    _tt(g, "pool", sqp[:, :m], yf[:, :m], yf[:, :m], ALU.mult)
                    ps = _psum(g)
                    _mm(g, ps[:, :m], blockones[:], sqp[:, :m])
                    _rstd_from_sum(g, rn[:, :m], ps[:, :m], 1.0, g.eps[:])
                    dst = (qT if kind == "q" else kT)[:, j, o0 + c0:o0 + c0 + m]
                    if kind == "q":
                        _stt(g, "dve", dst, yf[:, :m], 0.125, rn[:, :m], ALU.mult, ALU.mult)
                    else:
                        _tt(g, "dve", dst, yf[:, :m], rn[:, :m], ALU.mult)
        if kind in ("k", "v"):
            src = kT[:, j, :] if kind == "k" else ybf[:]
            for t4 in range(0, NT, 4):
                nt = min(4, NT - t4)
                for q in range(nt):
                    _tr(g, g.psbf[:, q * 128:(q + 1) * 128], src[:, (t4 + q) * 128:(t4 + q + 1) * 128], g.ident_bf[:])
                dst = (Kt[:, t4:t4 + nt, j * 128:(j + 1) * 128] if kind == "k" else Vt[:, t4:t4 + nt, j * 128:(j + 1) * 128])
                _copy(g, "act", dst, g.psbf[:, 0:nt * 128].rearrange("p (q c) -> p q c", c=128))


def _gdn_scan(g, d, nhp, aT, wab, dtb, negA, onorm, need_ctx, st):
    P = g.P
    NH = 2 * nhp
    W = NH * 128
    KW = NH * 64
    q0, qT, kT, Vt, Kt, Of, wg = d.q0, d.qT, d.kT, d.Vt, d.Kt, d.Of, d.wg
    bank_o, bank_s = d.banks
    S = _sb(g, "dS", [128, nhp, 128], F32, st)
    Sbf = _sb(g, "dSbf", [128, nhp, 128], BF16, st)
    sm = Pool_(g, "dsm", [128, 32], F32, 2, st)
    larep = _sb(g, "larep", [128, NH, 64], F32, st)
    Eg = _sb(g, "dEg", [128, nhp, 128], F32, st)
    gend = Pool_(g, "dgend", [128, nhp, 2], F32, 2, st)
    qdecT = _sb(g, "dqdec", [128, NH, 128], BF16, st)
    _memset(g, "pool", qdecT[:], 0.0)
    kz = _sb(g, "dkz", [128, NH, 128], BF16, st)
    _memset(g, "pool", kz[:], 0.0)
    CI4 = _cst(g, "CI").unsqueeze(2).to_broadcast([128, 2, KW])
    rhsd = _sb(g, "rhsd", [128, NH, 128], F32, st)
    dec = _sb(g, "ddec", [128, NH, 128], F32, st)
    tG = _sb(g, "dtG", [128, NH, 128], F32, st)
    qkm = _sb(g, "dqkm", [128, NH, 128], BF16, st)
    Mp = Pool_(g, "dM", [128, NH, 128], BF16, 2, st)
    Np = Pool_(g, "dN", [128, NH, 128], BF16, 2, st)
    Tnp = Pool_(g, "dTn", [128, NH, 128], BF16, 2, st)
    Amp = Pool_(g, "dAm", [128, NH, 128], BF16, 4, st)
    Y2p = Pool_(g, "dY2", [128, NH, 128], BF16, 2, st)
    NQ = Pool_(g, "dNQ", [128, NH, 256], BF16, 2, st)
    vb = _sb(g, "dvb", [128, W], BF16, st)
    kbg = _sb(g, "dkbg", [128, NH, 64], BF16, st)
    kend = _sb(g, "dkend", [128, NH, 64], BF16, st)
    kend2 = _sb(g, "dkend2", [128, 2, KW], BF16, st)
    u = _sb(g, "du", [128, W], F32, st)
    wT = _sb(g, "dwT", [128, NH, 128], BF16, st)
    _memset(g, "pool", wT[:], 0.0)
    vnew = _sb(g, "dvnew", [128, W], BF16, st)
    _memset(g, "pool", vnew[:], 0.0)
    G2 = _sb(g, "dG2", [128, W], F32, st)
    ob = _sb(g, "dob", [128, W], F32, st)
    sqo = _sb(g, "dsqo", [128, W], F32, st)
    mtok = _sb(g, "dmtok", [128, W], BF16, st)
    mt = Pool_(g, "dmt", [128, NH, 128], BF16, 2, st)
    ident_bc = g.ident.unsqueeze(1).to_broadcast([128, NH, 128])
    v3 = lambda ap: ap.rearrange("p (h c) -> p h c", c=128)
    halves = ((0, 64, 0), (64, 128, 1))
    yield
    for z in range(2):
        R_ = _cst(g, "gdnRf" if z == 0 else "gdnRb")
        Mz = _cst(g, "Mf" if z == 0 else "Mb")
        Sz = _cst(g, "Sf" if z == 0 else "Sb")
        Mz_bc = Mz.unsqueeze(1).to_broadcast([128, NH, 128])
        Sz_bc = Sz.unsqueeze(1).to_broadcast([128, NH, 128])
        _memset(g, "pool", S[:], 0.0)
        _memset(g, "pool", Sbf[:], 0.0)
        for t in (GLA_FWD if z == 0 else GLA_BWD):
            ts = slice(t * 128, (t + 1) * 128)
            at = [aT[:, k, ts] for k in range(KC)]
            want_out = need_ctx or t >= 2
            ps = _psum(g)
            _mmk(g, ps[:, 0:32], [(at[k], wab[:, k, :]) for k in range(KC)])
            s_ = sm.get()
            ca = z * 8 + 2 * q0
            la, beta = s_[:, 0:NH], s_[:, NH:2 * NH]
            egk = s_[:, 2 * NH:4 * NH]
            eg, ekend = s_[:, 2 * NH:3 * NH], s_[:, 3 * NH:4 * NH]
            bg = s_[:, 4 * NH:5 * NH]
            rs = s_[:, 5 * NH:6 * NH]
            _tt(g, "dve", la, ps[:, ca:ca + NH], dtb[:, ca:ca + NH], ALU.add)
            _act(g, beta, ps[:, 16 + ca:16 + ca + NH], AF.Sigmoid)
            yield
            _act(g, la, la, AF.Exp)
            _act(g, la, la, AF.Ln, scale=1.0, bias=1.0)
            _tt(g, "dve", la, la, negA[:, ca:ca + NH], ALU.mult)
            _copy(g, "pool", larep[:], la.unsqueeze(2).to_broadcast([128, NH, 64]))
            yield
            ps = _psum(g)
            lr = larep[:].rearrange("p h d -> p (h d)")
            for hp in range(nhp):
                _mm(g, ps[:, hp * 130:(hp + 1) * 130], lr[:, hp * 128:(hp + 1) * 128], R_)
            pv = ps[:, 0:130 * nhp].rearrange("p (h c) -> p h c", c=130)
            ge = gend.get()
            _act(g, Eg[:], pv[:, :, 0:128], AF.Exp)
            _act(g, ge[:], pv[:, :, 128:130], AF.Exp)
            yield
            for (lo, hi, par) in halves:
                _tt(g, "dve", qdecT[lo:hi, par::2, :], qT[lo:hi, :, ts], Eg[lo:hi, :, :], ALU.mult)
                _copy(g, "pool", kz[lo:hi, par::2, :], kT[lo:hi, :, ts])
            ps = _psum(g)
            _mm(g, ps[:, 0:NH], Mz, la)
            _mm(g, ps[:, NH:2 * NH], Sz, la)
            _act(g, egk, ps[:, 0:2 * NH], AF.Exp)
            yield
            _tt(g, "dve", bg, beta, eg, ALU.mult)
            kt3 = Kt[:, t, :].rearrange("p (h d) -> p h d", d=64)
            _tt(g, "pool", kend[:], kt3, ekend.unsqueeze(2).to_broadcast([128, NH, 64]), ALU.mult)
            _tt(g, "pool", kend2[:], kend[:].rearrange("p h d -> p (h d)").unsqueeze(1).to_broadcast([128, 2, KW]), CI4, ALU.mult)
            _tt(g, "pool", kbg[:], kt3, bg.unsqueeze(2).to_broadcast([128, NH, 64]), ALU.mult)
            _tt(g, "pool", v3(vb[:]), v3(Vt[:, t, :]), beta.unsqueeze(2).to_broadcast([128, NH, 128]), ALU.mult)
            la_bc = la.unsqueeze(2).to_broadcast([128, NH, 128])
            _tt(g, "pool", rhsd[:], Mz_bc, la_bc, ALU.mult)
            yield
            ps = _psum(g)
            _mm(g, ps[:, 0:W], Sz, rhsd[:].rearrange("p h c -> p (h c)"))
            _act(g, dec[:].rearrange("p h c -> p (h c)"), ps[:, 0:W], AF.Exp)
            yield
            ps = _psum(g)
            for h in range(NH):
                _mm(g, ps[:, h * 128:(h + 1) * 128], kz[:, h, :], qT[:, h // 2, ts])
            _tt(g, "dve", tG[:], v3(ps[:, 0:W]), Mz_bc, ALU.mult)
            yield
            _tt(g, "dve", qkm[:], tG[:], dec[:], ALU.mult)
            _tt(g, "pool", rhsd[:], Sz_bc, la_bc, ALU.mult)
            yield
            ps = _psum(g)
            _mm(g, ps[:, 0:W], Mz, rhsd[:].rearrange("p h c -> p (h c)"))
            _act(g, dec[:].rearrange("p h c -> p (h c)"), ps[:, 0:W], AF.Exp)
            yield
            _tt(g, "pool", dec[:], dec[:], beta.unsqueeze(2).to_broadcast([128, NH, 128]), ALU.mult)
            ps = _psum(g)
            for h in range(NH):
                _mm(g, ps[:, h * 128:(h + 1) * 128], kz[:, h, :], kz[:, h, :])
            _tt(g, "dve", tG[:], v3(ps[:, 0:W]), Sz_bc, ALU.mult)
            yield
            M_ = Mp.get()
            _tt(g, "dve", M_[:], tG[:], dec[:], ALU.mult)
            for h in range(NH):
                _tr(g, g.psbf[:, h * 128:(h + 1) * 128], M_[:, h, :], g.ident_bf[:])
            N_ = Np.get()
            _copy(g, "act", N_[:], v3(g.psbf[:, 0:W]))
            yield
            TT = NQ.get()
            Tn = Tnp.get()
            for lv, bsz in enumerate((1, 2, 4, 8, 16, 32)):
                last = (bsz == 32)
                mk_ = g.bmask[:, (lv if z == 0 else 6 + lv), :].unsqueeze(1).to_broadcast([128, NH, 128])
                mkT = g.bmask[:, (6 + lv if z == 0 else lv), :].unsqueeze(1).to_broadcast([128, NH, 128])
                Am = Amp.get()
                _tt(g, "pool", Am[:], M_[:], mkT, ALU.mult)
                if lv == 0 or not last:
                    Nm = Amp.get()
                    _tt(g, "pool", Nm[:], N_[:], mk_, ALU.mult)
                if lv == 0:
                    _tt(g, "dve", TT[:, :, 128:256], ident_bc, Nm[:], ALU.subtract)
                    _tt(g, "dve", Tn[:], ident_bc, Am[:], ALU.subtract)
                    yield
                    continue
                TT2 = NQ.get()
                psY = _psum(g)
                for h in range(NH):
                    _mm(g, psY[:, h * 128:(h + 1) * 128], Am[:, h, :], TT[:, h, 128:256])
                _copy(g, "act", TT[:, :, 0:128], v3(psY[:, 0:W]))
                yield
                if not last:
                    psY2 = _psum(g)
                    for h in range(NH):
                        _mm(g, psY2[:, h * 128:(h + 1) * 128], Nm[:, h, :], Tn[:, h, :])
                    y2 = Y2p.get()
                    _copy(g, "dve", y2[:], v3(psY2[:, 0:W]))
                    yield
                psZ = _psum(g)
                for h in range(NH):
                    _mm(g, psZ[:, h * 128:(h + 1) * 128], Tn[:, h, :], TT[:, h, 0:128])
                _tt(g, "dve", TT2[:, :, 128:256], TT[:, :, 128:256], v3(psZ[:, 0:W]), ALU.subtract)
                yield
                if not last:
                    psZ2 = _psum(g)
                    for h in range(NH):
                        _mm(g, psZ2[:, h * 128:(h + 1) * 128], TT[:, h, 128:256], y2[:, h, :])
                    Tn2 = Tnp.get()
                    _tt(g, "dve", Tn2[:], Tn[:], v3(psZ2[:, 0:W]), ALU.subtract)
                    Tn = Tn2
                    yield
                TT = TT2
            nq = TT
            ps = _psum(g)
            for h in range(NH):
                _mm(g, ps[:, h * 128:(h + 1) * 128], nq[:, h, 128:256], vb[:, h * 128:(h + 1) * 128])
            _copy(g, "act", u[:], ps[:, 0:W])
            yield
            ps = _psum(g)
            for h in range(NH):
                hp, hb = h // 2, (h % 2) * 64
                _mm(g, ps[hb:hb + 64, hp * 128:(hp + 1) * 128], kbg[:, h, :], nq[:, h, 128:256])
            for (lo, hi, par) in halves:
                _copy(g, "act", wT[lo:hi, par::2, :], v3(ps[lo:hi, 0:128 * nhp]))
            yield
            if z == 1 and want_out:
                ps = _psum(g)
                _mmk(g, ps[:, 0:W], [(at[k], wg[:, k, :]) for k in range(KC)])
                _act(g, G2[:], ps[:, 0:W], AF.Silu)
                _tt(g, "pool", v3(G2[:]), v3(G2[:]), onorm[:].unsqueeze(1).to_broadcast([128, NH, 128]), ALU.mult)
                yield
            pso = bank_o
            for c in ((0, 1) if z == 0 else (1, 0)):
                cb = c * 64
                psw = _psum(g)
                for h in range(NH):
                    _mm(g, psw[cb:cb + 64, h * 128:(h + 1) * 128], wT[:, h, cb:cb + 64], Sbf[:, h // 2, :])
                _tt(g, "dve", vnew[cb:cb + 64, :], u[cb:cb + 64, :], psw[cb:cb + 64, 0:W], ALU.subtract)
                yield
                psS = bank_s
                for h in range(NH):
                    hp, hb = h // 2, (h % 2) * 64
                    if want_out:
                        _mm(g, pso[cb:cb + 64, h * 128:(h + 1) * 128], qdecT[:, h, cb:cb + 64], Sbf[:, hp, :], start=True, stop=False)
                        _mm(g, pso[cb:cb + 64, h * 128:(h + 1) * 128], qkm[:, h, cb:cb + 64], vnew[:, h * 128:(h + 1) * 128], start=False, stop=True)
                    _mm(g, psS[hb:hb + 64, hp * 128:(hp + 1) * 128], kend2[:, c, h * 64:(h + 1) * 64], vnew[:, h * 128:(h + 1) * 128])
                for hp in range(nhp):
                    _stt(g, "dve", S[:, hp, :], S[:, hp, :], ge[:, hp, c:c + 1], psS[:, hp * 128:(hp + 1) * 128], ALU.mult, ALU.add)
                _copy(g, "pool", Sbf[:], S[:])
                yield
            if not want_out:
                continue
            if z == 0:
                _copy(g, "act", Of[:, t, :], pso[:, 0:W])
                yield
            else:
                _tt(g, "dve", ob[:], pso[:, 0:W], Of[:, t, :], ALU.add)
                _tt(g, "pool", sqo[:], ob[:], ob[:], ALU.mult)
                P.op("dve", lambda e: e.tensor_reduce(rs, v3(sqo[:]), axis=AX, op=ALU.add), [sqo[:]], [rs])
                _rstd_from_sum(g, rs, rs, 1.0 / 128, g.eps[:])
                yield
                _tt(g, "dve", v3(ob[:]), v3(ob[:]), rs.unsqueeze(2).to_broadcast([128, NH, 128]), ALU.mult)
                _tt(g, "pool", mtok[:], ob[:], G2[:], ALU.mult)
                for h in range(NH):
                    _tr(g, g.psbf[:, h * 128:(h + 1) * 128], mtok[:, h * 128:(h + 1) * 128], g.ident_bf[:])
                m_ = mt.get()
                _copy(g, "act", m_[:], v3(g.psbf[:, 0:W]))
                P.dma(g.dr["MIX"].rearrange("k p t -> p k t")[:, 2 * q0:2 * q0 + NH, ts], m_[:])
                yield


def make_in_maps(inputs):
    cst, rope, bmask = host_consts()
    maps = []
    for b in range(8):
        m = {
            "x": np.ascontiguousarray(inputs["x"][b], dtype=np.float32),
            "ctx": np.ascontiguousarray(inputs["ctx"][b], dtype=np.float32),
            "cc": np.ascontiguousarray(np.concatenate([np.asarray(inputs["c"][b]).reshape(8, 128),
                                                       np.asarray(inputs["c_ctx"]).reshape(8, 128)], 0), dtype=np.float32),
            "cst": cst, "rope": rope, "bmask": bmask,
        }
        for n in WEIGHT_NAMES:
            m[n] = np.ascontiguousarray(inputs[n], dtype=np.float32)
        maps.append(m)
    return maps


_PROG_CACHE = {}


def kernel(**inputs):
    if "nc" not in _PROG_CACHE:
        import os
        _PROG_CACHE["nc"] = build_program(n_layers=int(os.environ.get("K_LAYERS", DEPTH)))[0]
    nc = _PROG_CACHE["nc"]
    in_maps = make_in_maps(inputs)
    res = run_bass_kernel_spmd(nc, in_maps, core_ids=list(range(8)))
    out = np.stack([np.asarray(res.results[b]["out"], dtype=np.float32) for b in range(8)], 0)
    return out
```

```python
import contextlib
import numpy as np
import concourse.bass as bass
import concourse.mybir as mybir
from concourse.bass_utils import run_bass_kernel_spmd

F32 = mybir.dt.float32
BF16 = mybir.dt.bfloat16
AF = mybir.ActivationFunctionType
ALU = mybir.AluOpType

SAME_ENGINE_SYNC = True
SEM_ROTATE = 30000
DBG_SKIP = set()
USE_AB32 = False
DBG_STOP = None


class _Stop(Exception):
    pass


def _chk(name):
    if DBG_STOP == name:
        raise _Stop()


class _Eng:
    def __init__(self, name, handle, is_dma_only=False):
        self.name = name
        self.h = handle
        self.sem = None
        self.count = 0
        self.waited = {}
        self.pending = False


class Prog:
    def __init__(self, nc, es, n_dma_sems=40):
        self.nc = nc
        self.es = es
        self.engs = {
            "pe": _Eng("pe", nc.tensor),
            "act": _Eng("act", nc.scalar),
            "dve": _Eng("dve", nc.vector),
            "pool": _Eng("pool", nc.gpsimd),
            "sp": _Eng("sp", nc.sync),
        }
        self.semid = 0
        for e in self.engs.values():
            e.sem = self._newsem()
        self.dma_sems = [[self._newsem(), 0] for _ in range(n_dma_sems)]
        self.dma_rr = 0
        self.recs = {}
        self.snap = {}
        self.n_inst = 0
        self.n_wait = 0
        self.out_tokens = []

    def _newsem(self):
        self.semid += 1
        return self.es.enter_context(self.nc.semaphore("s%d" % self.semid))

    @staticmethod
    def _box(ap):
        t = ap.tensor
        name = t.name
        shape = tuple(t.shape)
        pat = ap.ap
        off = ap.offset
        sp = str(ap.space) if not isinstance(ap.space, str) else ap.space
        if "DRAM" in sp.upper() or "HBM" in sp.upper() or "Dram" in sp:
            W = shape[-1]
            r0, c0 = off // W, off % W
            r1, c1 = r0, c0
            ok = True
            for (s, c) in pat:
                s = abs(s)
                if c <= 1:
                    continue
                if s % W == 0:
                    r1 += (c - 1) * (s // W)
                elif s * (c - 1) < W:
                    c1 += (c - 1) * s
                else:
                    ok = False
            if (not ok) or c1 >= W:
                ext = 1
                for (s, c) in pat:
                    ext += (c - 1) * abs(s)
                return (name, 0, 1 << 30, 0, 1 << 30) if True else None
            return (name, r0, r1 + 1, c0, c1 + 1)
        fsz = 1
        for s in shape[1:]:
            fsz *= s
        pstep, pcnt = pat[0]
        if pstep != fsz and pcnt > 1:
            return (name, 0, 128, 0, fsz)
        p0 = off // fsz
        f0 = off % fsz
        ext = 1
        for (s, c) in pat[1:]:
            ext += (c - 1) * abs(s)
        if name.startswith("ps"):
            return (name, (p0 // 32) * 32, ((p0 + pcnt + 31) // 32) * 32, 0, fsz)
        return (name, p0, p0 + pcnt, f0, f0 + ext)

    def _deps(self, eng, reads, writes, is_dma):
        deps = {}
        boxes_r = [self._box(a) for a in reads]
        boxes_w = [self._box(a) for a in writes]
        for kind, boxes in (("r", boxes_r), ("w", boxes_w)):
            for b in boxes:
                lst = self.recs.get(b[0])
                if not lst:
                    continue
                for rec in lst:
                    if kind == "r" and rec[4] == "r":
                        continue
                    if rec[0] >= b[2] or b[1] >= rec[1] or rec[2] >= b[4] or b[3] >= rec[3]:
                        continue
                    if (not is_dma) and (not rec[8]) and rec[7] == eng.name and (eng.name == "pe" or not SAME_ENGINE_SYNC):
                        continue
                    s, v = rec[5], rec[6]
                    k = id(s)
                    if k not in deps or deps[k][1] < v:
                        deps[k] = (s, v)
        return deps, boxes_r, boxes_w

    def _record(self, eng, boxes_r, boxes_w, sem, val, is_dma):
        for b in boxes_w:
            lst = self.recs.setdefault(b[0], [])
            lst[:] = [r for r in lst if not (b[1] <= r[0] and r[1] <= b[2] and b[3] <= r[2] and r[3] <= b[4])]
            lst.append([b[1], b[2], b[3], b[4], "w", sem, val, eng.name, is_dma])
        for b in boxes_r:
            lst = self.recs.setdefault(b[0], [])
            hit = False
            if not is_dma:
                for r in lst:
                    if r[4] == "r" and r[7] == eng.name and not r[8] and r[0] == b[1] and r[1] == b[2] and r[2] == b[3] and r[3] == b[4]:
                        r[5], r[6] = sem, val
                        hit = True
                        break
            if not hit:
                lst.append([b[1], b[2], b[3], b[4], "r", sem, val, eng.name, is_dma])
                if len(lst) > 96:
                    self._compact(lst)

    @staticmethod
    def _compact(lst):
        keep = [r for r in lst if r[4] == "w"]
        merged = {}
        for r in lst:
            if r[4] != "r":
                continue
            k = (r[7], id(r[5]), r[8])
            m = merged.get(k)
            if m is None:
                merged[k] = list(r)
            else:
                m[0] = min(m[0], r[0]); m[1] = max(m[1], r[1]); m[2] = min(m[2], r[2]); m[3] = max(m[3], r[3])
                m[6] = max(m[6], r[6])
        lst[:] = keep + list(merged.values())

    def _emit_waits(self, eng, deps):
        for (s, v) in sorted(deps.values(), key=lambda sv: -sv[1]):
            k = id(s)
            if eng.waited.get(k, 0) >= v:
                continue
            eng.h.wait_ge(s, v)
            eng.waited[k] = v
            self.n_wait += 1
            sn = self.snap.get((k, v))
            if sn:
                w = eng.waited
                for k2, v2 in sn.items():
                    if w.get(k2, 0) < v2:
                        w[k2] = v2

    def op(self, en, fn, reads=(), writes=(), inc=True):
        eng = self.engs[en]
        deps, br, bw = self._deps(eng, reads, writes, False)
        self._emit_waits(eng, deps)
        ins = fn(eng.h)
        self.n_inst += 1
        if inc:
            if eng.count >= SEM_ROTATE and not eng.pending:
                eng.sem = self._newsem()
                eng.count = 0
            eng.count += 1
            ins.then_inc(eng.sem, 1)
            tok = (eng.sem, eng.count)
            eng.pending = False
            self.snap[(id(eng.sem), eng.count)] = dict(eng.waited)
        else:
            tok = (eng.sem, eng.count + 1)
            eng.pending = True
        self._record(eng, br, bw, tok[0], tok[1], False)
        return ins

    def dma(self, out, in_, q="sp", is_output=False, **kw):
        eng = self.engs[q]
        deps, br, bw = self._deps(eng, [in_], [out], True)
        slot = self.dma_sems[self.dma_rr]
        self.dma_rr = (self.dma_rr + 1) % len(self.dma_sems)
        if slot[1] > 0:
            deps[id(slot[0])] = (slot[0], slot[1])
        self._emit_waits(eng, deps)
        slot[1] += 16
        eng.h.dma_start(out=out, in_=in_, **kw).then_inc(slot[0], 16)
        self.snap[(id(slot[0]), slot[1])] = dict(eng.waited)
        self.n_inst += 1
        self._record(eng, br, bw, slot[0], slot[1], True)
        if is_output:
            self.out_tokens.append((slot[0], slot[1]))

    def finish(self):
        eng = self.engs["sp"]
        deps = {}
        for slot in self.dma_sems:
            if slot[1] > 0:
                deps[id(slot[0])] = (slot[0], slot[1])
        self._emit_waits(eng, deps)

    def barrier(self):
        deps = {}
        for e in self.engs.values():
            if e.count > 0:
                deps[id(e.sem)] = (e.sem, e.count)
        for slot in self.dma_sems:
            if slot[1] > 0:
                deps[id(slot[0])] = (slot[0], slot[1])
        for e in self.engs.values():
            d = {k: v for k, v in deps.items() if k != id(e.sem)}
            self._emit_waits(e, d)
        self.recs = {}


D = 1024
KC = 8
SEQ = 2048
NCTX = 256
TOK = NCTX + SEQ
NT = TOK // 128
DEPTH = 4
DFF = 4096
EPS = 1e-6
BLOCKS = [(0, 256), (256, 512), (768, 512), (1280, 512), (1792, 512)]
EVEN_IN = 2240
ODD_IN = 3104
AX = mybir.AxisListType.X

WEIGHT_NAMES = ["ada_w", "ada_b", "norm1_w", "norm2_w", "mlp_w1", "mlp_w2", "even_w_in", "mla_q_norm", "mla_w_uq",
                "mla_kv_norm", "mla_w_ukv", "gla_gate_up", "gla_gate_bias", "gla_o_norm", "even_w_out", "gdn_w_in",
                "gdn_conv_w", "gdn_a_log", "gdn_dt_bias", "gdn_o_norm", "gdn_w_out", "final_norm"]
WEIGHT_SHAPES = {
    "ada_w": [4, 1024, 6144], "ada_b": [4, 6144], "norm1_w": [4, 1024], "norm2_w": [4, 1024],
    "mlp_w1": [4, 1024, 4096], "mlp_w2": [4, 4096, 1024], "even_w_in": [2, 1024, 2240], "mla_q_norm": [2, 384],
    "mla_w_uq": [2, 384, 768], "mla_kv_norm": [2, 256], "mla_w_ukv": [2, 256, 1024], "gla_gate_up": [2, 2, 16, 256],
    "gla_gate_bias": [2, 2, 256], "gla_o_norm": [2, 128], "even_w_out": [2, 1024, 1024], "gdn_w_in": [2, 1024, 3104],
    "gdn_conv_w": [2, 3, 2048], "gdn_a_log": [2, 2, 8], "gdn_dt_bias": [2, 2, 8], "gdn_o_norm": [2, 128],
    "gdn_w_out": [2, 1024, 1024], "final_norm": [1024],
}


def host_consts():
    i = np.arange(128)
    same = (i[:, None] // 64) == (i[None, :] // 64)
    Mf = (same & (i[:, None] <= i[None, :])).astype(np.float32)
    Mb = (same & (i[:, None] >= i[None, :])).astype(np.float32)
    Sf = (same & (i[:, None] > i[None, :])).astype(np.float32)
    Sb = (same & (i[:, None] < i[None, :])).astype(np.float32)
    CI = np.stack([(i < 64), (i >= 64)], 1).astype(np.float32)
    ident = np.eye(128, dtype=np.float32)
    cols = [ident, Mf, Mb, Sf, Sb, CI,
            np.concatenate([Mf, CI], 1) / -16.0, np.concatenate([Mb, CI], 1) / -16.0, Sf / -16.0, Sb / -16.0,
            np.concatenate([Mf, CI], 1), np.concatenate([Mb, CI], 1)]
    cst = np.concatenate(cols, 1).astype(np.float32)
    rows = SEQ // 64
    row = np.repeat(np.arange(rows, dtype=np.float32), 64)
    col = np.tile(np.arange(64, dtype=np.float32), rows)
    inv = (10000.0 ** (-np.arange(0, 16, 2, dtype=np.float32) / 16)).astype(np.float32)
    ang = np.concatenate([row[:, None] * inv, col[:, None] * inv], -1).astype(np.float32)
    cos, sin = np.cos(ang).astype(np.float32), np.sin(ang).astype(np.float32)
    C = np.zeros((32, SEQ), np.float32)
    S = np.zeros((32, SEQ), np.float32)
    for ax in range(2):
        for half in range(2):
            for f in range(8):
                d = ax * 16 + half * 8 + f
                C[d] = cos[:, ax * 8 + f]
                S[d] = (-sin[:, ax * 8 + f]) if half == 0 else sin[:, ax * 8 + f]
    rope = np.stack([C, S], 0)
    r = np.arange(128)
    ms = []
    for b in (1, 2, 4, 8, 16, 32):
        same2 = (r[:, None] // (2 * b)) == (r[None, :] // (2 * b))
        ur = same2 & ((r[:, None] % (2 * b)) < b) & ((r[None, :] % (2 * b)) >= b)
        ms.append(ur.astype(np.float32))
    bmask = np.concatenate(ms + [m.T for m in ms], 1).astype(np.float32)
    return cst, rope, bmask


CST_OFF = {}
_o = 0
for _n, _w in [("ident", 128), ("Mf", 128), ("Mb", 128), ("Sf", 128), ("Sb", 128), ("CI", 2), ("glaRf", 130), ("glaRb", 130),
               ("glaAf", 128), ("glaAb", 128), ("gdnRf", 130), ("gdnRb", 130)]:
    CST_OFF[_n] = (_o, _w)
    _o += _w
CST_W = _o


class Ctx:
    pass


def build_program(n_layers=DEPTH, fake_mixer=False, debug_h=False):
    nc = bass.Bass("TRN2", target_bir_lowering=False)
    g = Ctx()
    g.nc = nc
    dr = {}
    dr["x"] = nc.dram_tensor("x", [SEQ, D], F32, kind="ExternalInput").ap()
    dr["ctx"] = nc.dram_tensor("ctx", [NCTX, D], F32, kind="ExternalInput").ap()
    dr["cc"] = nc.dram_tensor("cc", [16, 128], F32, kind="ExternalInput").ap()
    dr["cst"] = nc.dram_tensor("cst", [128, CST_W], F32, kind="ExternalInput").ap()
    dr["rope"] = nc.dram_tensor("rope", [2, 32, SEQ], F32, kind="ExternalInput").ap()
    dr["bmask"] = nc.dram_tensor("bmask", [128, 12 * 128], F32, kind="ExternalInput").ap()
    for n in WEIGHT_NAMES:
        dr[n] = nc.dram_tensor(n, WEIGHT_SHAPES[n], F32, kind="ExternalInput").ap()
    dr["out"] = nc.dram_tensor("out", [SEQ, D], F32, kind="ExternalOutput").ap()
    dr["H"] = nc.dram_tensor("Hs", [KC, 128, TOK], F32).ap()
    dr["MIX"] = nc.dram_tensor("MIXs", [KC, 128, TOK], BF16).ap()
    if debug_h:
        dr["dbg"] = nc.dram_tensor("dbg", [KC, 128, TOK], F32, kind="ExternalOutput").ap()
    g.dr = dr
    es = contextlib.ExitStack()
    with es:
        P = Prog(nc, es)
        g.P = P
        g.es = es
        g.ps = [es.enter_context(nc.psum_tensor("psb%d" % i, [128, 512], F32)) for i in range(7)]
        g.psbf = es.enter_context(nc.psum_tensor("psbf", [128, 1024], BF16))
        g.ps_rr = 0
        _emit(g, n_layers, fake_mixer, debug_h)
        P.finish()
        g.stats = (P.n_inst, P.n_wait, P.semid)
    return nc, g


_UNIQ = [0]


def _sb(g, name, shape, dt=F32, stack=None):
    _UNIQ[0] += 1
    return (stack or g.es).enter_context(g.nc.sbuf_tensor("sb_%s_%d" % (name, _UNIQ[0]), shape, dt))


def _psum(g, kind="rot"):
    if kind == "rot":
        t = g.ps[g.ps_rr]
        g.ps_rr = (g.ps_rr + 1) % getattr(g, "rot_n", 4)
        return t
    if kind == "acc":
        g.ps_acc = 1 - getattr(g, "ps_acc", 0)
        return g.ps[4 + g.ps_acc]
    return g.ps[6]


def _mm(g, out, lhsT, rhs, start=True, stop=True, inc=True):
    g.P.op("pe", lambda e: e.matmul(out, lhsT, rhs, start=start, stop=stop), [lhsT, rhs], [out], inc=inc)


def _mmk(g, out, pairs):
    n = len(pairs)
    for i, (l, r) in enumerate(pairs):
        _mm(g, out, l, r, start=(i == 0), stop=(i == n - 1), inc=(i == n - 1))


def _tr(g, out, in_, ident):
    g.P.op("pe", lambda e: e.transpose(out, in_, ident), [in_, ident], [out])


def _act(g, out, in_, func, scale=1.0, bias=0.0, extra_reads=()):
    rd = [in_] + [a for a in (scale, bias) if not isinstance(a, (int, float))] + list(extra_reads)
    g.P.op("act", lambda e: e.activation(out=out, in_=in_, func=func, bias=bias, scale=scale), rd, [out])


def _tt(g, en, out, in0, in1, op):
    g.P.op(en, lambda e: e.tensor_tensor(out, in0, in1, op=op), [in0, in1], [out])


def _stt(g, en, out, in0, scalar, in1, op0, op1):
    rd = [in0, in1] + ([scalar] if not isinstance(scalar, (int, float)) else [])
    g.P.op(en, lambda e: e.scalar_tensor_tensor(out=out, in0=in0, scalar=scalar, in1=in1, op0=op0, op1=op1), rd, [out])


def _ts(g, en, out, in0, s1, s2, op0, op1=None):
    rd = [in0] + [a for a in (s1, s2) if a is not None and not isinstance(a, (int, float))]
    if op1 is None:
        g.P.op(en, lambda e: e.tensor_scalar(out, in0, s1, None, op0=op0), rd, [out])
    else:
        g.P.op(en, lambda e: e.tensor_scalar(out, in0, s1, s2, op0=op0, op1=op1), rd, [out])


def _copy(g, en, out, in_):
    if en == "act":
        g.P.op("act", lambda e: e.copy(out, in_), [in_], [out])
    else:
        g.P.op(en, lambda e: e.tensor_copy(out, in_), [in_], [out])


def _memset(g, en, out, val):
    g.P.op(en, lambda e: e.memset(out, val), [], [out])


def _load_cast(g, dst, src):
    shp = list(dst.shape)
    if len(shp) == 2:
        pieces = [(dst[:, c:min(c + 1024, shp[1])], src[:, c:min(c + 1024, shp[1])]) for c in range(0, shp[1], 1024)]
    else:
        inner = shp[2]
        assert len(shp) == 3 and inner <= 1024
        step = max(1, 1024 // inner)
        pieces = [(dst[:, a:min(a + step, shp[1]), :], src[:, a:min(a + step, shp[1]), :]) for a in range(0, shp[1], step)]
    for d, s_ in pieces:
        stg = g.stage.get()
        n = 1
        for x in d.shape[1:]:
            n *= x
        if len(d.shape) == 3:
            v = stg[:, 0:n].rearrange("p (a b) -> p a b", b=d.shape[2])
        else:
            v = stg[:, 0:n]
        g.P.dma(v, s_)
        g.cast_rr = 1 - getattr(g, "cast_rr", 0)
        _copy(g, "pool" if g.cast_rr else "act", d, v)


def _rstd_from_sum(g, out_sb, ps_in, inv_n, eps_col):
    _act(g, out_sb, ps_in, AF.Sqrt, scale=inv_n, bias=eps_col)
    g.P.op("dve", lambda e: e.reciprocal(out_sb, out_sb), [out_sb], [out_sb])


class Pool_:
    def __init__(self, g, name, shape, dt, n, stack=None, zero=False):
        self.tiles = [_sb(g, "%s%d" % (name, i), shape, dt, stack) for i in range(n)]
        self.i = 0
        if zero:
            for t in self.tiles:
                _memset(g, "pool", t[:], 0.0)

    def get(self):
        t = self.tiles[self.i]
        self.i = (self.i + 1) % len(self.tiles)
        return t


def _cst(g, name):
    o, w = CST_OFF[name]
    return g.cst[:, o:o + w]


def _setup(g):
    P, dr = g.P, g.dr
    g.cst = _sb(g, "cst", [128, CST_W])
    P.dma(g.cst[:], dr["cst"])
    g.ident = _cst(g, "ident")
    g.ident_bf = _sb(g, "ident_bf", [128, 128], BF16)
    _copy(g, "dve", g.ident_bf[:], g.ident)
    g.ones_bf = _sb(g, "ones_bf", [128, 128], BF16)
    _memset(g, "pool", g.ones_bf[:], 1.0)
    g.ones_f = _sb(g, "ones_f", [128, 128])
    _memset(g, "pool", g.ones_f[:], 1.0)
    g.eps = _sb(g, "eps", [128, 1])
    _memset(g, "pool", g.eps[:], EPS)
    rows = []
    rows.append(("cc", dr["cc"], 16))
    for L in range(DEPTH):
        rows.append(("ada_b%d" % L, dr["ada_b"][L].rearrange("(r c) -> r c", c=128), 48))
        rows.append(("n1w%d" % L, dr["norm1_w"][L].rearrange("(r c) -> r c", c=128), 8))
        rows.append(("n2w%d" % L, dr["norm2_w"][L].rearrange("(r c) -> r c", c=128), 8))
    for i in range(2):
        rows.append(("qn%d" % i, dr["mla_q_norm"][i].rearrange("(r c) -> r c", c=128), 3))
        rows.append(("kvn%d" % i, dr["mla_kv_norm"][i].rearrange("(r c) -> r c", c=128), 2))
        rows.append(("conv%d" % i, dr["gdn_conv_w"][i].rearrange("t (r c) -> (t r) c", c=128), 48))
    rows.append(("fn", dr["final_norm"].rearrange("(r c) -> r c", c=128), 8))
    tot = sum(r[2] for r in rows)
    ntile = (tot + 127) // 128
    g.cols = _sb(g, "cols", [128, ntile * 128])
    g.coloff = {}
    with contextlib.ExitStack() as st:
        rt = [_sb(g, "rowst%d" % i, [128, 128], F32, st) for i in range(ntile)]
        for t in rt:
            _memset(g, "dve", t[:], 0.0)
        r = 0
        for key, ap, n in rows:
            g.coloff[key] = r
            done = 0
            while done < n:
                t, p = divmod(r + done, 128)
                m = min(n - done, 128 - p)
                P.dma(rt[t][p:p + m, :], ap[done:done + m, :])
                done += m
            r += n
        for t in range(ntile):
            ps = _psum(g)
            _tr(g, ps[:, 0:128], rt[t][:], g.ident)
            _copy(g, "dve", g.cols[:, t * 128:(t + 1) * 128], ps[:, 0:128])
        P.barrier()
    g.sc = _sb(g, "sc", [128, KC, 2])
    _act(g, g.sc[:, :, 0], g.cols[:, 0:8], AF.Silu)
    _act(g, g.sc[:, :, 1], g.cols[:, 8:16], AF.Silu)
    g.mod = [_sb(g, "mod%d" % L, [128, 48, 2]) for L in range(DEPTH)]
    g.comb1 = [_sb(g, "comb1_%d" % L, [128, KC, 2]) for L in range(DEPTH)]
    g.comb2 = [_sb(g, "comb2_%d" % L, [128, KC, 2]) for L in range(DEPTH)]
    g.adaw_pool = Pool_(g, "adaw", [128, KC, 128], F32, 2)
    g.stage = Pool_(g, "stage", [128, 1024], F32, 3)


def _col(g, key, n):
    o = g.coloff[key]
    return g.cols[:, o:o + n]


def _adaln_steps(g, L):
    P, dr = g.P, g.dr
    src = dr["ada_w"][L].rearrange("(k p) n -> p k n", p=128)
    for cb in range(48):
        wt = g.adaw_pool.get()
        P.dma(wt[:], src[:, :, cb * 128:(cb + 1) * 128])
        ps = _psum(g)
        _mmk(g, ps[:, 0:2], [(wt[:, k, :], g.sc[:, k, :]) for k in range(KC)])
        bias = _col(g, "ada_b%d" % L, 48)[:, cb:cb + 1]
        _tt(g, "dve", g.mod[L][:, cb, :], ps[:, 0:2], bias.to_broadcast([128, 2]), ALU.add)
        if cb == 47:
            n1 = _col(g, "n1w%d" % L, 8).unsqueeze(2).to_broadcast([128, KC, 2])
            n2 = _col(g, "n2w%d" % L, 8).unsqueeze(2).to_broadcast([128, KC, 2])
            _stt(g, "dve", g.comb1[L][:], g.mod[L][:, 8:16, :], 1.0, n1, ALU.add, ALU.mult)
            _stt(g, "dve", g.comb2[L][:], g.mod[L][:, 32:40, :], 1.0, n2, ALU.add, ALU.mult)
        yield


def _mix_view(g, s0, n):
    return g.dr["MIX"].rearrange("k p t -> p k t")[:, :, s0:s0 + n]


def _h_view(g, s0, n):
    return g.dr["H"].rearrange("k p t -> p k t")[:, :, s0:s0 + n]


def _load_inputs(g):
    P, dr = g.P, g.dr
    with contextlib.ExitStack() as st:
        xin = Pool_(g, "xin", [128, D], F32, 2, st)
        hb = Pool_(g, "hb0_", [128, KC, 128], F32, 2, st)
        for t in range(NT):
            src = dr["ctx"][t * 128:(t + 1) * 128, :] if t < 2 else dr["x"][(t - 2) * 128:(t - 1) * 128, :]
            xt = xin.get()
            P.dma(xt[:], src)
            ht = hb.get()
            for half in range(2):
                ps = _psum(g)
                for q in range(4):
                    k = half * 4 + q
                    _tr(g, ps[:, q * 128:(q + 1) * 128], xt[:, k * 128:(k + 1) * 128], g.ident)
                _copy(g, "act" if half == 0 else "dve", ht[:, half * 4:(half + 1) * 4, :],
                      ps[:].rearrange("p (q t) -> p q t", t=128))
            P.dma(_h_view(g, t * 128, 128), ht[:])
        P.barrier()


def _norm_tmps(g, st):
    return (Pool_(g, "sqn", [128, 512], BF16, 2, st), Pool_(g, "tmn", [128, 512], F32, 3, st), _sb(g, "rstdn", [128, 512], F32, st))


def _norm_mod(g, hblk, N, comb, shift, out_bf, tmps, ab=None):
    sqp, tp, rstd = tmps
    ps = _psum(g)
    for k in range(KC):
        sq = sqp.get()
        _act(g, sq[:, :N], hblk[:, k, :N], AF.Square)
        _mm(g, ps[:, :N], g.ones_bf[:], sq[:, :N], start=(k == 0), stop=(k == KC - 1))
    _rstd_from_sum(g, rstd[:, :N], ps[:, :N], 1.0 / D, g.eps[:])
    if ab is not None:
        w32, AB, t0, a32p = ab
        psab = _psum(g, "misc")
    for k in range(KC):
        t = tp.get()
        _stt(g, "dve", t[:, :N], hblk[:, k, :N], comb[:, k:k + 1], rstd[:, :N], ALU.mult, ALU.mult)
        if ab is None:
            _act(g, out_bf[:, k, :N], t[:, :N], AF.Identity, scale=1.0, bias=shift[:, k:k + 1])
        else:
            a32 = a32p.get()
            _act(g, a32[:, :N], t[:, :N], AF.Identity, scale=1.0, bias=shift[:, k:k + 1])
            _copy(g, "pool", out_bf[:, k, :N], a32[:, :N])
            for tt in range(N // 128):
                _mm(g, psab[:, tt * 32:(tt + 1) * 32], a32[:, tt * 128:(tt + 1) * 128], w32[:, k, :], start=(k == 0 and tt == 0), stop=(k == KC - 1 and tt == N // 128 - 1))
    if ab is not None:
        nt = N // 128
        _copy(g, "act", AB[:, t0:t0 + nt, :], psab[:, 0:nt * 32].rearrange("p (t c) -> p t c", c=32))


def _emit(g, n_layers, fake_mixer, debug_h):
    P, dr = g.P, g.dr
    _setup(g)
    ada = _adaln_steps(g, 0)
    for _ in ada:
        pass
    _load_inputs(g)
    for L in range(n_layers):
        last = (L == DEPTH - 1)
        need_ctx = not last
        nxt = _adaln_steps(g, L + 1) if L + 1 < n_layers else iter(())
        with contextlib.ExitStack() as lstA:
            lat = _mla_alloc(g, lstA) if (L % 2 == 0 and not fake_mixer) else None
            with contextlib.ExitStack() as lst:
                aT = _sb(g, "aT", [128, KC, TOK], BF16, lst)
                AB = None
                if L % 2 == 1 and not fake_mixer and USE_AB32:
                    AB = _sb(g, "AB", [128, NT, 32], F32, lst)
                with contextlib.ExitStack() as st:
                    hbp = Pool_(g, "hbn", [128, KC, 512], F32, 2, st)
                    tmps = _norm_tmps(g, st)
                    ab = None
                    if AB is not None:
                        w32 = _sb(g, "wab32", [128, KC, 32], F32, st)
                        P.dma(w32[:], dr["gdn_w_in"][L // 2].rearrange("(k p) n -> p k n", p=128)[:, :, O_A:O_A + 32])
                        a32p = Pool_(g, "a32", [128, 512], F32, 2, st)
                    for bi, (s0, N) in enumerate(BLOCKS):
                        s = 1 if bi == 0 else 0
                        hb = hbp.get()
                        P.dma(hb[:, :, :N], _h_view(g, s0, N))
                        if AB is not None:
                            ab = (w32, AB, s0 // 128, a32p)
                        _norm_mod(g, hb, N, g.comb1[L][:, :, s], g.mod[L][:, 0:8, s], aT[:, :, s0:s0 + N], tmps, ab)
                    P.barrier()
                if fake_mixer:
                    for (s0, N) in BLOCKS:
                        P.dma(_mix_view(g, s0, N), aT[:, :, s0:s0 + N])
                elif L % 2 == 0:
                    _even_mixer_a(g, L, aT, lat, lst)
                else:
                    _odd_mixer(g, L, aT, need_ctx, AB)
                P.barrier()
            if L % 2 == 0 and not fake_mixer and "attn" not in DBG_SKIP:
                _mla_attention(g, L, lat)
                P.barrier()
        wname = "even_w_out" if L % 2 == 0 else "gdn_w_out"
        with contextlib.ExitStack() as st:
            wo = _sb(g, "wo", [128, KC, D], BF16, st)
            _load_cast(g, wo[:], dr[wname][L // 2].rearrange("(k p) n -> p k n", p=128))
            hbp = Pool_(g, "hba", [128, KC, 512], F32, 2, st)
            mxp = Pool_(g, "mxb", [128, KC, 512], BF16, 2, st)
            for bi, (s0, N) in enumerate(BLOCKS):
                if bi == 0 and not need_ctx:
                    continue
                s = 1 if bi == 0 else 0
                hb = hbp.get()
                P.dma(hb[:, :, :N], _h_view(g, s0, N))
                mx = mxp.get()
                P.dma(mx[:, :, :N], _mix_view(g, s0, N))
                for m in range(KC):
                    ps = _psum(g)
                    _mmk(g, ps[:, :N], [(wo[:, k, m * 128:(m + 1) * 128], mx[:, k, :N]) for k in range(KC)])
                    _stt(g, "dve", hb[:, m, :N], ps[:, :N], g.mod[L][:, 16 + m, s:s + 1], hb[:, m, :N], ALU.mult, ALU.add)
                P.dma(_h_view(g, s0, N), hb[:, :, :N])
                for _ in range(5):
                    next(nxt, None)
            P.barrier()
        if debug_h == ("mix", L):
            _dump_h(g)
            return
        with contextlib.ExitStack() as st:
            w2 = _sb(g, "w2", [128, 32, D], BF16, st)
            w2src = dr["mlp_w2"][L].rearrange("(f p) n -> p f n", p=128)
            _load_cast(g, w2[:], w2src)
            w1p = Pool_(g, "w1p", [128, KC, 512], BF16, 2, st)
            w1src = dr["mlp_w1"][L].rearrange("(k p) n -> p k n", p=128)
            hbp = Pool_(g, "hbm", [128, KC, 512], F32, 1, st)
            a2 = _sb(g, "a2", [128, KC, 512], BF16, st)
            h1 = _sb(g, "h1", [128, 32, 512], BF16, st)
            rl = Pool_(g, "rl", [128, 512], F32, 3, st)
            tmps = _norm_tmps(g, st)
            for bi, (s0, N) in enumerate(BLOCKS):
                if bi == 0 and not need_ctx:
                    continue
                s = 1 if bi == 0 else 0
                hb = hbp.get()
                P.dma(hb[:, :, :N], _h_view(g, s0, N))
                _norm_mod(g, hb, N, g.comb2[L][:, :, s], g.mod[L][:, 24:32, s], a2, tmps)
                for fg in range(8):
                    w1 = w1p.get()
                    _load_cast(g, w1[:], w1src[:, :, fg * 512:(fg + 1) * 512])
                    for f in range(4):
                        ps = _psum(g)
                        _mmk(g, ps[:, :N], [(w1[:, k, f * 128:(f + 1) * 128], a2[:, k, :N]) for k in range(KC)])
                        r = rl.get()
                        _act(g, r[:, :N], ps[:, :N], AF.Relu)
                        _tt(g, "pool", h1[:, fg * 4 + f, :N], r[:, :N], r[:, :N], ALU.mult)
                for m in range(KC):
                    ps = _psum(g)
                    _mmk(g, ps[:, :N], [(w2[:, f, m * 128:(m + 1) * 128], h1[:, f, :N]) for f in range(32)])
                    _stt(g, "dve", hb[:, m, :N], ps[:, :N], g.mod[L][:, 40 + m, s:s + 1], hb[:, m, :N], ALU.mult, ALU.add)
                P.dma(_h_view(g, s0, N), hb[:, :, :N])
                for _ in range(5):
                    next(nxt, None)
            for _ in nxt:
                pass
            P.barrier()
        if debug_h == ("mlp", L):
            _dump_h(g)
            return
    if debug_h:
        _dump_h(g)
        return
    _final(g)


def _dump_h(g):
    P = g.P
    with contextlib.ExitStack() as st:
        hb = Pool_(g, "hbd", [128, KC, 512], F32, 2, st)
        for (s0, N) in BLOCKS:
            t = hb.get()
            P.dma(t[:, :, :N], _h_view(g, s0, N))
            P.dma(g.dr["dbg"].rearrange("k p t -> p k t")[:, :, s0:s0 + N], t[:, :, :N], is_output=True)
        P.dma(g.dr["out"][0:128, :], g.cst[:, 0:D], is_output=True)


def _final(g):
    P, dr = g.P, g.dr
    with contextlib.ExitStack() as st:
        hbp = Pool_(g, "hbf", [128, KC, 512], F32, 2, st)
        sq = _sb(g, "sqf", [128, KC, 512], BF16, st)
        yb = _sb(g, "yf", [128, KC, 512], F32, st)
        rstd = _sb(g, "rstdf", [128, 512], F32, st)
        otp = Pool_(g, "ot", [128, D], F32, 2, st)
        fn = _col(g, "fn", 8)
        for (s0, N) in BLOCKS[1:]:
            hb = hbp.get()
            P.dma(hb[:, :, :N], _h_view(g, s0, N))
            _act(g, sq[:, :, :N], hb[:, :, :N], AF.Square)
            ps = _psum(g)
            _mmk(g, ps[:, :N], [(g.ones_bf[:], sq[:, k, :N]) for k in range(KC)])
            _rstd_from_sum(g, rstd[:, :N], ps[:, :N], 1.0 / D, g.eps[:])
            for k in range(KC):
                _stt(g, "dve", yb[:, k, :N], hb[:, k, :N], fn[:, k:k + 1], rstd[:, :N], ALU.mult, ALU.mult)
            for tt in range(N // 128):
                ot = otp.get()
                for half in range(2):
                    ps = _psum(g)
                    for q in range(4):
                        k = half * 4 + q
                        _tr(g, ps[:, q * 128:(q + 1) * 128], yb[:, k, tt * 128:(tt + 1) * 128], g.ident)
                    _copy(g, "act" if half == 0 else "dve", ot[:, half * 512:(half + 1) * 512], ps[:])
                r0 = s0 - NCTX + tt * 128
                P.dma(dr["out"][r0:r0 + 128, :], ot[:], is_output=True)


C_CQ, C_CKV, C_KPE, C_GQ, C_GK, C_GV, C_GG, C_GLOW = 0, 384, 640, 672, 928, 1184, 1696, 2208
GLA_FWD = list(range(NT))
GLA_BWD = [1, 0] + list(range(NT - 1, 1, -1))


def _mla_alloc(g, st):
    lat = Ctx()
    lat.cqn = _sb(g, "cqn", [128, 3, TOK], BF16, st)
    lat.ckvn = _sb(g, "ckvn", [128, 2, TOK], BF16, st)
    lat.kpeT = _sb(g, "kpeT", [128, TOK], BF16, st)
    return lat


def _rope_tables(g, lat, st):
    lat.ropeC = _sb(g, "ropeC", [128, SEQ], F32, st)
    lat.ropeS = _sb(g, "ropeS", [128, SEQ], F32, st)
    g.P.dma(lat.ropeC[64:96, :], g.dr["rope"][0])
    g.P.dma(lat.ropeS[64:96, :], g.dr["rope"][1])


def _swap_halves(g, en, dst, src):
    d = dst.rearrange("p h (a t f) -> p h a t f", a=2, t=2)
    s_ = src.rearrange("p h (a t f) -> p h a t f", a=2, t=2)
    _copy(g, en, d[:, :, :, 0, :], s_[:, :, :, 1, :])
    _copy(g, en, d[:, :, :, 1, :], s_[:, :, :, 0, :])


def _even_mixer_a(g, L, aT, lat, st):
    P, dr = g.P, g.dr
    i = L // 2
    win = _sb(g, "win", [128, KC, EVEN_IN], BF16, st)
    wsrc = dr["even_w_in"][i].rearrange("(k p) n -> p k n", p=128)
    for k in range(KC):
        _load_cast(g, win[:, k, :], wsrc[:, k, :])
    with contextlib.ExitStack() as s1:
        _rope_tables(g, lat, s1)
        winrot = _sb(g, "winrot", [128, KC, 32], BF16, s1)
        _swap_halves(g, "pool", winrot[:], win[:, :, C_KPE:C_KPE + 32])
        cqf = _sb(g, "cqf", [128, 5, 512], F32, s1)
        sqp = Pool_(g, "sql", [128, 512], BF16, 2, s1)
        rq = _sb(g, "rq", [128, 512], F32, s1)
        rkv = _sb(g, "rkv", [128, 512], F32, s1)
        t1p = Pool_(g, "t1l", [128, 512], F32, 2, s1)
        qn = _col(g, "qn%d" % i, 3)
        kvn = _col(g, "kvn%d" % i, 2)
        for bi, (s0, N) in enumerate(BLOCKS):
            rhs = [aT[:, k, s0:s0 + N] for k in range(KC)]
            pss_q = _psum(g, "acc")
            pss_kv = _psum(g, "acc")
            for fc in range(5):
                ps = _psum(g)
                _mmk(g, ps[:, :N], [(win[:, k, fc * 128:(fc + 1) * 128], rhs[k]) for k in range(KC)])
                _copy(g, "act", cqf[:, fc, :N], ps[:, :N])
                sq = sqp.get()
                _tt(g, "pool", sq[:, :N], cqf[:, fc, :N], cqf[:, fc, :N], ALU.mult)
                if fc < 3:
                    _mm(g, pss_q[:, :N], g.ones_bf[:], sq[:, :N], start=(fc == 0), stop=(fc == 2))
                else:
                    _mm(g, pss_kv[:, :N], g.ones_bf[:], sq[:, :N], start=(fc == 3), stop=(fc == 4))
            _rstd_from_sum(g, rq[:, :N], pss_q[:, :N], 1.0 / 384, g.eps[:])
            _rstd_from_sum(g, rkv[:, :N], pss_kv[:, :N], 1.0 / 256, g.eps[:])
            for fc in range(3):
                _stt(g, "dve", lat.cqn[:, fc, s0:s0 + N], cqf[:, fc, :N], qn[:, fc:fc + 1], rq[:, :N], ALU.mult, ALU.mult)
            for fc in range(2):
                _stt(g, "dve", lat.ckvn[:, fc, s0:s0 + N], cqf[:, 3 + fc, :N], kvn[:, fc:fc + 1], rkv[:, :N], ALU.mult, ALU.mult)
            ps = _psum(g)
            _mmk(g, ps[64:96, :N], [(win[:, k, C_KPE:C_KPE + 32], rhs[k]) for k in range(KC)])
            if bi == 0:
                _copy(g, "act", lat.kpeT[64:96, s0:s0 + N], ps[64:96, :N])
            else:
                ps2 = _psum(g)
                _mmk(g, ps2[64:96, :N], [(winrot[:, k, :], rhs[k]) for k in range(KC)])
                p0 = s0 - NCTX
                t1 = t1p.get()
                t2 = t1p.get()
                _tt(g, "dve", t1[64:96, :N], ps[64:96, :N], lat.ropeC[64:96, p0:p0 + N], ALU.mult)
                _tt(g, "dve", t2[64:96, :N], ps2[64:96, :N], lat.ropeS[64:96, p0:p0 + N], ALU.mult)
                _tt(g, "pool", lat.kpeT[64:96, s0:s0 + N], t1[64:96, :N], t2[64:96, :N], ALU.add)
        P.barrier()
    with contextlib.ExitStack() as s2:
        if "gla" not in DBG_SKIP:
            try:
                _gla(g, i, aT, win, s2)
            except _Stop:
                pass
        P.barrier()


def _gla(g, i, aT, win, st):
    P, dr = g.P, g.dr
    gup = _sb(g, "gup", [16, 2, 256], F32, st)
    P.dma(gup[:], dr["gla_gate_up"][i].rearrange("z r n -> r z n"))
    gbias = _sb(g, "gbias", [128, 2, 256], F32, st)
    P.dma(gbias[:], dr["gla_gate_bias"][i].partition_broadcast(128))
    onorm = _sb(g, "onorm", [128, 128], F32, st)
    P.dma(onorm[:], dr["gla_o_norm"][i].partition_broadcast(128))
    Of = _sb(g, "Of", [128, NT, 512], F32, st)
    S = _sb(g, "S", [128, 2, 128], F32, st)
    Sbf = _sb(g, "Sbf", [128, 2, 128], BF16, st)
    glowT = Pool_(g, "glowT", [16, 128], F32, 2, st)
    tl = Pool_(g, "tl", [128, 256], F32, 2, st)
    Lz = Pool_(g, "Lz", [128, 256], F32, 2, st)
    Eq = Pool_(g, "Eq", [128, 2, 128], F32, 2, st)
    Ek = Pool_(g, "Ek", [128, 2, 128], F32, 2, st)
    gend = Pool_(g, "gend", [128, 2, 2], F32, 2, st)
    kendE = Pool_(g, "kendE", [128, 256], F32, 2, st)
    qdecT = Pool_(g, "qdecT", [128, 4, 128], BF16, 2, st, zero=True)
    kinvT = Pool_(g, "kinvT", [128, 4, 128], BF16, 2, st, zero=True)
    kend = Pool_(g, "kend", [128, 2, 256], BF16, 2, st)
    CI2 = _cst(g, "CI").unsqueeze(2).to_broadcast([128, 2, 256])
    Vg = Pool_(g, "Vg", [128, 512], BF16, 2, st)
    AmT = Pool_(g, "AmT", [128, 4, 128], BF16, 2, st)
    G2 = Pool_(g, "G2", [128, 512], F32, 1, st)
    ob = Pool_(g, "ob", [128, 512], F32, 1, st)
    sqo = Pool_(g, "sqo", [128, 512], F32, 1, st)
    ss4 = Pool_(g, "ss4", [128, 4], F32, 2, st)
    mtok = Pool_(g, "mtok", [128, 512], BF16, 2, st)
    mt = Pool_(g, "mt", [128, 4, 128], BF16, 2, st)
    LN8 = float(np.log(0.125))
    for z in range(2):
        R_ = _cst(g, "glaRf" if z == 0 else "glaRb")
        A_ = _cst(g, "glaAf" if z == 0 else "glaAb")
        Mz = _cst(g, "Mf" if z == 0 else "Mb")
        _memset(g, "pool", S[:], 0.0)
        _memset(g, "pool", Sbf[:], 0.0)
        for t in (GLA_FWD if z == 0 else GLA_BWD):
            ts = slice(t * 128, (t + 1) * 128)
            at = [aT[:, k, ts] for k in range(KC)]
            ps = _psum(g)
            c0 = C_GLOW + 16 * z
            _mmk(g, ps[0:16, 0:128], [(win[:, k, c0:c0 + 16], at[k]) for k in range(KC)])
            gl = glowT.get()
            _copy(g, "act", gl[:], ps[0:16, 0:128])
            _chk("gla_%s_%d" % ("a", z))
            ps = _psum(g)
            _mm(g, ps[:, 0:256], gl[:], gup[:, z, :])
            tt_ = tl.get()
            _tt(g, "dve", tt_[:], ps[:, 0:256], gbias[:, z, :], ALU.add)
            L_ = Lz.get()
            _act(g, tt_[:], tt_[:], AF.Exp, scale=-1.0)
            _act(g, L_[:], tt_[:], AF.Ln, scale=1.0, bias=1.0)
            _chk("gla_%s_%d" % ("b", z))
            ps = _psum(g)
            for hp in range(2):
                _mm(g, ps[:, hp * 130:(hp + 1) * 130], L_[:, hp * 128:(hp + 1) * 128], R_)
            pv = ps[:, 0:260].rearrange("p (h c) -> p h c", c=130)
            eq, ek, ge = Eq.get(), Ek.get(), gend.get()
            _act(g, eq[:], pv[:, :, 0:128], AF.Exp, scale=1.0, bias=LN8)
            _act(g, ek[:], pv[:, :, 0:128], AF.Exp, scale=-1.0)
            _act(g, ge[:], pv[:, :, 128:130], AF.Exp)
            _chk("gla_%s_%d" % ("c", z))
            ps = _psum(g)
            _mm(g, ps[:, 0:256], A_, L_[:])
            ke = kendE.get()
            _act(g, ke[:], ps[:, 0:256], AF.Exp)
            _chk("gla_%s_%d" % ("d", z))
            ps = _psum(g)
            for hp in range(2):
                _mmk(g, ps[:, hp * 128:(hp + 1) * 128], [(win[:, k, C_GQ + hp * 128:C_GQ + (hp + 1) * 128], at[k]) for k in range(KC)])
                _mmk(g, ps[:, 256 + hp * 128:256 + (hp + 1) * 128], [(win[:, k, C_GK + hp * 128:C_GK + (hp + 1) * 128], at[k]) for k in range(KC)])
            qd, ki = qdecT.get(), kinvT.get()
            for (lo, hi, par) in ((0, 64, 0), (64, 128, 1)):
                _tt(g, "dve", qd[lo:hi, par::2, :], ps[lo:hi, 0:256].rearrange("p (h c) -> p h c", c=128), eq[lo:hi, :, :], ALU.mult)
                _tt(g, "dve", ki[lo:hi, par::2, :], ps[lo:hi, 256:512].rearrange("p (h c) -> p h c", c=128), ek[lo:hi, :, :], ALU.mult)
            _chk("gla_%s_%d" % ("e", z))
            ps = _psum(g)
            _mmk(g, ps[:, 0:256], [(at[k], win[:, k, C_GK:C_GK + 256]) for k in range(KC)])
            kn = kend.get()
            _tt(g, "dve", ke[:], ps[:, 0:256], ke[:], ALU.mult)
            _tt(g, "pool", kn[:], ke[:].unsqueeze(1).to_broadcast([128, 2, 256]), CI2, ALU.mult)
            _chk("gla_%s_%d" % ("f", z))
            ps = _psum(g)
            _mmk(g, ps[:, 0:512], [(at[k], win[:, k, C_GV:C_GV + 512]) for k in range(KC)])
            vg = Vg.get()
            _copy(g, "act", vg[:], ps[:, 0:512])
            _chk("gla_%s_%d" % ("g", z))
            ps = _psum(g)
            for h in range(4):
                hp, hb = h // 2, (h % 2) * 64
                _mm(g, ps[:, h * 128:(h + 1) * 128], ki[:, h, :], qd[:, h, :])
            am = AmT.get()
            if "h_dve" not in DBG_SKIP:
                _tt(g, "dve", am[:], ps[:].rearrange("p (h c) -> p h c", c=128), Mz.unsqueeze(1).to_broadcast([128, 4, 128]), ALU.mult)
            else:
                _copy(g, "act", am[:], ps[:].rearrange("p (h c) -> p h c", c=128))
            _chk("gla_%s_%d" % ("h", z))
            if z == 1:
                ps = _psum(g)
                _mmk(g, ps[:, 0:512], [(at[k], win[:, k, C_GG:C_GG + 512]) for k in range(KC)])
                g2 = G2.get()
                _act(g, g2[:], ps[:, 0:512], AF.Silu)
                _tt(g, "pool", g2[:].rearrange("p (h c) -> p h c", c=128), g2[:].rearrange("p (h c) -> p h c", c=128),
                    onorm[:].unsqueeze(1).to_broadcast([128, 4, 128]), ALU.mult)
            _chk("gla_%s_%d" % ("i", z))
            pso = _psum(g, "acc")
            for c in ((0, 1) if z == 0 else (1, 0)):
                cb = c * 64
                psS = _psum(g, "misc")
                for h in range(4):
                    hp, hb = h // 2, (h % 2) * 64
                    _mm(g, pso[cb:cb + 64, h * 128:(h + 1) * 128], qd[:, h, cb:cb + 64], Sbf[:, hp, :], start=True, stop=False)
                    _mm(g, pso[cb:cb + 64, h * 128:(h + 1) * 128], am[:, h, cb:cb + 64], vg[:, h * 128:(h + 1) * 128], start=False, stop=True)
                    _mm(g, psS[hb:hb + 64, hp * 128:(hp + 1) * 128], kn[:, c, h * 64:(h + 1) * 64], vg[:, h * 128:(h + 1) * 128])
                for hp in range(2):
                    _stt(g, "dve", S[:, hp, :], S[:, hp, :], ge[:, hp, c:c + 1], psS[:, hp * 128:(hp + 1) * 128], ALU.mult, ALU.add)
                _copy(g, "pool", Sbf[:], S[:])
            _chk("gla_%s_%d" % ("j", z))
            if z == 0:
                _copy(g, "act", Of[:, t, :], pso[:])
            else:
                o = ob.get()
                _tt(g, "dve", o[:], pso[:], Of[:, t, :], ALU.add)
                sq = sqo.get()
                _tt(g, "pool", sq[:], o[:], o[:], ALU.mult)
                s4 = ss4.get()
                P.op("dve", lambda e: e.tensor_reduce(s4[:], sq[:].rearrange("p (h c) -> p h c", c=128), axis=AX, op=ALU.add), [sq[:]], [s4[:]])
                _rstd_from_sum(g, s4[:], s4[:], 1.0 / 128, g.eps[:])
                o3 = o[:].rearrange("p (h c) -> p h c", c=128)
                _tt(g, "dve", o3, o3, s4[:].unsqueeze(2).to_broadcast([128, 4, 128]), ALU.mult)
                mk = mtok.get()
                _tt(g, "pool", mk[:], o[:], g2[:], ALU.mult)
                for h in range(4):
                    _tr(g, g.psbf[:, h * 128:(h + 1) * 128], mk[:, h * 128:(h + 1) * 128], g.ident_bf[:])
                m_ = mt.get()
                _copy(g, "act", m_[:], g.psbf[:, 0:512].rearrange("p (h c) -> p h c", c=128))
                P.dma(g.dr["MIX"].rearrange("k p t -> p k t")[:, 4:8, ts], m_[:])
            _chk("gla_k_%d" % z)


def _mla_attention(g, L, lat):
    P, dr = g.P, g.dr
    i = L // 2
    with contextlib.ExitStack() as st0:
      QT = _sb(g, "QT", [128, 8, TOK], BF16, st0)
      KT = _sb(g, "KT", [128, 8, TOK], BF16, st0)
      VA = _sb(g, "VA", [128, NT, 8, 128], BF16, st0)
      with contextlib.ExitStack() as st:
        _rope_tables(g, lat, st)
        wuq = _sb(g, "wuq", [128, 3, 768], BF16, st)
        _load_cast(g, wuq[:], dr["mla_w_uq"][i].rearrange("(k p) n -> p k n", p=128))
        wuqr = _sb(g, "wuqr", [128, 3, 768], BF16, st)
        _copy(g, "pool", wuqr[:], wuq[:])
        for k in range(3):
            _swap_halves(g, "pool", wuqr[:, k, :].rearrange("p (h c) -> p h c", c=96)[:, :, 64:96],
                         wuq[:, k, :].rearrange("p (h c) -> p h c", c=96)[:, :, 64:96])
        wukv = _sb(g, "wukv", [128, 2, 1024], BF16, st)
        _load_cast(g, wukv[:], dr["mla_w_ukv"][i].rearrange("(k p) n -> p k n", p=128))
        wv = _sb(g, "wv", [128, 2, 512], BF16, st)
        for k in range(2):
            _copy(g, "pool", wv[:, k, :].rearrange("p (h c) -> p h c", c=64), wukv[:, k, :].rearrange("p (h c) -> p h c", c=128)[:, :, 64:128])
        _memset(g, "pool", VA[:, :, :, 64:128].rearrange("p t h c -> p (t h) c"), 1.0)
        t1p = Pool_(g, "t1a", [128, 512], F32, 2, st)
        for bi, (s0, N) in enumerate(BLOCKS):
            sl = slice(s0, s0 + N)
            p0 = s0 - NCTX
            for h in range(8):
                ps = _psum(g)
                _mmk(g, ps[0:96, :N], [(wuq[:, fc, h * 96:(h + 1) * 96], lat.cqn[:, fc, sl]) for fc in range(3)])
                _copy(g, "act", QT[0:64, h, sl], ps[0:64, :N])
                if bi == 0:
                    _copy(g, "act", QT[64:96, h, sl], ps[64:96, :N])
                else:
                    ps2 = _psum(g)
                    _mmk(g, ps2[0:96, :N], [(wuqr[:, fc, h * 96:(h + 1) * 96], lat.cqn[:, fc, sl]) for fc in range(3)])
                    t1, t2 = t1p.get(), t1p.get()
                    _tt(g, "dve", t1[64:96, :N], ps[64:96, :N], lat.ropeC[64:96, p0:p0 + N], ALU.mult)
                    _tt(g, "dve", t2[64:96, :N], ps2[64:96, :N], lat.ropeS[64:96, p0:p0 + N], ALU.mult)
                    _tt(g, "pool", QT[64:96, h, sl], t1[64:96, :N], t2[64:96, :N], ALU.add)
                ps = _psum(g)
                _mmk(g, ps[0:64, :N], [(wukv[:, c, h * 128:h * 128 + 64], lat.ckvn[:, c, sl]) for c in range(2)])
                _copy(g, "act", KT[0:64, h, sl], ps[0:64, :N])
                _copy(g, "pool", KT[64:96, h, sl], lat.kpeT[64:96, sl])
            for tt_ in range(N // 128):
                t = s0 // 128 + tt_
                ps = _psum(g)
                _mmk(g, ps[:, 0:512], [(lat.ckvn[:, c, t * 128:(t + 1) * 128], wv[:, c, :]) for c in range(2)])
                _copy(g, "dve", VA[:, t, :, 0:64], ps[:, 0:512].rearrange("p (h c) -> p h c", c=64))
        P.barrier()
      with contextlib.ExitStack() as st:
        scale = float(96 ** -0.5)
        pT = Pool_(g, "pT", [128, 512], BF16, 3, st)
        rec = Pool_(g, "rec", [128, 512], F32, 2, st)
        atile = Pool_(g, "atile", [128, 512], BF16, 2, st)
        for bi, (s0, N) in enumerate(BLOCKS):
            sl = slice(s0, s0 + N)
            nk = 2 if bi == 0 else NT
            for h in range(8):
                if h % 2 == 0:
                    at_ = atile.get()
                pso = _psum(g, "acc")
                for kc in range(nk):
                    pss = _psum(g)
                    _mm(g, pss[:, :N], KT[0:96, h, kc * 128:(kc + 1) * 128], QT[0:96, h, sl])
                    p_ = pT.get()
                    _act(g, p_[:, :N], pss[:, :N], AF.Exp, scale=scale)
                    _mm(g, pso[:, :N], VA[:, kc, h, :], p_[:, :N], start=(kc == 0), stop=(kc == nk - 1))
                r_ = rec.get()
                P.op("dve", lambda e: e.reciprocal(r_[64:128, :N], pso[64:128, :N]), [pso[64:128, :N]], [r_[64:128, :N]])
                hb = (h % 2) * 64
                _tt(g, "dve", at_[hb:hb + 64, :N], pso[0:64, :N], r_[64:128, :N], ALU.mult)
                if h % 2 == 1:
                    P.dma(g.dr["MIX"].rearrange("k p t -> p k t")[:, h // 2, sl], at_[:, :N])


O_Q, O_K, O_V, O_G, O_A, O_B = 0, 512, 1024, 2048, 3072, 3088


def _odd_mixer(g, L, aT, need_ctx, AB):
    import itertools
    P, dr = g.P, g.dr
    i = L // 2
    wsrc = dr["gdn_w_in"][i].rearrange("(k p) n -> p k n", p=128)
    with contextlib.ExitStack() as st:
        blockones = _sb(g, "blockones", [128, 128], BF16, st)
        _memset(g, "pool", blockones[:], 0.0)
        _memset(g, "pool", blockones[0:64, 0:64], 1.0)
        _memset(g, "pool", blockones[64:128, 64:128], 1.0)
        wab = _sb(g, "wab", [128, KC, 32], BF16, st)
        _load_cast(g, wab[:], wsrc[:, :, O_A:O_A + 32])
        dtb = _sb(g, "dtb", [128, 16], F32, st)
        P.dma(dtb[:], dr["gdn_dt_bias"][i].rearrange("z h -> (z h)").partition_broadcast(128))
        negA = _sb(g, "negA", [128, 16], F32, st)
        P.dma(negA[:], dr["gdn_a_log"][i].rearrange("z h -> (z h)").partition_broadcast(128))
        _act(g, negA[:], negA[:], AF.Exp)
        _ts(g, "dve", negA[:], negA[:], -1.0, None, ALU.mult)
        onorm = _sb(g, "onormd", [128, 128], F32, st)
        P.dma(onorm[:], dr["gdn_o_norm"][i].partition_broadcast(128))
        g.bmask = _sb(g, "bmask", [128, 12, 128], BF16, st)
        _load_cast(g, g.bmask[:], dr["bmask"].rearrange("p (m c) -> p m c", c=128))
        nhp = 1
        NH = 2 * nhp
        for pair in range(2):
            with contextlib.ExitStack() as sh:
                units = []
                for u_ in range(2):
                    q0 = (2 * pair + u_) * nhp
                    d = Ctx()
                    d.q0 = q0
                    d.qT = _sb(g, "gqT", [128, nhp, TOK], BF16, sh)
                    d.kT = _sb(g, "gkT", [128, nhp, TOK], BF16, sh)
                    d.Vt = _sb(g, "gVt", [128, NT, NH * 128], BF16, sh)
                    d.Kt = _sb(g, "gKt", [128, NT, NH * 64], BF16, sh)
                    d.Of = _sb(g, "gOf", [128, NT, NH * 128], F32, sh)
                    d.wg = _sb(g, "gwg", [128, KC, NH * 128], BF16, sh)
                    _load_cast(g, d.wg[:], wsrc[:, :, O_G + 2 * q0 * 128:O_G + (2 * q0 + NH) * 128])
                    d.banks = (g.ps[3 + 2 * u_], g.ps[4 + 2 * u_])
                    units.append(d)
                with contextlib.ExitStack() as s1:
                    tmp = _gdn_conv_tmps(g, s1)
                    for d in units:
                        _gdn_conv_stage(g, i, d.q0, nhp, aT, wsrc, d.qT, d.kT, d.Vt, d.Kt, blockones, tmp)
                    P.barrier()
                with contextlib.ExitStack() as s2:
                    g.rot_n = 3
                    g.ps_rr = 0
                    gens = [_gdn_scan(g, d, nhp, aT, wab, dtb, negA, onorm, need_ctx, s2) for d in units]
                    for _ in itertools.zip_longest(*gens):
                        pass
                    g.rot_n = 4
                    g.ps_rr = 0
                    P.barrier()


def _gdn_conv_tmps(g, st):
    t = Ctx()
    t.xl = _sb(g, "xl", [128, SEQ + 2], F32, st)
    t.xc = _sb(g, "xc", [128, NCTX + 2], F32, st)
    for buf, n in ((t.xl, SEQ), (t.xc, NCTX)):
        _memset(g, "pool", buf[:, 0:1], 0.0)
        _memset(g, "pool", buf[:, n + 1:n + 2], 0.0)
    t.tcv = _sb(g, "tcv", [128, SEQ], F32, st)
    t.ybf = _sb(g, "ybf", [128, TOK], BF16, st)
    t.yf = _sb(g, "yf32", [128, 512], F32, st)
    t.sqp = _sb(g, "sqc", [128, 512], BF16, st)
    t.rn = _sb(g, "rnc", [128, 512], F32, st)
    t.wtp = Pool_(g, "wcv", [128, KC, 128], BF16, 2, st)
    return t


def _gdn_conv_stage(g, i, q0, nhp, aT, wsrc, qT, kT, Vt, Kt, blockones, tmp):
    P = g.P
    xl, xc, tcv, ybf, yf, sqp, rn, wtp = tmp.xl, tmp.xc, tmp.tcv, tmp.ybf, tmp.yf, tmp.sqp, tmp.rn, tmp.wtp
    cols = _col(g, "conv%d" % i, 48)
    chunks = [("q", O_Q // 128 + q0 + j, j) for j in range(nhp)] + [("k", O_K // 128 + q0 + j, j) for j in range(nhp)] + \
             [("v", O_V // 128 + 2 * q0 + j, j) for j in range(2 * nhp)]
    for kind, cc, j in chunks:
        wt = wtp.get()
        _load_cast(g, wt[:], wsrc[:, :, cc * 128:(cc + 1) * 128])
        for bi, (s0, N) in enumerate(BLOCKS):
            ps = _psum(g)
            _mmk(g, ps[:, :N], [(wt[:, k, :], aT[:, k, s0:s0 + N]) for k in range(KC)])
            if bi == 0:
                _copy(g, "act", xc[:, 1:1 + N], ps[:, :N])
            else:
                p0 = s0 - NCTX
                _copy(g, "act", xl[:, 1 + p0:1 + p0 + N], ps[:, :N])
        w0, w1, w2 = cols[:, cc:cc + 1], cols[:, 16 + cc:17 + cc], cols[:, 32 + cc:33 + cc]
        for buf, n, o0 in ((xc, NCTX, 0), (xl, SEQ, NCTX)):
            t = tcv[:, 0:n]
            _ts(g, "pool", t, buf[:, 1:1 + n], w1, None, ALU.mult)
            _stt(g, "dve", t, buf[:, 0:n], w0, t, ALU.mult, ALU.add)
            _stt(g, "dve", t, buf[:, 2:2 + n], w2, t, ALU.mult, ALU.add)
            if kind == "v":
                _act(g, ybf[:, o0:o0 + n], t, AF.Silu)
            else:
                for c0 in range(0, n, 512):
                    m = min(512, n - c0)
                    _act(g, yf[:, :m], t[:, c0:c0 + m], AF.Silu)
                    _tt(g, "pool", sqp[:, :m], yf[:, :m], yf[:, :m], ALU.mult)
                    ps = _psum(g)
                    _mm(g, ps[:, :m], blockones[:], sqp[:, :m])
                    _rstd_from_sum(g, rn[:, :m], ps[:, :m], 1.0, g.eps[:])
                    dst = (qT if kind == "q" else kT)[:, j, o0 + c0:o0 + c0 + m]
                    if kind == "q":
                        _stt(g, "dve", dst, yf[:, :m], 0.125, rn[:, :m], ALU.mult, ALU.mult)
                    else:
                        _tt(g, "dve", dst, yf[:, :m], rn[:, :m], ALU.mult)
        if kind in ("k", "v"):
            src = kT[:, j, :] if kind == "k" else ybf[:]
            for t4 in range(0, NT, 4):
                nt = min(4, NT - t4)
                for q in range(nt):
                    _tr(g, g.psbf[:, q * 128:(q + 1) * 128], src[:, (t4 + q) * 128:(t4 + q + 1) * 128], g.ident_bf[:])
                dst = (Kt[:, t4:t4 + nt, j * 128:(j + 1) * 128] if kind == "k" else Vt[:, t4:t4 + nt, j * 128:(j + 1) * 128])
                _copy(g, "act", dst, g.psbf[:, 0:nt * 128].rearrange("p (q c) -> p q c", c=128))


def _gdn_scan(g, d, nhp, aT, wab, dtb, negA, onorm, need_ctx, st):
    P = g.P
    NH = 2 * nhp
    W = NH * 128
    KW = NH * 64
    q0, qT, kT, Vt, Kt, Of, wg = d.q0, d.qT, d.kT, d.Vt, d.Kt, d.Of, d.wg
    bank_o, bank_s = d.banks
    S = _sb(g, "dS", [128, nhp, 128], F32, st)
    Sbf = _sb(g, "dSbf", [128, nhp, 128], BF16, st)
    sm = Pool_(g, "dsm", [128, 32], F32, 2, st)
    larep = _sb(g, "larep", [128, NH, 64], F32, st)
    Eg = _sb(g, "dEg", [128, nhp, 128], F32, st)
    gend = Pool_(g, "dgend", [128, nhp, 2], F32, 2, st)
    qdecT = _sb(g, "dqdec", [128, NH, 128], BF16, st)
    _memset(g, "pool", qdecT[:], 0.0)
    kz = _sb(g, "dkz", [128, NH, 128], BF16, st)
    _memset(g, "pool", kz[:], 0.0)
    CI4 = _cst(g, "CI").unsqueeze(2).to_broadcast([128, 2, KW])
    rhsd = _sb(g, "rhsd", [128, NH, 128], F32, st)
    dec = _sb(g, "ddec", [128, NH, 128], F32, st)
    tG = _sb(g, "dtG", [128, NH, 128], F32, st)
    qkm = _sb(g, "dqkm", [128, NH, 128], BF16, st)
    Mp = Pool_(g, "dM", [128, NH, 128], BF16, 2, st)
    Np = Pool_(g, "dN", [128, NH, 128], BF16, 2, st)
    Tnp = Pool_(g, "dTn", [128, NH, 128], BF16, 2, st)
    Amp = Pool_(g, "dAm", [128, NH, 128], BF16, 4, st)
    Y2p = Pool_(g, "dY2", [128, NH, 128], BF16, 2, st)
    NQ = Pool_(g, "dNQ", [128, NH, 256], BF16, 2, st)
    vb = _sb(g, "dvb", [128, W], BF16, st)
    kbg = _sb(g, "dkbg", [128, NH, 64], BF16, st)
    kend = _sb(g, "dkend", [128, NH, 64], BF16, st)
    kend2 = _sb(g, "dkend2", [128, 2, KW], BF16, st)
    u = _sb(g, "du", [128, W], F32, st)
    wT = _sb(g, "dwT", [128, NH, 128], BF16, st)
    _memset(g, "pool", wT[:], 0.0)
    vnew = _sb(g, "dvnew", [128, W], BF16, st)
    _memset(g, "pool", vnew[:], 0.0)
    G2 = _sb(g, "dG2", [128, W], F32, st)
    ob = _sb(g, "dob", [128, W], F32, st)
    sqo = _sb(g, "dsqo", [128, W], F32, st)
    mtok = _sb(g, "dmtok", [128, W], BF16, st)
    mt = Pool_(g, "dmt", [128, NH, 128], BF16, 2, st)
    ident_bc = g.ident.unsqueeze(1).to_broadcast([128, NH, 128])
    v3 = lambda ap: ap.rearrange("p (h c) -> p h c", c=128)
    halves = ((0, 64, 0), (64, 128, 1))
    yield
    for z in range(2):
        R_ = _cst(g, "gdnRf" if z == 0 else "gdnRb")
        Mz = _cst(g, "Mf" if z == 0 else "Mb")
        Sz = _cst(g, "Sf" if z == 0 else "Sb")
        Mz_bc = Mz.unsqueeze(1).to_broadcast([128, NH, 128])
        Sz_bc = Sz.unsqueeze(1).to_broadcast([128, NH, 128])
        _memset(g, "pool", S[:], 0.0)
        _memset(g, "pool", Sbf[:], 0.0)
        for t in (GLA_FWD if z == 0 else GLA_BWD):
            ts = slice(t * 128, (t + 1) * 128)
            at = [aT[:, k, ts] for k in range(KC)]
            want_out = need_ctx or t >= 2
            ps = _psum(g)
            _mmk(g, ps[:, 0:32], [(at[k], wab[:, k, :]) for k in range(KC)])
            s_ = sm.get()
            ca = z * 8 + 2 * q0
            la, beta = s_[:, 0:NH], s_[:, NH:2 * NH]
            egk = s_[:, 2 * NH:4 * NH]
            eg, ekend = s_[:, 2 * NH:3 * NH], s_[:, 3 * NH:4 * NH]
            bg = s_[:, 4 * NH:5 * NH]
            rs = s_[:, 5 * NH:6 * NH]
            _tt(g, "dve", la, ps[:, ca:ca + NH], dtb[:, ca:ca + NH], ALU.add)
            _act(g, beta, ps[:, 16 + ca:16 + ca + NH], AF.Sigmoid)
            yield
            _act(g, la, la, AF.Exp)
            _act(g, la, la, AF.Ln, scale=1.0, bias=1.0)
            _tt(g, "dve", la, la, negA[:, ca:ca + NH], ALU.mult)
            _copy(g, "pool", larep[:], la.unsqueeze(2).to_broadcast([128, NH, 64]))
            yield
            ps = _psum(g)
            lr = larep[:].rearrange("p h d -> p (h d)")
            for hp in range(nhp):
                _mm(g, ps[:, hp * 130:(hp + 1) * 130], lr[:, hp * 128:(hp + 1) * 128], R_)
            pv = ps[:, 0:130 * nhp].rearrange("p (h c) -> p h c", c=130)
            ge = gend.get()
            _act(g, Eg[:], pv[:, :, 0:128], AF.Exp)
            _act(g, ge[:], pv[:, :, 128:130], AF.Exp)
            yield
            for (lo, hi, par) in halves:
                _tt(g, "dve", qdecT[lo:hi, par::2, :], qT[lo:hi, :, ts], Eg[lo:hi, :, :], ALU.mult)
                _copy(g, "pool", kz[lo:hi, par::2, :], kT[lo:hi, :, ts])
            ps = _psum(g)
            _mm(g, ps[:, 0:NH], Mz, la)
            _mm(g, ps[:, NH:2 * NH], Sz, la)
            _act(g, egk, ps[:, 0:2 * NH], AF.Exp)
            yield
            _tt(g, "dve", bg, beta, eg, ALU.mult)
            kt3 = Kt[:, t, :].rearrange("p (h d) -> p h d", d=64)
            _tt(g, "pool", kend[:], kt3, ekend.unsqueeze(2).to_broadcast([128, NH, 64]), ALU.mult)
            _tt(g, "pool", kend2[:], kend[:].rearrange("p h d -> p (h d)").unsqueeze(1).to_broadcast([128, 2, KW]), CI4, ALU.mult)
            _tt(g, "pool", kbg[:], kt3, bg.unsqueeze(2).to_broadcast([128, NH, 64]), ALU.mult)
            _tt(g, "pool", v3(vb[:]), v3(Vt[:, t, :]), beta.unsqueeze(2).to_broadcast([128, NH, 128]), ALU.mult)
            la_bc = la.unsqueeze(2).to_broadcast([128, NH, 128])
            _tt(g, "pool", rhsd[:], Mz_bc, la_bc, ALU.mult)
            yield
            ps = _psum(g)
            _mm(g, ps[:, 0:W], Sz, rhsd[:].rearrange("p h c -> p (h c)"))
            _act(g, dec[:].rearrange("p h c -> p (h c)"), ps[:, 0:W], AF.Exp)
            yield
            ps = _psum(g)
            for h in range(NH):
                _mm(g, ps[:, h * 128:(h + 1) * 128], kz[:, h, :], qT[:, h // 2, ts])
            _tt(g, "dve", tG[:], v3(ps[:, 0:W]), Mz_bc, ALU.mult)
            yield
            _tt(g, "dve", qkm[:], tG[:], dec[:], ALU.mult)
            _tt(g, "pool", rhsd[:], Sz_bc, la_bc, ALU.mult)
            yield
            ps = _psum(g)
            _mm(g, ps[:, 0:W], Mz, rhsd[:].rearrange("p h c -> p (h c)"))
            _act(g, dec[:].rearrange("p h c -> p (h c)"), ps[:, 0:W], AF.Exp)
            yield
            _tt(g, "pool", dec[:], dec[:], beta.unsqueeze(2).to_broadcast([128, NH, 128]), ALU.mult)
            ps = _psum(g)
            for h in range(NH):
                _mm(g, ps[:, h * 128:(h + 1) * 128], kz[:, h, :], kz[:, h, :])
            _tt(g, "dve", tG[:], v3(ps[:, 0:W]), Sz_bc, ALU.mult)
            yield
            M_ = Mp.get()
            _tt(g, "dve", M_[:], tG[:], dec[:], ALU.mult)
            for h in range(NH):
                _tr(g, g.psbf[:, h * 128:(h + 1) * 128], M_[:, h, :], g.ident_bf[:])
            N_ = Np.get()
            _copy(g, "act", N_[:], v3(g.psbf[:, 0:W]))
            yield
            TT = NQ.get()
            Tn = Tnp.get()
            for lv, bsz in enumerate((1, 2, 4, 8, 16, 32)):
                last = (bsz == 32)
                mk_ = g.bmask[:, (lv if z == 0 else 6 + lv), :].unsqueeze(1).to_broadcast([128, NH, 128])
                mkT = g.bmask[:, (6 + lv if z == 0 else lv), :].unsqueeze(1).to_broadcast([128, NH, 128])
                Am = Amp.get()
                _tt(g, "pool", Am[:], M_[:], mkT, ALU.mult)
                if lv == 0 or not last:
                    Nm = Amp.get()
                    _tt(g, "pool", Nm[:], N_[:], mk_, ALU.mult)
                if lv == 0:
                    _tt(g, "dve", TT[:, :, 128:256], ident_bc, Nm[:], ALU.subtract)
                    _tt(g, "dve", Tn[:], ident_bc, Am[:], ALU.subtract)
                    yield
                    continue
                TT2 = NQ.get()
                psY = _psum(g)
                for h in range(NH):
                    _mm(g, psY[:, h * 128:(h + 1) * 128], Am[:, h, :], TT[:, h, 128:256])
                _copy(g, "act", TT[:, :, 0:128], v3(psY[:, 0:W]))
                yield
                if not last:
                    psY2 = _psum(g)
                    for h in range(NH):
                        _mm(g, psY2[:, h * 128:(h + 1) * 128], Nm[:, h, :], Tn[:, h, :])
                    y2 = Y2p.get()
                    _copy(g, "dve", y2[:], v3(psY2[:, 0:W]))
                    yield
                psZ = _psum(g)
                for h in range(NH):
                    _mm(g, psZ[:, h * 128:(h + 1) * 128], Tn[:, h, :], TT[:, h, 0:128])
                _tt(g, "dve", TT2[:, :, 128:256], TT[:, :, 128:256], v3(psZ[:, 0:W]), ALU.subtract)
                yield
                if not last:
                    psZ2 = _psum(g)
                    for h in range(NH):
                        _mm(g, psZ2[:, h * 128:(h + 1) * 128], TT[:, h, 128:256], y2[:, h, :])
                    Tn2 = Tnp.get()
                    _tt(g, "dve", Tn2[:], Tn[:], v3(psZ2[:, 0:W]), ALU.subtract)
                    Tn = Tn2
                    yield
                TT = TT2
            nq = TT
            ps = _psum(g)
            for h in range(NH):
                _mm(g, ps[:, h * 128:(h + 1) * 128], nq[:, h, 128:256], vb[:, h * 128:(h + 1) * 128])
            _copy(g, "act", u[:], ps[:, 0:W])
            yield
            ps = _psum(g)
            for h in range(NH):
                hp, hb = h // 2, (h % 2) * 64
                _mm(g, ps[hb:hb + 64, hp * 128:(hp + 1) * 128], kbg[:, h, :], nq[:, h, 128:256])
            for (lo, hi, par) in halves:
                _copy(g, "act", wT[lo:hi, par::2, :], v3(ps[lo:hi, 0:128 * nhp]))
            yield
            if z == 1 and want_out:
                ps = _psum(g)
                _mmk(g, ps[:, 0:W], [(at[k], wg[:, k, :]) for k in range(KC)])
                _act(g, G2[:], ps[:, 0:W], AF.Silu)
                _tt(g, "pool", v3(G2[:]), v3(G2[:]), onorm[:].unsqueeze(1).to_broadcast([128, NH, 128]), ALU.mult)
                yield
            pso = bank_o
            for c in ((0, 1) if z == 0 else (1, 0)):
                cb = c * 64
                psw = _psum(g)
                for h in range(NH):
                    _mm(g, psw[cb:cb + 64, h * 128:(h + 1) * 128], wT[:, h, cb:cb + 64], Sbf[:, h // 2, :])
                _tt(g, "dve", vnew[cb:cb + 64, :], u[cb:cb + 64, :], psw[cb:cb + 64, 0:W], ALU.subtract)
                yield
                psS = bank_s
                for h in range(NH):
                    hp, hb = h // 2, (h % 2) * 64
                    if want_out:
                        _mm(g, pso[cb:cb + 64, h * 128:(h + 1) * 128], qdecT[:, h, cb:cb + 64], Sbf[:, hp, :], start=True, stop=False)
                        _mm(g, pso[cb:cb + 64, h * 128:(h + 1) * 128], qkm[:, h, cb:cb + 64], vnew[:, h * 128:(h + 1) * 128], start=False, stop=True)
                    _mm(g, psS[hb:hb + 64, hp * 128:(hp + 1) * 128], kend2[:, c, h * 64:(h + 1) * 64], vnew[:, h * 128:(h + 1) * 128])
                for hp in range(nhp):
                    _stt(g, "dve", S[:, hp, :], S[:, hp, :], ge[:, hp, c:c + 1], psS[:, hp * 128:(hp + 1) * 128], ALU.mult, ALU.add)
                _copy(g, "pool", Sbf[:], S[:])
                yield
            if not want_out:
                continue
            if z == 0:
                _copy(g, "act", Of[:, t, :], pso[:, 0:W])
                yield
            else:
                _tt(g, "dve", ob[:], pso[:, 0:W], Of[:, t, :], ALU.add)
                _tt(g, "pool", sqo[:], ob[:], ob[:], ALU.mult)
                P.op("dve", lambda e: e.tensor_reduce(rs, v3(sqo[:]), axis=AX, op=ALU.add), [sqo[:]], [rs])
                _rstd_from_sum(g, rs, rs, 1.0 / 128, g.eps[:])
                yield
                _tt(g, "dve", v3(ob[:]), v3(ob[:]), rs.unsqueeze(2).to_broadcast([128, NH, 128]), ALU.mult)
                _tt(g, "pool", mtok[:], ob[:], G2[:], ALU.mult)
                for h in range(NH):
                    _tr(g, g.psbf[:, h * 128:(h + 1) * 128], mtok[:, h * 128:(h + 1) * 128], g.ident_bf[:])
                m_ = mt.get()
                _copy(g, "act", m_[:], v3(g.psbf[:, 0:W]))
                P.dma(g.dr["MIX"].rearrange("k p t -> p k t")[:, 2 * q0:2 * q0 + NH, ts], m_[:])
                yield


def make_in_maps(inputs):
    cst, rope, bmask = host_consts()
    maps = []
    for b in range(8):
        m = {
            "x": np.ascontiguousarray(inputs["x"][b], dtype=np.float32),
            "ctx": np.ascontiguousarray(inputs["ctx"][b], dtype=np.float32),
            "cc": np.ascontiguousarray(np.concatenate([np.asarray(inputs["c"][b]).reshape(8, 128),
                                                       np.asarray(inputs["c_ctx"]).reshape(8, 128)], 0), dtype=np.float32),
            "cst": cst, "rope": rope, "bmask": bmask,
        }
        for n in WEIGHT_NAMES:
            m[n] = np.ascontiguousarray(inputs[n], dtype=np.float32)
        maps.append(m)
    return maps


_PROG_CACHE = {}


def kernel(**inputs):
    if "nc" not in _PROG_CACHE:
        import os
        _PROG_CACHE["nc"] = build_program(n_layers=int(os.environ.get("K_LAYERS", DEPTH)))[0]
    nc = _PROG_CACHE["nc"]
    in_maps = make_in_maps(inputs)
    res = run_bass_kernel_spmd(nc, in_maps, core_ids=list(range(8)))
    out = np.stack([np.asarray(res.results[b]["out"], dtype=np.float32) for b in range(8)], 0)
    return out
```

```python
import contextlib
import numpy as np
import concourse.bass as bass
import concourse.mybir as mybir
from concourse.bass_utils import run_bass_kernel_spmd

F32 = mybir.dt.float32
BF16 = mybir.dt.bfloat16
AF = mybir.ActivationFunctionType
ALU = mybir.AluOpType

SAME_ENGINE_SYNC = True
SEM_ROTATE = 30000
DBG_SKIP = set()
USE_AB32 = False
DBG_STOP = None


class _Stop(Exception):
    pass


def _chk(name):
    if DBG_STOP == name:
        raise _Stop()


class _Eng:
    def __init__(self, name, handle, is_dma_only=False):
        self.name = name
        self.h = handle
        self.sem = None
        self.count = 0
        self.waited = {}
        self.pending = False


class Prog:
    def __init__(self, nc, es, n_dma_sems=40):
        self.nc = nc
        self.es = es
        self.engs = {
            "pe": _Eng("pe", nc.tensor),
            "act": _Eng("act", nc.scalar),
            "dve": _Eng("dve", nc.vector),
            "pool": _Eng("pool", nc.gpsimd),
            "sp": _Eng("sp", nc.sync),
        }
        self.semid = 0
        for e in self.engs.values():
            e.sem = self._newsem()
        self.dma_sems = [[self._newsem(), 0] for _ in range(n_dma_sems)]
        self.dma_rr = 0
        self.recs = {}
        self.snap = {}
        self.n_inst = 0
        self.n_wait = 0
        self.out_tokens = []

    def _newsem(self):
        self.semid += 1
        return self.es.enter_context(self.nc.semaphore("s%d" % self.semid))

    @staticmethod
    def _box(ap):
        t = ap.tensor
        name = t.name
        shape = tuple(t.shape)
        pat = ap.ap
        off = ap.offset
        sp = str(ap.space) if not isinstance(ap.space, str) else ap.space
        if "DRAM" in sp.upper() or "HBM" in sp.upper() or "Dram" in sp:
            W = shape[-1]
            r0, c0 = off // W, off % W
            r1, c1 = r0, c0
            ok = True
            for (s, c) in pat:
                s = abs(s)
                if c <= 1:
                    continue
                if s % W == 0:
                    r1 += (c - 1) * (s // W)
                elif s * (c - 1) < W:
                    c1 += (c - 1) * s
                else:
                    ok = False
            if (not ok) or c1 >= W:
                ext = 1
                for (s, c) in pat:
                    ext += (c - 1) * abs(s)
                return (name, 0, 1 << 30, 0, 1 << 30) if True else None
            return (name, r0, r1 + 1, c0, c1 + 1)
        fsz = 1
        for s in shape[1:]:
            fsz *= s
        pstep, pcnt = pat[0]
        if pstep != fsz and pcnt > 1:
            return (name, 0, 128, 0, fsz)
        p0 = off // fsz
        f0 = off % fsz
        ext = 1
        for (s, c) in pat[1:]:
            ext += (c - 1) * abs(s)
        if name.startswith("ps"):
            return (name, (p0 // 32) * 32, ((p0 + pcnt + 31) // 32) * 32, 0, fsz)
        return (name, p0, p0 + pcnt, f0, f0 + ext)

    def _deps(self, eng, reads, writes, is_dma):
        deps = {}
        boxes_r = [self._box(a) for a in reads]
        boxes_w = [self._box(a) for a in writes]
        for kind, boxes in (("r", boxes_r), ("w", boxes_w)):
            for b in boxes:
                lst = self.recs.get(b[0])
                if not lst:
                    continue
                for rec in lst:
                    if kind == "r" and rec[4] == "r":
                        continue
                    if rec[0] >= b[2] or b[1] >= rec[1] or rec[2] >= b[4] or b[3] >= rec[3]:
                        continue
                    if (not is_dma) and (not rec[8]) and rec[7] == eng.name and (eng.name == "pe" or not SAME_ENGINE_SYNC):
                        continue
                    s, v = rec[5], rec[6]
                    k = id(s)
                    if k not in deps or deps[k][1] < v:
                        deps[k] = (s, v)
        return deps, boxes_r, boxes_w

    def _record(self, eng, boxes_r, boxes_w, sem, val, is_dma):
        for b in boxes_w:
            lst = self.recs.setdefault(b[0], [])
            lst[:] = [r for r in lst if not (b[1] <= r[0] and r[1] <= b[2] and b[3] <= r[2] and r[3] <= b[4])]
            lst.append([b[1], b[2], b[3], b[4], "w", sem, val, eng.name, is_dma])
        for b in boxes_r:
            lst = self.recs.setdefault(b[0], [])
            hit = False
            if not is_dma:
                for r in lst:
                    if r[4] == "r" and r[7] == eng.name and not r[8] and r[0] == b[1] and r[1] == b[2] and r[2] == b[3] and r[3] == b[4]:
                        r[5], r[6] = sem, val
                        hit = True
                        break
            if not hit:
                lst.append([b[1], b[2], b[3], b[4], "r", sem, val, eng.name, is_dma])
                if len(lst) > 96:
                    self._compact(lst)

    @staticmethod
    def _compact(lst):
        keep = [r for r in lst if r[4] == "w"]
        merged = {}
        for r in lst:
            if r[4] != "r":
                continue
            k = (r[7], id(r[5]), r[8])
            m = merged.get(k)
            if m is None:
                merged[k] = list(r)
            else:
                m[0] = min(m[0], r[0]); m[1] = max(m[1], r[1]); m[2] = min(m[2], r[2]); m[3] = max(m[3], r[3])
                m[6] = max(m[6], r[6])
        lst[:] = keep + list(merged.values())

    def _emit_waits(self, eng, deps):
        for (s, v) in sorted(deps.values(), key=lambda sv: -sv[1]):
            k = id(s)
            if eng.waited.get(k, 0) >= v:
                continue
            eng.h.wait_ge(s, v)
            eng.waited[k] = v
            self.n_wait += 1
            sn = self.snap.get((k, v))
            if sn:
                w = eng.waited
                for k2, v2 in sn.items():
                    if w.get(k2, 0) < v2:
                        w[k2] = v2

    def op(self, en, fn, reads=(), writes=(), inc=True):
        eng = self.engs[en]
        deps, br, bw = self._deps(eng, reads, writes, False)
        self._emit_waits(eng, deps)
        ins = fn(eng.h)
        self.n_inst += 1
        if inc:
            if eng.count >= SEM_ROTATE and not eng.pending:
                eng.sem = self._newsem()
                eng.count = 0
            eng.count += 1
            ins.then_inc(eng.sem, 1)
            tok = (eng.sem, eng.count)
            eng.pending = False
            self.snap[(id(eng.sem), eng.count)] = dict(eng.waited)
        else:
            tok = (eng.sem, eng.count + 1)
            eng.pending = True
        self._record(eng, br, bw, tok[0], tok[1], False)
        return ins

    def dma(self, out, in_, q="sp", is_output=False, **kw):
        eng = self.engs[q]
        deps, br, bw = self._deps(eng, [in_], [out], True)
        slot = self.dma_sems[self.dma_rr]
        self.dma_rr = (self.dma_rr + 1) % len(self.dma_sems)
        if slot[1] > 0:
            deps[id(slot[0])] = (slot[0], slot[1])
        self._emit_waits(eng, deps)
        slot[1] += 16
        eng.h.dma_start(out=out, in_=in_, **kw).then_inc(slot[0], 16)
        self.snap[(id(slot[0]), slot[1])] = dict(eng.waited)
        self.n_inst += 1
        self._record(eng, br, bw, slot[0], slot[1], True)
        if is_output:
            self.out_tokens.append((slot[0], slot[1]))

    def finish(self):
        eng = self.engs["sp"]
        deps = {}
        for slot in self.dma_sems:
            if slot[1] > 0:
                deps[id(slot[0])] = (slot[0], slot[1])
        self._emit_waits(eng, deps)

    def barrier(self):
        deps = {}
        for e in self.engs.values():
            if e.count > 0:
                deps[id(e.sem)] = (e.sem, e.count)
        for slot in self.dma_sems:
            if slot[1] > 0:
                deps[id(slot[0])] = (slot[0], slot[1])
        for e in self.engs.values():
            d = {k: v for k, v in deps.items() if k != id(e.sem)}
            self._emit_waits(e, d)
        self.recs = {}


D = 1024
KC = 8
SEQ = 2048
NCTX = 256
TOK = NCTX + SEQ
NT = TOK // 128
DEPTH = 4
DFF = 4096
EPS = 1e-6
BLOCKS = [(0, 256), (256, 512), (768, 512), (1280, 512), (1792, 512)]
EVEN_IN = 2240
ODD_IN = 3104
AX = mybir.AxisListType.X

WEIGHT_NAMES = ["ada_w", "ada_b", "norm1_w", "norm2_w", "mlp_w1", "mlp_w2", "even_w_in", "mla_q_norm", "mla_w_uq",
                "mla_kv_norm", "mla_w_ukv", "gla_gate_up", "gla_gate_bias", "gla_o_norm", "even_w_out", "gdn_w_in",
                "gdn_conv_w", "gdn_a_log", "gdn_dt_bias", "gdn_o_norm", "gdn_w_out", "final_norm"]
WEIGHT_SHAPES = {
    "ada_w": [4, 1024, 6144], "ada_b": [4, 6144], "norm1_w": [4, 1024], "norm2_w": [4, 1024],
    "mlp_w1": [4, 1024, 4096], "mlp_w2": [4, 4096, 1024], "even_w_in": [2, 1024, 2240], "mla_q_norm": [2, 384],
    "mla_w_uq": [2, 384, 768], "mla_kv_norm": [2, 256], "mla_w_ukv": [2, 256, 1024], "gla_gate_up": [2, 2, 16, 256],
    "gla_gate_bias": [2, 2, 256], "gla_o_norm": [2, 128], "even_w_out": [2, 1024, 1024], "gdn_w_in": [2, 1024, 3104],
    "gdn_conv_w": [2, 3, 2048], "gdn_a_log": [2, 2, 8], "gdn_dt_bias": [2, 2, 8], "gdn_o_norm": [2, 128],
    "gdn_w_out": [2, 1024, 1024], "final_norm": [1024],
}


def host_consts():
    i = np.arange(128)
    same = (i[:, None] // 64) == (i[None, :] // 64)
    Mf = (same & (i[:, None] <= i[None, :])).astype(np.float32)
    Mb = (same & (i[:, None] >= i[None, :])).astype(np.float32)
    Sf = (same & (i[:, None] > i[None, :])).astype(np.float32)
    Sb = (same & (i[:, None] < i[None, :])).astype(np.float32)
    CI = np.stack([(i < 64), (i >= 64)], 1).astype(np.float32)
    ident = np.eye(128, dtype=np.float32)
    cols = [ident, Mf, Mb, Sf, Sb, CI,
            np.concatenate([Mf, CI], 1) / -16.0, np.concatenate([Mb, CI], 1) / -16.0, Sf / -16.0, Sb / -16.0,
            np.concatenate([Mf, CI], 1), np.concatenate([Mb, CI], 1)]
    cst = np.concatenate(cols, 1).astype(np.float32)
    rows = SEQ // 64
    row = np.repeat(np.arange(rows, dtype=np.float32), 64)
    col = np.tile(np.arange(64, dtype=np.float32), rows)
    inv = (10000.0 ** (-np.arange(0, 16, 2, dtype=np.float32) / 16)).astype(np.float32)
    ang = np.concatenate([row[:, None] * inv, col[:, None] * inv], -1).astype(np.float32)
    cos, sin = np.cos(ang).astype(np.float32), np.sin(ang).astype(np.float32)
    C = np.zeros((32, SEQ), np.float32)
    S = np.zeros((32, SEQ), np.float32)
    for ax in range(2):
        for half in range(2):
            for f in range(8):
                d = ax * 16 + half * 8 + f
                C[d] = cos[:, ax * 8 + f]
                S[d] = (-sin[:, ax * 8 + f]) if half == 0 else sin[:, ax * 8 + f]
    rope = np.stack([C, S], 0)
    r = np.arange(128)
    ms = []
    for b in (1, 2, 4, 8, 16, 32):
        same2 = (r[:, None] // (2 * b)) == (r[None, :] // (2 * b))
        ur = same2 & ((r[:, None] % (2 * b)) < b) & ((r[None, :] % (2 * b)) >= b)
        ms.append(ur.astype(np.float32))
    bmask = np.concatenate(ms + [m.T for m in ms], 1).astype(np.float32)
    return cst, rope, bmask


CST_OFF = {}
_o = 0
for _n, _w in [("ident", 128), ("Mf", 128), ("Mb", 128), ("Sf", 128), ("Sb", 128), ("CI", 2), ("glaRf", 130), ("glaRb", 130),
               ("glaAf", 128), ("glaAb", 128), ("gdnRf", 130), ("gdnRb", 130)]:
    CST_OFF[_n] = (_o, _w)
    _o += _w
CST_W = _o


class Ctx:
    pass


def build_program(n_layers=DEPTH, fake_mixer=False, debug_h=False):
    nc = bass.Bass("TRN2", target_bir_lowering=False)
    g = Ctx()
    g.nc = nc
    dr = {}
    dr["x"] = nc.dram_tensor("x", [SEQ, D], F32, kind="ExternalInput").ap()
    dr["ctx"] = nc.dram_tensor("ctx", [NCTX, D], F32, kind="ExternalInput").ap()
    dr["cc"] = nc.dram_tensor("cc", [16, 128], F32, kind="ExternalInput").ap()
    dr["cst"] = nc.dram_tensor("cst", [128, CST_W], F32, kind="ExternalInput").ap()
    dr["rope"] = nc.dram_tensor("rope", [2, 32, SEQ], F32, kind="ExternalInput").ap()
    dr["bmask"] = nc.dram_tensor("bmask", [128, 12 * 128], F32, kind="ExternalInput").ap()
    for n in WEIGHT_NAMES:
        dr[n] = nc.dram_tensor(n, WEIGHT_SHAPES[n], F32, kind="ExternalInput").ap()
    dr["out"] = nc.dram_tensor("out", [SEQ, D], F32, kind="ExternalOutput").ap()
    dr["H"] = nc.dram_tensor("Hs", [KC, 128, TOK], F32).ap()
    dr["MIX"] = nc.dram_tensor("MIXs", [KC, 128, TOK], BF16).ap()
    if debug_h:
        dr["dbg"] = nc.dram_tensor("dbg", [KC, 128, TOK], F32, kind="ExternalOutput").ap()
    g.dr = dr
    es = contextlib.ExitStack()
    with es:
        P = Prog(nc, es)
        g.P = P
        g.es = es
        g.ps = [es.enter_context(nc.psum_tensor("psb%d" % i, [128, 512], F32)) for i in range(7)]
        g.psbf = es.enter_context(nc.psum_tensor("psbf", [128, 1024], BF16))
        g.ps_rr = 0
        _emit(g, n_layers, fake_mixer, debug_h)
        P.finish()
        g.stats = (P.n_inst, P.n_wait, P.semid)
    return nc, g


_UNIQ = [0]


def _sb(g, name, shape, dt=F32, stack=None):
    _UNIQ[0] += 1
    return (stack or g.es).enter_context(g.nc.sbuf_tensor("sb_%s_%d" % (name, _UNIQ[0]), shape, dt))


def _psum(g, kind="rot"):
    if kind == "rot":
        t = g.ps[g.ps_rr]
        g.ps_rr = (g.ps_rr + 1) % getattr(g, "rot_n", 4)
        return t
    if kind == "acc":
        g.ps_acc = 1 - getattr(g, "ps_acc", 0)
        return g.ps[4 + g.ps_acc]
    return g.ps[6]


def _mm(g, out, lhsT, rhs, start=True, stop=True, inc=True):
    g.P.op("pe", lambda e: e.matmul(out, lhsT, rhs, start=start, stop=stop), [lhsT, rhs], [out], inc=inc)


def _mmk(g, out, pairs):
    n = len(pairs)
    for i, (l, r) in enumerate(pairs):
        _mm(g, out, l, r, start=(i == 0), stop=(i == n - 1), inc=(i == n - 1))


def _tr(g, out, in_, ident):
    g.P.op("pe", lambda e: e.transpose(out, in_, ident), [in_, ident], [out])


def _act(g, out, in_, func, scale=1.0, bias=0.0, extra_reads=()):
    rd = [in_] + [a for a in (scale, bias) if not isinstance(a, (int, float))] + list(extra_reads)
    g.P.op("act", lambda e: e.activation(out=out, in_=in_, func=func, bias=bias, scale=scale), rd, [out])


def _tt(g, en, out, in0, in1, op):
    g.P.op(en, lambda e: e.tensor_tensor(out, in0, in1, op=op), [in0, in1], [out])


def _stt(g, en, out, in0, scalar, in1, op0, op1):
    rd = [in0, in1] + ([scalar] if not isinstance(scalar, (int, float)) else [])
    g.P.op(en, lambda e: e.scalar_tensor_tensor(out=out, in0=in0, scalar=scalar, in1=in1, op0=op0, op1=op1), rd, [out])


def _ts(g, en, out, in0, s1, s2, op0, op1=None):
    rd = [in0] + [a for a in (s1, s2) if a is not None and not isinstance(a, (int, float))]
    if op1 is None:
        g.P.op(en, lambda e: e.tensor_scalar(out, in0, s1, None, op0=op0), rd, [out])
    else:
        g.P.op(en, lambda e: e.tensor_scalar(out, in0, s1, s2, op0=op0, op1=op1), rd, [out])


def _copy(g, en, out, in_):
    if en == "act":
        g.P.op("act", lambda e: e.copy(out, in_), [in_], [out])
    else:
        g.P.op(en, lambda e: e.tensor_copy(out, in_), [in_], [out])


def _memset(g, en, out, val):
    g.P.op(en, lambda e: e.memset(out, val), [], [out])


def _load_cast(g, dst, src):
    shp = list(dst.shape)
    if len(shp) == 2:
        pieces = [(dst[:, c:min(c + 1024, shp[1])], src[:, c:min(c + 1024, shp[1])]) for c in range(0, shp[1], 1024)]
    else:
        inner = shp[2]
        assert len(shp) == 3 and inner <= 1024
        step = max(1, 1024 // inner)
        pieces = [(dst[:, a:min(a + step, shp[1]), :], src[:, a:min(a + step, shp[1]), :]) for a in range(0, shp[1], step)]
    for d, s_ in pieces:
        stg = g.stage.get()
        n = 1
        for x in d.shape[1:]:
            n *= x
        if len(d.shape) == 3:
            v = stg[:, 0:n].rearrange("p (a b) -> p a b", b=d.shape[2])
        else:
            v = stg[:, 0:n]
        g.P.dma(v, s_)
        g.cast_rr = 1 - getattr(g, "cast_rr", 0)
        _copy(g, "pool" if g.cast_rr else "act", d, v)


def _rstd_from_sum(g, out_sb, ps_in, inv_n, eps_col):
    _act(g, out_sb, ps_in, AF.Sqrt, scale=inv_n, bias=eps_col)
    g.P.op("dve", lambda e: e.reciprocal(out_sb, out_sb), [out_sb], [out_sb])


class Pool_:
    def __init__(self, g, name, shape, dt, n, stack=None, zero=False):
        self.tiles = [_sb(g, "%s%d" % (name, i), shape, dt, stack) for i in range(n)]
        self.i = 0
        if zero:
            for t in self.tiles:
                _memset(g, "pool", t[:], 0.0)

    def get(self):
        t = self.tiles[self.i]
        self.i = (self.i + 1) % len(self.tiles)
        return t


def _cst(g, name):
    o, w = CST_OFF[name]
    return g.cst[:, o:o + w]


def _setup(g):
    P, dr = g.P, g.dr
    g.cst = _sb(g, "cst", [128, CST_W])
    P.dma(g.cst[:], dr["cst"])
    g.ident = _cst(g, "ident")
    g.ident_bf = _sb(g, "ident_bf", [128, 128], BF16)
    _copy(g, "dve", g.ident_bf[:], g.ident)
    g.ones_bf = _sb(g, "ones_bf", [128, 128], BF16)
    _memset(g, "pool", g.ones_bf[:], 1.0)
    g.ones_f = _sb(g, "ones_f", [128, 128])
    _memset(g, "pool", g.ones_f[:], 1.0)
    g.eps = _sb(g, "eps", [128, 1])
    _memset(g, "pool", g.eps[:], EPS)
    rows = []
    rows.append(("cc", dr["cc"], 16))
    for L in range(DEPTH):
        rows.append(("ada_b%d" % L, dr["ada_b"][L].rearrange("(r c) -> r c", c=128), 48))
        rows.append(("n1w%d" % L, dr["norm1_w"][L].rearrange("(r c) -> r c", c=128), 8))
        rows.append(("n2w%d" % L, dr["norm2_w"][L].rearrange("(r c) -> r c", c=128), 8))
    for i in range(2):
        rows.append(("qn%d" % i, dr["mla_q_norm"][i].rearrange("(r c) -> r c", c=128), 3))
        rows.append(("kvn%d" % i, dr["mla_kv_norm"][i].rearrange("(r c) -> r c", c=128), 2))
        rows.append(("conv%d" % i, dr["gdn_conv_w"][i].rearrange("t (r c) -> (t r) c", c=128), 48))
    rows.append(("fn", dr["final_norm"].rearrange("(r c) -> r c", c=128), 8))
    tot = sum(r[2] for r in rows)
    ntile = (tot + 127) // 128
    g.cols = _sb(g, "cols", [128, ntile * 128])
    g.coloff = {}
    with contextlib.ExitStack() as st:
        rt = [_sb(g, "rowst%d" % i, [128, 128], F32, st) for i in range(ntile)]
        for t in rt:
            _memset(g, "dve", t[:], 0.0)
        r = 0
        for key, ap, n in rows:
            g.coloff[key] = r
            done = 0
            while done < n:
                t, p = divmod(r + done, 128)
                m = min(n - done, 128 - p)
                P.dma(rt[t][p:p + m, :], ap[done:done + m, :])
                done += m
            r += n
        for t in range(ntile):
            ps = _psum(g)
            _tr(g, ps[:, 0:128], rt[t][:], g.ident)
            _copy(g, "dve", g.cols[:, t * 128:(t + 1) * 128], ps[:, 0:128])
        P.barrier()
    g.sc = _sb(g, "sc", [128, KC, 2])
    _act(g, g.sc[:, :, 0], g.cols[:, 0:8], AF.Silu)
    _act(g, g.sc[:, :, 1], g.cols[:, 8:16], AF.Silu)
    g.mod = [_sb(g, "mod%d" % L, [128, 48, 2]) for L in range(DEPTH)]
    g.comb1 = [_sb(g, "comb1_%d" % L, [128, KC, 2]) for L in range(DEPTH)]
    g.comb2 = [_sb(g, "comb2_%d" % L, [128, KC, 2]) for L in range(DEPTH)]
    g.adaw_pool = Pool_(g, "adaw", [128, KC, 128], F32, 2)
    g.stage = Pool_(g, "stage", [128, 1024], F32, 3)


def _col(g, key, n):
    o = g.coloff[key]
    return g.cols[:, o:o + n]


def _adaln_steps(g, L):
    P, dr = g.P, g.dr
    src = dr["ada_w"][L].rearrange("(k p) n -> p k n", p=128)
    for cb in range(48):
        wt = g.adaw_pool.get()
        P.dma(wt[:], src[:, :, cb * 128:(cb + 1) * 128])
        ps = _psum(g)
        _mmk(g, ps[:, 0:2], [(wt[:, k, :], g.sc[:, k, :]) for k in range(KC)])
        bias = _col(g, "ada_b%d" % L, 48)[:, cb:cb + 1]
        _tt(g, "dve", g.mod[L][:, cb, :], ps[:, 0:2], bias.to_broadcast([128, 2]), ALU.add)
        if cb == 47:
            n1 = _col(g, "n1w%d" % L, 8).unsqueeze(2).to_broadcast([128, KC, 2])
            n2 = _col(g, "n2w%d" % L, 8).unsqueeze(2).to_broadcast([128, KC, 2])
            _stt(g, "dve", g.comb1[L][:], g.mod[L][:, 8:16, :], 1.0, n1, ALU.add, ALU.mult)
            _stt(g, "dve", g.comb2[L][:], g.mod[L][:, 32:40, :], 1.0, n2, ALU.add, ALU.mult)
        yield


def _mix_view(g, s0, n):
    return g.dr["MIX"].rearrange("k p t -> p k t")[:, :, s0:s0 + n]


def _h_view(g, s0, n):
    return g.dr["H"].rearrange("k p t -> p k t")[:, :, s0:s0 + n]


def _load_inputs(g):
    P, dr = g.P, g.dr
    with contextlib.ExitStack() as st:
        xin = Pool_(g, "xin", [128, D], F32, 2, st)
        hb = Pool_(g, "hb0_", [128, KC, 128], F32, 2, st)
        for t in range(NT):
            src = dr["ctx"][t * 128:(t + 1) * 128, :] if t < 2 else dr["x"][(t - 2) * 128:(t - 1) * 128, :]
            xt = xin.get()
            P.dma(xt[:], src)
            ht = hb.get()
            for half in range(2):
                ps = _psum(g)
                for q in range(4):
                    k = half * 4 + q
                    _tr(g, ps[:, q * 128:(q + 1) * 128], xt[:, k * 128:(k + 1) * 128], g.ident)
                _copy(g, "act" if half == 0 else "dve", ht[:, half * 4:(half + 1) * 4, :],
                      ps[:].rearrange("p (q t) -> p q t", t=128))
            P.dma(_h_view(g, t * 128, 128), ht[:])
        P.barrier()


def _norm_tmps(g, st):
    return (Pool_(g, "sqn", [128, 512], BF16, 2, st), Pool_(g, "tmn", [128, 512], F32, 3, st), _sb(g, "rstdn", [128, 512], F32, st))


def _norm_mod(g, hblk, N, comb, shift, out_bf, tmps, ab=None):
    sqp, tp, rstd = tmps
    ps = _psum(g)
    for k in range(KC):
        sq = sqp.get()
        _act(g, sq[:, :N], hblk[:, k, :N], AF.Square)
        _mm(g, ps[:, :N], g.ones_bf[:], sq[:, :N], start=(k == 0), stop=(k == KC - 1))
    _rstd_from_sum(g, rstd[:, :N], ps[:, :N], 1.0 / D, g.eps[:])
    if ab is not None:
        w32, AB, t0, a32p = ab
        psab = _psum(g, "misc")
    for k in range(KC):
        t = tp.get()
        _stt(g, "dve", t[:, :N], hblk[:, k, :N], comb[:, k:k + 1], rstd[:, :N], ALU.mult, ALU.mult)
        if ab is None:
            _act(g, out_bf[:, k, :N], t[:, :N], AF.Identity, scale=1.0, bias=shift[:, k:k + 1])
        else:
            a32 = a32p.get()
            _act(g, a32[:, :N], t[:, :N], AF.Identity, scale=1.0, bias=shift[:, k:k + 1])
            _copy(g, "pool", out_bf[:, k, :N], a32[:, :N])
            for tt in range(N // 128):
                _mm(g, psab[:, tt * 32:(tt + 1) * 32], a32[:, tt * 128:(tt + 1) * 128], w32[:, k, :], start=(k == 0 and tt == 0), stop=(k == KC - 1 and tt == N // 128 - 1))
    if ab is not None:
        nt = N // 128
        _copy(g, "act", AB[:, t0:t0 + nt, :], psab[:, 0:nt * 32].rearrange("p (t c) -> p t c", c=32))


def _emit(g, n_layers, fake_mixer, debug_h):
    P, dr = g.P, g.dr
    _setup(g)
    ada = _adaln_steps(g, 0)
    for _ in ada:
        pass
    _load_inputs(g)
    for L in range(n_layers):
        last = (L == DEPTH - 1)
        need_ctx = not last
        nxt = _adaln_steps(g, L + 1) if L + 1 < n_layers else iter(())
        with contextlib.ExitStack() as lstA:
            lat = _mla_alloc(g, lstA) if (L % 2 == 0 and not fake_mixer) else None
            with contextlib.ExitStack() as lst:
                aT = _sb(g, "aT", [128, KC, TOK], BF16, lst)
                AB = None
                if L % 2 == 1 and not fake_mixer and USE_AB32:
                    AB = _sb(g, "AB", [128, NT, 32], F32, lst)
                with contextlib.ExitStack() as st:
                    hbp = Pool_(g, "hbn", [128, KC, 512], F32, 2, st)
                    tmps = _norm_tmps(g, st)
                    ab = None
                    if AB is not None:
                        w32 = _sb(g, "wab32", [128, KC, 32], F32, st)
                        P.dma(w32[:], dr["gdn_w_in"][L // 2].rearrange("(k p) n -> p k n", p=128)[:, :, O_A:O_A + 32])
                        a32p = Pool_(g, "a32", [128, 512], F32, 2, st)
                    for bi, (s0, N) in enumerate(BLOCKS):
                        s = 1 if bi == 0 else 0
                        hb = hbp.get()
                        P.dma(hb[:, :, :N], _h_view(g, s0, N))
                        if AB is not None:
                            ab = (w32, AB, s0 // 128, a32p)
                        _norm_mod(g, hb, N, g.comb1[L][:, :, s], g.mod[L][:, 0:8, s], aT[:, :, s0:s0 + N], tmps, ab)
                    P.barrier()
                if fake_mixer:
                    for (s0, N) in BLOCKS:
                        P.dma(_mix_view(g, s0, N), aT[:, :, s0:s0 + N])
                elif L % 2 == 0:
                    _even_mixer_a(g, L, aT, lat, lst)
                else:
                    _odd_mixer(g, L, aT, need_ctx, AB)
                P.barrier()
            if L % 2 == 0 and not fake_mixer and "attn" not in DBG_SKIP:
                _mla_attention(g, L, lat)
                P.barrier()
        wname = "even_w_out" if L % 2 == 0 else "gdn_w_out"
        with contextlib.ExitStack() as st:
            wo = _sb(g, "wo", [128, KC, D], BF16, st)
            _load_cast(g, wo[:], dr[wname][L // 2].rearrange("(k p) n -> p k n", p=128))
            hbp = Pool_(g, "hba", [128, KC, 512], F32, 2, st)
            mxp = Pool_(g, "mxb", [128, KC, 512], BF16, 2, st)
            for bi, (s0, N) in enumerate(BLOCKS):
                if bi == 0 and not need_ctx:
                    continue
                s = 1 if bi == 0 else 0
                hb = hbp.get()
                P.dma(hb[:, :, :N], _h_view(g, s0, N))
                mx = mxp.get()
                P.dma(mx[:, :, :N], _mix_view(g, s0, N))
                for m in range(KC):
                    ps = _psum(g)
                    _mmk(g, ps[:, :N], [(wo[:, k, m * 128:(m + 1) * 128], mx[:, k, :N]) for k in range(KC)])
                    _stt(g, "dve", hb[:, m, :N], ps[:, :N], g.mod[L][:, 16 + m, s:s + 1], hb[:, m, :N], ALU.mult, ALU.add)
                P.dma(_h_view(g, s0, N), hb[:, :, :N])
                for _ in range(5):
                    next(nxt, None)
            P.barrier()
        if debug_h == ("mix", L):
            _dump_h(g)
            return
        with contextlib.ExitStack() as st:
            w2 = _sb(g, "w2", [128, 32, D], BF16, st)
            stage_keep = g.stage
            g.stage = Pool_(g, "stgm", [128, 1024], F32, 5, st)
            w2src = dr["mlp_w2"][L].rearrange("(f p) n -> p f n", p=128)
            _load_cast(g, w2[:], w2src)
            w1p = Pool_(g, "w1p", [128, KC, 512], BF16, 2, st)
            w1src = dr["mlp_w1"][L].rearrange("(k p) n -> p k n", p=128)
            hbp = Pool_(g, "hbm", [128, KC, 512], F32, 1, st)
            a2 = _sb(g, "a2", [128, KC, 512], BF16, st)
            h1 = _sb(g, "h1", [128, 32, 512], BF16, st)
            rl = Pool_(g, "rl", [128, 512], F32, 3, st)
            tmps = _norm_tmps(g, st)
            for bi, (s0, N) in enumerate(BLOCKS):
                if bi == 0 and not need_ctx:
                    continue
                s = 1 if bi == 0 else 0
                hb = hbp.get()
                P.dma(hb[:, :, :N], _h_view(g, s0, N))
                _norm_mod(g, hb, N, g.comb2[L][:, :, s], g.mod[L][:, 24:32, s], a2, tmps)
                for fg in range(8):
                    w1 = w1p.get()
                    _load_cast(g, w1[:], w1src[:, :, fg * 512:(fg + 1) * 512])
                    for f in range(4):
                        ps = _psum(g)
                        _mmk(g, ps[:, :N], [(w1[:, k, f * 128:(f + 1) * 128], a2[:, k, :N]) for k in range(KC)])
                        r = rl.get()
                        _act(g, r[:, :N], ps[:, :N], AF.Relu)
                        _tt(g, "pool", h1[:, fg * 4 + f, :N], r[:, :N], r[:, :N], ALU.mult)
                for m in range(KC):
                    ps = _psum(g)
                    _mmk(g, ps[:, :N], [(w2[:, f, m * 128:(m + 1) * 128], h1[:, f, :N]) for f in range(32)])
                    _stt(g, "dve", hb[:, m, :N], ps[:, :N], g.mod[L][:, 40 + m, s:s + 1], hb[:, m, :N], ALU.mult, ALU.add)
                P.dma(_h_view(g, s0, N), hb[:, :, :N])
                for _ in range(5):
                    next(nxt, None)
            for _ in nxt:
                pass
            P.barrier()
            g.stage = stage_keep
        if debug_h == ("mlp", L):
            _dump_h(g)
            return
    if debug_h:
        _dump_h(g)
        return
    _final(g)


def _dump_h(g):
    P = g.P
    with contextlib.ExitStack() as st:
        hb = Pool_(g, "hbd", [128, KC, 512], F32, 2, st)
        for (s0, N) in BLOCKS:
            t = hb.get()
            P.dma(t[:, :, :N], _h_view(g, s0, N))
            P.dma(g.dr["dbg"].rearrange("k p t -> p k t")[:, :, s0:s0 + N], t[:, :, :N], is_output=True)
        P.dma(g.dr["out"][0:128, :], g.cst[:, 0:D], is_output=True)


def _final(g):
    P, dr = g.P, g.dr
    with contextlib.ExitStack() as st:
        hbp = Pool_(g, "hbf", [128, KC, 512], F32, 2, st)
        sq = _sb(g, "sqf", [128, KC, 512], BF16, st)
        yb = _sb(g, "yf", [128, KC, 512], F32, st)
        rstd = _sb(g, "rstdf", [128, 512], F32, st)
        otp = Pool_(g, "ot", [128, D], F32, 2, st)
        fn = _col(g, "fn", 8)
        for (s0, N) in BLOCKS[1:]:
            hb = hbp.get()
            P.dma(hb[:, :, :N], _h_view(g, s0, N))
            _act(g, sq[:, :, :N], hb[:, :, :N], AF.Square)
            ps = _psum(g)
            _mmk(g, ps[:, :N], [(g.ones_bf[:], sq[:, k, :N]) for k in range(KC)])
            _rstd_from_sum(g, rstd[:, :N], ps[:, :N], 1.0 / D, g.eps[:])
            for k in range(KC):
                _stt(g, "dve", yb[:, k, :N], hb[:, k, :N], fn[:, k:k + 1], rstd[:, :N], ALU.mult, ALU.mult)
            for tt in range(N // 128):
                ot = otp.get()
                for half in range(2):
                    ps = _psum(g)
                    for q in range(4):
                        k = half * 4 + q
                        _tr(g, ps[:, q * 128:(q + 1) * 128], yb[:, k, tt * 128:(tt + 1) * 128], g.ident)
                    _copy(g, "act" if half == 0 else "dve", ot[:, half * 512:(half + 1) * 512], ps[:])
                r0 = s0 - NCTX + tt * 128
                P.dma(dr["out"][r0:r0 + 128, :], ot[:], is_output=True)


C_CQ, C_CKV, C_KPE, C_GQ, C_GK, C_GV, C_GG, C_GLOW = 0, 384, 640, 672, 928, 1184, 1696, 2208
GLA_FWD = list(range(NT))
GLA_BWD = [1, 0] + list(range(NT - 1, 1, -1))


def _mla_alloc(g, st):
    lat = Ctx()
    lat.cqn = _sb(g, "cqn", [128, 3, TOK], BF16, st)
    lat.ckvn = _sb(g, "ckvn", [128, 2, TOK], BF16, st)
    lat.kpeT = _sb(g, "kpeT", [128, TOK], BF16, st)
    return lat


def _rope_tables(g, lat, st):
    lat.ropeC = _sb(g, "ropeC", [128, SEQ], F32, st)
    lat.ropeS = _sb(g, "ropeS", [128, SEQ], F32, st)
    g.P.dma(lat.ropeC[64:96, :], g.dr["rope"][0])
    g.P.dma(lat.ropeS[64:96, :], g.dr["rope"][1])


def _swap_halves(g, en, dst, src):
    d = dst.rearrange("p h (a t f) -> p h a t f", a=2, t=2)
    s_ = src.rearrange("p h (a t f) -> p h a t f", a=2, t=2)
    _copy(g, en, d[:, :, :, 0, :], s_[:, :, :, 1, :])
    _copy(g, en, d[:, :, :, 1, :], s_[:, :, :, 0, :])


def _even_mixer_a(g, L, aT, lat, st):
    P, dr = g.P, g.dr
    i = L // 2
    win = _sb(g, "win", [128, KC, EVEN_IN], BF16, st)
    wsrc = dr["even_w_in"][i].rearrange("(k p) n -> p k n", p=128)
    for k in range(KC):
        _load_cast(g, win[:, k, :], wsrc[:, k, :])
    with contextlib.ExitStack() as s1:
        _rope_tables(g, lat, s1)
        winrot = _sb(g, "winrot", [128, KC, 32], BF16, s1)
        _swap_halves(g, "pool", winrot[:], win[:, :, C_KPE:C_KPE + 32])
        cqf = _sb(g, "cqf", [128, 5, 512], F32, s1)
        sqp = Pool_(g, "sql", [128, 512], BF16, 2, s1)
        rq = _sb(g, "rq", [128, 512], F32, s1)
        rkv = _sb(g, "rkv", [128, 512], F32, s1)
        t1p = Pool_(g, "t1l", [128, 512], F32, 2, s1)
        qn = _col(g, "qn%d" % i, 3)
        kvn = _col(g, "kvn%d" % i, 2)
        for bi, (s0, N) in enumerate(BLOCKS):
            rhs = [aT[:, k, s0:s0 + N] for k in range(KC)]
            pss_q = _psum(g, "acc")
            pss_kv = _psum(g, "acc")
            for fc in range(5):
                ps = _psum(g)
                _mmk(g, ps[:, :N], [(win[:, k, fc * 128:(fc + 1) * 128], rhs[k]) for k in range(KC)])
                _copy(g, "act", cqf[:, fc, :N], ps[:, :N])
                sq = sqp.get()
                _tt(g, "pool", sq[:, :N], cqf[:, fc, :N], cqf[:, fc, :N], ALU.mult)
                if fc < 3:
                    _mm(g, pss_q[:, :N], g.ones_bf[:], sq[:, :N], start=(fc == 0), stop=(fc == 2))
                else:
                    _mm(g, pss_kv[:, :N], g.ones_bf[:], sq[:, :N], start=(fc == 3), stop=(fc == 4))
            _rstd_from_sum(g, rq[:, :N], pss_q[:, :N], 1.0 / 384, g.eps[:])
            _rstd_from_sum(g, rkv[:, :N], pss_kv[:, :N], 1.0 / 256, g.eps[:])
            for fc in range(3):
                _stt(g, "dve", lat.cqn[:, fc, s0:s0 + N], cqf[:, fc, :N], qn[:, fc:fc + 1], rq[:, :N], ALU.mult, ALU.mult)
            for fc in range(2):
                _stt(g, "dve", lat.ckvn[:, fc, s0:s0 + N], cqf[:, 3 + fc, :N], kvn[:, fc:fc + 1], rkv[:, :N], ALU.mult, ALU.mult)
            ps = _psum(g)
            _mmk(g, ps[64:96, :N], [(win[:, k, C_KPE:C_KPE + 32], rhs[k]) for k in range(KC)])
            if bi == 0:
                _copy(g, "act", lat.kpeT[64:96, s0:s0 + N], ps[64:96, :N])
            else:
                ps2 = _psum(g)
                _mmk(g, ps2[64:96, :N], [(winrot[:, k, :], rhs[k]) for k in range(KC)])
                p0 = s0 - NCTX
                t1 = t1p.get()
                t2 = t1p.get()
                _tt(g, "dve", t1[64:96, :N], ps[64:96, :N], lat.ropeC[64:96, p0:p0 + N], ALU.mult)
                _tt(g, "dve", t2[64:96, :N], ps2[64:96, :N], lat.ropeS[64:96, p0:p0 + N], ALU.mult)
                _tt(g, "pool", lat.kpeT[64:96, s0:s0 + N], t1[64:96, :N], t2[64:96, :N], ALU.add)
        P.barrier()
    with contextlib.ExitStack() as s2:
        if "gla" not in DBG_SKIP:
            try:
                _gla(g, i, aT, win, s2)
            except _Stop:
                pass
        P.barrier()


def _gla(g, i, aT, win, st):
    P, dr = g.P, g.dr
    gup = _sb(g, "gup", [16, 2, 256], F32, st)
    P.dma(gup[:], dr["gla_gate_up"][i].rearrange("z r n -> r z n"))
    gbias = _sb(g, "gbias", [128, 2, 256], F32, st)
    P.dma(gbias[:], dr["gla_gate_bias"][i].partition_broadcast(128))
    onorm = _sb(g, "onorm", [128, 128], F32, st)
    P.dma(onorm[:], dr["gla_o_norm"][i].partition_broadcast(128))
    Of = _sb(g, "Of", [128, NT, 512], F32, st)
    S = _sb(g, "S", [128, 2, 128], F32, st)
    Sbf = _sb(g, "Sbf", [128, 2, 128], BF16, st)
    glowT = Pool_(g, "glowT", [16, 128], F32, 2, st)
    tl = Pool_(g, "tl", [128, 256], F32, 2, st)
    Lz = Pool_(g, "Lz", [128, 256], F32, 2, st)
    Eq = Pool_(g, "Eq", [128, 2, 128], F32, 2, st)
    Ek = Pool_(g, "Ek", [128, 2, 128], F32, 2, st)
    gend = Pool_(g, "gend", [128, 2, 2], F32, 2, st)
    kendE = Pool_(g, "kendE", [128, 256], F32, 2, st)
    qdecT = Pool_(g, "qdecT", [128, 4, 128], BF16, 2, st, zero=True)
    kinvT = Pool_(g, "kinvT", [128, 4, 128], BF16, 2, st, zero=True)
    kend = Pool_(g, "kend", [128, 2, 256], BF16, 2, st)
    CI2 = _cst(g, "CI").unsqueeze(2).to_broadcast([128, 2, 256])
    Vg = Pool_(g, "Vg", [128, 512], BF16, 2, st)
    AmT = Pool_(g, "AmT", [128, 4, 128], BF16, 2, st)
    G2 = Pool_(g, "G2", [128, 512], F32, 1, st)
    ob = Pool_(g, "ob", [128, 512], F32, 1, st)
    sqo = Pool_(g, "sqo", [128, 512], F32, 1, st)
    ss4 = Pool_(g, "ss4", [128, 4], F32, 2, st)
    mtok = Pool_(g, "mtok", [128, 512], BF16, 2, st)
    mt = Pool_(g, "mt", [128, 4, 128], BF16, 2, st)
    LN8 = float(np.log(0.125))
    for z in range(2):
        R_ = _cst(g, "glaRf" if z == 0 else "glaRb")
        A_ = _cst(g, "glaAf" if z == 0 else "glaAb")
        Mz = _cst(g, "Mf" if z == 0 else "Mb")
        _memset(g, "pool", S[:], 0.0)
        _memset(g, "pool", Sbf[:], 0.0)
        for t in (GLA_FWD if z == 0 else GLA_BWD):
            ts = slice(t * 128, (t + 1) * 128)
            at = [aT[:, k, ts] for k in range(KC)]
            ps = _psum(g)
            c0 = C_GLOW + 16 * z
            _mmk(g, ps[0:16, 0:128], [(win[:, k, c0:c0 + 16], at[k]) for k in range(KC)])
            gl = glowT.get()
            _copy(g, "act", gl[:], ps[0:16, 0:128])
            _chk("gla_%s_%d" % ("a", z))
            ps = _psum(g)
            _mm(g, ps[:, 0:256], gl[:], gup[:, z, :])
            tt_ = tl.get()
            _tt(g, "dve", tt_[:], ps[:, 0:256], gbias[:, z, :], ALU.add)
            L_ = Lz.get()
            _act(g, tt_[:], tt_[:], AF.Exp, scale=-1.0)
            _act(g, L_[:], tt_[:], AF.Ln, scale=1.0, bias=1.0)
            _chk("gla_%s_%d" % ("b", z))
            ps = _psum(g)
            for hp in range(2):
                _mm(g, ps[:, hp * 130:(hp + 1) * 130], L_[:, hp * 128:(hp + 1) * 128], R_)
            pv = ps[:, 0:260].rearrange("p (h c) -> p h c", c=130)
            eq, ek, ge = Eq.get(), Ek.get(), gend.get()
            _act(g, eq[:], pv[:, :, 0:128], AF.Exp, scale=1.0, bias=LN8)
            _act(g, ek[:], pv[:, :, 0:128], AF.Exp, scale=-1.0)
            _act(g, ge[:], pv[:, :, 128:130], AF.Exp)
            _chk("gla_%s_%d" % ("c", z))
            ps = _psum(g)
            _mm(g, ps[:, 0:256], A_, L_[:])
            ke = kendE.get()
            _act(g, ke[:], ps[:, 0:256], AF.Exp)
            _chk("gla_%s_%d" % ("d", z))
            ps = _psum(g)
            for hp in range(2):
                _mmk(g, ps[:, hp * 128:(hp + 1) * 128], [(win[:, k, C_GQ + hp * 128:C_GQ + (hp + 1) * 128], at[k]) for k in range(KC)])
                _mmk(g, ps[:, 256 + hp * 128:256 + (hp + 1) * 128], [(win[:, k, C_GK + hp * 128:C_GK + (hp + 1) * 128], at[k]) for k in range(KC)])
            qd, ki = qdecT.get(), kinvT.get()
            for (lo, hi, par) in ((0, 64, 0), (64, 128, 1)):
                _tt(g, "dve", qd[lo:hi, par::2, :], ps[lo:hi, 0:256].rearrange("p (h c) -> p h c", c=128), eq[lo:hi, :, :], ALU.mult)
                _tt(g, "dve", ki[lo:hi, par::2, :], ps[lo:hi, 256:512].rearrange("p (h c) -> p h c", c=128), ek[lo:hi, :, :], ALU.mult)
            _chk("gla_%s_%d" % ("e", z))
            ps = _psum(g)
            _mmk(g, ps[:, 0:256], [(at[k], win[:, k, C_GK:C_GK + 256]) for k in range(KC)])
            kn = kend.get()
            _tt(g, "dve", ke[:], ps[:, 0:256], ke[:], ALU.mult)
            _tt(g, "pool", kn[:], ke[:].unsqueeze(1).to_broadcast([128, 2, 256]), CI2, ALU.mult)
            _chk("gla_%s_%d" % ("f", z))
            ps = _psum(g)
            _mmk(g, ps[:, 0:512], [(at[k], win[:, k, C_GV:C_GV + 512]) for k in range(KC)])
            vg = Vg.get()
            _copy(g, "act", vg[:], ps[:, 0:512])
            _chk("gla_%s_%d" % ("g", z))
            ps = _psum(g)
            for h in range(4):
                hp, hb = h // 2, (h % 2) * 64
                _mm(g, ps[:, h * 128:(h + 1) * 128], ki[:, h, :], qd[:, h, :])
            am = AmT.get()
            if "h_dve" not in DBG_SKIP:
                _tt(g, "dve", am[:], ps[:].rearrange("p (h c) -> p h c", c=128), Mz.unsqueeze(1).to_broadcast([128, 4, 128]), ALU.mult)
            else:
                _copy(g, "act", am[:], ps[:].rearrange("p (h c) -> p h c", c=128))
            _chk("gla_%s_%d" % ("h", z))
            if z == 1:
                ps = _psum(g)
                _mmk(g, ps[:, 0:512], [(at[k], win[:, k, C_GG:C_GG + 512]) for k in range(KC)])
                g2 = G2.get()
                _act(g, g2[:], ps[:, 0:512], AF.Silu)
                _tt(g, "pool", g2[:].rearrange("p (h c) -> p h c", c=128), g2[:].rearrange("p (h c) -> p h c", c=128),
                    onorm[:].unsqueeze(1).to_broadcast([128, 4, 128]), ALU.mult)
            _chk("gla_%s_%d" % ("i", z))
            pso = _psum(g, "acc")
            for c in ((0, 1) if z == 0 else (1, 0)):
                cb = c * 64
                psS = _psum(g, "misc")
                for h in range(4):
                    hp, hb = h // 2, (h % 2) * 64
                    _mm(g, pso[cb:cb + 64, h * 128:(h + 1) * 128], qd[:, h, cb:cb + 64], Sbf[:, hp, :], start=True, stop=False)
                    _mm(g, pso[cb:cb + 64, h * 128:(h + 1) * 128], am[:, h, cb:cb + 64], vg[:, h * 128:(h + 1) * 128], start=False, stop=True)
                    _mm(g, psS[hb:hb + 64, hp * 128:(hp + 1) * 128], kn[:, c, h * 64:(h + 1) * 64], vg[:, h * 128:(h + 1) * 128])
                for hp in range(2):
                    _stt(g, "dve", S[:, hp, :], S[:, hp, :], ge[:, hp, c:c + 1], psS[:, hp * 128:(hp + 1) * 128], ALU.mult, ALU.add)
                _copy(g, "pool", Sbf[:], S[:])
            _chk("gla_%s_%d" % ("j", z))
            if z == 0:
                _copy(g, "act", Of[:, t, :], pso[:])
            else:
                o = ob.get()
                _tt(g, "dve", o[:], pso[:], Of[:, t, :], ALU.add)
                sq = sqo.get()
                _tt(g, "pool", sq[:], o[:], o[:], ALU.mult)
                s4 = ss4.get()
                P.op("dve", lambda e: e.tensor_reduce(s4[:], sq[:].rearrange("p (h c) -> p h c", c=128), axis=AX, op=ALU.add), [sq[:]], [s4[:]])
                _rstd_from_sum(g, s4[:], s4[:], 1.0 / 128, g.eps[:])
                o3 = o[:].rearrange("p (h c) -> p h c", c=128)
                _tt(g, "dve", o3, o3, s4[:].unsqueeze(2).to_broadcast([128, 4, 128]), ALU.mult)
                mk = mtok.get()
                _tt(g, "pool", mk[:], o[:], g2[:], ALU.mult)
                for h in range(4):
                    _tr(g, g.psbf[:, h * 128:(h + 1) * 128], mk[:, h * 128:(h + 1) * 128], g.ident_bf[:])
                m_ = mt.get()
                _copy(g, "act", m_[:], g.psbf[:, 0:512].rearrange("p (h c) -> p h c", c=128))
                P.dma(g.dr["MIX"].rearrange("k p t -> p k t")[:, 4:8, ts], m_[:])
            _chk("gla_k_%d" % z)


def _mla_attention(g, L, lat):
    P, dr = g.P, g.dr
    i = L // 2
    with contextlib.ExitStack() as st0:
      QT = _sb(g, "QT", [128, 8, TOK], BF16, st0)
      KT = _sb(g, "KT", [128, 8, TOK], BF16, st0)
      VA = _sb(g, "VA", [128, NT, 8, 128], BF16, st0)
      with contextlib.ExitStack() as st:
        _rope_tables(g, lat, st)
        wuq = _sb(g, "wuq", [128, 3, 768], BF16, st)
        _load_cast(g, wuq[:], dr["mla_w_uq"][i].rearrange("(k p) n -> p k n", p=128))
        wuqr = _sb(g, "wuqr", [128, 3, 768], BF16, st)
        _copy(g, "pool", wuqr[:], wuq[:])
        for k in range(3):
            _swap_halves(g, "pool", wuqr[:, k, :].rearrange("p (h c) -> p h c", c=96)[:, :, 64:96],
                         wuq[:, k, :].rearrange("p (h c) -> p h c", c=96)[:, :, 64:96])
        wukv = _sb(g, "wukv", [128, 2, 1024], BF16, st)
        _load_cast(g, wukv[:], dr["mla_w_ukv"][i].rearrange("(k p) n -> p k n", p=128))
        wv = _sb(g, "wv", [128, 2, 512], BF16, st)
        for k in range(2):
            _copy(g, "pool", wv[:, k, :].rearrange("p (h c) -> p h c", c=64), wukv[:, k, :].rearrange("p (h c) -> p h c", c=128)[:, :, 64:128])
        _memset(g, "pool", VA[:, :, :, 64:128].rearrange("p t h c -> p (t h) c"), 1.0)
        t1p = Pool_(g, "t1a", [128, 512], F32, 2, st)
        for bi, (s0, N) in enumerate(BLOCKS):
            sl = slice(s0, s0 + N)
            p0 = s0 - NCTX
            for h in range(8):
                ps = _psum(g)
                _mmk(g, ps[0:96, :N], [(wuq[:, fc, h * 96:(h + 1) * 96], lat.cqn[:, fc, sl]) for fc in range(3)])
                _copy(g, "act", QT[0:64, h, sl], ps[0:64, :N])
                if bi == 0:
                    _copy(g, "act", QT[64:96, h, sl], ps[64:96, :N])
                else:
                    ps2 = _psum(g)
                    _mmk(g, ps2[0:96, :N], [(wuqr[:, fc, h * 96:(h + 1) * 96], lat.cqn[:, fc, sl]) for fc in range(3)])
                    t1, t2 = t1p.get(), t1p.get()
                    _tt(g, "dve", t1[64:96, :N], ps[64:96, :N], lat.ropeC[64:96, p0:p0 + N], ALU.mult)
                    _tt(g, "dve", t2[64:96, :N], ps2[64:96, :N], lat.ropeS[64:96, p0:p0 + N], ALU.mult)
                    _tt(g, "pool", QT[64:96, h, sl], t1[64:96, :N], t2[64:96, :N], ALU.add)
                ps = _psum(g)
                _mmk(g, ps[0:64, :N], [(wukv[:, c, h * 128:h * 128 + 64], lat.ckvn[:, c, sl]) for c in range(2)])
                _copy(g, "act", KT[0:64, h, sl], ps[0:64, :N])
                _copy(g, "pool", KT[64:96, h, sl], lat.kpeT[64:96, sl])
            for tt_ in range(N // 128):
                t = s0 // 128 + tt_
                ps = _psum(g)
                _mmk(g, ps[:, 0:512], [(lat.ckvn[:, c, t * 128:(t + 1) * 128], wv[:, c, :]) for c in range(2)])
                _copy(g, "dve", VA[:, t, :, 0:64], ps[:, 0:512].rearrange("p (h c) -> p h c", c=64))
        P.barrier()
      with contextlib.ExitStack() as st:
        scale = float(96 ** -0.5)
        pT = Pool_(g, "pT", [128, 512], BF16, 3, st)
        rec = Pool_(g, "rec", [128, 512], F32, 2, st)
        atile = Pool_(g, "atile", [128, 512], BF16, 2, st)
        for bi, (s0, N) in enumerate(BLOCKS):
            sl = slice(s0, s0 + N)
            nk = 2 if bi == 0 else NT
            for h in range(8):
                if h % 2 == 0:
                    at_ = atile.get()
                pso = _psum(g, "acc")
                for kc in range(nk):
                    pss = _psum(g)
                    _mm(g, pss[:, :N], KT[0:96, h, kc * 128:(kc + 1) * 128], QT[0:96, h, sl])
                    p_ = pT.get()
                    _act(g, p_[:, :N], pss[:, :N], AF.Exp, scale=scale)
                    _mm(g, pso[:, :N], VA[:, kc, h, :], p_[:, :N], start=(kc == 0), stop=(kc == nk - 1))
                r_ = rec.get()
                P.op("dve", lambda e: e.reciprocal(r_[64:128, :N], pso[64:128, :N]), [pso[64:128, :N]], [r_[64:128, :N]])
                hb = (h % 2) * 64
                _tt(g, "dve", at_[hb:hb + 64, :N], pso[0:64, :N], r_[64:128, :N], ALU.mult)
                if h % 2 == 1:
                    P.dma(g.dr["MIX"].rearrange("k p t -> p k t")[:, h // 2, sl], at_[:, :N])


O_Q, O_K, O_V, O_G, O_A, O_B = 0, 512, 1024, 2048, 3072, 3088


def _odd_mixer(g, L, aT, need_ctx, AB):
    import itertools
    P, dr = g.P, g.dr
    i = L // 2
    wsrc = dr["gdn_w_in"][i].rearrange("(k p) n -> p k n", p=128)
    with contextlib.ExitStack() as st:
        blockones = _sb(g, "blockones", [128, 128], BF16, st)
        _memset(g, "pool", blockones[:], 0.0)
        _memset(g, "pool", blockones[0:64, 0:64], 1.0)
        _memset(g, "pool", blockones[64:128, 64:128], 1.0)
        wab = _sb(g, "wab", [128, KC, 32], BF16, st)
        _load_cast(g, wab[:], wsrc[:, :, O_A:O_A + 32])
        dtb = _sb(g, "dtb", [128, 16], F32, st)
        P.dma(dtb[:], dr["gdn_dt_bias"][i].rearrange("z h -> (z h)").partition_broadcast(128))
        negA = _sb(g, "negA", [128, 16], F32, st)
        P.dma(negA[:], dr["gdn_a_log"][i].rearrange("z h -> (z h)").partition_broadcast(128))
        _act(g, negA[:], negA[:], AF.Exp)
        _ts(g, "dve", negA[:], negA[:], -1.0, None, ALU.mult)
        onorm = _sb(g, "onormd", [128, 128], F32, st)
        P.dma(onorm[:], dr["gdn_o_norm"][i].partition_broadcast(128))
        g.bmask = _sb(g, "bmask", [128, 12, 128], BF16, st)
        _load_cast(g, g.bmask[:], dr["bmask"].rearrange("p (m c) -> p m c", c=128))
        nhp = 1
        NH = 2 * nhp
        for pair in range(2):
            with contextlib.ExitStack() as sh:
                units = []
                for u_ in range(2):
                    q0 = (2 * pair + u_) * nhp
                    d = Ctx()
                    d.q0 = q0
                    d.qT = _sb(g, "gqT", [128, nhp, TOK], BF16, sh)
                    d.kT = _sb(g, "gkT", [128, nhp, TOK], BF16, sh)
                    d.Vt = _sb(g, "gVt", [128, NT, NH * 128], BF16, sh)
                    d.Kt = _sb(g, "gKt", [128, NT, NH * 64], BF16, sh)
                    d.Of = _sb(g, "gOf", [128, NT, NH * 128], F32, sh)
                    d.wg = _sb(g, "gwg", [128, KC, NH * 128], BF16, sh)
                    _load_cast(g, d.wg[:], wsrc[:, :, O_G + 2 * q0 * 128:O_G + (2 * q0 + NH) * 128])
                    d.banks = (g.ps[3 + 2 * u_], g.ps[4 + 2 * u_])
                    units.append(d)
                with contextlib.ExitStack() as s1:
                    tmp = _gdn_conv_tmps(g, s1)
                    for d in units:
                        _gdn_conv_stage(g, i, d.q0, nhp, aT, wsrc, d.qT, d.kT, d.Vt, d.Kt, blockones, tmp)
                    P.barrier()
                with contextlib.ExitStack() as s2:
                    g.rot_n = 3
                    g.ps_rr = 0
                    gens = [_gdn_scan(g, d, nhp, aT, wab, dtb, negA, onorm, need_ctx, s2) for d in units]
                    for _ in itertools.zip_longest(*gens):
                        pass
                    g.rot_n = 4
                    g.ps_rr = 0
                    P.barrier()


def _gdn_conv_tmps(g, st):
    t = Ctx()
    t.xl = _sb(g, "xl", [128, SEQ + 2], F32, st)
    t.xc = _sb(g, "xc", [128, NCTX + 2], F32, st)
    for buf, n in ((t.xl, SEQ), (t.xc, NCTX)):
        _memset(g, "pool", buf[:, 0:1], 0.0)
        _memset(g, "pool", buf[:, n + 1:n + 2], 0.0)
    t.tcv = _sb(g, "tcv", [128, SEQ], F32, st)
    t.ybf = _sb(g, "ybf", [128, TOK], BF16, st)
    t.yf = _sb(g, "yf32", [128, 512], F32, st)
    t.sqp = _sb(g, "sqc", [128, 512], BF16, st)
    t.rn = _sb(g, "rnc", [128, 512], F32, st)
    t.wtp = Pool_(g, "wcv", [128, KC, 128], BF16, 2, st)
    return t


def _gdn_conv_stage(g, i, q0, nhp, aT, wsrc, qT, kT, Vt, Kt, blockones, tmp):
    P = g.P
    xl, xc, tcv, ybf, yf, sqp, rn, wtp = tmp.xl, tmp.xc, tmp.tcv, tmp.ybf, tmp.yf, tmp.sqp, tmp.rn, tmp.wtp
    cols = _col(g, "conv%d" % i, 48)
    chunks = [("q", O_Q // 128 + q0 + j, j) for j in range(nhp)] + [("k", O_K // 128 + q0 + j, j) for j in range(nhp)] + \
             [("v", O_V // 128 + 2 * q0 + j, j) for j in range(2 * nhp)]
    for kind, cc, j in chunks:
        wt = wtp.get()
        _load_cast(g, wt[:], wsrc[:, :, cc * 128:(cc + 1) * 128])
        for bi, (s0, N) in enumerate(BLOCKS):
            ps = _psum(g)
            _mmk(g, ps[:, :N], [(wt[:, k, :], aT[:, k, s0:s0 + N]) for k in range(KC)])
            if bi == 0:
                _copy(g, "act", xc[:, 1:1 + N], ps[:, :N])
            else:
                p0 = s0 - NCTX
                _copy(g, "act", xl[:, 1 + p0:1 + p0 + N], ps[:, :N])
        w0, w1, w2 = cols[:, cc:cc + 1], cols[:, 16 + cc:17 + cc], cols[:, 32 + cc:33 + cc]
        for buf, n, o0 in ((xc, NCTX, 0), (xl, SEQ, NCTX)):
            t = tcv[:, 0:n]
            _ts(g, "pool", t, buf[:, 1:1 + n], w1, None, ALU.mult)
            _stt(g, "dve", t, buf[:, 0:n], w0, t, ALU.mult, ALU.add)
            _stt(g, "dve", t, buf[:, 2:2 + n], w2, t, ALU.mult, ALU.add)
            if kind == "v":
                _act(g, ybf[:, o0:o0 + n], t, AF.Silu)
            else:
                for c0 in range(0, n, 512):
                    m = min(512, n - c0)
                    _act(g, yf[:, :m], t[:, c0:c0 + m], AF.Silu)
                    _tt(g, "pool", sqp[:, :m], yf[:, :m], yf[:, :m], ALU.mult)
                    ps = _psum(g)
                    _mm(g, ps[:, :m], blockones[:], sqp[:, :m])
                    _rstd_from_sum(g, rn[:, :m], ps[:, :m], 1.0, g.eps[:])
                    dst = (qT if kind == "q" else kT)[:, j, o0 + c0:o0 + c0 + m]
                    if kind == "q":
                        _stt(g, "dve", dst, yf[:, :m], 0.125, rn[:, :m], ALU.mult, ALU.mult)
                    else:
                        _tt(g, "dve", dst, yf[:, :m], rn[:, :m], ALU.mult)
        if kind in ("k", "v"):
            src = kT[:, j, :] if kind == "k" else ybf[:]
            for t4 in range(0, NT, 4):
                nt = min(4, NT - t4)
                for q in range(nt):
                    _tr(g, g.psbf[:, q * 128:(q + 1) * 128], src[:, (t4 + q) * 128:(t4 + q + 1) * 128], g.ident_bf[:])
                dst = (Kt[:, t4:t4 + nt, j * 128:(j + 1) * 128] if kind == "k" else Vt[:, t4:t4 + nt, j * 128:(j + 1) * 128])
                _copy(g, "act", dst, g.psbf[:, 0:nt * 128].rearrange("p (q c) -> p q c", c=128))


def _gdn_scan(g, d, nhp, aT, wab, dtb, negA, onorm, need_ctx, st):
    P = g.P
    NH = 2 * nhp
    W = NH * 128
    KW = NH * 64
    q0, qT, kT, Vt, Kt, Of, wg = d.q0, d.qT, d.kT, d.Vt, d.Kt, d.Of, d.wg
    bank_o, bank_s = d.banks
    S = _sb(g, "dS", [128, nhp, 128], F32, st)
    Sbf = _sb(g, "dSbf", [128, nhp, 128], BF16, st)
    sm = Pool_(g, "dsm", [128, 32], F32, 2, st)
    larep = _sb(g, "larep", [128, NH, 64], F32, st)
    Eg = _sb(g, "dEg", [128, nhp, 128], F32, st)
    gend = Pool_(g, "dgend", [128, nhp, 2], F32, 2, st)
    qdecT = _sb(g, "dqdec", [128, NH, 128], BF16, st)
    _memset(g, "pool", qdecT[:], 0.0)
    kz = _sb(g, "dkz", [128, NH, 128], BF16, st)
    _memset(g, "pool", kz[:], 0.0)
    CI4 = _cst(g, "CI").unsqueeze(2).to_broadcast([128, 2, KW])
    rhsd = _sb(g, "rhsd", [128, NH, 128], F32, st)
    dec = _sb(g, "ddec", [128, NH, 128], F32, st)
    tG = _sb(g, "dtG", [128, NH, 128], F32, st)
    qkm = _sb(g, "dqkm", [128, NH, 128], BF16, st)
    Mp = Pool_(g, "dM", [128, NH, 128], BF16, 2, st)
    Np = Pool_(g, "dN", [128, NH, 128], BF16, 2, st)
    Tnp = Pool_(g, "dTn", [128, NH, 128], BF16, 2, st)
    Amp = Pool_(g, "dAm", [128, NH, 128], BF16, 4, st)
    Y2p = Pool_(g, "dY2", [128, NH, 128], BF16, 2, st)
    NQ = Pool_(g, "dNQ", [128, NH, 256], BF16, 2, st)
    vb = _sb(g, "dvb", [128, W], BF16, st)
    kbg = _sb(g, "dkbg", [128, NH, 64], BF16, st)
    kend = _sb(g, "dkend", [128, NH, 64], BF16, st)
    kend2 = _sb(g, "dkend2", [128, 2, KW], BF16, st)
    u = _sb(g, "du", [128, W], F32, st)
    wT = _sb(g, "dwT", [128, NH, 128], BF16, st)
    _memset(g, "pool", wT[:], 0.0)
    vnew = _sb(g, "dvnew", [128, W], BF16, st)
    _memset(g, "pool", vnew[:], 0.0)
    G2 = _sb(g, "dG2", [128, W], F32, st)
    ob = _sb(g, "dob", [128, W], F32, st)
    sqo = _sb(g, "dsqo", [128, W], F32, st)
    mtok = _sb(g, "dmtok", [128, W], BF16, st)
    mt = Pool_(g, "dmt", [128, NH, 128], BF16, 2, st)
    ident_bc = g.ident.unsqueeze(1).to_broadcast([128, NH, 128])
    v3 = lambda ap: ap.rearrange("p (h c) -> p h c", c=128)
    halves = ((0, 64, 0), (64, 128, 1))
    yield
    for z in range(2):
        R_ = _cst(g, "gdnRf" if z == 0 else "gdnRb")
        Mz = _cst(g, "Mf" if z == 0 else "Mb")
        Sz = _cst(g, "Sf" if z == 0 else "Sb")
        Mz_bc = Mz.unsqueeze(1).to_broadcast([128, NH, 128])
        Sz_bc = Sz.unsqueeze(1).to_broadcast([128, NH, 128])
        _memset(g, "pool", S[:], 0.0)
        _memset(g, "pool", Sbf[:], 0.0)
        for t in (GLA_FWD if z == 0 else GLA_BWD):
            ts = slice(t * 128, (t + 1) * 128)
            at = [aT[:, k, ts] for k in range(KC)]
            want_out = need_ctx or t >= 2
            ps = _psum(g)
            _mmk(g, ps[:, 0:32], [(at[k], wab[:, k, :]) for k in range(KC)])
            s_ = sm.get()
            ca = z * 8 + 2 * q0
            la, beta = s_[:, 0:NH], s_[:, NH:2 * NH]
            egk = s_[:, 2 * NH:4 * NH]
            eg, ekend = s_[:, 2 * NH:3 * NH], s_[:, 3 * NH:4 * NH]
            bg = s_[:, 4 * NH:5 * NH]
            rs = s_[:, 5 * NH:6 * NH]
            _tt(g, "dve", la, ps[:, ca:ca + NH], dtb[:, ca:ca + NH], ALU.add)
            _act(g, beta, ps[:, 16 + ca:16 + ca + NH], AF.Sigmoid)
            yield
            _act(g, la, la, AF.Exp)
            _act(g, la, la, AF.Ln, scale=1.0, bias=1.0)
            _tt(g, "dve", la, la, negA[:, ca:ca + NH], ALU.mult)
            _copy(g, "pool", larep[:], la.unsqueeze(2).to_broadcast([128, NH, 64]))
            yield
            ps = _psum(g)
            lr = larep[:].rearrange("p h d -> p (h d)")
            for hp in range(nhp):
                _mm(g, ps[:, hp * 130:(hp + 1) * 130], lr[:, hp * 128:(hp + 1) * 128], R_)
            pv = ps[:, 0:130 * nhp].rearrange("p (h c) -> p h c", c=130)
            ge = gend.get()
            _act(g, Eg[:], pv[:, :, 0:128], AF.Exp)
            _act(g, ge[:], pv[:, :, 128:130], AF.Exp)
            yield
            for (lo, hi, par) in halves:
                _tt(g, "dve", qdecT[lo:hi, par::2, :], qT[lo:hi, :, ts], Eg[lo:hi, :, :], ALU.mult)
                _copy(g, "pool", kz[lo:hi, par::2, :], kT[lo:hi, :, ts])
            ps = _psum(g)
            _mm(g, ps[:, 0:NH], Mz, la)
            _mm(g, ps[:, NH:2 * NH], Sz, la)
            _act(g, egk, ps[:, 0:2 * NH], AF.Exp)
            yield
            _tt(g, "dve", bg, beta, eg, ALU.mult)
            kt3 = Kt[:, t, :].rearrange("p (h d) -> p h d", d=64)
            _tt(g, "pool", kend[:], kt3, ekend.unsqueeze(2).to_broadcast([128, NH, 64]), ALU.mult)
            _tt(g, "pool", kend2[:], kend[:].rearrange("p h d -> p (h d)").unsqueeze(1).to_broadcast([128, 2, KW]), CI4, ALU.mult)
            _tt(g, "pool", kbg[:], kt3, bg.unsqueeze(2).to_broadcast([128, NH, 64]), ALU.mult)
            _tt(g, "pool", v3(vb[:]), v3(Vt[:, t, :]), beta.unsqueeze(2).to_broadcast([128, NH, 128]), ALU.mult)
            la_bc = la.unsqueeze(2).to_broadcast([128, NH, 128])
            _tt(g, "pool", rhsd[:], Mz_bc, la_bc, ALU.mult)
            yield
            ps = _psum(g)
            _mm(g, ps[:, 0:W], Sz, rhsd[:].rearrange("p h c -> p (h c)"))
            _act(g, dec[:].rearrange("p h c -> p (h c)"), ps[:, 0:W], AF.Exp)
            yield
            ps = _psum(g)
            for h in range(NH):
                _mm(g, ps[:, h * 128:(h + 1) * 128], kz[:, h, :], qT[:, h // 2, ts])
            _tt(g, "dve", tG[:], v3(ps[:, 0:W]), Mz_bc, ALU.mult)
            yield
            _tt(g, "dve", qkm[:], tG[:], dec[:], ALU.mult)
            _tt(g, "pool", rhsd[:], Sz_bc, la_bc, ALU.mult)
            yield
            ps = _psum(g)
            _mm(g, ps[:, 0:W], Mz, rhsd[:].rearrange("p h c -> p (h c)"))
            _act(g, dec[:].rearrange("p h c -> p (h c)"), ps[:, 0:W], AF.Exp)
            yield
            _tt(g, "pool", dec[:], dec[:], beta.unsqueeze(2).to_broadcast([128, NH, 128]), ALU.mult)
            ps = _psum(g)
            for h in range(NH):
                _mm(g, ps[:, h * 128:(h + 1) * 128], kz[:, h, :], kz[:, h, :])
            _tt(g, "dve", tG[:], v3(ps[:, 0:W]), Sz_bc, ALU.mult)
            yield
            M_ = Mp.get()
            _tt(g, "dve", M_[:], tG[:], dec[:], ALU.mult)
            for h in range(NH):
                _tr(g, g.psbf[:, h * 128:(h + 1) * 128], M_[:, h, :], g.ident_bf[:])
            N_ = Np.get()
            _copy(g, "act", N_[:], v3(g.psbf[:, 0:W]))
            yield
            TT = NQ.get()
            Tn = Tnp.get()
            for lv, bsz in enumerate((1, 2, 4, 8, 16, 32)):
                last = (bsz == 32)
                mk_ = g.bmask[:, (lv if z == 0 else 6 + lv), :].unsqueeze(1).to_broadcast([128, NH, 128])
                mkT = g.bmask[:, (6 + lv if z == 0 else lv), :].unsqueeze(1).to_broadcast([128, NH, 128])
                Am = Amp.get()
                _tt(g, "pool", Am[:], M_[:], mkT, ALU.mult)
                if lv == 0 or not last:
                    Nm = Amp.get()
                    _tt(g, "pool", Nm[:], N_[:], mk_, ALU.mult)
                if lv == 0:
                    _tt(g, "dve", TT[:, :, 128:256], ident_bc, Nm[:], ALU.subtract)
                    _tt(g, "dve", Tn[:], ident_bc, Am[:], ALU.subtract)
                    yield
                    continue
                TT2 = NQ.get()
                psY = _psum(g)
                for h in range(NH):
                    _mm(g, psY[:, h * 128:(h + 1) * 128], Am[:, h, :], TT[:, h, 128:256])
                _copy(g, "act", TT[:, :, 0:128], v3(psY[:, 0:W]))
                yield
                if not last:
                    psY2 = _psum(g)
                    for h in range(NH):
                        _mm(g, psY2[:, h * 128:(h + 1) * 128], Nm[:, h, :], Tn[:, h, :])
                    y2 = Y2p.get()
                    _copy(g, "dve", y2[:], v3(psY2[:, 0:W]))
                    yield
                psZ = _psum(g)
                for h in range(NH):
                    _mm(g, psZ[:, h * 128:(h + 1) * 128], Tn[:, h, :], TT[:, h, 0:128])
                _tt(g, "dve", TT2[:, :, 128:256], TT[:, :, 128:256], v3(psZ[:, 0:W]), ALU.subtract)
                yield
                if not last:
                    psZ2 = _psum(g)
                    for h in range(NH):
                        _mm(g, psZ2[:, h * 128:(h + 1) * 128], TT[:, h, 128:256], y2[:, h, :])
                    Tn2 = Tnp.get()
                    _tt(g, "dve", Tn2[:], Tn[:], v3(psZ2[:, 0:W]), ALU.subtract)
                    Tn = Tn2
                    yield
                TT = TT2
            nq = TT
            ps = _psum(g)
            for h in range(NH):
                _mm(g, ps[:, h * 128:(h + 1) * 128], nq[:, h, 128:256], vb[:, h * 128:(h + 1) * 128])
            _copy(g, "act", u[:], ps[:, 0:W])
            yield
            ps = _psum(g)
            for h in range(NH):
                hp, hb = h // 2, (h % 2) * 64
                _mm(g, ps[hb:hb + 64, hp * 128:(hp + 1) * 128], kbg[:, h, :], nq[:, h, 128:256])
            for (lo, hi, par) in halves:
                _copy(g, "act", wT[lo:hi, par::2, :], v3(ps[lo:hi, 0:128 * nhp]))
            yield
            if z == 1 and want_out:
                ps = _psum(g)
                _mmk(g, ps[:, 0:W], [(at[k], wg[:, k, :]) for k in range(KC)])
                _act(g, G2[:], ps[:, 0:W], AF.Silu)
                _tt(g, "pool", v3(G2[:]), v3(G2[:]), onorm[:].unsqueeze(1).to_broadcast([128, NH, 128]), ALU.mult)
                yield
            pso = bank_o
            for c in ((0, 1) if z == 0 else (1, 0)):
                cb = c * 64
                psw = _psum(g)
                for h in range(NH):
                    _mm(g, psw[cb:cb + 64, h * 128:(h + 1) * 128], wT[:, h, cb:cb + 64], Sbf[:, h // 2, :])
                _tt(g, "dve", vnew[cb:cb + 64, :], u[cb:cb + 64, :], psw[cb:cb + 64, 0:W], ALU.subtract)
                yield
                psS = bank_s
                for h in range(NH):
                    hp, hb = h // 2, (h % 2) * 64
                    if want_out:
                        _mm(g, pso[cb:cb + 64, h * 128:(h + 1) * 128], qdecT[:, h, cb:cb + 64], Sbf[:, hp, :], start=True, stop=False)
                        _mm(g, pso[cb:cb + 64, h * 128:(h + 1) * 128], qkm[:, h, cb:cb + 64], vnew[:, h * 128:(h + 1) * 128], start=False, stop=True)
                    _mm(g, psS[hb:hb + 64, hp * 128:(hp + 1) * 128], kend2[:, c, h * 64:(h + 1) * 64], vnew[:, h * 128:(h + 1) * 128])
                for hp in range(nhp):
                    _stt(g, "dve", S[:, hp, :], S[:, hp, :], ge[:, hp, c:c + 1], psS[:, hp * 128:(hp + 1) * 128], ALU.mult, ALU.add)
                _copy(g, "pool", Sbf[:], S[:])
                yield
            if not want_out:
                continue
            if z == 0:
                _copy(g, "act", Of[:, t, :], pso[:, 0:W])
                yield
            else:
                _tt(g, "dve", ob[:], pso[:, 0:W], Of[:, t, :], ALU.add)
                _tt(g, "pool", sqo[:], ob[:], ob[:], ALU.mult)
                P.op("dve", lambda e: e.tensor_reduce(rs, v3(sqo[:]), axis=AX, op=ALU.add), [sqo[:]], [rs])
                _rstd_from_sum(g, rs, rs, 1.0 / 128, g.eps[:])
                yield
                _tt(g, "dve", v3(ob[:]), v3(ob[:]), rs.unsqueeze(2).to_broadcast([128, NH, 128]), ALU.mult)
                _tt(g, "pool", mtok[:], ob[:], G2[:], ALU.mult)
                for h in range(NH):
                    _tr(g, g.psbf[:, h * 128:(h + 1) * 128], mtok[:, h * 128:(h + 1) * 128], g.ident_bf[:])
                m_ = mt.get()
                _copy(g, "act", m_[:], v3(g.psbf[:, 0:W]))
                P.dma(g.dr["MIX"].rearrange("k p t -> p k t")[:, 2 * q0:2 * q0 + NH, ts], m_[:])
                yield


def make_in_maps(inputs):
    cst, rope, bmask = host_consts()
    maps = []
    for b in range(8):
        m = {
            "x": np.ascontiguousarray(inputs["x"][b], dtype=np.float32),
            "ctx": np.ascontiguousarray(inputs["ctx"][b], dtype=np.float32),
            "cc": np.ascontiguousarray(np.concatenate([np.asarray(inputs["c"][b]).reshape(8, 128),
                                                       np.asarray(inputs["c_ctx"]).reshape(8, 128)], 0), dtype=np.float32),
            "cst": cst, "rope": rope, "bmask": bmask,
        }
        for n in WEIGHT_NAMES:
            m[n] = np.ascontiguousarray(inputs[n], dtype=np.float32)
        maps.append(m)
    return maps


_PROG_CACHE = {}


def kernel(**inputs):
    if "nc" not in _PROG_CACHE:
        import os
        _PROG_CACHE["nc"] = build_program(n_layers=int(os.environ.get("K_LAYERS", DEPTH)))[0]
    nc = _PROG_CACHE["nc"]
    in_maps = make_in_maps(inputs)
    res = run_bass_kernel_spmd(nc, in_maps, core_ids=list(range(8)))
    out = np.stack([np.asarray(res.results[b]["out"], dtype=np.float32) for b in range(8)], 0)
    return out
```
